# Optimizing a Trainium2 kernel written in Bass

```python
import jax, jax.numpy as jnp
from jax import lax
import numpy as np

D_MODEL = 1024
BATCH = 2
SEQ = 8192
DEPTH = 1

MIX_WIDTH = D_MODEL
HEAD_DIM = 64
ATTN_WIDTH = MIX_WIDTH // 2
N_Q_HEADS = ATTN_WIDTH // HEAD_DIM
N_KV_HEADS = 2
KV_WIDTH = N_KV_HEADS * HEAD_DIM
TOK_WIDTH = MIX_WIDTH - ATTN_WIDTH
N_TOK_HEADS = TOK_WIDTH // HEAD_DIM
N_IN = ATTN_WIDTH + 2 * KV_WIDTH + 2 * TOK_WIDTH
CHUNK = 128
Q_BLOCK = 128
GRID_W = 64
ROPE_THETA = 10000.0
D_FF = 2816
CONV_W = 3
N_MOD = 6
EPS = 1e-6

kernel_name = "hybrid_attn_gmlp_convffn_encoder_layer"


def rmsnorm(x, g):
    xf = x.astype(jnp.float32)
    y = xf * lax.rsqrt(jnp.mean(xf * xf, axis=-1, keepdims=True) + EPS)
    return (y * g.astype(jnp.float32)).astype(x.dtype)


def axial_rope_tables(seq_len):
    rows = seq_len // GRID_W
    row_id = jnp.broadcast_to(jnp.arange(rows)[:, None], (rows, GRID_W)).reshape(-1).astype(jnp.float32)
    col_id = jnp.broadcast_to(jnp.arange(GRID_W)[None, :], (rows, GRID_W)).reshape(-1).astype(jnp.float32)
    axis_dim = HEAD_DIM // 2
    inv_freq = jnp.power(jnp.float32(ROPE_THETA), -jnp.arange(0, axis_dim, 2, dtype=jnp.float32) / axis_dim)
    ang_r = row_id[:, None] * inv_freq[None, :]
    ang_c = col_id[:, None] * inv_freq[None, :]
    ang = jnp.concatenate([ang_r, ang_r, ang_c, ang_c], axis=-1)
    return jnp.cos(ang), jnp.sin(ang)


def rotate_half(a):
    a1, a2 = jnp.split(a, 2, axis=-1)
    return jnp.concatenate([-a2, a1], axis=-1)


def apply_axial_rope(x, cos, sin):
    xr, xc = jnp.split(x, 2, axis=-1)
    rot = jnp.concatenate([rotate_half(xr), rotate_half(xc)], axis=-1)
    return x * cos + rot * sin


def block_attention(q, k, v):
    B, S = q.shape[0], q.shape[1]
    groups = N_Q_HEADS // N_KV_HEADS
    nb = S // Q_BLOCK
    qb = q.reshape(B, nb, Q_BLOCK, N_KV_HEADS, groups, HEAD_DIM).transpose(1, 0, 2, 3, 4, 5)
    scale = jnp.float32(HEAD_DIM ** -0.5)

    def one_block(qi):
        s = jnp.einsum('bqkgd,bskd->bkgqs', qi, k, preferred_element_type=jnp.float32) * scale
        p = jax.nn.softmax(s, axis=-1)
        return jnp.einsum('bkgqs,bskd->bqkgd', p.astype(v.dtype), v)

    ob = lax.map(one_block, qb)
    return ob.transpose(1, 0, 2, 3, 4, 5).reshape(B, S, ATTN_WIDTH)


def chunked_token_mlp(z, g_v, w_s, b_s):
    B, S = z.shape[0], z.shape[1]
    nc = S // CHUNK
    z = jax.nn.gelu(z)
    u, v = jnp.split(z, 2, axis=-1)
    v = rmsnorm(v.reshape(B, S, N_TOK_HEADS, HEAD_DIM), g_v)
    v = v.reshape(B, nc, CHUNK, N_TOK_HEADS, HEAD_DIM)
    mixed = jnp.einsum('hpq,bcqhd->bcphd', w_s, v) + b_s.T[None, None, :, :, None]
    out = u.reshape(B, nc, CHUNK, N_TOK_HEADS, HEAD_DIM) * mixed
    return out.reshape(B, S, TOK_WIDTH)


def conv_glu_ffn(h, w_up, w_conv, b_conv, w_down):
    z = h @ w_up
    zp = jnp.pad(z, ((0, 0), (1, 1), (0, 0)))
    z = zp[:, :-2] * w_conv[0] + zp[:, 1:-1] * w_conv[1] + zp[:, 2:] * w_conv[2] + b_conv
    a, b = jnp.split(z, 2, axis=-1)
    return (jax.nn.silu(a) * b) @ w_down


def setup_inputs(seed: int = 0) -> dict:
    key = jax.random.key(seed)
    ks = jax.random.split(key, 24)
    f32 = jnp.float32
    L, D = DEPTH, D_MODEL

    def nrm(k, shape, scale):
        return jax.random.normal(k, shape, f32) * scale

    def gain(k, shape):
        return 1.0 + 0.05 * jax.random.normal(k, shape, f32)

    return {
        "x": nrm(ks[0], (BATCH, SEQ, D), 1.0),
        "c": nrm(ks[1], (BATCH, D), 1.0),
        "w_ada": nrm(ks[2], (L, D, N_MOD * D), 0.5 * D ** -0.5),
        "b_ada": nrm(ks[3], (L, N_MOD * D), 0.01),
        "g_norm1": gain(ks[4], (L, D)),
        "w_in": nrm(ks[5], (L, D, N_IN), D ** -0.5),
        "g_q": gain(ks[6], (L, HEAD_DIM)),
        "g_k": gain(ks[7], (L, HEAD_DIM)),
        "g_tok_v": gain(ks[8], (L, HEAD_DIM)),
        "w_s": nrm(ks[9], (L, N_TOK_HEADS, CHUNK, CHUNK), CHUNK ** -0.5),
        "b_s": 1.0 + nrm(ks[10], (L, N_TOK_HEADS, CHUNK), 0.01),
        "g_attn_out": gain(ks[11], (L, ATTN_WIDTH)),
        "g_tok_out": gain(ks[12], (L, TOK_WIDTH)),
        "w_out": nrm(ks[13], (L, MIX_WIDTH, D), MIX_WIDTH ** -0.5),
        "g_norm2": gain(ks[14], (L, D)),
        "w_up": nrm(ks[15], (L, D, 2 * D_FF), D ** -0.5),
        "w_conv": nrm(ks[16], (L, CONV_W, 2 * D_FF), CONV_W ** -0.5),
        "b_conv": nrm(ks[17], (L, 2 * D_FF), 0.01),
        "w_down": nrm(ks[18], (L, D_FF, D), D_FF ** -0.5),
        "g_final": gain(ks[19], (D,)),
    }


def reference(x, c, w_ada, b_ada, g_norm1, w_in, g_q, g_k, g_tok_v, w_s, b_s,
              g_attn_out, g_tok_out, w_out, g_norm2, w_up, w_conv, b_conv, w_down, g_final):
    B, S, _ = x.shape
    cos, sin = axial_rope_tables(S)
    cos = cos.astype(x.dtype)[None, :, None, :]
    sin = sin.astype(x.dtype)[None, :, None, :]
    mod_all = jnp.einsum('bd,ldm->lbm', jax.nn.silu(c), w_ada) + b_ada[:, None, :]

    for l in range(DEPTH):
        sh1, sc1, gt1, sh2, sc2, gt2 = jnp.split(mod_all[l][:, None, :], N_MOD, axis=-1)

        h = rmsnorm(x, g_norm1[l]) * (1.0 + sc1) + sh1
        proj = h @ w_in[l]
        q, k, v, z = jnp.split(
            proj, [ATTN_WIDTH, ATTN_WIDTH + KV_WIDTH, ATTN_WIDTH + 2 * KV_WIDTH], axis=-1)
        q = rmsnorm(q.reshape(B, S, N_Q_HEADS, HEAD_DIM), g_q[l])
        k = rmsnorm(k.reshape(B, S, N_KV_HEADS, HEAD_DIM), g_k[l])
        q = apply_axial_rope(q, cos, sin)
        k = apply_axial_rope(k, cos, sin)
        v = v.reshape(B, S, N_KV_HEADS, HEAD_DIM)
        attn = block_attention(q, k, v)
        tok = chunked_token_mlp(z, g_tok_v[l], w_s[l], b_s[l])
        mixed = jnp.concatenate(
            [rmsnorm(attn, g_attn_out[l]), rmsnorm(tok, g_tok_out[l])], axis=-1)
        x = x + gt1 * (mixed @ w_out[l])

        h = rmsnorm(x, g_norm2[l]) * (1.0 + sc2) + sh2
        x = x + gt2 * conv_glu_ffn(h, w_up[l], w_conv[l], b_conv[l], w_down[l])

    return rmsnorm(x, g_final)
```

```python
import numpy as np
import concourse.bass as bass
import concourse.mybir as mybir
from concourse.bass_utils import run_bass_kernel_spmd
from contextlib import ExitStack

F32 = mybir.dt.float32
BF16 = mybir.dt.bfloat16
AF = mybir.ActivationFunctionType
ALU = mybir.AluOpType
AX = mybir.AxisListType

ENGS = ("sync", "scalar", "vector", "gpsimd", "tensor")
D = 1024
S = 8192
NT = 64
NX = 18
NCOLS = NX * 128
DFF = 2816
NCT = 22
EPS = 1e-6
AB_TILES = NT


class _Stop(Exception):
    pass


class Prog:
    def __init__(self, nc, stack):
        self.nc = nc
        self.stack = stack
        self.sem = {}
        self.cnt = {}
        self.waited = {e: {} for e in ENGS}
        self.lastw = {}
        self.readers = {}
        self.ninst = 0
        self.nwait = 0

    def getsem(self, name):
        if name not in self.sem:
            self.sem[name] = self.stack.enter_context(self.nc.semaphore(name))
            self.cnt[name] = 0
        return self.sem[name]

    def _wait(self, eng, tok):
        s, v = tok
        if self.waited[eng].get(s, 0) >= v:
            return
        self.waited[eng][s] = v
        getattr(self.nc, eng).wait_ge(self.sem[s], v)
        self.nwait += 1

    def op(self, eng, fn, reads=(), writes=(), dma=None, inc=True):
        deps = []
        for r in reads:
            if r in self.lastw:
                deps.append(self.lastw[r])
        for w in writes:
            if w in self.lastw:
                deps.append(self.lastw[w])
            for s, v in self.readers.get(w, {}).items():
                deps.append((s, v))
        for t in deps:
            if eng == "tensor" and t[0] == "e_tensor":
                continue
            self._wait(eng, t)
        e = getattr(self.nc, eng)
        tok = None
        if dma is not None:
            self.getsem(dma)
            self.cnt[dma] += 16
            tok = (dma, self.cnt[dma])
            fn(e).then_inc(self.sem[dma], 16)
        elif inc:
            sn = "e_" + eng
            self.getsem(sn)
            self.cnt[sn] += 1
            tok = (sn, self.cnt[sn])
            fn(e).then_inc(self.sem[sn], 1)
        else:
            fn(e)
        self.ninst += 1
        if tok is not None:
            for r in reads:
                d = self.readers.setdefault(r, {})
                d[tok[0]] = max(d.get(tok[0], 0), tok[1])
            for w in writes:
                self.lastw[w] = tok
                self.readers[w] = {}
        return tok

    def barrier(self):
        for eng in ENGS:
            for s, v in self.cnt.items():
                if v > 0:
                    self._wait(eng, (s, v))
        self.lastw = {}
        self.readers = {}


def build_nc(debug_stage=0):
    nc = bass.Bass("TRN2", target_bir_lowering=False)
    di = lambda name, shape: nc.dram_tensor(name, shape, F32, kind="ExternalInput").ap()
    x_rot = di("x_rot", [S, D])
    cos_t = di("cos_t", [128, NT, 64])
    sin_t = di("sin_t", [128, NT, 64])
    cb = di("cb", [128, 8])
    hmask = di("hmask", [128, 2])
    w_ada = di("w_ada", [D, 6 * D])
    b_ada = di("b_ada", [1, 6 * D])
    g1_bd = di("g1_b", [128, D])
    g2_bd = di("g2_b", [128, D])
    gfin_bd = di("gfin_b", [128, D])
    w_in = di("w_in", [D, 1792])
    w_out = di("w_out", [D, D])
    w_up = di("w_up", [D, 2 * DFF])
    w_down = di("w_down", [DFF, D])
    gq_bd = di("gq_b", [128, 64])
    gk_bd = di("gk_b", [128, 64])
    gtv_bd = di("gtv_b", [128, 64])
    wsT_d = di("wsT", [128, 8, 128])
    bsbT_d = di("bsbT", [128, 512])
    gmix_d = di("gmix", [128, 8])
    wconv_d = di("wconv", [128, 3, 44])
    bconv_d = di("bconv", [128, 44])
    ident_d = di("ident", [128, 128])
    out_d = nc.dram_tensor("out", [2048, D], F32, kind="ExternalOutput").ap()
    scr = nc.dram_tensor("scr", [2, 128, D], F32, kind="Internal").ap()
    x1s = nc.dram_tensor("x1s", [2048, D], F32, kind=("ExternalOutput" if debug_stage else "Internal")).ap()
    wub = nc.dram_tensor("wub", [NCT, 128, 8 * 2 * 128], BF16, kind="Internal").ap()
    if debug_stage:
        dbgf = nc.dram_tensor("dbgf", [128, 8192], F32, kind="ExternalOutput").ap()
        dbgb = nc.dram_tensor("dbgb", [128, 65536], BF16, kind="ExternalOutput").ap()

    try:
      with ExitStack() as st0:
        P = Prog(nc, st0)

        def dump(dst, src, res):
            tok = P.op("sync", lambda e: e.dma_start(out=dst, in_=src), reads=[res], dma="d_dbg")
            P._wait("sync", tok)

        def stage_end(k, dumps):
            if debug_stage == k:
                for (dst, src, res) in dumps():
                    dump(dst, src, res)
                print("STOP at stage", k, "inst", P.ninst, "waits", P.nwait, "cnt", P.cnt)
                raise _Stop()

        def SB(st, name, shape, dt):
            return st.enter_context(nc.sbuf_tensor("s_" + name, shape, dt))

        def PS(st, name, shape, dt):
            return st.enter_context(nc.psum_tensor("p_" + name, shape, dt))

        V = lambda fn, r=(), w=(): P.op("vector", fn, r, w)
        A = lambda fn, r=(), w=(): P.op("scalar", fn, r, w)
        G = lambda fn, r=(), w=(): P.op("gpsimd", fn, r, w)
        T = lambda fn, r=(), w=(), inc=True: P.op("tensor", fn, r, w, inc=inc)

        def LD(dst, src, res, sem=None, eng="sync"):
            return P.op(eng, lambda e: e.dma_start(out=dst, in_=src), writes=[res], dma="d_" + res)

        ident_b = SB(st0, "ident_b", [128, 128], BF16)
        ones_f = SB(st0, "ones_f", [128, 128], F32)
        epsc = SB(st0, "epsc", [128, 1], F32)
        G1_b = SB(st0, "G1_b", [128, D], F32)
        gt1_b = SB(st0, "gt1_b", [128, D], F32)
        bias_q = SB(st0, "bias_q", [128, 512], F32)
        bias_kv = SB(st0, "bias_kv", [128, 256], F32)
        bias_zv = SB(st0, "bias_zv", [128, 512], F32)
        biasU = SB(st0, "biasU", [128, 4], F32)
        sh1T = SB(st0, "sh1T", [128, 8], BF16)
        sh2T = SB(st0, "sh2T", [128, 8], BF16)
        gq_b = SB(st0, "gq_b", [128, 64], F32)
        gk_b = SB(st0, "gk_b", [128, 64], F32)
        gtv_b = SB(st0, "gtv_b", [128, 64], F32)
        gmix = SB(st0, "gmix", [128, 8], F32)
        wconv = SB(st0, "wconv", [128, 3, 44], F32)
        bconv = SB(st0, "bconv", [128, 44], F32)
        bias2 = SB(st0, "bias2", [128, 44], F32)
        hm = SB(st0, "hm", [128, 2], F32)
        ss = SB(st0, "ss", [128, NT], F32)
        lnv = SB(st0, "lnv", [128, NT], F32)
        rstd = SB(st0, "rstd", [128, NT], F32)
        sstok = SB(st0, "sstok", [128, NX], F32)
        ssatt = SB(st0, "ssatt", [128, 20], F32)
        rs_t = SB(st0, "rs_t", [128, NX], F32)
        rs_a = SB(st0, "rs_a", [128, 20], F32)
        ss2 = SB(st0, "ss2", [128, NX], F32)
        rs2 = SB(st0, "rs2", [128, NX], F32)
        ssf = SB(st0, "ssf", [128, 16], F32)
        rsf = SB(st0, "rsf", [128, 16], F32)
        sm = SB(st0, "sm", [128, 16], F32)

        LD(ident_b[:], ident_d, "ident_b", "d_c0", eng="gpsimd")
        V(lambda e: e.memset(ones_f[:], 1.0), w=["ones_f"])
        V(lambda e: e.memset(epsc[:], EPS), w=["epsc"])
        for nm, tl, src in [("gq_b", gq_b, gq_bd), ("gk_b", gk_b, gk_bd), ("gtv_b", gtv_b, gtv_bd),
                            ("gmix", gmix, gmix_d), ("wconv", wconv, wconv_d), ("bconv", bconv, bconv_d),
                            ("hm", hm, hmask)]:
            LD(tl[:], src, nm, "d_c1")
        for nm, tl in [("ss", ss), ("sstok", sstok), ("ssatt", ssatt), ("ss2", ss2), ("ssf", ssf), ("rstd", rstd), ("sm", sm)]:
            V(lambda e, tl=tl: e.memset(tl[:], 0.0), w=[nm])

        def rsqrt_cols(dst, src, lo, hi, scale, rn, wn, tmp=None):
            A(lambda e: e.activation(dst[:, lo:hi], src[:, lo:hi], AF.Ln, bias=epsc[:, 0:1], scale=scale),
              r=[rn, "epsc"], w=[wn])
            A(lambda e: e.activation(dst[:, lo:hi], dst[:, lo:hi], AF.Exp, scale=-0.5), r=[wn], w=[wn])

        with ExitStack() as st:
            wada = [SB(st, f"wada{i}", [128, 8, D], BF16) for i in range(2)]
            cbt = SB(st, "cbt", [128, 8], F32)
            scT = SB(st, "scT", [128, 8], BF16)
            brow = SB(st, "brow", [1, 6 * D], F32)
            mrow = [SB(st, f"mrow{i}", [1, D], F32) for i in range(2)]
            gtmp = SB(st, "gtmp", [128, D], F32)
            pMod = PS(st, "pMod", [128, D], F32)
            pB = PS(st, "pB", [128, D], F32)
            pX = PS(st, "pX", [128, 8], F32)
            LD(cbt[:], cb, "cbt", "d_c1")
            LD(brow[:], b_ada, "brow", "d_c1")
            A(lambda e: e.activation(scT[:], cbt[:], AF.Silu), r=["cbt"], w=["scT"])
            for m in range(6):
                wb = wada[m % 2]
                wn = f"wada{m % 2}"
                for kk in range(2):
                    P.op("gpsimd", lambda e, kk=kk, m=m, wb=wb: e.dma_start(
                        out=wb[:, 4 * kk:4 * kk + 4, :],
                        in_=w_ada[512 * kk:512 * kk + 512, m * D:(m + 1) * D].rearrange("(k p) n -> p k n", p=128)),
                        writes=[wn], dma=f"d_wada{m % 2}")
                for c2 in range(2):
                    for k in range(8):
                        T(lambda e, k=k, c2=c2, wb=wb: e.matmul(pMod[0:1, c2 * 512:(c2 + 1) * 512], scT[:, k:k + 1],
                                                                wb[:, k, c2 * 512:(c2 + 1) * 512],
                                                                start=(k == 0), stop=(k == 7)),
                          r=[wn, "scT"], w=["pMod"], inc=(k == 7))
                mr = mrow[m % 2]
                mn = f"mrow{m % 2}"
                V(lambda e, m=m, mr=mr: e.tensor_tensor(mr[:], pMod[0:1, :], brow[0:1, m * D:(m + 1) * D], ALU.add),
                  r=["pMod", "brow"], w=[mn])
                if m in (0, 3):
                    for k in range(8):
                        T(lambda e, k=k, mr=mr: e.matmul(pX[:, k:k + 1], mr[0:1, k * 128:(k + 1) * 128],
                                                         ones_f[0:1, 0:1], start=True, stop=True),
                          r=[mn, "ones_f"], w=["pX"], inc=(k == 7))
                    dst, dn = (sh1T, "sh1T") if m == 0 else (sh2T, "sh2T")
                    V(lambda e, dst=dst: e.tensor_copy(dst[:], pX[:]), r=["pX"], w=[dn])
                else:
                    for c2 in range(2):
                        T(lambda e, c2=c2, mr=mr: e.matmul(pB[:, c2 * 512:(c2 + 1) * 512], ones_f[0:1, :],
                                                           mr[0:1, c2 * 512:(c2 + 1) * 512], start=True, stop=True),
                          r=[mn, "ones_f"], w=["pB"], inc=(c2 == 1))
                    if m in (1, 4):
                        LD(gtmp[:], g1_bd if m == 1 else g2_bd, "gtmp", "d_c2")
                        if m == 1:
                            V(lambda e: e.scalar_tensor_tensor(G1_b[:], pB[:], 1.0, gtmp[:], ALU.add, ALU.mult),
                              r=["pB", "gtmp"], w=["G1_b"])
                        else:
                            V(lambda e: e.scalar_tensor_tensor(gtmp[:], pB[:], 1.0, gtmp[:], ALU.add, ALU.mult),
                              r=["pB", "gtmp"], w=["gtmp"])
                            P.op("sync", lambda e: e.dma_start(out=scr[0], in_=gtmp[:]), reads=["gtmp"], dma="d_scr")
                    elif m == 2:
                        V(lambda e: e.tensor_copy(gt1_b[:], pB[:]), r=["pB"], w=["gt1_b"])
                    else:
                        V(lambda e: e.tensor_copy(gtmp[:], pB[:]), r=["pB"], w=["gtmp"])
                        P.op("sync", lambda e: e.dma_start(out=scr[1], in_=gtmp[:]), reads=["gtmp"], dma="d_scr")
            P.barrier()
            stage_end(1, lambda: [(dbgf[:, 0:1024], G1_b[:], "G1_b"), (dbgf[:, 1024:2048], gt1_b[:], "gt1_b"), (dbgb[:, 0:8], sh1T[:], "sh1T"), (dbgb[:, 8:16], sh2T[:], "sh2T")])

        with ExitStack() as stAR, ExitStack() as stA:
            AR1 = SB(stAR, "AR1", [128, S + NT * 2 * 129], BF16)
            KT = AR1[:, 0:S]
            VVf = AR1[:, S:S + NT * 2 * 129]
            VV = VVf.rearrange("p (t g c) -> p t g c", t=NT, g=2)
            VVk = VVf.rearrange("p (t c) -> p t c", t=NT)
            QT = SB(stA, "QT", [128, 4, NCOLS], BF16)
            mixT = SB(stA, "mixT", [128, 8, NCOLS], BF16)
            V(lambda e: e.memset(AR1[:], 0.0), w=["VV", "KT"])
            V(lambda e: e.memset(mixT[:, 0:4, 0:128], 0.0), w=["mixT"])
            V(lambda e: e.memset(mixT[:, 0:4, 2176:2304], 0.0), w=["mixT"])
            if debug_stage:
                V(lambda e: e.memset(QT[:], 0.0), w=["QT"])
            V(lambda e: e.memset(VV[:, :, :, 0:1], 1.0), w=["VV"])
            V(lambda e: e.memset(VV[:, :, :, 128:129], 1.0), w=["VV"])

            with ExitStack() as st:
                Win = SB(st, "Win", [128, 8, 1792], BF16)
                wsT = SB(st, "wsTb", [128, 8, 128], BF16)
                bsbT = SB(st, "bsbT", [128, 512], F32)
                xt = [SB(st, f"xt{i}", [128, D], F32) for i in range(2)]
                junk = SB(st, "junk", [128, D], BF16)
                xn = [SB(st, f"xn{i}", [128, D], BF16) for i in range(2)]
                xnT = [SB(st, f"xnT{i}", [128, 8, 128], BF16) for i in range(2)]
                cst = [SB(st, f"cst{i}", [128, 64], F32) for i in range(2)]
                snt = [SB(st, f"snt{i}", [128, 64], F32) for i in range(2)]
                kvf = SB(st, "kvf", [128, 256], F32)
                ksq = SB(st, "ksq", [128, 128], F32)
                kra = SB(st, "kra", [128, 128], F32)
                krb = SB(st, "krb", [128, 128], F32)
                kbf = SB(st, "kbf", [128, 128], BF16)
                f1 = SB(st, "f1", [128, 512], F32)
                f2 = SB(st, "f2", [128, 512], F32)
                f3 = SB(st, "f3", [128, 512], F32)
                qbf = SB(st, "qbf", [128, 512], BF16)
                vnpad = SB(st, "vnpad", [128, 8, 128], BF16)
                uT = SB(st, "uT", [128, 512], F32)
                tk = SB(st, "tk", [128, 512], F32)
                sqt = SB(st, "sqt", [128, 512], F32)
                pT = [PS(st, f"pT{i}", [128, D], BF16) for i in range(2)]
                pKV = PS(st, "pKV", [128, 512], F32)
                pQ = PS(st, "pQ", [128, 512], F32)
                pZV = PS(st, "pZV", [128, 512], F32)
                pU = PS(st, "pU", [128, 512], F32)
                pTQ = PS(st, "pTQ", [128, D], BF16)
                pM = PS(st, "pM", [128, 512], F32)

                for kk in range(2):
                    P.op("gpsimd", lambda e, kk=kk: e.dma_start(
                        out=Win[:, 4 * kk:4 * kk + 4, :],
                        in_=w_in[512 * kk:512 * kk + 512, :].rearrange("(k p) n -> p k n", p=128)),
                        writes=["Win"], dma="d_win")
                LD(wsT[:], wsT_d, "wsT", "d_c3", eng="gpsimd")
                LD(bsbT[:], bsbT_d, "bsbT", "d_c1")
                V(lambda e: e.memset(vnpad[:], 0.0), w=["vnpad"])

                colgrp = [(0, 512, bias_q, "bias_q", 0), (512, 256, bias_kv, "bias_kv", 512),
                          (1280, 512, bias_zv, "bias_zv", 768)]
                for (c0, n, dst, dn, r0) in colgrp:
                    for k in range(8):
                        T(lambda e, k=k, c0=c0, n=n: e.matmul(pQ[0:1, 0:n], sh1T[:, k:k + 1], Win[:, k, c0:c0 + n],
                                                              start=(k == 0), stop=(k == 7)),
                          r=["Win", "sh1T"], w=["pQ"], inc=(k == 7))
                    V(lambda e, n=n, r0=r0: e.tensor_copy(tk[0:1, 0:n], pQ[0:1, 0:n]), r=["pQ"], w=["tk"])
                    T(lambda e, n=n, r0=r0: e.matmul(pZV[:, 0:n], ones_f[0:1, :], tk[0:1, 0:n],
                                                     start=True, stop=True), r=["tk", "ones_f"], w=["pZV"])
                    V(lambda e, n=n, dst=dst: e.tensor_copy(dst[:, 0:n], pZV[:, 0:n]), r=["pZV"], w=[dn])
                for c4 in range(4):
                    for k in range(8):
                        T(lambda e, k=k, c4=c4: e.matmul(pU[:, c4:c4 + 1], Win[:, k, 768 + c4 * 128:768 + (c4 + 1) * 128],
                                                         sh1T[:, k:k + 1], start=(k == 0), stop=(k == 7)),
                          r=["Win", "sh1T"], w=["pU"], inc=(k == 7 and c4 == 3))
                V(lambda e: e.tensor_copy(biasU[:], pU[:, 0:4]), r=["pU"], w=["biasU"])


                qf = SB(st, "qf", [128, 512], F32)
                zvf = SB(st, "zvf", [128, 512], F32)
                sm18 = SB(st, "sm18", [128, 18], F32)
                V(lambda e: e.memset(sm18[:], 1.0), w=["sm18"])
                print("sbuf remaining at AB:", nc.sbuf_bytes_remaining)

                def v3(a, nh):
                    return a[:, 0:nh * 64].rearrange("p (h d) -> p h d", h=nh)

                def front(t):
                    s = t % 2
                    own = t < NX
                    LD(xt[s][:], x_rot[t * 128:(t + 1) * 128, :], f"xt{s}")
                    LD(cst[s][:], cos_t[:, t, :], f"cst{s}")
                    LD(snt[s][:], sin_t[:, t, :], f"snt{s}")
                    A(lambda e: e.activation(junk[:], xt[s][:], AF.Square, accum_out=ss[:, t:t + 1]),
                      r=[f"xt{s}"], w=["junk", "ss"])
                    rsqrt_cols(rstd, ss, t, t + 1, 1.0 / D, "ss", "rstd")
                    V(lambda e: e.scalar_tensor_tensor(xn[s][:], xt[s][:], rstd[:, t:t + 1], G1_b[:], ALU.mult, ALU.mult),
                      r=[f"xt{s}", "rstd", "G1_b"], w=[f"xn{s}"])
                    for k in range(8):
                        T(lambda e, k=k: e.transpose(pT[s][:, k * 128:(k + 1) * 128], xn[s][:, k * 128:(k + 1) * 128], ident_b[:]),
                          r=[f"xn{s}", "ident_b"], w=[f"pT{s}"], inc=(k == 7))
                    A(lambda e: e.copy(xnT[s][:].rearrange("p k n -> p (k n)"), pT[s][:]), r=[f"pT{s}"], w=[f"xnT{s}"])
                    for k in range(8):
                        T(lambda e, k=k: e.matmul(pKV[:, 0:256], xnT[s][:, k, :], Win[:, k, 512:768],
                                                  start=(k == 0), stop=(k == 7)),
                          r=[f"xnT{s}", "Win"], w=["pKV"], inc=(k == 7))
                    if own:
                        for k in range(8):
                            T(lambda e, k=k: e.matmul(pQ[:, :], xnT[s][:, k, :], Win[:, k, 0:512],
                                                      start=(k == 0), stop=(k == 7)),
                              r=[f"xnT{s}", "Win"], w=["pQ"], inc=(k == 7))
                        for k in range(8):
                            T(lambda e, k=k: e.matmul(pZV[:, :], xnT[s][:, k, :], Win[:, k, 1280:1792],
                                                      start=(k == 0), stop=(k == 7)),
                              r=[f"xnT{s}", "Win"], w=["pZV"], inc=(k == 7))
                        for c4 in range(4):
                            for k in range(8):
                                T(lambda e, k=k, c4=c4: e.matmul(pU[:, c4 * 128:(c4 + 1) * 128],
                                                                 Win[:, k, 768 + c4 * 128:768 + (c4 + 1) * 128],
                                                                 xnT[s][:, k, :], start=(k == 0), stop=(k == 7)),
                                  r=[f"xnT{s}", "Win"], w=["pU"], inc=(k == 7 and c4 == 3))

                def mid(t):
                    own = t < NX
                    V(lambda e: e.tensor_tensor(kvf[:], pKV[:, 0:256], bias_kv[:], ALU.add), r=["pKV", "bias_kv"], w=["kvf"])
                    if own:
                        V(lambda e: e.tensor_tensor(zvf[:], pZV[:], bias_zv[:], ALU.add), r=["pZV", "bias_zv"], w=["zvf"])
                        A(lambda e: e.activation(zvf[:], zvf[:], AF.Gelu_apprx_tanh), r=["zvf"], w=["zvf"])
                        V(lambda e: e.tensor_tensor(qf[:], pQ[:], bias_q[:], ALU.add), r=["pQ", "bias_q"], w=["qf"])
                        for c4 in range(4):
                            A(lambda e, c4=c4: e.activation(uT[:, c4 * 128:(c4 + 1) * 128], pU[:, c4 * 128:(c4 + 1) * 128],
                                                            AF.Gelu_apprx_tanh, bias=biasU[:, c4:c4 + 1]),
                              r=["pU", "biasU"], w=["uT"])

                def rope2(src, sname, nh, ra, raname, rb, rbname, dst, dname, cs, csn, sn, snn):
                    V(lambda e: e.tensor_tensor(v3(ra, nh), v3(src, nh), cs[:].unsqueeze(1).to_broadcast([128, nh, 64]), ALU.mult),
                      r=[sname, csn], w=[raname])
                    for blk in range(4):
                        pb = blk ^ 1
                        G(lambda e, blk=blk, pb=pb: e.tensor_tensor(
                            v3(rb, nh)[:, :, blk * 16:(blk + 1) * 16], v3(src, nh)[:, :, pb * 16:(pb + 1) * 16],
                            sn[:, blk * 16:(blk + 1) * 16].unsqueeze(1).to_broadcast([128, nh, 16]), ALU.mult),
                          r=[sname, snn], w=[rbname])
                    V(lambda e: e.tensor_tensor(dst[:, 0:nh * 64], ra[:, 0:nh * 64], rb[:, 0:nh * 64], ALU.add),
                      r=[raname, rbname], w=[dname])

                def back_a(t):
                    s = t % 2
                    own = t < NX
                    G(lambda e: e.tensor_copy(VV[:, t, :, 64:128], kvf[:, 128:256].rearrange("p (g d) -> p g d", g=2)),
                      r=["kvf"], w=["VV"])
                    G(lambda e: e.tensor_tensor(ksq[:], kvf[:, 0:128], kvf[:, 0:128], ALU.mult), r=["kvf"], w=["ksq"])
                    V(lambda e: e.tensor_reduce(sm18[:, 0:2], v3(ksq, 2), axis=AX.X, op=ALU.add), r=["ksq"], w=["sm18"])
                    if own:
                        G(lambda e: e.tensor_tensor(f3[:], qf[:], qf[:], ALU.mult), r=["qf"], w=["f3"])
                        V(lambda e: e.tensor_reduce(sm18[:, 2:10], v3(f3, 8), axis=AX.X, op=ALU.add), r=["f3"], w=["sm18"])
                        G(lambda e: e.tensor_tensor(f1[:], zvf[:], zvf[:], ALU.mult), r=["zvf"], w=["f1"])
                        V(lambda e: e.tensor_reduce(sm18[:, 10:18], v3(f1, 8), axis=AX.X, op=ALU.add), r=["f1"], w=["sm18"])
                    nc_ = 18 if own else 2
                    rsqrt_cols(sm18, sm18, 0, nc_, 1.0 / 64, "sm18", "sm18")

                def back_b(t):
                    s = t % 2
                    own = t < NX
                    V(lambda e: e.tensor_tensor(v3(kra, 2), v3(kvf, 2), sm18[:, 0:2].unsqueeze(2).to_broadcast([128, 2, 64]), ALU.mult),
                      r=["kvf", "sm18"], w=["kra"])
                    V(lambda e: e.tensor_tensor(v3(kra, 2), v3(kra, 2), gk_b[:].unsqueeze(1).to_broadcast([128, 2, 64]), ALU.mult),
                      r=["kra", "gk_b"], w=["kra"])
                    rope2(kra, "kra", 2, ksq, "ksq", krb, "krb", kbf, "kbf", cst[s], f"cst{s}", snt[s], f"snt{s}")
                    T(lambda e: e.transpose(pTQ[:, 512:640], kbf[:], ident_b[:]), r=["kbf", "ident_b"], w=["pTQk"])
                    kt_copy = lambda: V(lambda e: e.tensor_copy(KT[:, t * 128:(t + 1) * 128], pTQ[:, 512:640]), r=["pTQk"], w=["KT"])
                    if not own:
                        pending.append(kt_copy)
                        return
                    kt_copy()
                    V(lambda e: e.tensor_tensor(v3(f2, 8), v3(qf, 8), sm18[:, 2:10].unsqueeze(2).to_broadcast([128, 8, 64]), ALU.mult),
                      r=["qf", "sm18"], w=["f2"])
                    V(lambda e: e.tensor_tensor(v3(f2, 8), v3(f2, 8), gq_b[:].unsqueeze(1).to_broadcast([128, 8, 64]), ALU.mult),
                      r=["f2", "gq_b"], w=["f2"])
                    rope2(f2, "f2", 8, f3, "f3", f1, "f1", qbf, "qbf", cst[s], f"cst{s}", snt[s], f"snt{s}")
                    for p in range(4):
                        T(lambda e, p=p: e.transpose(pTQ[:, p * 128:(p + 1) * 128], qbf[:, p * 128:(p + 1) * 128], ident_b[:]),
                          r=["qbf", "ident_b"], w=["pTQ"], inc=(p == 3))
                    V(lambda e: e.tensor_tensor(v3(f2, 8), v3(zvf, 8), sm18[:, 10:18].unsqueeze(2).to_broadcast([128, 8, 64]), ALU.mult),
                      r=["zvf", "sm18"], w=["f2"])
                    for sl in range(2):
                        G(lambda e, sl=sl: e.tensor_tensor(
                            vnpad[:].rearrange("p (c s) n -> p c s n", s=2)[:, :, sl, sl * 64:(sl + 1) * 64],
                            f2[:].rearrange("p (c s d) -> p c s d", s=2, d=64)[:, :, sl, :],
                            gtv_b[:].unsqueeze(1).to_broadcast([128, 4, 64]), ALU.mult),
                          r=["f2", "gtv_b"], w=["vnpad"])
                    for c4 in range(4):
                        for sl in range(2):
                            T(lambda e, c4=c4, sl=sl: e.matmul(pM[:, c4 * 128:(c4 + 1) * 128], vnpad[:, 2 * c4 + sl, :],
                                                               wsT[:, 2 * c4 + sl, :], start=(sl == 0), stop=(sl == 1)),
                              r=["vnpad", "wsT"], w=["pM"], inc=(sl == 1 and c4 == 3))
                    V(lambda e: e.tensor_copy(QT[:, :, t * 128:(t + 1) * 128], pTQ[:, 0:512].rearrange("p (a n) -> p a n", a=4)),
                      r=["pTQ"], w=["QT"])
                    V(lambda e: e.tensor_tensor(tk[:], pM[:], bsbT[:], ALU.add), r=["pM", "bsbT"], w=["tk"])
                    V(lambda e: e.tensor_tensor(tk[:], tk[:], uT[:], ALU.mult), r=["tk", "uT"], w=["tk"])
                    G(lambda e: e.tensor_tensor(sqt[:], tk[:], tk[:], ALU.mult), r=["tk"], w=["sqt"])
                    for c4 in range(4):
                        T(lambda e, c4=c4: e.matmul(pM[:, 0:1] if False else pTS[:, 0:1], sqt[:, c4 * 128:(c4 + 1) * 128], ones_f[:, 0:1],
                                                    start=(c4 == 0), stop=(c4 == 3)),
                          r=["sqt", "ones_f"], w=["pTS"], inc=(c4 == 3))
                    V(lambda e: e.tensor_tensor(mixT[:, 4:8, t * 128:(t + 1) * 128],
                                                tk[:].rearrange("p (c n) -> p c n", c=4),
                                                gmix[:, 4:8].unsqueeze(2).to_broadcast([128, 4, 128]), ALU.mult),
                      r=["tk", "gmix"], w=["mixT"])
                    V(lambda e: e.tensor_copy(sstok[:, t:t + 1], pTS[:, 0:1]), r=["pTS"], w=["sstok"])

                pTS = pKV[:, 256:512]
                pending = []
                front(0)
                mid(0)
                for t in range(AB_TILES):
                    if t + 1 < AB_TILES:
                        front(t + 1)
                    back_a(t)
                    back_b(t)
                    if t + 1 < AB_TILES:
                        mid(t + 1)
                    for f_ in pending:
                        f_()
                    pending.clear()

                rsqrt_cols(rs_t, sstok, 0, NX, 1.0 / 512, "sstok", "rs_t")
                P.barrier()
                stage_end(2, lambda: [(dbgb[:, 0:8192], KT, "KT"), (dbgb[:, 8192:17408], QT[:].rearrange("p a n -> p (a n)"), "QT"), (dbgb[:, 17408:26624], mixT[:, 4:8, :].rearrange("p a n -> p (a n)"), "mixT"), (dbgb[:, 26624:43136], VVf, "VV"), (dbgf[:, 0:18], rs_t[:, 0:18], "rs_t"), (dbgf[:, 64:128], rstd[:, :], "rstd"), (dbgf[:, 128:132], biasU[:], "biasU"), (dbgf[:, 1024:1536], bias_q[:], "bias_q"), (dbgf[:, 1536:1792], bias_kv[:], "bias_kv"), (dbgf[:, 2048:2560], bias_zv[:], "bias_zv")])

            with ExitStack() as st:
                Pb = [SB(st, f"Pb{i}", [128, 1024], BF16) for i in range(3)]
                Oa = SB(st, "Oa", [128, 512], F32)
                Ob = SB(st, "Ob", [128, 512], F32)
                rden = SB(st, "rden", [128, 512], F32)
                at = SB(st, "at", [128, 512], F32)
                sq = SB(st, "sq", [128, 512], F32)
                acc = SB(st, "acc", [128, 512], F32)
                pS = [PS(st, f"pS{i}", [128, 1024], F32) for i in range(2)]
                pOa = PS(st, "pOa", [128, 512], F32)
                pOb = PS(st, "pOb", [128, 512], F32)
                pBc = PS(st, "pBc", [128, 512], F32)
                pSA = PS(st, "pSA", [128, 512], F32)
                for ct in range(NCT):
                    for ab in range(2):
                        P.op("gpsimd", lambda e: e.dma_start(
                            out=wub[ct].rearrange("p (k a n) -> p k a n", k=8, a=2)[:, :, ab, :],
                            in_=w_up[:, ab * DFF + ct * 128:ab * DFF + (ct + 1) * 128].rearrange("(k p) n -> p k n", p=128)),
                            writes=[f"wub{ct}_{ab}"], dma="d_wub")
                QH = SB(st, "QH", [128, 4, 2], BF16)
                hs = SB(st, "hs", [128, 2], F32)

                def qk(kt, i, p, q0, qn):
                    b = pS[i % 2]
                    T(lambda e: e.matmul(b[:, 0:qn], KT[0:64, kt * 128:(kt + 1) * 128], QT[0:64, p, q0:q0 + qn],
                                         start=True, stop=True), r=["KT", "QT"], w=[f"pS{i % 2}"], inc=False)
                    T(lambda e: e.matmul(b[:, 512:512 + qn], KT[64:128, kt * 128:(kt + 1) * 128],
                                         QT[64:128, p, q0:q0 + qn], start=True, stop=True),
                      r=["KT", "QT"], w=[f"pS{i % 2}"])

                def ex(kt, i):
                    A(lambda e: e.activation(Pb[i % 3][:], pS[i % 2][:], AF.Exp, scale=0.125),
                      r=[f"pS{i % 2}"], w=[f"Pb{i % 3}"])

                def pv(kt, i, qn):
                    pb = Pb[i % 3]
                    T(lambda e: e.matmul(pOa[:, 0:qn], VVk[:, kt, 64:192], pb[:, 0:qn],
                                         start=(kt == 0), stop=(kt == NT - 1)),
                      r=[f"Pb{i % 3}", "VV"], w=["pOa"], inc=False)
                    T(lambda e: e.matmul(pOb[:, 0:qn], VV[:, kt, 1, 0:128], pb[:, 512:512 + qn],
                                         start=(kt == 0), stop=(kt == NT - 1)),
                      r=[f"Pb{i % 3}", "VV"], w=["pOa", "pOb"])

                def epi_a(qn):
                    V(lambda e: e.tensor_copy(Oa[0:65, 0:qn], pOa[0:65, 0:qn]), r=["pOa"], w=["Oa"])
                    V(lambda e: e.tensor_copy(Ob[:, 0:qn], pOb[:, 0:qn]), r=["pOb"], w=["Ob"])
                    V(lambda e: e.reciprocal(rden[64:65, 0:qn], Oa[64:65, 0:qn]), r=["Oa"], w=["rdenA"])
                    V(lambda e: e.reciprocal(rden[0:1, 0:qn], Ob[0:1, 0:qn]), r=["Ob"], w=["rdenB"])

                def epi_b(qn):
                    T(lambda e: e.matmul(pBc[:, 0:qn], ones_f[64:65, :], rden[64:65, 0:qn], start=True, stop=True),
                      r=["rdenA", "ones_f"], w=["pBc"], inc=False)
                    T(lambda e: e.matmul(pSA[:, 0:qn], ones_f[0:1, :], rden[0:1, 0:qn], start=True, stop=True),
                      r=["rdenB", "ones_f"], w=["pBc", "pSA"])
                    V(lambda e: e.tensor_tensor(at[0:64, 0:qn], Oa[0:64, 0:qn], pBc[0:64, 0:qn], ALU.mult),
                      r=["Oa", "pBc"], w=["at"])
                    V(lambda e: e.tensor_tensor(at[64:128, 0:qn], Ob[64:128, 0:qn], pSA[64:128, 0:qn], ALU.mult),
                      r=["Ob", "pSA"], w=["at"])

                def epi_main(qi, p, q0):
                    epi_b(512)
                    V(lambda e: e.tensor_scalar(mixT[:, p, q0:q0 + 512], at[:, :], gmix[:, p:p + 1], None, ALU.mult),
                      r=["at", "gmix"], w=["mixT"])
                    if p == 0:
                        G(lambda e: e.tensor_tensor(acc[:, :], at[:, :], at[:, :], ALU.mult), r=["at"], w=["acc"])
                    else:
                        G(lambda e: e.tensor_tensor(sq[:, :], at[:, :], at[:, :], ALU.mult), r=["at"], w=["sq"])
                        G(lambda e: e.tensor_tensor(acc[:, :], acc[:, :], sq[:, :], ALU.add), r=["acc", "sq"], w=["acc"])

                def epi_ss(qi):
                    for w_ in range(4):
                        T(lambda e, w_=w_: e.matmul(pSA[:, w_:w_ + 1], acc[:, w_ * 128:(w_ + 1) * 128], ones_f[:, 0:1],
                                                    start=True, stop=True), r=["acc", "ones_f"], w=["pSA"], inc=(w_ == 3))
                    V(lambda e: e.tensor_copy(ssatt[:, 1 + qi * 4:1 + qi * 4 + 4], pSA[:, 0:4]), r=["pSA"], w=["ssatt"])

                V(lambda e: e.tensor_copy(QH[:, :, 0:1], QT[:, :, 127:128]), r=["QT"], w=["QH"])
                V(lambda e: e.tensor_copy(QH[:, :, 1:2], QT[:, :, 2176:2177]), r=["QT"], w=["QH"])
                for g in range(2):
                    for kt in range(NT):
                        T(lambda e: e.matmul(pS[0][:, g * 512 + kt * 8:g * 512 + kt * 8 + 8],
                                             KT[g * 64:(g + 1) * 64, kt * 128:(kt + 1) * 128], QH[g * 64:(g + 1) * 64, :, :],
                                             start=True, stop=True), r=["KT", "QH"], w=["pS0"], inc=(kt == NT - 1))
                ex(0, 0)
                for kt in range(NT):
                    T(lambda e: e.matmul(pOa[0:65, 0:8], VV[:, kt, 0, 64:129], Pb[0][:, kt * 8:kt * 8 + 8],
                                         start=(kt == 0), stop=(kt == NT - 1)), r=["Pb0", "VV"], w=["pOa"], inc=(kt == NT - 1))
                for kt in range(NT):
                    T(lambda e: e.matmul(pOb[:, 0:8], VV[:, kt, 1, 0:128], Pb[0][:, 512 + kt * 8:512 + kt * 8 + 8],
                                         start=(kt == 0), stop=(kt == NT - 1)), r=["Pb0", "VV"], w=["pOb"], inc=(kt == NT - 1))
                epi_a(8)
                epi_b(8)
                at3 = at[:, 0:8].rearrange("p (a t) -> p a t", t=2)
                for tk_, col in ((0, 127), (1, 2176)):
                    V(lambda e: e.tensor_tensor(mixT[:, 0:4, col:col + 1], at3[:, :, tk_:tk_ + 1], gmix[:, 0:4].unsqueeze(2), ALU.mult),
                      r=["at", "gmix"], w=["mixT"])
                V(lambda e: e.tensor_tensor(sq[:, 0:8], at[:, 0:8], at[:, 0:8], ALU.mult), r=["at"], w=["sq"])
                V(lambda e: e.tensor_reduce(hs[:, 0:2], sq[:, 0:8].rearrange("p (a t) -> p t a", t=2), axis=AX.X, op=ALU.add),
                  r=["sq"], w=["hs"])
                V(lambda e: e.memset(acc[:, 0:256], 0.0), w=["acc"])
                V(lambda e: e.tensor_copy(acc[:, 127:129], hs[:, 0:2]), r=["hs"], w=["acc"])
                for w_ in range(2):
                    T(lambda e, w_=w_: e.matmul(pSA[:, w_:w_ + 1], acc[:, w_ * 128:(w_ + 1) * 128], ones_f[:, 0:1],
                                                start=True, stop=True), r=["acc", "ones_f"], w=["pSA"], inc=(w_ == 1))
                V(lambda e: e.tensor_copy(ssatt[:, 0:1], pSA[:, 0:1]), r=["pSA"], w=["ssatt"])
                V(lambda e: e.tensor_copy(ssatt[:, 17:18], pSA[:, 1:2]), r=["pSA"], w=["ssatt"])

                iters = [(qi, p) for qi in range(4) for p in range(4)]
                step = 1
                pend = None
                for (qi, p) in iters:
                    q0 = 128 + qi * 512
                    qk(0, step, p, q0, 512)
                    for kt in range(NT):
                        ex(kt, step + kt)
                        if kt + 1 < NT:
                            qk(kt + 1, step + kt + 1, p, q0, 512)
                        pv(kt, step + kt, 512)
                        if kt == 3 and pend is not None:
                            epi_main(*pend)
                        if kt == 16 and pend is not None:
                            if pend[1] == 3:
                                epi_ss(pend[0])
                            pend = None
                    step += NT
                    epi_a(512)
                    pend = (qi, p, q0)
                epi_main(*pend)
                epi_ss(3)

                rsqrt_cols(rs_a, ssatt, 0, NX, 1.0 / 512, "ssatt", "rs_a")
                P.barrier()
                stage_end(3, lambda: [(dbgb[:, 0:9216], mixT[:, 0:4, :].rearrange("p a n -> p (a n)"), "mixT"), (dbgf[:, 0:18], rs_a[:, 0:18], "rs_a")])

            x1nT = AR1[:, 0:8 * NCOLS].rearrange("p (k n) -> p k n", k=8)
            with ExitStack() as st2:
                G2_b = SB(st2, "G2_b", [128, D], F32)
                Wout = SB(st2, "Wout", [128, 8, D], BF16)
                xe = [SB(st2, f"xe{i}", [128, D], F32) for i in range(2)]
                x1h = [SB(st2, f"x1h{i}", [128, D], F32) for i in range(2)]
                t1 = SB(st2, "t1", [128, D], F32)
                x1n = [SB(st2, f"x1n{i}", [128, D], BF16) for i in range(2)]
                junk2 = SB(st2, "junk2", [128, D], BF16)
                pYa = PS(st2, "pYa", [128, D], F32)
                pYb = PS(st2, "pYb", [128, D], F32)
                pT2 = [PS(st2, f"pT2{i}", [128, D], BF16) for i in range(2)]
                LD(G2_b[:], scr[0], "G2_b", "d_c2")
                for kk in range(2):
                    P.op("gpsimd", lambda e, kk=kk: e.dma_start(
                        out=Wout[:, 4 * kk:4 * kk + 4, :],
                        in_=w_out[512 * kk:512 * kk + 512, :].rearrange("(k p) n -> p k n", p=128)),
                        writes=["Wout"], dma="d_wout")
                def out_mm(t):
                    s = t % 2
                    LD(xe[s][:], x_rot[t * 128:(t + 1) * 128, :], f"xe{s}")
                    for c2 in range(2):
                        for k in range(4):
                            T(lambda e, k=k, c2=c2: e.matmul(pYa[:, c2 * 512:(c2 + 1) * 512], mixT[:, k, t * 128:(t + 1) * 128],
                                                             Wout[:, k, c2 * 512:(c2 + 1) * 512], start=(k == 0), stop=(k == 3)),
                              r=["mixT", "Wout"], w=["pYa"], inc=(k == 3 and c2 == 1))
                    for c2 in range(2):
                        for k in range(4, 8):
                            T(lambda e, k=k, c2=c2: e.matmul(pYb[:, c2 * 512:(c2 + 1) * 512], mixT[:, k, t * 128:(t + 1) * 128],
                                                             Wout[:, k, c2 * 512:(c2 + 1) * 512], start=(k == 4), stop=(k == 7)),
                              r=["mixT", "Wout"], w=["pYb"], inc=(k == 7 and c2 == 1))

                def out_ep1(t):
                    s = t % 2
                    xd = x1h[s][:]
                    xdn = f"x1h{s}"
                    A(lambda e: e.activation(t1[:], pYa[:], AF.Identity, scale=rs_a[:, t:t + 1]), r=["pYa", "rs_a"], w=["t1"])
                    V(lambda e: e.scalar_tensor_tensor(t1[:], pYb[:], rs_t[:, t:t + 1], t1[:], ALU.mult, ALU.add),
                      r=["pYb", "rs_t", "t1"], w=["t1"])

                def out_ep2(t):
                    s = t % 2
                    xd = x1h[s][:]
                    xdn = f"x1h{s}"
                    V(lambda e: e.tensor_tensor(t1[:], t1[:], gt1_b[:], ALU.mult), r=["t1", "gt1_b"], w=["t1"])
                    V(lambda e: e.tensor_tensor(xd, t1[:], xe[s][:], ALU.add), r=["t1", f"xe{s}"], w=[xdn])
                    if 1 <= t <= 16:
                        P.op("sync", lambda e: e.dma_start(out=x1s[(t - 1) * 128:t * 128, :], in_=xd), reads=[xdn], dma=f"d_x1s{s}")
                    A(lambda e: e.activation(junk2[:], xd, AF.Square, accum_out=ss2[:, t:t + 1]), r=[xdn], w=["junk2", "ss2"])
                    rsqrt_cols(rs2, ss2, t, t + 1, 1.0 / D, "ss2", "rs2")
                    V(lambda e: e.scalar_tensor_tensor(x1n[s][:], xd, rs2[:, t:t + 1], G2_b[:], ALU.mult, ALU.mult),
                      r=[xdn, "rs2", "G2_b"], w=[f"x1n{s}"])
                    for k in range(8):
                        T(lambda e, k=k: e.transpose(pT2[s][:, k * 128:(k + 1) * 128], x1n[s][:, k * 128:(k + 1) * 128], ident_b[:]),
                          r=[f"x1n{s}", "ident_b"], w=[f"pT2{s}"], inc=(k == 7))
                    V(lambda e: e.tensor_copy(x1nT[:, :, t * 128:(t + 1) * 128], pT2[s][:].rearrange("p (k n) -> p k n", k=8)),
                      r=[f"pT2{s}"], w=["x1nT"])

                out_mm(0)
                for t in range(NX):
                    out_ep1(t)
                    if t + 1 < NX:
                        out_mm(t + 1)
                    out_ep2(t)
                P.barrier()
                stage_end(4, lambda: [(dbgb[:, 0:18432], AR1[:, 0:18432], "x1nT")])
                print("sbuf remaining at OUT:", nc.sbuf_bytes_remaining)

            stA.close()
            with ExitStack() as st2:
                NWU = 2
                GW = 2
                groups = [(c, min(GW, NCT - c)) for c in range(0, NCT, GW)]
                NG = len(groups)
                gt2_b = SB(st2, "gt2_b", [128, D], F32)
                gfin_b = SB(st2, "gfin_b", [128, D], F32)
                gT = SB(st2, "gT", [128, NCT, 512], BF16)
                tail0 = 8 * NCOLS
                wu = [AR1[:, tail0:tail0 + 8 * 2 * GW * 128].rearrange("p (c f) -> p c f", c=GW),
                      SB(st2, "wu1", [128, GW, 8 * 2 * 128], BF16)]
                Wd = SB(st2, "Wd", [128, NCT, D], BF16)
                zb = [SB(st2, f"zb{i}", [128, 2, 514], F32) for i in range(2)]
                cv = [SB(st2, f"cv{i}", [128, 2, 512], F32) for i in range(2)]
                sl_ = [SB(st2, f"sl{i}", [128, 512], F32) for i in range(2)]
                y2q = SB(st2, "y2q", [128, 4, D], F32)
                ptmp = SB(st2, "ptmp", [128, 512], F32)
                xr = [SB(st2, f"xr{i}", [128, D], F32) for i in range(2)]
                junk3 = AR1[:, tail0 + 8 * 2 * GW * 128:tail0 + 8 * 2 * GW * 128 + D]
                ot = [SB(st2, f"ot{i}", [128, D], F32) for i in range(2)]
                pZ = [PS(st2, f"pZ{i}", [128, 512], F32) for i in range(4)]
                pY = [PS(st2, f"pY{i}", [128, 512], F32) for i in range(4)]
                LD(gt2_b[:], scr[1], "gt2_b")
                LD(gfin_b[:], gfin_bd, "gfin_b")
                print("sbuf remaining at FFN:", nc.sbuf_bytes_remaining)

                def ld_wu(gg):
                    c0_, n_ = groups[gg % NG]
                    s = gg % NWU
                    P.op("sync", lambda e: e.dma_start(out=wu[s][:, 0:n_, :], in_=wub[c0_:c0_ + n_].rearrange("c p f -> p c f")),
                         writes=[f"wu{s}"], dma=f"d_wu{s}")

                def wsl(qtr, ct, k, ab):
                    gi = ct // GW
                    s = (qtr * NG + gi) % NWU
                    j = ct - groups[gi][0]
                    return wu[s][:, j, :].rearrange("p (k a n) -> p k a n", k=8, a=2)[:, k, ab, :], f"wu{s}"

                def ld_wd(ct):
                    P.op("gpsimd", lambda e: e.dma_start(out=Wd[:, ct, :], in_=w_down[ct * 128:(ct + 1) * 128, :]),
                         writes=[f"Wd{ct}"], dma=f"d_Wd{ct}")

                def up_tail(qtr, ct):
                    ci = ct % 2
                    A(lambda e: e.activation(sl_[ci][:], cv[ci][:, 0, :], AF.Silu), r=[f"cv{ci}"], w=[f"sl{ci}"])
                    V(lambda e: e.tensor_tensor(gT[:, ct, :], sl_[ci][:], cv[ci][:, 1, :], ALU.mult),
                      r=[f"sl{ci}", f"cv{ci}b"], w=["gT"])
                    if qtr >= 1:
                        down_mm(ct, 0)

                def down_mm(ct, c2):
                    for tt in range(4):
                        T(lambda e, tt=tt: e.matmul(pY[tt][:, :], gT[:, ct, tt * 128:(tt + 1) * 128],
                                                    Wd[:, ct, c2 * 512:(c2 + 1) * 512],
                                                    start=(ct == 0), stop=(ct == NCT - 1)),
                          r=["gT", f"Wd{ct}"], w=[f"pY{tt}"], inc=(ct == NCT - 1 or tt == 3))

                def bias2_for(ct):
                    pb = pY[ct % 4]
                    for ab in range(2):
                        for k in range(8):
                            wl, wn = wsl(0, ct, k, ab)
                            T(lambda e, k=k, ab=ab, wl=wl: e.matmul(pb[:, ab:ab + 1], wl, sh2T[:, k:k + 1], start=(k == 0), stop=(k == 7)),
                              r=[wn, "sh2T"], w=[f"pY{ct % 4}"], inc=(k == 7 and ab == 1))
                    A(lambda e: e.copy(bias2[:, ct:ct + 1], pb[:, 0:1]), r=[f"pY{ct % 4}"], w=["bias2"])
                    A(lambda e: e.copy(bias2[:, 22 + ct:23 + ct], pb[:, 1:2]), r=[f"pY{ct % 4}"], w=["bias2"])

                def fin_tail(qtr, tt):
                    ti = qtr * 4 + tt
                    os_ = ti % 2
                    V(lambda e: e.tensor_tensor(y2q[:, tt, :], y2q[:, tt, :], xr[tt % 2][:], ALU.add),
                      r=[f"y2q{tt}", f"xr{tt % 2}"], w=[f"y2q{tt}"])
                    if tt < 2:
                        LD(xr[tt % 2][:], x1s[(ti + 2) * 128:(ti + 3) * 128, :], f"xr{tt % 2}")
                    A(lambda e: e.activation(junk3, y2q[:, tt, :], AF.Square, accum_out=ssf[:, ti:ti + 1]),
                      r=[f"y2q{tt}"], w=["junk3", "ssf"])
                    rsqrt_cols(rsf, ssf, ti, ti + 1, 1.0 / D, "ssf", "rsf")
                    V(lambda e: e.scalar_tensor_tensor(ot[os_][:], y2q[:, tt, :], rsf[:, ti:ti + 1], gfin_b[:],
                                                       ALU.mult, ALU.mult),
                      r=[f"y2q{tt}", "rsf", "gfin_b"], w=[f"ot{os_}"])
                    P.op("sync", lambda e: e.dma_start(out=out_d[ti * 128:(ti + 1) * 128, :], in_=ot[os_][:]),
                         reads=[f"ot{os_}"], dma=f"d_out{os_}")

                ld_wu(0)
                for qtr in range(4):
                    c0 = 127 + qtr * 512
                    for ct in range(NCT):
                        zi = ct % 2
                        ci = ct % 2
                        if ct % GW == 0:
                            gg = qtr * NG + ct // GW
                            if gg + 1 < 4 * NG:
                                ld_wu(gg + 1)
                        if qtr == 0 and ct == 0:
                            bias2_for(0)
                        for ab in range(2):
                            for blk in range(2):
                                pz = pZ[ab * 2 + blk]
                                for k in range(8):
                                    wl, wn = wsl(qtr, ct, k, ab)
                                    T(lambda e, k=k, wl=wl: e.matmul(
                                        pz[:, 0:257], wl,
                                        x1nT[:, k, c0 + blk * 257:c0 + (blk + 1) * 257], start=(k == 0), stop=(k == 7)),
                                      r=[wn, "x1nT"], w=[f"pZ{ab * 2 + blk}"], inc=(k == 7))
                                A(lambda e: e.activation(
                                    zb[zi][:, ab, blk * 257:(blk + 1) * 257], pz[:, 0:257], AF.Identity,
                                    bias=bias2[:, ab * 22 + ct:ab * 22 + ct + 1]),
                                  r=[f"pZ{ab * 2 + blk}", "bias2"], w=[f"zb{zi}"])
                        if qtr == 0 and ct + 1 < NCT:
                            bias2_for(ct + 1)
                        if qtr == 0:
                            ld_wd(ct)
                        if qtr > 0 and ct < 4:
                            fin_tail(qtr - 1, ct)
                        if ct == 4:
                            for tt in range(2):
                                LD(xr[tt][:], x1s[(qtr * 4 + tt) * 128:(qtr * 4 + tt + 1) * 128, :], f"xr{tt}")
                        if ct > 0:
                            up_tail(qtr, ct - 1)
                        if qtr == 0:
                            V(lambda e: e.tensor_scalar(zb[zi][:, :, 0:1], zb[zi][:, :, 0:1], hm[:, 0:1], None, ALU.mult),
                              r=[f"zb{zi}", "hm"], w=[f"zb{zi}"])
                        if qtr == 3:
                            V(lambda e: e.tensor_scalar(zb[zi][:, :, 513:514], zb[zi][:, :, 513:514], hm[:, 1:2], None, ALU.mult),
                              r=[f"zb{zi}", "hm"], w=[f"zb{zi}"])
                        for ab in range(2):
                            ch = ab * 22 + ct
                            if ab == 0:
                                A(lambda e: e.activation(cv[ci][:, ab, :], zb[zi][:, ab, 1:513], AF.Identity,
                                                         bias=bconv[:, ch:ch + 1], scale=wconv[:, 1, ch:ch + 1]),
                                  r=[f"zb{zi}", "wconv", "bconv"], w=[f"cv{ci}"])
                            else:
                                G(lambda e: e.tensor_scalar(cv[ci][:, ab, :], zb[zi][:, ab, 1:513], wconv[:, 1, ch:ch + 1],
                                                            bconv[:, ch:ch + 1], ALU.mult, ALU.add),
                                  r=[f"zb{zi}", "wconv", "bconv"], w=[f"cv{ci}b"])
                        chb = 22 + ct
                        G(lambda e: e.tensor_scalar(ptmp[:], zb[zi][:, 1, 2:514], wconv[:, 2, chb:chb + 1], None, ALU.mult),
                          r=[f"zb{zi}", "wconv"], w=["ptmp"])
                        G(lambda e: e.tensor_tensor(cv[ci][:, 1, :], cv[ci][:, 1, :], ptmp[:], ALU.add),
                          r=[f"cv{ci}b", "ptmp"], w=[f"cv{ci}b"])
                        cha = ct
                        V(lambda e: e.scalar_tensor_tensor(cv[ci][:, 0, :], zb[zi][:, 0, 0:512], wconv[:, 0, cha:cha + 1],
                                                           cv[ci][:, 0, :], ALU.mult, ALU.add),
                          r=[f"zb{zi}", "wconv", f"cv{ci}"], w=[f"cv{ci}"])
                        V(lambda e: e.scalar_tensor_tensor(cv[ci][:, 0, :], zb[zi][:, 0, 2:514], wconv[:, 2, cha:cha + 1],
                                                           cv[ci][:, 0, :], ALU.mult, ALU.add),
                          r=[f"zb{zi}", "wconv", f"cv{ci}"], w=[f"cv{ci}"])
                        V(lambda e: e.scalar_tensor_tensor(cv[ci][:, 1, :], zb[zi][:, 1, 0:512], wconv[:, 0, chb:chb + 1],
                                                           cv[ci][:, 1, :], ALU.mult, ALU.add),
                          r=[f"zb{zi}", "wconv", f"cv{ci}b"], w=[f"cv{ci}b"])
                    up_tail(qtr, NCT - 1)
                    for c2 in range(2):
                        if not (qtr >= 1 and c2 == 0):
                            for ct in range(NCT):
                                down_mm(ct, c2)
                        for tt in range(4):
                            V(lambda e, tt=tt: e.tensor_tensor(y2q[:, tt, c2 * 512:(c2 + 1) * 512], pY[tt][:, :],
                                                               gt2_b[:, c2 * 512:(c2 + 1) * 512], ALU.mult),
                              r=[f"pY{tt}", "gt2_b"], w=[f"y2q{tt}"])
                    if qtr == 3:
                        for tt in range(4):
                            fin_tail(3, tt)
                P.barrier()
        print("kernel build: inst", P.ninst, "waits", P.nwait, "sems", len(P.sem), P.cnt)
    except _Stop:
        pass
    return nc


def _prep_inputs(inp):
    x = np.asarray(inp["x"], np.float32)
    c = np.asarray(inp["c"], np.float32)
    f = lambda k: np.asarray(inp[k], np.float32)
    rows = S // 64
    row_id = np.repeat(np.arange(rows), 64).astype(np.float32)
    col_id = np.tile(np.arange(64), rows).astype(np.float32)
    inv_freq = np.power(np.float32(10000.0), -np.arange(0, 32, 2, dtype=np.float32) / np.float32(32)).astype(np.float32)
    ang_r = row_id[:, None] * inv_freq[None, :]
    ang_c = col_id[:, None] * inv_freq[None, :]
    ang = np.concatenate([ang_r, ang_r, ang_c, ang_c], axis=-1).astype(np.float32)
    cos = np.cos(ang).astype(np.float32)
    sin = np.sin(ang).astype(np.float32)
    sgn = np.concatenate([-np.ones(16), np.ones(16), -np.ones(16), np.ones(16)]).astype(np.float32)
    sinS = sin * sgn[None, :]
    perm_h = [0, 4, 1, 5, 2, 6, 3, 7]
    qcols = np.concatenate([np.arange(h * 64, (h + 1) * 64) for h in perm_h])
    w_in = f("w_in")[0].copy()
    w_in[:, 0:512] = w_in[:, qcols]
    w_out = f("w_out")[0].copy()
    w_out[0:512, :] = w_out[qcols, :]
    g_attn = f("g_attn_out")[0][qcols]
    gmixv = np.concatenate([g_attn, f("g_tok_out")[0]])
    gmix = np.ascontiguousarray(gmixv.reshape(8, 128).T)
    rep = lambda v, n=128: np.ascontiguousarray(np.broadcast_to(v[None, :], (n, v.shape[0])))
    wsT = np.ascontiguousarray(f("w_s")[0].transpose(2, 0, 1))
    b_s = f("b_s")[0]
    bsbT = np.zeros((128, 4, 128), np.float32)
    for c4 in range(4):
        bsbT[0:64, c4, :] = b_s[2 * c4][None, :]
        bsbT[64:128, c4, :] = b_s[2 * c4 + 1][None, :]
    wconv = np.ascontiguousarray(f("w_conv")[0].reshape(3, 44, 128).transpose(2, 0, 1))
    bconv = np.ascontiguousarray(f("b_conv")[0].reshape(44, 128).T)
    shared = {
        "w_ada": f("w_ada")[0], "b_ada": f("b_ada"), "g1_b": rep(f("g_norm1")[0]), "g2_b": rep(f("g_norm2")[0]),
        "gfin_b": rep(f("g_final")), "w_in": w_in, "w_out": w_out, "w_up": f("w_up")[0], "w_down": f("w_down")[0],
        "gq_b": rep(f("g_q")[0]), "gk_b": rep(f("g_k")[0]), "gtv_b": rep(f("g_tok_v")[0]), "wsT": wsT,
        "bsbT": bsbT.reshape(128, 512), "gmix": gmix, "wconv": wconv, "bconv": bconv,
        "ident": np.eye(128, dtype=np.float32),
    }
    in_maps = []
    for core in range(8):
        b, j = core // 4, core % 4
        sh = j * 2048 - 128
        xr = np.roll(x[b], -sh, axis=0)
        cr = np.roll(cos, -sh, axis=0).reshape(NT, 128, 64).transpose(1, 0, 2)
        sr = np.roll(sinS, -sh, axis=0).reshape(NT, 128, 64).transpose(1, 0, 2)
        hmk = np.ones((128, 2), np.float32)
        if j == 0:
            hmk[:, 0] = 0.0
        if j == 3:
            hmk[:, 1] = 0.0
        m = dict(shared)
        m.update({"x_rot": np.ascontiguousarray(xr), "cos_t": np.ascontiguousarray(cr), "sin_t": np.ascontiguousarray(sr),
                  "cb": np.ascontiguousarray(c[b].reshape(8, 128).T), "hmask": hmk})
        in_maps.append(m)
    return in_maps


_NC_CACHE = {}


def kernel(**inputs):
    in_maps = _prep_inputs(inputs)
    if "nc" not in _NC_CACHE:
        _NC_CACHE["nc"] = build_nc()
    nc = _NC_CACHE["nc"]
    res = run_bass_kernel_spmd(nc, in_maps, core_ids=list(range(8)))
    out = np.zeros((2, S, D), np.float32)
    for core in range(8):
        b, j = core // 4, core % 4
        out[b, j * 2048:(j + 1) * 2048, :] = res.results[core]["out"]
    return out
```

```python
import numpy as np
import concourse.bass as bass
import concourse.mybir as mybir
from concourse.bass_utils import run_bass_kernel_spmd
from contextlib import ExitStack

F32 = mybir.dt.float32
BF16 = mybir.dt.bfloat16
AF = mybir.ActivationFunctionType
ALU = mybir.AluOpType
AX = mybir.AxisListType

ENGS = ("sync", "scalar", "vector", "gpsimd", "tensor")
D = 1024
S = 8192
NT = 64
NX = 18
NCOLS = NX * 128
DFF = 2816
NCT = 22
EPS = 1e-6
AB_TILES = NT


class _Stop(Exception):
    pass


class Prog:
    def __init__(self, nc, stack):
        self.nc = nc
        self.stack = stack
        self.sem = {}
        self.cnt = {}
        self.waited = {e: {} for e in ENGS}
        self.lastw = {}
        self.readers = {}
        self.ninst = 0
        self.nwait = 0

    def getsem(self, name):
        if name not in self.sem:
            self.sem[name] = self.stack.enter_context(self.nc.semaphore(name))
            self.cnt[name] = 0
        return self.sem[name]

    def _wait(self, eng, tok):
        s, v = tok
        if self.waited[eng].get(s, 0) >= v:
            return
        self.waited[eng][s] = v
        getattr(self.nc, eng).wait_ge(self.sem[s], v)
        self.nwait += 1

    def op(self, eng, fn, reads=(), writes=(), dma=None, inc=True):
        deps = []
        for r in reads:
            if r in self.lastw:
                deps.append(self.lastw[r])
        for w in writes:
            if w in self.lastw:
                deps.append(self.lastw[w])
            for s, v in self.readers.get(w, {}).items():
                deps.append((s, v))
        for t in deps:
            if eng == "tensor" and t[0] == "e_tensor":
                continue
            self._wait(eng, t)
        e = getattr(self.nc, eng)
        tok = None
        if dma is not None:
            self.getsem(dma)
            self.cnt[dma] += 16
            tok = (dma, self.cnt[dma])
            fn(e).then_inc(self.sem[dma], 16)
        elif inc:
            sn = "e_" + eng
            self.getsem(sn)
            self.cnt[sn] += 1
            tok = (sn, self.cnt[sn])
            fn(e).then_inc(self.sem[sn], 1)
        else:
            fn(e)
        self.ninst += 1
        if tok is not None:
            for r in reads:
                d = self.readers.setdefault(r, {})
                d[tok[0]] = max(d.get(tok[0], 0), tok[1])
            for w in writes:
                self.lastw[w] = tok
                self.readers[w] = {}
        return tok

    def barrier(self):
        for eng in ENGS:
            for s, v in self.cnt.items():
                if v > 0:
                    self._wait(eng, (s, v))
        self.lastw = {}
        self.readers = {}


def build_nc(debug_stage=0):
    nc = bass.Bass("TRN2", target_bir_lowering=False)
    di = lambda name, shape: nc.dram_tensor(name, shape, F32, kind="ExternalInput").ap()
    x_rot = di("x_rot", [S, D])
    cos_t = di("cos_t", [128, NT, 64])
    sin_t = di("sin_t", [128, NT, 64])
    cb = di("cb", [128, 8])
    hmask = di("hmask", [128, 2])
    w_ada = di("w_ada", [D, 6 * D])
    b_ada = di("b_ada", [1, 6 * D])
    g1_bd = di("g1_b", [128, D])
    g2_bd = di("g2_b", [128, D])
    gfin_bd = di("gfin_b", [128, D])
    w_in = di("w_in", [D, 1792])
    w_out = di("w_out", [D, D])
    w_up = di("w_up", [D, 2 * DFF])
    w_down = di("w_down", [DFF, D])
    gq_bd = di("gq_b", [128, 64])
    gk_bd = di("gk_b", [128, 64])
    gtv_bd = di("gtv_b", [128, 64])
    wsT_d = di("wsT", [128, 8, 128])
    bsbT_d = di("bsbT", [128, 512])
    gmix_d = di("gmix", [128, 8])
    wconv_d = di("wconv", [128, 3, 44])
    bconv_d = di("bconv", [128, 44])
    ident_d = di("ident", [128, 128])
    out_d = nc.dram_tensor("out", [2048, D], F32, kind="ExternalOutput").ap()
    scr = nc.dram_tensor("scr", [2, 128, D], F32, kind="Internal").ap()
    x1s = nc.dram_tensor("x1s", [2048, D], F32, kind=("ExternalOutput" if debug_stage else "Internal")).ap()
    wub = nc.dram_tensor("wub", [NCT, 128, 8 * 2 * 128], BF16, kind="Internal").ap()
    if debug_stage:
        dbgf = nc.dram_tensor("dbgf", [128, 8192], F32, kind="ExternalOutput").ap()
        dbgb = nc.dram_tensor("dbgb", [128, 65536], BF16, kind="ExternalOutput").ap()

    try:
      with ExitStack() as st0:
        P = Prog(nc, st0)

        def dump(dst, src, res):
            tok = P.op("sync", lambda e: e.dma_start(out=dst, in_=src), reads=[res], dma="d_dbg")
            P._wait("sync", tok)

        def stage_end(k, dumps):
            if debug_stage == k:
                for (dst, src, res) in dumps():
                    dump(dst, src, res)
                print("STOP at stage", k, "inst", P.ninst, "waits", P.nwait, "cnt", P.cnt)
                raise _Stop()

        def SB(st, name, shape, dt):
            return st.enter_context(nc.sbuf_tensor("s_" + name, shape, dt))

        def PS(st, name, shape, dt):
            return st.enter_context(nc.psum_tensor("p_" + name, shape, dt))

        V = lambda fn, r=(), w=(): P.op("vector", fn, r, w)
        A = lambda fn, r=(), w=(): P.op("scalar", fn, r, w)
        G = lambda fn, r=(), w=(): P.op("gpsimd", fn, r, w)
        T = lambda fn, r=(), w=(), inc=True: P.op("tensor", fn, r, w, inc=inc)

        def LD(dst, src, res, sem=None, eng="sync"):
            return P.op(eng, lambda e: e.dma_start(out=dst, in_=src), writes=[res], dma="d_" + res)

        ident_b = SB(st0, "ident_b", [128, 128], BF16)
        ones_f = SB(st0, "ones_f", [128, 128], F32)
        epsc = SB(st0, "epsc", [128, 1], F32)
        G1_b = SB(st0, "G1_b", [128, D], F32)
        gt1_b = SB(st0, "gt1_b", [128, D], F32)
        bias_q = SB(st0, "bias_q", [128, 512], F32)
        bias_kv = SB(st0, "bias_kv", [128, 256], F32)
        bias_zv = SB(st0, "bias_zv", [128, 512], F32)
        biasU = SB(st0, "biasU", [128, 4], F32)
        sh1T = SB(st0, "sh1T", [128, 8], BF16)
        sh2T = SB(st0, "sh2T", [128, 8], BF16)
        gq_b = SB(st0, "gq_b", [128, 64], F32)
        gk_b = SB(st0, "gk_b", [128, 64], F32)
        gtv_b = SB(st0, "gtv_b", [128, 64], F32)
        gmix = SB(st0, "gmix", [128, 8], F32)
        wconv = SB(st0, "wconv", [128, 3, 44], F32)
        bconv = SB(st0, "bconv", [128, 44], F32)
        bias2 = SB(st0, "bias2", [128, 44], F32)
        hm = SB(st0, "hm", [128, 2], F32)
        ss = SB(st0, "ss", [128, NT], F32)
        lnv = SB(st0, "lnv", [128, NT], F32)
        rstd = SB(st0, "rstd", [128, NT], F32)
        sstok = SB(st0, "sstok", [128, NX], F32)
        ssatt = SB(st0, "ssatt", [128, 20], F32)
        rs_t = SB(st0, "rs_t", [128, NX], F32)
        rs_a = SB(st0, "rs_a", [128, 20], F32)
        ss2 = SB(st0, "ss2", [128, NX], F32)
        rs2 = SB(st0, "rs2", [128, NX], F32)
        ssf = SB(st0, "ssf", [128, 16], F32)
        rsf = SB(st0, "rsf", [128, 16], F32)
        sm = SB(st0, "sm", [128, 16], F32)

        LD(ident_b[:], ident_d, "ident_b", "d_c0", eng="gpsimd")
        V(lambda e: e.memset(ones_f[:], 1.0), w=["ones_f"])
        V(lambda e: e.memset(epsc[:], EPS), w=["epsc"])
        for nm, tl, src in [("gq_b", gq_b, gq_bd), ("gk_b", gk_b, gk_bd), ("gtv_b", gtv_b, gtv_bd),
                            ("gmix", gmix, gmix_d), ("wconv", wconv, wconv_d), ("bconv", bconv, bconv_d),
                            ("hm", hm, hmask)]:
            LD(tl[:], src, nm, "d_c1")
        for nm, tl in [("ss", ss), ("sstok", sstok), ("ssatt", ssatt), ("ss2", ss2), ("ssf", ssf), ("rstd", rstd), ("sm", sm)]:
            V(lambda e, tl=tl: e.memset(tl[:], 0.0), w=[nm])

        def rsqrt_cols(dst, src, lo, hi, scale, rn, wn, tmp=None):
            A(lambda e: e.activation(dst[:, lo:hi], src[:, lo:hi], AF.Ln, bias=epsc[:, 0:1], scale=scale),
              r=[rn, "epsc"], w=[wn])
            A(lambda e: e.activation(dst[:, lo:hi], dst[:, lo:hi], AF.Exp, scale=-0.5), r=[wn], w=[wn])

        with ExitStack() as st:
            wada = [SB(st, f"wada{i}", [128, 8, D], BF16) for i in range(2)]
            cbt = SB(st, "cbt", [128, 8], F32)
            scT = SB(st, "scT", [128, 8], BF16)
            brow = SB(st, "brow", [1, 6 * D], F32)
            mrow = [SB(st, f"mrow{i}", [1, D], F32) for i in range(2)]
            gtmp = SB(st, "gtmp", [128, D], F32)
            pMod = PS(st, "pMod", [128, D], F32)
            pB = PS(st, "pB", [128, D], F32)
            pX = PS(st, "pX", [128, 8], F32)
            LD(cbt[:], cb, "cbt", "d_c1")
            LD(brow[:], b_ada, "brow", "d_c1")
            A(lambda e: e.activation(scT[:], cbt[:], AF.Silu), r=["cbt"], w=["scT"])
            for m in range(6):
                wb = wada[m % 2]
                wn = f"wada{m % 2}"
                for kk in range(2):
                    P.op("gpsimd", lambda e, kk=kk, m=m, wb=wb: e.dma_start(
                        out=wb[:, 4 * kk:4 * kk + 4, :],
                        in_=w_ada[512 * kk:512 * kk + 512, m * D:(m + 1) * D].rearrange("(k p) n -> p k n", p=128)),
                        writes=[wn], dma=f"d_wada{m % 2}")
                for c2 in range(2):
                    for k in range(8):
                        T(lambda e, k=k, c2=c2, wb=wb: e.matmul(pMod[0:1, c2 * 512:(c2 + 1) * 512], scT[:, k:k + 1],
                                                                wb[:, k, c2 * 512:(c2 + 1) * 512],
                                                                start=(k == 0), stop=(k == 7)),
                          r=[wn, "scT"], w=["pMod"], inc=(k == 7))
                mr = mrow[m % 2]
                mn = f"mrow{m % 2}"
                V(lambda e, m=m, mr=mr: e.tensor_tensor(mr[:], pMod[0:1, :], brow[0:1, m * D:(m + 1) * D], ALU.add),
                  r=["pMod", "brow"], w=[mn])
                if m in (0, 3):
                    for k in range(8):
                        T(lambda e, k=k, mr=mr: e.matmul(pX[:, k:k + 1], mr[0:1, k * 128:(k + 1) * 128],
                                                         ones_f[0:1, 0:1], start=True, stop=True),
                          r=[mn, "ones_f"], w=["pX"], inc=(k == 7))
                    dst, dn = (sh1T, "sh1T") if m == 0 else (sh2T, "sh2T")
                    V(lambda e, dst=dst: e.tensor_copy(dst[:], pX[:]), r=["pX"], w=[dn])
                else:
                    for c2 in range(2):
                        T(lambda e, c2=c2, mr=mr: e.matmul(pB[:, c2 * 512:(c2 + 1) * 512], ones_f[0:1, :],
                                                           mr[0:1, c2 * 512:(c2 + 1) * 512], start=True, stop=True),
                          r=[mn, "ones_f"], w=["pB"], inc=(c2 == 1))
                    if m in (1, 4):
                        LD(gtmp[:], g1_bd if m == 1 else g2_bd, "gtmp", "d_c2")
                        if m == 1:
                            V(lambda e: e.scalar_tensor_tensor(G1_b[:], pB[:], 1.0, gtmp[:], ALU.add, ALU.mult),
                              r=["pB", "gtmp"], w=["G1_b"])
                        else:
                            V(lambda e: e.scalar_tensor_tensor(gtmp[:], pB[:], 1.0, gtmp[:], ALU.add, ALU.mult),
                              r=["pB", "gtmp"], w=["gtmp"])
                            P.op("sync", lambda e: e.dma_start(out=scr[0], in_=gtmp[:]), reads=["gtmp"], dma="d_scr")
                    elif m == 2:
                        V(lambda e: e.tensor_copy(gt1_b[:], pB[:]), r=["pB"], w=["gt1_b"])
                    else:
                        V(lambda e: e.tensor_copy(gtmp[:], pB[:]), r=["pB"], w=["gtmp"])
                        P.op("sync", lambda e: e.dma_start(out=scr[1], in_=gtmp[:]), reads=["gtmp"], dma="d_scr")
            P.barrier()
            stage_end(1, lambda: [(dbgf[:, 0:1024], G1_b[:], "G1_b"), (dbgf[:, 1024:2048], gt1_b[:], "gt1_b"), (dbgb[:, 0:8], sh1T[:], "sh1T"), (dbgb[:, 8:16], sh2T[:], "sh2T")])

        with ExitStack() as stAR, ExitStack() as stA:
            AR1 = SB(stAR, "AR1", [128, S + NT * 2 * 129], BF16)
            KT = AR1[:, 0:S]
            VVf = AR1[:, S:S + NT * 2 * 129]
            VV = VVf.rearrange("p (t g c) -> p t g c", t=NT, g=2)
            VVk = VVf.rearrange("p (t c) -> p t c", t=NT)
            QT = SB(stA, "QT", [128, 4, NCOLS], BF16)
            mixT = SB(stA, "mixT", [128, 8, NCOLS], BF16)
            V(lambda e: e.memset(AR1[:], 0.0), w=["VV", "KT"])
            V(lambda e: e.memset(mixT[:, 0:4, 0:128], 0.0), w=["mixT"])
            V(lambda e: e.memset(mixT[:, 0:4, 2176:2304], 0.0), w=["mixT"])
            if debug_stage:
                V(lambda e: e.memset(QT[:], 0.0), w=["QT"])
            V(lambda e: e.memset(VV[:, :, :, 0:1], 1.0), w=["VV"])
            V(lambda e: e.memset(VV[:, :, :, 128:129], 1.0), w=["VV"])

            with ExitStack() as st:
                Win = SB(st, "Win", [128, 8, 1792], BF16)
                wsT = SB(st, "wsTb", [128, 8, 128], BF16)
                bsbT = SB(st, "bsbT", [128, 512], F32)
                xt = [SB(st, f"xt{i}", [128, D], F32) for i in range(2)]
                junk = SB(st, "junk", [128, D], BF16)
                xn = [SB(st, f"xn{i}", [128, D], BF16) for i in range(2)]
                xnT = [SB(st, f"xnT{i}", [128, 8, 128], BF16) for i in range(2)]
                cst = [SB(st, f"cst{i}", [128, 64], F32) for i in range(2)]
                snt = [SB(st, f"snt{i}", [128, 64], F32) for i in range(2)]
                kvf = SB(st, "kvf", [128, 256], F32)
                ksq = SB(st, "ksq", [128, 128], F32)
                kra = SB(st, "kra", [128, 128], F32)
                krb = SB(st, "krb", [128, 128], F32)
                kbf = SB(st, "kbf", [128, 128], BF16)
                f1 = SB(st, "f1", [128, 512], F32)
                f2 = SB(st, "f2", [128, 512], F32)
                f3 = SB(st, "f3", [128, 512], F32)
                qbf = SB(st, "qbf", [128, 512], BF16)
                vnpad = SB(st, "vnpad", [128, 8, 128], BF16)
                uT = SB(st, "uT", [128, 512], F32)
                tk = SB(st, "tk", [128, 512], F32)
                sqt = SB(st, "sqt", [128, 512], F32)
                pT = [PS(st, f"pT{i}", [128, D], BF16) for i in range(2)]
                pKV = PS(st, "pKV", [128, 512], F32)
                pQ = PS(st, "pQ", [128, 512], F32)
                pZV = PS(st, "pZV", [128, 512], F32)
                pU = PS(st, "pU", [128, 512], F32)
                pTQ = PS(st, "pTQ", [128, D], BF16)
                pM = PS(st, "pM", [128, 512], F32)

                for kk in range(2):
                    P.op("gpsimd", lambda e, kk=kk: e.dma_start(
                        out=Win[:, 4 * kk:4 * kk + 4, :],
                        in_=w_in[512 * kk:512 * kk + 512, :].rearrange("(k p) n -> p k n", p=128)),
                        writes=["Win"], dma="d_win")
                LD(wsT[:], wsT_d, "wsT", "d_c3", eng="gpsimd")
                LD(bsbT[:], bsbT_d, "bsbT", "d_c1")
                V(lambda e: e.memset(vnpad[:], 0.0), w=["vnpad"])

                colgrp = [(0, 512, bias_q, "bias_q", 0), (512, 256, bias_kv, "bias_kv", 512),
                          (1280, 512, bias_zv, "bias_zv", 768)]
                for (c0, n, dst, dn, r0) in colgrp:
                    for k in range(8):
                        T(lambda e, k=k, c0=c0, n=n: e.matmul(pQ[0:1, 0:n], sh1T[:, k:k + 1], Win[:, k, c0:c0 + n],
                                                              start=(k == 0), stop=(k == 7)),
                          r=["Win", "sh1T"], w=["pQ"], inc=(k == 7))
                    V(lambda e, n=n, r0=r0: e.tensor_copy(tk[0:1, 0:n], pQ[0:1, 0:n]), r=["pQ"], w=["tk"])
                    T(lambda e, n=n, r0=r0: e.matmul(pZV[:, 0:n], ones_f[0:1, :], tk[0:1, 0:n],
                                                     start=True, stop=True), r=["tk", "ones_f"], w=["pZV"])
                    V(lambda e, n=n, dst=dst: e.tensor_copy(dst[:, 0:n], pZV[:, 0:n]), r=["pZV"], w=[dn])
                for c4 in range(4):
                    for k in range(8):
                        T(lambda e, k=k, c4=c4: e.matmul(pU[:, c4:c4 + 1], Win[:, k, 768 + c4 * 128:768 + (c4 + 1) * 128],
                                                         sh1T[:, k:k + 1], start=(k == 0), stop=(k == 7)),
                          r=["Win", "sh1T"], w=["pU"], inc=(k == 7 and c4 == 3))
                V(lambda e: e.tensor_copy(biasU[:], pU[:, 0:4]), r=["pU"], w=["biasU"])


                qf = SB(st, "qf", [128, 512], F32)
                zvf = SB(st, "zvf", [128, 512], F32)
                sm18 = SB(st, "sm18", [128, 18], F32)
                V(lambda e: e.memset(sm18[:], 1.0), w=["sm18"])
                print("sbuf remaining at AB:", nc.sbuf_bytes_remaining)

                def v3(a, nh):
                    return a[:, 0:nh * 64].rearrange("p (h d) -> p h d", h=nh)

                def front(t):
                    s = t % 2
                    own = t < NX
                    LD(xt[s][:], x_rot[t * 128:(t + 1) * 128, :], f"xt{s}")
                    LD(cst[s][:], cos_t[:, t, :], f"cst{s}")
                    LD(snt[s][:], sin_t[:, t, :], f"snt{s}")
                    A(lambda e: e.activation(junk[:], xt[s][:], AF.Square, accum_out=ss[:, t:t + 1]),
                      r=[f"xt{s}"], w=["junk", "ss"])
                    rsqrt_cols(rstd, ss, t, t + 1, 1.0 / D, "ss", "rstd")
                    V(lambda e: e.scalar_tensor_tensor(xn[s][:], xt[s][:], rstd[:, t:t + 1], G1_b[:], ALU.mult, ALU.mult),
                      r=[f"xt{s}", "rstd", "G1_b"], w=[f"xn{s}"])
                    for k in range(8):
                        T(lambda e, k=k: e.transpose(pT[s][:, k * 128:(k + 1) * 128], xn[s][:, k * 128:(k + 1) * 128], ident_b[:]),
                          r=[f"xn{s}", "ident_b"], w=[f"pT{s}"], inc=(k == 7))
                    A(lambda e: e.copy(xnT[s][:].rearrange("p k n -> p (k n)"), pT[s][:]), r=[f"pT{s}"], w=[f"xnT{s}"])
                    for k in range(8):
                        T(lambda e, k=k: e.matmul(pKV[:, 0:256], xnT[s][:, k, :], Win[:, k, 512:768],
                                                  start=(k == 0), stop=(k == 7)),
                          r=[f"xnT{s}", "Win"], w=["pKV"], inc=(k == 7))
                    if own:
                        for k in range(8):
                            T(lambda e, k=k: e.matmul(pQ[:, :], xnT[s][:, k, :], Win[:, k, 0:512],
                                                      start=(k == 0), stop=(k == 7)),
                              r=[f"xnT{s}", "Win"], w=["pQ"], inc=(k == 7))
                        for k in range(8):
                            T(lambda e, k=k: e.matmul(pZV[:, :], xnT[s][:, k, :], Win[:, k, 1280:1792],
                                                      start=(k == 0), stop=(k == 7)),
                              r=[f"xnT{s}", "Win"], w=["pZV"], inc=(k == 7))
                        for c4 in range(4):
                            for k in range(8):
                                T(lambda e, k=k, c4=c4: e.matmul(pU[:, c4 * 128:(c4 + 1) * 128],
                                                                 Win[:, k, 768 + c4 * 128:768 + (c4 + 1) * 128],
                                                                 xnT[s][:, k, :], start=(k == 0), stop=(k == 7)),
                                  r=[f"xnT{s}", "Win"], w=["pU"], inc=(k == 7 and c4 == 3))

                def mid(t):
                    own = t < NX
                    V(lambda e: e.tensor_tensor(kvf[:], pKV[:, 0:256], bias_kv[:], ALU.add), r=["pKV", "bias_kv"], w=["kvf"])
                    if own:
                        V(lambda e: e.tensor_tensor(zvf[:], pZV[:], bias_zv[:], ALU.add), r=["pZV", "bias_zv"], w=["zvf"])
                        A(lambda e: e.activation(zvf[:], zvf[:], AF.Gelu_apprx_tanh), r=["zvf"], w=["zvf"])
                        V(lambda e: e.tensor_tensor(qf[:], pQ[:], bias_q[:], ALU.add), r=["pQ", "bias_q"], w=["qf"])
                        for c4 in range(4):
                            A(lambda e, c4=c4: e.activation(uT[:, c4 * 128:(c4 + 1) * 128], pU[:, c4 * 128:(c4 + 1) * 128],
                                                            AF.Gelu_apprx_tanh, bias=biasU[:, c4:c4 + 1]),
                              r=["pU", "biasU"], w=["uT"])

                def rope2(src, sname, nh, ra, raname, rb, rbname, dst, dname, cs, csn, sn, snn):
                    V(lambda e: e.tensor_tensor(v3(ra, nh), v3(src, nh), cs[:].unsqueeze(1).to_broadcast([128, nh, 64]), ALU.mult),
                      r=[sname, csn], w=[raname])
                    for blk in range(4):
                        pb = blk ^ 1
                        G(lambda e, blk=blk, pb=pb: e.tensor_tensor(
                            v3(rb, nh)[:, :, blk * 16:(blk + 1) * 16], v3(src, nh)[:, :, pb * 16:(pb + 1) * 16],
                            sn[:, blk * 16:(blk + 1) * 16].unsqueeze(1).to_broadcast([128, nh, 16]), ALU.mult),
                          r=[sname, snn], w=[rbname])
                    V(lambda e: e.tensor_tensor(dst[:, 0:nh * 64], ra[:, 0:nh * 64], rb[:, 0:nh * 64], ALU.add),
                      r=[raname, rbname], w=[dname])

                def back_a(t):
                    s = t % 2
                    own = t < NX
                    G(lambda e: e.tensor_copy(VV[:, t, :, 64:128], kvf[:, 128:256].rearrange("p (g d) -> p g d", g=2)),
                      r=["kvf"], w=["VV"])
                    G(lambda e: e.tensor_tensor(ksq[:], kvf[:, 0:128], kvf[:, 0:128], ALU.mult), r=["kvf"], w=["ksq"])
                    V(lambda e: e.tensor_reduce(sm18[:, 0:2], v3(ksq, 2), axis=AX.X, op=ALU.add), r=["ksq"], w=["sm18"])
                    if own:
                        G(lambda e: e.tensor_tensor(f3[:], qf[:], qf[:], ALU.mult), r=["qf"], w=["f3"])
                        V(lambda e: e.tensor_reduce(sm18[:, 2:10], v3(f3, 8), axis=AX.X, op=ALU.add), r=["f3"], w=["sm18"])
                        G(lambda e: e.tensor_tensor(f1[:], zvf[:], zvf[:], ALU.mult), r=["zvf"], w=["f1"])
                        V(lambda e: e.tensor_reduce(sm18[:, 10:18], v3(f1, 8), axis=AX.X, op=ALU.add), r=["f1"], w=["sm18"])
                    nc_ = 18 if own else 2
                    rsqrt_cols(sm18, sm18, 0, nc_, 1.0 / 64, "sm18", "sm18")

                def back_b(t):
                    s = t % 2
                    own = t < NX
                    V(lambda e: e.tensor_tensor(v3(kra, 2), v3(kvf, 2), sm18[:, 0:2].unsqueeze(2).to_broadcast([128, 2, 64]), ALU.mult),
                      r=["kvf", "sm18"], w=["kra"])
                    V(lambda e: e.tensor_tensor(v3(kra, 2), v3(kra, 2), gk_b[:].unsqueeze(1).to_broadcast([128, 2, 64]), ALU.mult),
                      r=["kra", "gk_b"], w=["kra"])
                    rope2(kra, "kra", 2, ksq, "ksq", krb, "krb", kbf, "kbf", cst[s], f"cst{s}", snt[s], f"snt{s}")
                    T(lambda e: e.transpose(pTQ[:, 512:640], kbf[:], ident_b[:]), r=["kbf", "ident_b"], w=["pTQk"])
                    kt_copy = lambda: V(lambda e: e.tensor_copy(KT[:, t * 128:(t + 1) * 128], pTQ[:, 512:640]), r=["pTQk"], w=["KT"])
                    if not own:
                        pending.append(kt_copy)
                        return
                    kt_copy()
                    V(lambda e: e.tensor_tensor(v3(f2, 8), v3(qf, 8), sm18[:, 2:10].unsqueeze(2).to_broadcast([128, 8, 64]), ALU.mult),
                      r=["qf", "sm18"], w=["f2"])
                    V(lambda e: e.tensor_tensor(v3(f2, 8), v3(f2, 8), gq_b[:].unsqueeze(1).to_broadcast([128, 8, 64]), ALU.mult),
                      r=["f2", "gq_b"], w=["f2"])
                    rope2(f2, "f2", 8, f3, "f3", f1, "f1", qbf, "qbf", cst[s], f"cst{s}", snt[s], f"snt{s}")
                    for p in range(4):
                        T(lambda e, p=p: e.transpose(pTQ[:, p * 128:(p + 1) * 128], qbf[:, p * 128:(p + 1) * 128], ident_b[:]),
                          r=["qbf", "ident_b"], w=["pTQ"], inc=(p == 3))
                    V(lambda e: e.tensor_tensor(v3(f2, 8), v3(zvf, 8), sm18[:, 10:18].unsqueeze(2).to_broadcast([128, 8, 64]), ALU.mult),
                      r=["zvf", "sm18"], w=["f2"])
                    for sl in range(2):
                        G(lambda e, sl=sl: e.tensor_tensor(
                            vnpad[:].rearrange("p (c s) n -> p c s n", s=2)[:, :, sl, sl * 64:(sl + 1) * 64],
                            f2[:].rearrange("p (c s d) -> p c s d", s=2, d=64)[:, :, sl, :],
                            gtv_b[:].unsqueeze(1).to_broadcast([128, 4, 64]), ALU.mult),
                          r=["f2", "gtv_b"], w=["vnpad"])
                    for c4 in range(4):
                        for sl in range(2):
                            T(lambda e, c4=c4, sl=sl: e.matmul(pM[:, c4 * 128:(c4 + 1) * 128], vnpad[:, 2 * c4 + sl, :],
                                                               wsT[:, 2 * c4 + sl, :], start=(sl == 0), stop=(sl == 1)),
                              r=["vnpad", "wsT"], w=["pM"], inc=(sl == 1 and c4 == 3))
                    V(lambda e: e.tensor_copy(QT[:, :, t * 128:(t + 1) * 128], pTQ[:, 0:512].rearrange("p (a n) -> p a n", a=4)),
                      r=["pTQ"], w=["QT"])
                    V(lambda e: e.tensor_tensor(tk[:], pM[:], bsbT[:], ALU.add), r=["pM", "bsbT"], w=["tk"])
                    V(lambda e: e.tensor_tensor(tk[:], tk[:], uT[:], ALU.mult), r=["tk", "uT"], w=["tk"])
                    G(lambda e: e.tensor_tensor(sqt[:], tk[:], tk[:], ALU.mult), r=["tk"], w=["sqt"])
                    for c4 in range(4):
                        T(lambda e, c4=c4: e.matmul(pM[:, 0:1] if False else pTS[:, 0:1], sqt[:, c4 * 128:(c4 + 1) * 128], ones_f[:, 0:1],
                                                    start=(c4 == 0), stop=(c4 == 3)),
                          r=["sqt", "ones_f"], w=["pTS"], inc=(c4 == 3))
                    V(lambda e: e.tensor_tensor(mixT[:, 4:8, t * 128:(t + 1) * 128],
                                                tk[:].rearrange("p (c n) -> p c n", c=4),
                                                gmix[:, 4:8].unsqueeze(2).to_broadcast([128, 4, 128]), ALU.mult),
                      r=["tk", "gmix"], w=["mixT"])
                    V(lambda e: e.tensor_copy(sstok[:, t:t + 1], pTS[:, 0:1]), r=["pTS"], w=["sstok"])

                pTS = pKV[:, 256:512]
                pending = []
                front(0)
                mid(0)
                for t in range(AB_TILES):
                    if t + 1 < AB_TILES:
                        front(t + 1)
                    back_a(t)
                    back_b(t)
                    if t + 1 < AB_TILES:
                        mid(t + 1)
                    for f_ in pending:
                        f_()
                    pending.clear()

                rsqrt_cols(rs_t, sstok, 0, NX, 1.0 / 512, "sstok", "rs_t")
                P.barrier()
                stage_end(2, lambda: [(dbgb[:, 0:8192], KT, "KT"), (dbgb[:, 8192:17408], QT[:].rearrange("p a n -> p (a n)"), "QT"), (dbgb[:, 17408:26624], mixT[:, 4:8, :].rearrange("p a n -> p (a n)"), "mixT"), (dbgb[:, 26624:43136], VVf, "VV"), (dbgf[:, 0:18], rs_t[:, 0:18], "rs_t"), (dbgf[:, 64:128], rstd[:, :], "rstd"), (dbgf[:, 128:132], biasU[:], "biasU"), (dbgf[:, 1024:1536], bias_q[:], "bias_q"), (dbgf[:, 1536:1792], bias_kv[:], "bias_kv"), (dbgf[:, 2048:2560], bias_zv[:], "bias_zv")])

            with ExitStack() as st:
                Pb = [SB(st, f"Pb{i}", [128, 1024], BF16) for i in range(3)]
                Oa = SB(st, "Oa", [128, 512], F32)
                Ob = SB(st, "Ob", [128, 512], F32)
                rden = SB(st, "rden", [128, 512], F32)
                at = SB(st, "at", [128, 512], F32)
                sq = SB(st, "sq", [128, 512], F32)
                acc = SB(st, "acc", [128, 512], F32)
                pS = [PS(st, f"pS{i}", [128, 1024], F32) for i in range(2)]
                pOa = PS(st, "pOa", [128, 512], F32)
                pOb = PS(st, "pOb", [128, 512], F32)
                pBc = PS(st, "pBc", [128, 512], F32)
                pSA = PS(st, "pSA", [128, 512], F32)
                for ct in range(NCT):
                    for ab in range(2):
                        P.op("gpsimd", lambda e: e.dma_start(
                            out=wub[ct].rearrange("p (k a n) -> p k a n", k=8, a=2)[:, :, ab, :],
                            in_=w_up[:, ab * DFF + ct * 128:ab * DFF + (ct + 1) * 128].rearrange("(k p) n -> p k n", p=128)),
                            writes=[f"wub{ct}_{ab}"], dma="d_wub")
                QH = SB(st, "QH", [128, 4, 2], BF16)
                hs = SB(st, "hs", [128, 2], F32)

                def qk(kt, i, p, q0, qn):
                    b = pS[i % 2]
                    T(lambda e: e.matmul(b[:, 0:qn], KT[0:64, kt * 128:(kt + 1) * 128], QT[0:64, p, q0:q0 + qn],
                                         start=True, stop=True), r=["KT", "QT"], w=[f"pS{i % 2}"], inc=False)
                    T(lambda e: e.matmul(b[:, 512:512 + qn], KT[64:128, kt * 128:(kt + 1) * 128],
                                         QT[64:128, p, q0:q0 + qn], start=True, stop=True),
                      r=["KT", "QT"], w=[f"pS{i % 2}"])

                def ex(kt, i):
                    A(lambda e: e.activation(Pb[i % 3][:], pS[i % 2][:], AF.Exp, scale=0.125),
                      r=[f"pS{i % 2}"], w=[f"Pb{i % 3}"])

                def pv(kt, i, qn):
                    pb = Pb[i % 3]
                    T(lambda e: e.matmul(pOa[:, 0:qn], VVk[:, kt, 64:192], pb[:, 0:qn],
                                         start=(kt == 0), stop=(kt == NT - 1)),
                      r=[f"Pb{i % 3}", "VV"], w=["pOa"], inc=False)
                    T(lambda e: e.matmul(pOb[:, 0:qn], VV[:, kt, 1, 0:128], pb[:, 512:512 + qn],
                                         start=(kt == 0), stop=(kt == NT - 1)),
                      r=[f"Pb{i % 3}", "VV"], w=["pOa", "pOb"])

                def epi_a(qn):
                    V(lambda e: e.tensor_copy(Oa[0:65, 0:qn], pOa[0:65, 0:qn]), r=["pOa"], w=["Oa"])
                    V(lambda e: e.tensor_copy(Ob[:, 0:qn], pOb[:, 0:qn]), r=["pOb"], w=["Ob"])
                    V(lambda e: e.reciprocal(rden[64:65, 0:qn], Oa[64:65, 0:qn]), r=["Oa"], w=["rdenA"])
                    V(lambda e: e.reciprocal(rden[0:1, 0:qn], Ob[0:1, 0:qn]), r=["Ob"], w=["rdenB"])

                def epi_b(qn):
                    T(lambda e: e.matmul(pBc[:, 0:qn], ones_f[64:65, :], rden[64:65, 0:qn], start=True, stop=True),
                      r=["rdenA", "ones_f"], w=["pBc"], inc=False)
                    T(lambda e: e.matmul(pSA[:, 0:qn], ones_f[0:1, :], rden[0:1, 0:qn], start=True, stop=True),
                      r=["rdenB", "ones_f"], w=["pBc", "pSA"])
                    V(lambda e: e.tensor_tensor(at[0:64, 0:qn], Oa[0:64, 0:qn], pBc[0:64, 0:qn], ALU.mult),
                      r=["Oa", "pBc"], w=["at"])
                    V(lambda e: e.tensor_tensor(at[64:128, 0:qn], Ob[64:128, 0:qn], pSA[64:128, 0:qn], ALU.mult),
                      r=["Ob", "pSA"], w=["at"])

                def epi_main(qi, p, q0):
                    epi_b(512)
                    V(lambda e: e.tensor_scalar(mixT[:, p, q0:q0 + 512], at[:, :], gmix[:, p:p + 1], None, ALU.mult),
                      r=["at", "gmix"], w=["mixT"])
                    if p == 0:
                        G(lambda e: e.tensor_tensor(acc[:, :], at[:, :], at[:, :], ALU.mult), r=["at"], w=["acc"])
                    else:
                        G(lambda e: e.tensor_tensor(sq[:, :], at[:, :], at[:, :], ALU.mult), r=["at"], w=["sq"])
                        G(lambda e: e.tensor_tensor(acc[:, :], acc[:, :], sq[:, :], ALU.add), r=["acc", "sq"], w=["acc"])

                def epi_ss(qi):
                    for w_ in range(4):
                        T(lambda e, w_=w_: e.matmul(pSA[:, w_:w_ + 1], acc[:, w_ * 128:(w_ + 1) * 128], ones_f[:, 0:1],
                                                    start=True, stop=True), r=["acc", "ones_f"], w=["pSA"], inc=(w_ == 3))
                    V(lambda e: e.tensor_copy(ssatt[:, 1 + qi * 4:1 + qi * 4 + 4], pSA[:, 0:4]), r=["pSA"], w=["ssatt"])

                V(lambda e: e.tensor_copy(QH[:, :, 0:1], QT[:, :, 127:128]), r=["QT"], w=["QH"])
                V(lambda e: e.tensor_copy(QH[:, :, 1:2], QT[:, :, 2176:2177]), r=["QT"], w=["QH"])
                for g in range(2):
                    for kt in range(NT):
                        T(lambda e: e.matmul(pS[0][:, g * 512 + kt * 8:g * 512 + kt * 8 + 8],
                                             KT[g * 64:(g + 1) * 64, kt * 128:(kt + 1) * 128], QH[g * 64:(g + 1) * 64, :, :],
                                             start=True, stop=True), r=["KT", "QH"], w=["pS0"], inc=(kt == NT - 1))
                ex(0, 0)
                for kt in range(NT):
                    T(lambda e: e.matmul(pOa[0:65, 0:8], VV[:, kt, 0, 64:129], Pb[0][:, kt * 8:kt * 8 + 8],
                                         start=(kt == 0), stop=(kt == NT - 1)), r=["Pb0", "VV"], w=["pOa"], inc=(kt == NT - 1))
                for kt in range(NT):
                    T(lambda e: e.matmul(pOb[:, 0:8], VV[:, kt, 1, 0:128], Pb[0][:, 512 + kt * 8:512 + kt * 8 + 8],
                                         start=(kt == 0), stop=(kt == NT - 1)), r=["Pb0", "VV"], w=["pOb"], inc=(kt == NT - 1))
                epi_a(8)
                epi_b(8)
                at3 = at[:, 0:8].rearrange("p (a t) -> p a t", t=2)
                for tk_, col in ((0, 127), (1, 2176)):
                    V(lambda e: e.tensor_tensor(mixT[:, 0:4, col:col + 1], at3[:, :, tk_:tk_ + 1], gmix[:, 0:4].unsqueeze(2), ALU.mult),
                      r=["at", "gmix"], w=["mixT"])
                V(lambda e: e.tensor_tensor(sq[:, 0:8], at[:, 0:8], at[:, 0:8], ALU.mult), r=["at"], w=["sq"])
                V(lambda e: e.tensor_reduce(hs[:, 0:2], sq[:, 0:8].rearrange("p (a t) -> p t a", t=2), axis=AX.X, op=ALU.add),
                  r=["sq"], w=["hs"])
                V(lambda e: e.memset(acc[:, 0:256], 0.0), w=["acc"])
                V(lambda e: e.tensor_copy(acc[:, 127:129], hs[:, 0:2]), r=["hs"], w=["acc"])
                for w_ in range(2):
                    T(lambda e, w_=w_: e.matmul(pSA[:, w_:w_ + 1], acc[:, w_ * 128:(w_ + 1) * 128], ones_f[:, 0:1],
                                                start=True, stop=True), r=["acc", "ones_f"], w=["pSA"], inc=(w_ == 1))
                V(lambda e: e.tensor_copy(ssatt[:, 0:1], pSA[:, 0:1]), r=["pSA"], w=["ssatt"])
                V(lambda e: e.tensor_copy(ssatt[:, 17:18], pSA[:, 1:2]), r=["pSA"], w=["ssatt"])

                iters = [(qi, p) for qi in range(4) for p in range(4)]
                step = 1
                pend = None
                for (qi, p) in iters:
                    q0 = 128 + qi * 512
                    qk(0, step, p, q0, 512)
                    for kt in range(NT):
                        ex(kt, step + kt)
                        if kt + 1 < NT:
                            qk(kt + 1, step + kt + 1, p, q0, 512)
                        pv(kt, step + kt, 512)
                        if kt == 3 and pend is not None:
                            epi_main(*pend)
                        if kt == 16 and pend is not None:
                            if pend[1] == 3:
                                epi_ss(pend[0])
                            pend = None
                    step += NT
                    epi_a(512)
                    pend = (qi, p, q0)
                epi_main(*pend)
                epi_ss(3)

                rsqrt_cols(rs_a, ssatt, 0, NX, 1.0 / 512, "ssatt", "rs_a")
                P.barrier()
                stage_end(3, lambda: [(dbgb[:, 0:9216], mixT[:, 0:4, :].rearrange("p a n -> p (a n)"), "mixT"), (dbgf[:, 0:18], rs_a[:, 0:18], "rs_a")])

            x1nT = AR1[:, 0:8 * NCOLS].rearrange("p (k n) -> p k n", k=8)
            with ExitStack() as st2:
                G2_b = SB(st2, "G2_b", [128, D], F32)
                Wout = SB(st2, "Wout", [128, 8, D], BF16)
                xe = [SB(st2, f"xe{i}", [128, D], F32) for i in range(2)]
                x1h = [SB(st2, f"x1h{i}", [128, D], F32) for i in range(2)]
                t1 = SB(st2, "t1", [128, D], F32)
                x1n = [SB(st2, f"x1n{i}", [128, D], BF16) for i in range(2)]
                junk2 = SB(st2, "junk2", [128, D], BF16)
                pYa = PS(st2, "pYa", [128, D], F32)
                pYb = PS(st2, "pYb", [128, D], F32)
                pT2 = [PS(st2, f"pT2{i}", [128, D], BF16) for i in range(2)]
                LD(G2_b[:], scr[0], "G2_b", "d_c2")
                for kk in range(2):
                    P.op("gpsimd", lambda e, kk=kk: e.dma_start(
                        out=Wout[:, 4 * kk:4 * kk + 4, :],
                        in_=w_out[512 * kk:512 * kk + 512, :].rearrange("(k p) n -> p k n", p=128)),
                        writes=["Wout"], dma="d_wout")
                def out_mm(t):
                    s = t % 2
                    LD(xe[s][:], x_rot[t * 128:(t + 1) * 128, :], f"xe{s}")
                    for c2 in range(2):
                        for k in range(4):
                            T(lambda e, k=k, c2=c2: e.matmul(pYa[:, c2 * 512:(c2 + 1) * 512], mixT[:, k, t * 128:(t + 1) * 128],
                                                             Wout[:, k, c2 * 512:(c2 + 1) * 512], start=(k == 0), stop=(k == 3)),
                              r=["mixT", "Wout"], w=["pYa"], inc=(k == 3 and c2 == 1))
                    for c2 in range(2):
                        for k in range(4, 8):
                            T(lambda e, k=k, c2=c2: e.matmul(pYb[:, c2 * 512:(c2 + 1) * 512], mixT[:, k, t * 128:(t + 1) * 128],
                                                             Wout[:, k, c2 * 512:(c2 + 1) * 512], start=(k == 4), stop=(k == 7)),
                              r=["mixT", "Wout"], w=["pYb"], inc=(k == 7 and c2 == 1))

                def out_ep1(t):
                    s = t % 2
                    xd = x1h[s][:]
                    xdn = f"x1h{s}"
                    A(lambda e: e.activation(t1[:], pYa[:], AF.Identity, scale=rs_a[:, t:t + 1]), r=["pYa", "rs_a"], w=["t1"])
                    V(lambda e: e.scalar_tensor_tensor(t1[:], pYb[:], rs_t[:, t:t + 1], t1[:], ALU.mult, ALU.add),
                      r=["pYb", "rs_t", "t1"], w=["t1"])

                def out_ep2(t):
                    s = t % 2
                    xd = x1h[s][:]
                    xdn = f"x1h{s}"
                    V(lambda e: e.tensor_tensor(t1[:], t1[:], gt1_b[:], ALU.mult), r=["t1", "gt1_b"], w=["t1"])
                    V(lambda e: e.tensor_tensor(xd, t1[:], xe[s][:], ALU.add), r=["t1", f"xe{s}"], w=[xdn])
                    if 1 <= t <= 16:
                        P.op("sync", lambda e: e.dma_start(out=x1s[(t - 1) * 128:t * 128, :], in_=xd), reads=[xdn], dma=f"d_x1s{s}")
                    A(lambda e: e.activation(junk2[:], xd, AF.Square, accum_out=ss2[:, t:t + 1]), r=[xdn], w=["junk2", "ss2"])
                    rsqrt_cols(rs2, ss2, t, t + 1, 1.0 / D, "ss2", "rs2")
                    V(lambda e: e.scalar_tensor_tensor(x1n[s][:], xd, rs2[:, t:t + 1], G2_b[:], ALU.mult, ALU.mult),
                      r=[xdn, "rs2", "G2_b"], w=[f"x1n{s}"])
                    for k in range(8):
                        T(lambda e, k=k: e.transpose(pT2[s][:, k * 128:(k + 1) * 128], x1n[s][:, k * 128:(k + 1) * 128], ident_b[:]),
                          r=[f"x1n{s}", "ident_b"], w=[f"pT2{s}"], inc=(k == 7))
                    V(lambda e: e.tensor_copy(x1nT[:, :, t * 128:(t + 1) * 128], pT2[s][:].rearrange("p (k n) -> p k n", k=8)),
                      r=[f"pT2{s}"], w=["x1nT"])

                out_mm(0)
                for t in range(NX):
                    out_ep1(t)
                    if t + 1 < NX:
                        out_mm(t + 1)
                    out_ep2(t)
                P.barrier()
                stage_end(4, lambda: [(dbgb[:, 0:18432], AR1[:, 0:18432], "x1nT")])
                print("sbuf remaining at OUT:", nc.sbuf_bytes_remaining)

            stA.close()
            with ExitStack() as st2:
                NWU = 2
                GW = 2
                groups = [(c, min(GW, NCT - c)) for c in range(0, NCT, GW)]
                NG = len(groups)
                gt2_b = SB(st2, "gt2_b", [128, D], F32)
                gfin_b = SB(st2, "gfin_b", [128, D], F32)
                gT = SB(st2, "gT", [128, NCT, 512], BF16)
                tail0 = 8 * NCOLS
                wu = [AR1[:, tail0:tail0 + 8 * 2 * GW * 128].rearrange("p (c f) -> p c f", c=GW),
                      SB(st2, "wu1", [128, GW, 8 * 2 * 128], BF16)]
                Wd = SB(st2, "Wd", [128, NCT, D], BF16)
                zb = [SB(st2, f"zb{i}", [128, 2, 514], F32) for i in range(2)]
                cv = [SB(st2, f"cv{i}", [128, 2, 512], F32) for i in range(2)]
                sl_ = [SB(st2, f"sl{i}", [128, 512], F32) for i in range(2)]
                y2q = SB(st2, "y2q", [128, 4, D], F32)
                xr = [SB(st2, f"xr{i}", [128, D], F32) for i in range(2)]
                junk3 = AR1[:, tail0 + 8 * 2 * GW * 128:tail0 + 8 * 2 * GW * 128 + D]
                ot = [SB(st2, f"ot{i}", [128, D], F32) for i in range(2)]
                pZ = [PS(st2, f"pZ{i}", [128, 512], F32) for i in range(4)]
                pY = [PS(st2, f"pY{i}", [128, 512], F32) for i in range(4)]
                LD(gt2_b[:], scr[1], "gt2_b")
                LD(gfin_b[:], gfin_bd, "gfin_b")
                print("sbuf remaining at FFN:", nc.sbuf_bytes_remaining)

                def ld_wu(gg):
                    c0_, n_ = groups[gg % NG]
                    s = gg % NWU
                    P.op("sync", lambda e: e.dma_start(out=wu[s][:, 0:n_, :], in_=wub[c0_:c0_ + n_].rearrange("c p f -> p c f")),
                         writes=[f"wu{s}"], dma=f"d_wu{s}")

                def wsl(qtr, ct, k, ab):
                    gi = ct // GW
                    s = (qtr * NG + gi) % NWU
                    j = ct - groups[gi][0]
                    return wu[s][:, j, :].rearrange("p (k a n) -> p k a n", k=8, a=2)[:, k, ab, :], f"wu{s}"

                def ld_wd(ct):
                    P.op("gpsimd", lambda e: e.dma_start(out=Wd[:, ct, :], in_=w_down[ct * 128:(ct + 1) * 128, :]),
                         writes=[f"Wd{ct}"], dma=f"d_Wd{ct}")

                def up_tail(qtr, ct):
                    ci = ct % 2
                    A(lambda e: e.activation(sl_[ci][:], cv[ci][:, 0, :], AF.Silu), r=[f"cv{ci}"], w=[f"sl{ci}"])
                    V(lambda e: e.tensor_tensor(gT[:, ct, :], sl_[ci][:], cv[ci][:, 1, :], ALU.mult),
                      r=[f"sl{ci}", f"cv{ci}"], w=["gT"])

                def bias2_for(ct):
                    pb = pY[ct % 4]
                    for ab in range(2):
                        for k in range(8):
                            wl, wn = wsl(0, ct, k, ab)
                            T(lambda e, k=k, ab=ab, wl=wl: e.matmul(pb[:, ab:ab + 1], wl, sh2T[:, k:k + 1], start=(k == 0), stop=(k == 7)),
                              r=[wn, "sh2T"], w=[f"pY{ct % 4}"], inc=(k == 7 and ab == 1))
                    A(lambda e: e.copy(bias2[:, ct:ct + 1], pb[:, 0:1]), r=[f"pY{ct % 4}"], w=["bias2"])
                    A(lambda e: e.copy(bias2[:, 22 + ct:23 + ct], pb[:, 1:2]), r=[f"pY{ct % 4}"], w=["bias2"])

                def fin_tail(qtr, tt):
                    ti = qtr * 4 + tt
                    os_ = ti % 2
                    V(lambda e: e.tensor_tensor(y2q[:, tt, :], y2q[:, tt, :], xr[tt % 2][:], ALU.add),
                      r=[f"y2q{tt}", f"xr{tt % 2}"], w=[f"y2q{tt}"])
                    if tt < 2:
                        LD(xr[tt % 2][:], x1s[(ti + 2) * 128:(ti + 3) * 128, :], f"xr{tt % 2}")
                    A(lambda e: e.activation(junk3, y2q[:, tt, :], AF.Square, accum_out=ssf[:, ti:ti + 1]),
                      r=[f"y2q{tt}"], w=["junk3", "ssf"])
                    rsqrt_cols(rsf, ssf, ti, ti + 1, 1.0 / D, "ssf", "rsf")
                    V(lambda e: e.scalar_tensor_tensor(ot[os_][:], y2q[:, tt, :], rsf[:, ti:ti + 1], gfin_b[:],
                                                       ALU.mult, ALU.mult),
                      r=[f"y2q{tt}", "rsf", "gfin_b"], w=[f"ot{os_}"])
                    P.op("sync", lambda e: e.dma_start(out=out_d[ti * 128:(ti + 1) * 128, :], in_=ot[os_][:]),
                         reads=[f"ot{os_}"], dma=f"d_out{os_}")

                ld_wu(0)
                for qtr in range(4):
                    c0 = 127 + qtr * 512
                    for ct in range(NCT):
                        zi = ct % 2
                        ci = ct % 2
                        if ct % GW == 0:
                            gg = qtr * NG + ct // GW
                            if gg + 1 < 4 * NG:
                                ld_wu(gg + 1)
                        if qtr == 0 and ct == 0:
                            bias2_for(0)
                        for ab in range(2):
                            for blk in range(2):
                                pz = pZ[ab * 2 + blk]
                                for k in range(8):
                                    wl, wn = wsl(qtr, ct, k, ab)
                                    T(lambda e, k=k, wl=wl: e.matmul(
                                        pz[:, 0:257], wl,
                                        x1nT[:, k, c0 + blk * 257:c0 + (blk + 1) * 257], start=(k == 0), stop=(k == 7)),
                                      r=[wn, "x1nT"], w=[f"pZ{ab * 2 + blk}"], inc=(k == 7))
                                A(lambda e: e.activation(
                                    zb[zi][:, ab, blk * 257:(blk + 1) * 257], pz[:, 0:257], AF.Identity,
                                    bias=bias2[:, ab * 22 + ct:ab * 22 + ct + 1]),
                                  r=[f"pZ{ab * 2 + blk}", "bias2"], w=[f"zb{zi}"])
                        if qtr == 0 and ct + 1 < NCT:
                            bias2_for(ct + 1)
                        if qtr == 0:
                            ld_wd(ct)
                        if qtr > 0 and ct < 4:
                            fin_tail(qtr - 1, ct)
                        if ct == 4:
                            for tt in range(2):
                                LD(xr[tt][:], x1s[(qtr * 4 + tt) * 128:(qtr * 4 + tt + 1) * 128, :], f"xr{tt}")
                        if ct > 0:
                            up_tail(qtr, ct - 1)
                        if qtr == 0:
                            V(lambda e: e.tensor_scalar(zb[zi][:, :, 0:1], zb[zi][:, :, 0:1], hm[:, 0:1], None, ALU.mult),
                              r=[f"zb{zi}", "hm"], w=[f"zb{zi}"])
                        if qtr == 3:
                            V(lambda e: e.tensor_scalar(zb[zi][:, :, 513:514], zb[zi][:, :, 513:514], hm[:, 1:2], None, ALU.mult),
                              r=[f"zb{zi}", "hm"], w=[f"zb{zi}"])
                        for ab in range(2):
                            ch = ab * 22 + ct
                            A(lambda e: e.activation(cv[ci][:, ab, :], zb[zi][:, ab, 1:513], AF.Identity,
                                                     bias=bconv[:, ch:ch + 1], scale=wconv[:, 1, ch:ch + 1]),
                              r=[f"zb{zi}", "wconv", "bconv"], w=[f"cv{ci}"])
                        for ab in range(2):
                            ch = ab * 22 + ct
                            V(lambda e: e.scalar_tensor_tensor(cv[ci][:, ab, :], zb[zi][:, ab, 0:512], wconv[:, 0, ch:ch + 1],
                                                               cv[ci][:, ab, :], ALU.mult, ALU.add),
                              r=[f"zb{zi}", "wconv", f"cv{ci}"], w=[f"cv{ci}"])
                            V(lambda e: e.scalar_tensor_tensor(cv[ci][:, ab, :], zb[zi][:, ab, 2:514], wconv[:, 2, ch:ch + 1],
                                                               cv[ci][:, ab, :], ALU.mult, ALU.add),
                              r=[f"zb{zi}", "wconv", f"cv{ci}"], w=[f"cv{ci}"])
                    up_tail(qtr, NCT - 1)
                    for c2 in range(2):
                        for ct in range(NCT):
                            for tt in range(4):
                                T(lambda e, tt=tt: e.matmul(pY[tt][:, :], gT[:, ct, tt * 128:(tt + 1) * 128],
                                                            Wd[:, ct, c2 * 512:(c2 + 1) * 512],
                                                            start=(ct == 0), stop=(ct == NCT - 1)),
                                  r=["gT", f"Wd{ct}"], w=[f"pY{tt}"], inc=(ct == NCT - 1))
                        for tt in range(4):
                            V(lambda e, tt=tt: e.tensor_tensor(y2q[:, tt, c2 * 512:(c2 + 1) * 512], pY[tt][:, :],
                                                               gt2_b[:, c2 * 512:(c2 + 1) * 512], ALU.mult),
                              r=[f"pY{tt}", "gt2_b"], w=[f"y2q{tt}"])
                    if qtr == 3:
                        for tt in range(4):
                            fin_tail(3, tt)
                P.barrier()
        print("kernel build: inst", P.ninst, "waits", P.nwait, "sems", len(P.sem), P.cnt)
    except _Stop:
        pass
    return nc


def _prep_inputs(inp):
    x = np.asarray(inp["x"], np.float32)
    c = np.asarray(inp["c"], np.float32)
    f = lambda k: np.asarray(inp[k], np.float32)
    rows = S // 64
    row_id = np.repeat(np.arange(rows), 64).astype(np.float32)
    col_id = np.tile(np.arange(64), rows).astype(np.float32)
    inv_freq = np.power(np.float32(10000.0), -np.arange(0, 32, 2, dtype=np.float32) / np.float32(32)).astype(np.float32)
    ang_r = row_id[:, None] * inv_freq[None, :]
    ang_c = col_id[:, None] * inv_freq[None, :]
    ang = np.concatenate([ang_r, ang_r, ang_c, ang_c], axis=-1).astype(np.float32)
    cos = np.cos(ang).astype(np.float32)
    sin = np.sin(ang).astype(np.float32)
    sgn = np.concatenate([-np.ones(16), np.ones(16), -np.ones(16), np.ones(16)]).astype(np.float32)
    sinS = sin * sgn[None, :]
    perm_h = [0, 4, 1, 5, 2, 6, 3, 7]
    qcols = np.concatenate([np.arange(h * 64, (h + 1) * 64) for h in perm_h])
    w_in = f("w_in")[0].copy()
    w_in[:, 0:512] = w_in[:, qcols]
    w_out = f("w_out")[0].copy()
    w_out[0:512, :] = w_out[qcols, :]
    g_attn = f("g_attn_out")[0][qcols]
    gmixv = np.concatenate([g_attn, f("g_tok_out")[0]])
    gmix = np.ascontiguousarray(gmixv.reshape(8, 128).T)
    rep = lambda v, n=128: np.ascontiguousarray(np.broadcast_to(v[None, :], (n, v.shape[0])))
    wsT = np.ascontiguousarray(f("w_s")[0].transpose(2, 0, 1))
    b_s = f("b_s")[0]
    bsbT = np.zeros((128, 4, 128), np.float32)
    for c4 in range(4):
        bsbT[0:64, c4, :] = b_s[2 * c4][None, :]
        bsbT[64:128, c4, :] = b_s[2 * c4 + 1][None, :]
    wconv = np.ascontiguousarray(f("w_conv")[0].reshape(3, 44, 128).transpose(2, 0, 1))
    bconv = np.ascontiguousarray(f("b_conv")[0].reshape(44, 128).T)
    shared = {
        "w_ada": f("w_ada")[0], "b_ada": f("b_ada"), "g1_b": rep(f("g_norm1")[0]), "g2_b": rep(f("g_norm2")[0]),
        "gfin_b": rep(f("g_final")), "w_in": w_in, "w_out": w_out, "w_up": f("w_up")[0], "w_down": f("w_down")[0],
        "gq_b": rep(f("g_q")[0]), "gk_b": rep(f("g_k")[0]), "gtv_b": rep(f("g_tok_v")[0]), "wsT": wsT,
        "bsbT": bsbT.reshape(128, 512), "gmix": gmix, "wconv": wconv, "bconv": bconv,
        "ident": np.eye(128, dtype=np.float32),
    }
    in_maps = []
    for core in range(8):
        b, j = core // 4, core % 4
        sh = j * 2048 - 128
        xr = np.roll(x[b], -sh, axis=0)
        cr = np.roll(cos, -sh, axis=0).reshape(NT, 128, 64).transpose(1, 0, 2)
        sr = np.roll(sinS, -sh, axis=0).reshape(NT, 128, 64).transpose(1, 0, 2)
        hmk = np.ones((128, 2), np.float32)
        if j == 0:
            hmk[:, 0] = 0.0
        if j == 3:
            hmk[:, 1] = 0.0
        m = dict(shared)
        m.update({"x_rot": np.ascontiguousarray(xr), "cos_t": np.ascontiguousarray(cr), "sin_t": np.ascontiguousarray(sr),
                  "cb": np.ascontiguousarray(c[b].reshape(8, 128).T), "hmask": hmk})
        in_maps.append(m)
    return in_maps


_NC_CACHE = {}


def kernel(**inputs):
    in_maps = _prep_inputs(inputs)
    if "nc" not in _NC_CACHE:
        _NC_CACHE["nc"] = build_nc()
    nc = _NC_CACHE["nc"]
    res = run_bass_kernel_spmd(nc, in_maps, core_ids=list(range(8)))
    out = np.zeros((2, S, D), np.float32)
    for core in range(8):
        b, j = core // 4, core % 4
        out[b, j * 2048:(j + 1) * 2048, :] = res.results[core]["out"]
    return out
```

```python
import numpy as np
import concourse.bass as bass
import concourse.mybir as mybir
from concourse.bass_utils import run_bass_kernel_spmd
from contextlib import ExitStack

F32 = mybir.dt.float32
BF16 = mybir.dt.bfloat16
AF = mybir.ActivationFunctionType
ALU = mybir.AluOpType
AX = mybir.AxisListType

ENGS = ("sync", "scalar", "vector", "gpsimd", "tensor")
D = 1024
S = 8192
NT = 64
NX = 18
NCOLS = NX * 128
DFF = 2816
NCT = 22
EPS = 1e-6
AB_TILES = NT


class _Stop(Exception):
    pass


class Prog:
    def __init__(self, nc, stack):
        self.nc = nc
        self.stack = stack
        self.sem = {}
        self.cnt = {}
        self.waited = {e: {} for e in ENGS}
        self.lastw = {}
        self.readers = {}
        self.ninst = 0
        self.nwait = 0

    def getsem(self, name):
        if name not in self.sem:
            self.sem[name] = self.stack.enter_context(self.nc.semaphore(name))
            self.cnt[name] = 0
        return self.sem[name]

    def _wait(self, eng, tok):
        s, v = tok
        if self.waited[eng].get(s, 0) >= v:
            return
        self.waited[eng][s] = v
        getattr(self.nc, eng).wait_ge(self.sem[s], v)
        self.nwait += 1

    def op(self, eng, fn, reads=(), writes=(), dma=None, inc=True):
        deps = []
        for r in reads:
            if r in self.lastw:
                deps.append(self.lastw[r])
        for w in writes:
            if w in self.lastw:
                deps.append(self.lastw[w])
            for s, v in self.readers.get(w, {}).items():
                deps.append((s, v))
        for t in deps:
            if eng == "tensor" and t[0] == "e_tensor":
                continue
            self._wait(eng, t)
        e = getattr(self.nc, eng)
        tok = None
        if dma is not None:
            self.getsem(dma)
            self.cnt[dma] += 16
            tok = (dma, self.cnt[dma])
            fn(e).then_inc(self.sem[dma], 16)
        elif inc:
            sn = "e_" + eng
            self.getsem(sn)
            self.cnt[sn] += 1
            tok = (sn, self.cnt[sn])
            fn(e).then_inc(self.sem[sn], 1)
        else:
            fn(e)
        self.ninst += 1
        if tok is not None:
            for r in reads:
                d = self.readers.setdefault(r, {})
                d[tok[0]] = max(d.get(tok[0], 0), tok[1])
            for w in writes:
                self.lastw[w] = tok
                self.readers[w] = {}
        return tok

    def barrier(self):
        for eng in ENGS:
            for s, v in self.cnt.items():
                if v > 0:
                    self._wait(eng, (s, v))
        self.lastw = {}
        self.readers = {}


def build_nc(debug_stage=0):
    nc = bass.Bass("TRN2", target_bir_lowering=False)
    di = lambda name, shape: nc.dram_tensor(name, shape, F32, kind="ExternalInput").ap()
    x_rot = di("x_rot", [S, D])
    cos_t = di("cos_t", [128, NT, 64])
    sin_t = di("sin_t", [128, NT, 64])
    cb = di("cb", [128, 8])
    hmask = di("hmask", [128, 2])
    w_ada = di("w_ada", [D, 6 * D])
    b_ada = di("b_ada", [1, 6 * D])
    g1_bd = di("g1_b", [128, D])
    g2_bd = di("g2_b", [128, D])
    gfin_bd = di("gfin_b", [128, D])
    w_in = di("w_in", [D, 1792])
    w_out = di("w_out", [D, D])
    w_up = di("w_up", [D, 2 * DFF])
    w_down = di("w_down", [DFF, D])
    gq_bd = di("gq_b", [128, 64])
    gk_bd = di("gk_b", [128, 64])
    gtv_bd = di("gtv_b", [128, 64])
    wsT_d = di("wsT", [128, 8, 128])
    bsbT_d = di("bsbT", [128, 512])
    gmix_d = di("gmix", [128, 8])
    wconv_d = di("wconv", [128, 3, 44])
    bconv_d = di("bconv", [128, 44])
    ident_d = di("ident", [128, 128])
    out_d = nc.dram_tensor("out", [2048, D], F32, kind="ExternalOutput").ap()
    scr = nc.dram_tensor("scr", [2, 128, D], F32, kind="Internal").ap()
    x1s = nc.dram_tensor("x1s", [2048, D], F32, kind=("ExternalOutput" if debug_stage else "Internal")).ap()
    wub = nc.dram_tensor("wub", [NCT, 128, 8 * 2 * 128], BF16, kind="Internal").ap()
    if debug_stage:
        dbgf = nc.dram_tensor("dbgf", [128, 8192], F32, kind="ExternalOutput").ap()
        dbgb = nc.dram_tensor("dbgb", [128, 65536], BF16, kind="ExternalOutput").ap()

    try:
      with ExitStack() as st0:
        P = Prog(nc, st0)

        def dump(dst, src, res):
            tok = P.op("sync", lambda e: e.dma_start(out=dst, in_=src), reads=[res], dma="d_dbg")
            P._wait("sync", tok)

        def stage_end(k, dumps):
            if debug_stage == k:
                for (dst, src, res) in dumps():
                    dump(dst, src, res)
                print("STOP at stage", k, "inst", P.ninst, "waits", P.nwait, "cnt", P.cnt)
                raise _Stop()

        def SB(st, name, shape, dt):
            return st.enter_context(nc.sbuf_tensor("s_" + name, shape, dt))

        def PS(st, name, shape, dt):
            return st.enter_context(nc.psum_tensor("p_" + name, shape, dt))

        V = lambda fn, r=(), w=(): P.op("vector", fn, r, w)
        A = lambda fn, r=(), w=(): P.op("scalar", fn, r, w)
        G = lambda fn, r=(), w=(): P.op("gpsimd", fn, r, w)
        T = lambda fn, r=(), w=(), inc=True: P.op("tensor", fn, r, w, inc=inc)

        def LD(dst, src, res, sem=None, eng="sync"):
            return P.op(eng, lambda e: e.dma_start(out=dst, in_=src), writes=[res], dma="d_" + res)

        ident_b = SB(st0, "ident_b", [128, 128], BF16)
        ones_f = SB(st0, "ones_f", [128, 128], F32)
        epsc = SB(st0, "epsc", [128, 1], F32)
        G1_b = SB(st0, "G1_b", [128, D], F32)
        gt1_b = SB(st0, "gt1_b", [128, D], F32)
        bias_q = SB(st0, "bias_q", [128, 512], F32)
        bias_kv = SB(st0, "bias_kv", [128, 256], F32)
        bias_zv = SB(st0, "bias_zv", [128, 512], F32)
        biasU = SB(st0, "biasU", [128, 4], F32)
        sh1T = SB(st0, "sh1T", [128, 8], BF16)
        sh2T = SB(st0, "sh2T", [128, 8], BF16)
        gq_b = SB(st0, "gq_b", [128, 64], F32)
        gk_b = SB(st0, "gk_b", [128, 64], F32)
        gtv_b = SB(st0, "gtv_b", [128, 64], F32)
        gmix = SB(st0, "gmix", [128, 8], F32)
        wconv = SB(st0, "wconv", [128, 3, 44], F32)
        bconv = SB(st0, "bconv", [128, 44], F32)
        bias2 = SB(st0, "bias2", [128, 44], F32)
        hm = SB(st0, "hm", [128, 2], F32)
        ss = SB(st0, "ss", [128, NT], F32)
        lnv = SB(st0, "lnv", [128, NT], F32)
        rstd = SB(st0, "rstd", [128, NT], F32)
        sstok = SB(st0, "sstok", [128, NX], F32)
        ssatt = SB(st0, "ssatt", [128, 20], F32)
        rs_t = SB(st0, "rs_t", [128, NX], F32)
        rs_a = SB(st0, "rs_a", [128, 20], F32)
        ss2 = SB(st0, "ss2", [128, NX], F32)
        rs2 = SB(st0, "rs2", [128, NX], F32)
        ssf = SB(st0, "ssf", [128, 16], F32)
        rsf = SB(st0, "rsf", [128, 16], F32)
        sm = SB(st0, "sm", [128, 16], F32)

        LD(ident_b[:], ident_d, "ident_b", "d_c0", eng="gpsimd")
        V(lambda e: e.memset(ones_f[:], 1.0), w=["ones_f"])
        V(lambda e: e.memset(epsc[:], EPS), w=["epsc"])
        for nm, tl, src in [("gq_b", gq_b, gq_bd), ("gk_b", gk_b, gk_bd), ("gtv_b", gtv_b, gtv_bd),
                            ("gmix", gmix, gmix_d), ("wconv", wconv, wconv_d), ("bconv", bconv, bconv_d),
                            ("hm", hm, hmask)]:
            LD(tl[:], src, nm, "d_c1")
        for nm, tl in [("ss", ss), ("sstok", sstok), ("ssatt", ssatt), ("ss2", ss2), ("ssf", ssf), ("rstd", rstd), ("sm", sm)]:
            V(lambda e, tl=tl: e.memset(tl[:], 0.0), w=[nm])

        def rsqrt_cols(dst, src, lo, hi, scale, rn, wn, tmp=None):
            A(lambda e: e.activation(dst[:, lo:hi], src[:, lo:hi], AF.Ln, bias=epsc[:, 0:1], scale=scale),
              r=[rn, "epsc"], w=[wn])
            A(lambda e: e.activation(dst[:, lo:hi], dst[:, lo:hi], AF.Exp, scale=-0.5), r=[wn], w=[wn])

        with ExitStack() as st:
            wada = [SB(st, f"wada{i}", [128, 8, D], BF16) for i in range(2)]
            cbt = SB(st, "cbt", [128, 8], F32)
            scT = SB(st, "scT", [128, 8], BF16)
            brow = SB(st, "brow", [1, 6 * D], F32)
            mrow = [SB(st, f"mrow{i}", [1, D], F32) for i in range(2)]
            gtmp = SB(st, "gtmp", [128, D], F32)
            pMod = PS(st, "pMod", [128, D], F32)
            pB = PS(st, "pB", [128, D], F32)
            pX = PS(st, "pX", [128, 8], F32)
            LD(cbt[:], cb, "cbt", "d_c1")
            LD(brow[:], b_ada, "brow", "d_c1")
            A(lambda e: e.activation(scT[:], cbt[:], AF.Silu), r=["cbt"], w=["scT"])
            for m in range(6):
                wb = wada[m % 2]
                wn = f"wada{m % 2}"
                for kk in range(2):
                    P.op("gpsimd", lambda e, kk=kk, m=m, wb=wb: e.dma_start(
                        out=wb[:, 4 * kk:4 * kk + 4, :],
                        in_=w_ada[512 * kk:512 * kk + 512, m * D:(m + 1) * D].rearrange("(k p) n -> p k n", p=128)),
                        writes=[wn], dma=f"d_wada{m % 2}")
                for c2 in range(2):
                    for k in range(8):
                        T(lambda e, k=k, c2=c2, wb=wb: e.matmul(pMod[0:1, c2 * 512:(c2 + 1) * 512], scT[:, k:k + 1],
                                                                wb[:, k, c2 * 512:(c2 + 1) * 512],
                                                                start=(k == 0), stop=(k == 7)),
                          r=[wn, "scT"], w=["pMod"], inc=(k == 7))
                mr = mrow[m % 2]
                mn = f"mrow{m % 2}"
                V(lambda e, m=m, mr=mr: e.tensor_tensor(mr[:], pMod[0:1, :], brow[0:1, m * D:(m + 1) * D], ALU.add),
                  r=["pMod", "brow"], w=[mn])
                if m in (0, 3):
                    for k in range(8):
                        T(lambda e, k=k, mr=mr: e.matmul(pX[:, k:k + 1], mr[0:1, k * 128:(k + 1) * 128],
                                                         ones_f[0:1, 0:1], start=True, stop=True),
                          r=[mn, "ones_f"], w=["pX"], inc=(k == 7))
                    dst, dn = (sh1T, "sh1T") if m == 0 else (sh2T, "sh2T")
                    V(lambda e, dst=dst: e.tensor_copy(dst[:], pX[:]), r=["pX"], w=[dn])
                else:
                    for c2 in range(2):
                        T(lambda e, c2=c2, mr=mr: e.matmul(pB[:, c2 * 512:(c2 + 1) * 512], ones_f[0:1, :],
                                                           mr[0:1, c2 * 512:(c2 + 1) * 512], start=True, stop=True),
                          r=[mn, "ones_f"], w=["pB"], inc=(c2 == 1))
                    if m in (1, 4):
                        LD(gtmp[:], g1_bd if m == 1 else g2_bd, "gtmp", "d_c2")
                        if m == 1:
                            V(lambda e: e.scalar_tensor_tensor(G1_b[:], pB[:], 1.0, gtmp[:], ALU.add, ALU.mult),
                              r=["pB", "gtmp"], w=["G1_b"])
                        else:
                            V(lambda e: e.scalar_tensor_tensor(gtmp[:], pB[:], 1.0, gtmp[:], ALU.add, ALU.mult),
                              r=["pB", "gtmp"], w=["gtmp"])
                            P.op("sync", lambda e: e.dma_start(out=scr[0], in_=gtmp[:]), reads=["gtmp"], dma="d_scr")
                    elif m == 2:
                        V(lambda e: e.tensor_copy(gt1_b[:], pB[:]), r=["pB"], w=["gt1_b"])
                    else:
                        V(lambda e: e.tensor_copy(gtmp[:], pB[:]), r=["pB"], w=["gtmp"])
                        P.op("sync", lambda e: e.dma_start(out=scr[1], in_=gtmp[:]), reads=["gtmp"], dma="d_scr")
            P.barrier()
            stage_end(1, lambda: [(dbgf[:, 0:1024], G1_b[:], "G1_b"), (dbgf[:, 1024:2048], gt1_b[:], "gt1_b"), (dbgb[:, 0:8], sh1T[:], "sh1T"), (dbgb[:, 8:16], sh2T[:], "sh2T")])

        with ExitStack() as stAR, ExitStack() as stA:
            AR1 = SB(stAR, "AR1", [128, S + NT * 2 * 129], BF16)
            KT = AR1[:, 0:S]
            VVf = AR1[:, S:S + NT * 2 * 129]
            VV = VVf.rearrange("p (t g c) -> p t g c", t=NT, g=2)
            VVk = VVf.rearrange("p (t c) -> p t c", t=NT)
            QT = SB(stA, "QT", [128, 4, NCOLS], BF16)
            mixT = SB(stA, "mixT", [128, 8, NCOLS], BF16)
            V(lambda e: e.memset(AR1[:], 0.0), w=["VV", "KT"])
            V(lambda e: e.memset(mixT[:, 0:4, 0:128], 0.0), w=["mixT"])
            V(lambda e: e.memset(mixT[:, 0:4, 2176:2304], 0.0), w=["mixT"])
            if debug_stage:
                V(lambda e: e.memset(QT[:], 0.0), w=["QT"])
            V(lambda e: e.memset(VV[:, :, :, 0:1], 1.0), w=["VV"])
            V(lambda e: e.memset(VV[:, :, :, 128:129], 1.0), w=["VV"])

            with ExitStack() as st:
                Win = SB(st, "Win", [128, 8, 1792], BF16)
                wsT = SB(st, "wsTb", [128, 8, 128], BF16)
                bsbT = SB(st, "bsbT", [128, 512], F32)
                xt = [SB(st, f"xt{i}", [128, D], F32) for i in range(2)]
                junk = SB(st, "junk", [128, D], BF16)
                xn = [SB(st, f"xn{i}", [128, D], BF16) for i in range(2)]
                xnT = [SB(st, f"xnT{i}", [128, 8, 128], BF16) for i in range(2)]
                cst = [SB(st, f"cst{i}", [128, 64], F32) for i in range(2)]
                snt = [SB(st, f"snt{i}", [128, 64], F32) for i in range(2)]
                kvf = SB(st, "kvf", [128, 256], F32)
                ksq = SB(st, "ksq", [128, 128], F32)
                kra = SB(st, "kra", [128, 128], F32)
                krb = SB(st, "krb", [128, 128], F32)
                kbf = SB(st, "kbf", [128, 128], BF16)
                f1 = SB(st, "f1", [128, 512], F32)
                f2 = SB(st, "f2", [128, 512], F32)
                f3 = SB(st, "f3", [128, 512], F32)
                qbf = SB(st, "qbf", [128, 512], BF16)
                vnpad = SB(st, "vnpad", [128, 8, 128], BF16)
                uT = SB(st, "uT", [128, 512], F32)
                tk = SB(st, "tk", [128, 512], F32)
                sqt = SB(st, "sqt", [128, 512], F32)
                pT = [PS(st, f"pT{i}", [128, D], BF16) for i in range(2)]
                pKV = PS(st, "pKV", [128, 512], F32)
                pQ = PS(st, "pQ", [128, 512], F32)
                pZV = PS(st, "pZV", [128, 512], F32)
                pU = PS(st, "pU", [128, 512], F32)
                pTQ = PS(st, "pTQ", [128, D], BF16)
                pM = PS(st, "pM", [128, 512], F32)

                for kk in range(2):
                    P.op("gpsimd", lambda e, kk=kk: e.dma_start(
                        out=Win[:, 4 * kk:4 * kk + 4, :],
                        in_=w_in[512 * kk:512 * kk + 512, :].rearrange("(k p) n -> p k n", p=128)),
                        writes=["Win"], dma="d_win")
                LD(wsT[:], wsT_d, "wsT", "d_c3", eng="gpsimd")
                LD(bsbT[:], bsbT_d, "bsbT", "d_c1")
                V(lambda e: e.memset(vnpad[:], 0.0), w=["vnpad"])

                colgrp = [(0, 512, bias_q, "bias_q", 0), (512, 256, bias_kv, "bias_kv", 512),
                          (1280, 512, bias_zv, "bias_zv", 768)]
                for (c0, n, dst, dn, r0) in colgrp:
                    for k in range(8):
                        T(lambda e, k=k, c0=c0, n=n: e.matmul(pQ[0:1, 0:n], sh1T[:, k:k + 1], Win[:, k, c0:c0 + n],
                                                              start=(k == 0), stop=(k == 7)),
                          r=["Win", "sh1T"], w=["pQ"], inc=(k == 7))
                    V(lambda e, n=n, r0=r0: e.tensor_copy(tk[0:1, 0:n], pQ[0:1, 0:n]), r=["pQ"], w=["tk"])
                    T(lambda e, n=n, r0=r0: e.matmul(pZV[:, 0:n], ones_f[0:1, :], tk[0:1, 0:n],
                                                     start=True, stop=True), r=["tk", "ones_f"], w=["pZV"])
                    V(lambda e, n=n, dst=dst: e.tensor_copy(dst[:, 0:n], pZV[:, 0:n]), r=["pZV"], w=[dn])
                for c4 in range(4):
                    for k in range(8):
                        T(lambda e, k=k, c4=c4: e.matmul(pU[:, c4:c4 + 1], Win[:, k, 768 + c4 * 128:768 + (c4 + 1) * 128],
                                                         sh1T[:, k:k + 1], start=(k == 0), stop=(k == 7)),
                          r=["Win", "sh1T"], w=["pU"], inc=(k == 7 and c4 == 3))
                V(lambda e: e.tensor_copy(biasU[:], pU[:, 0:4]), r=["pU"], w=["biasU"])


                qf = SB(st, "qf", [128, 512], F32)
                zvf = SB(st, "zvf", [128, 512], F32)
                sm18 = SB(st, "sm18", [128, 18], F32)
                V(lambda e: e.memset(sm18[:], 1.0), w=["sm18"])
                print("sbuf remaining at AB:", nc.sbuf_bytes_remaining)

                def v3(a, nh):
                    return a[:, 0:nh * 64].rearrange("p (h d) -> p h d", h=nh)

                def front(t):
                    s = t % 2
                    own = t < NX
                    LD(xt[s][:], x_rot[t * 128:(t + 1) * 128, :], f"xt{s}")
                    LD(cst[s][:], cos_t[:, t, :], f"cst{s}")
                    LD(snt[s][:], sin_t[:, t, :], f"snt{s}")
                    A(lambda e: e.activation(junk[:], xt[s][:], AF.Square, accum_out=ss[:, t:t + 1]),
                      r=[f"xt{s}"], w=["junk", "ss"])
                    rsqrt_cols(rstd, ss, t, t + 1, 1.0 / D, "ss", "rstd")
                    V(lambda e: e.scalar_tensor_tensor(xn[s][:], xt[s][:], rstd[:, t:t + 1], G1_b[:], ALU.mult, ALU.mult),
                      r=[f"xt{s}", "rstd", "G1_b"], w=[f"xn{s}"])
                    for k in range(8):
                        T(lambda e, k=k: e.transpose(pT[s][:, k * 128:(k + 1) * 128], xn[s][:, k * 128:(k + 1) * 128], ident_b[:]),
                          r=[f"xn{s}", "ident_b"], w=[f"pT{s}"], inc=(k == 7))
                    A(lambda e: e.copy(xnT[s][:].rearrange("p k n -> p (k n)"), pT[s][:]), r=[f"pT{s}"], w=[f"xnT{s}"])
                    for k in range(8):
                        T(lambda e, k=k: e.matmul(pKV[:, 0:256], xnT[s][:, k, :], Win[:, k, 512:768],
                                                  start=(k == 0), stop=(k == 7)),
                          r=[f"xnT{s}", "Win"], w=["pKV"], inc=(k == 7))
                    if own:
                        for k in range(8):
                            T(lambda e, k=k: e.matmul(pQ[:, :], xnT[s][:, k, :], Win[:, k, 0:512],
                                                      start=(k == 0), stop=(k == 7)),
                              r=[f"xnT{s}", "Win"], w=["pQ"], inc=(k == 7))
                        for k in range(8):
                            T(lambda e, k=k: e.matmul(pZV[:, :], xnT[s][:, k, :], Win[:, k, 1280:1792],
                                                      start=(k == 0), stop=(k == 7)),
                              r=[f"xnT{s}", "Win"], w=["pZV"], inc=(k == 7))
                        for c4 in range(4):
                            for k in range(8):
                                T(lambda e, k=k, c4=c4: e.matmul(pU[:, c4 * 128:(c4 + 1) * 128],
                                                                 Win[:, k, 768 + c4 * 128:768 + (c4 + 1) * 128],
                                                                 xnT[s][:, k, :], start=(k == 0), stop=(k == 7)),
                                  r=[f"xnT{s}", "Win"], w=["pU"], inc=(k == 7 and c4 == 3))

                def mid(t):
                    own = t < NX
                    V(lambda e: e.tensor_tensor(kvf[:], pKV[:, 0:256], bias_kv[:], ALU.add), r=["pKV", "bias_kv"], w=["kvf"])
                    if own:
                        V(lambda e: e.tensor_tensor(zvf[:], pZV[:], bias_zv[:], ALU.add), r=["pZV", "bias_zv"], w=["zvf"])
                        A(lambda e: e.activation(zvf[:], zvf[:], AF.Gelu_apprx_tanh), r=["zvf"], w=["zvf"])
                        V(lambda e: e.tensor_tensor(qf[:], pQ[:], bias_q[:], ALU.add), r=["pQ", "bias_q"], w=["qf"])
                        for c4 in range(4):
                            A(lambda e, c4=c4: e.activation(uT[:, c4 * 128:(c4 + 1) * 128], pU[:, c4 * 128:(c4 + 1) * 128],
                                                            AF.Gelu_apprx_tanh, bias=biasU[:, c4:c4 + 1]),
                              r=["pU", "biasU"], w=["uT"])

                def rope2(src, sname, nh, ra, raname, rb, rbname, dst, dname, cs, csn, sn, snn):
                    V(lambda e: e.tensor_tensor(v3(ra, nh), v3(src, nh), cs[:].unsqueeze(1).to_broadcast([128, nh, 64]), ALU.mult),
                      r=[sname, csn], w=[raname])
                    for blk in range(4):
                        pb = blk ^ 1
                        G(lambda e, blk=blk, pb=pb: e.tensor_tensor(
                            v3(rb, nh)[:, :, blk * 16:(blk + 1) * 16], v3(src, nh)[:, :, pb * 16:(pb + 1) * 16],
                            sn[:, blk * 16:(blk + 1) * 16].unsqueeze(1).to_broadcast([128, nh, 16]), ALU.mult),
                          r=[sname, snn], w=[rbname])
                    V(lambda e: e.tensor_tensor(dst[:, 0:nh * 64], ra[:, 0:nh * 64], rb[:, 0:nh * 64], ALU.add),
                      r=[raname, rbname], w=[dname])

                def back_a(t):
                    s = t % 2
                    own = t < NX
                    G(lambda e: e.tensor_copy(VV[:, t, :, 64:128], kvf[:, 128:256].rearrange("p (g d) -> p g d", g=2)),
                      r=["kvf"], w=["VV"])
                    G(lambda e: e.tensor_tensor(ksq[:], kvf[:, 0:128], kvf[:, 0:128], ALU.mult), r=["kvf"], w=["ksq"])
                    V(lambda e: e.tensor_reduce(sm18[:, 0:2], v3(ksq, 2), axis=AX.X, op=ALU.add), r=["ksq"], w=["sm18"])
                    if own:
                        G(lambda e: e.tensor_tensor(f3[:], qf[:], qf[:], ALU.mult), r=["qf"], w=["f3"])
                        V(lambda e: e.tensor_reduce(sm18[:, 2:10], v3(f3, 8), axis=AX.X, op=ALU.add), r=["f3"], w=["sm18"])
                        G(lambda e: e.tensor_tensor(f1[:], zvf[:], zvf[:], ALU.mult), r=["zvf"], w=["f1"])
                        V(lambda e: e.tensor_reduce(sm18[:, 10:18], v3(f1, 8), axis=AX.X, op=ALU.add), r=["f1"], w=["sm18"])
                    nc_ = 18 if own else 2
                    rsqrt_cols(sm18, sm18, 0, nc_, 1.0 / 64, "sm18", "sm18")

                def back_b(t):
                    s = t % 2
                    own = t < NX
                    V(lambda e: e.tensor_tensor(v3(kra, 2), v3(kvf, 2), sm18[:, 0:2].unsqueeze(2).to_broadcast([128, 2, 64]), ALU.mult),
                      r=["kvf", "sm18"], w=["kra"])
                    V(lambda e: e.tensor_tensor(v3(kra, 2), v3(kra, 2), gk_b[:].unsqueeze(1).to_broadcast([128, 2, 64]), ALU.mult),
                      r=["kra", "gk_b"], w=["kra"])
                    rope2(kra, "kra", 2, ksq, "ksq", krb, "krb", kbf, "kbf", cst[s], f"cst{s}", snt[s], f"snt{s}")
                    T(lambda e: e.transpose(pTQ[:, 512:640], kbf[:], ident_b[:]), r=["kbf", "ident_b"], w=["pTQk"])
                    kt_copy = lambda: V(lambda e: e.tensor_copy(KT[:, t * 128:(t + 1) * 128], pTQ[:, 512:640]), r=["pTQk"], w=["KT"])
                    if not own:
                        pending.append(kt_copy)
                        return
                    kt_copy()
                    V(lambda e: e.tensor_tensor(v3(f2, 8), v3(qf, 8), sm18[:, 2:10].unsqueeze(2).to_broadcast([128, 8, 64]), ALU.mult),
                      r=["qf", "sm18"], w=["f2"])
                    V(lambda e: e.tensor_tensor(v3(f2, 8), v3(f2, 8), gq_b[:].unsqueeze(1).to_broadcast([128, 8, 64]), ALU.mult),
                      r=["f2", "gq_b"], w=["f2"])
                    rope2(f2, "f2", 8, f3, "f3", f1, "f1", qbf, "qbf", cst[s], f"cst{s}", snt[s], f"snt{s}")
                    for p in range(4):
                        T(lambda e, p=p: e.transpose(pTQ[:, p * 128:(p + 1) * 128], qbf[:, p * 128:(p + 1) * 128], ident_b[:]),
                          r=["qbf", "ident_b"], w=["pTQ"], inc=(p == 3))
                    V(lambda e: e.tensor_tensor(v3(f2, 8), v3(zvf, 8), sm18[:, 10:18].unsqueeze(2).to_broadcast([128, 8, 64]), ALU.mult),
                      r=["zvf", "sm18"], w=["f2"])
                    for sl in range(2):
                        G(lambda e, sl=sl: e.tensor_tensor(
                            vnpad[:].rearrange("p (c s) n -> p c s n", s=2)[:, :, sl, sl * 64:(sl + 1) * 64],
                            f2[:].rearrange("p (c s d) -> p c s d", s=2, d=64)[:, :, sl, :],
                            gtv_b[:].unsqueeze(1).to_broadcast([128, 4, 64]), ALU.mult),
                          r=["f2", "gtv_b"], w=["vnpad"])
                    for c4 in range(4):
                        for sl in range(2):
                            T(lambda e, c4=c4, sl=sl: e.matmul(pM[:, c4 * 128:(c4 + 1) * 128], vnpad[:, 2 * c4 + sl, :],
                                                               wsT[:, 2 * c4 + sl, :], start=(sl == 0), stop=(sl == 1)),
                              r=["vnpad", "wsT"], w=["pM"], inc=(sl == 1 and c4 == 3))
                    V(lambda e: e.tensor_copy(QT[:, :, t * 128:(t + 1) * 128], pTQ[:, 0:512].rearrange("p (a n) -> p a n", a=4)),
                      r=["pTQ"], w=["QT"])
                    V(lambda e: e.tensor_tensor(tk[:], pM[:], bsbT[:], ALU.add), r=["pM", "bsbT"], w=["tk"])
                    V(lambda e: e.tensor_tensor(tk[:], tk[:], uT[:], ALU.mult), r=["tk", "uT"], w=["tk"])
                    G(lambda e: e.tensor_tensor(sqt[:], tk[:], tk[:], ALU.mult), r=["tk"], w=["sqt"])
                    for c4 in range(4):
                        T(lambda e, c4=c4: e.matmul(pM[:, 0:1] if False else pTS[:, 0:1], sqt[:, c4 * 128:(c4 + 1) * 128], ones_f[:, 0:1],
                                                    start=(c4 == 0), stop=(c4 == 3)),
                          r=["sqt", "ones_f"], w=["pTS"], inc=(c4 == 3))
                    V(lambda e: e.tensor_tensor(mixT[:, 4:8, t * 128:(t + 1) * 128],
                                                tk[:].rearrange("p (c n) -> p c n", c=4),
                                                gmix[:, 4:8].unsqueeze(2).to_broadcast([128, 4, 128]), ALU.mult),
                      r=["tk", "gmix"], w=["mixT"])
                    V(lambda e: e.tensor_copy(sstok[:, t:t + 1], pTS[:, 0:1]), r=["pTS"], w=["sstok"])

                pTS = pKV[:, 256:512]
                pending = []
                front(0)
                mid(0)
                for t in range(AB_TILES):
                    if t + 1 < AB_TILES:
                        front(t + 1)
                    back_a(t)
                    back_b(t)
                    if t + 1 < AB_TILES:
                        mid(t + 1)
                    for f_ in pending:
                        f_()
                    pending.clear()

                rsqrt_cols(rs_t, sstok, 0, NX, 1.0 / 512, "sstok", "rs_t")
                P.barrier()
                stage_end(2, lambda: [(dbgb[:, 0:8192], KT, "KT"), (dbgb[:, 8192:17408], QT[:].rearrange("p a n -> p (a n)"), "QT"), (dbgb[:, 17408:26624], mixT[:, 4:8, :].rearrange("p a n -> p (a n)"), "mixT"), (dbgb[:, 26624:43136], VVf, "VV"), (dbgf[:, 0:18], rs_t[:, 0:18], "rs_t"), (dbgf[:, 64:128], rstd[:, :], "rstd"), (dbgf[:, 128:132], biasU[:], "biasU"), (dbgf[:, 1024:1536], bias_q[:], "bias_q"), (dbgf[:, 1536:1792], bias_kv[:], "bias_kv"), (dbgf[:, 2048:2560], bias_zv[:], "bias_zv")])

            with ExitStack() as st:
                Pb = [SB(st, f"Pb{i}", [128, 1024], BF16) for i in range(3)]
                Oa = SB(st, "Oa", [128, 512], F32)
                Ob = SB(st, "Ob", [128, 512], F32)
                rden = SB(st, "rden", [128, 512], F32)
                at = SB(st, "at", [128, 512], F32)
                sq = SB(st, "sq", [128, 512], F32)
                acc = SB(st, "acc", [128, 512], F32)
                pS = [PS(st, f"pS{i}", [128, 1024], F32) for i in range(2)]
                pOa = PS(st, "pOa", [128, 512], F32)
                pOb = PS(st, "pOb", [128, 512], F32)
                pBc = PS(st, "pBc", [128, 512], F32)
                pSA = PS(st, "pSA", [128, 512], F32)
                for ct in range(NCT):
                    for ab in range(2):
                        P.op("gpsimd", lambda e: e.dma_start(
                            out=wub[ct].rearrange("p (k a n) -> p k a n", k=8, a=2)[:, :, ab, :],
                            in_=w_up[:, ab * DFF + ct * 128:ab * DFF + (ct + 1) * 128].rearrange("(k p) n -> p k n", p=128)),
                            writes=[f"wub{ct}_{ab}"], dma="d_wub")
                QH = SB(st, "QH", [128, 4, 2], BF16)
                hs = SB(st, "hs", [128, 2], F32)

                def qk(kt, i, p, q0, qn):
                    b = pS[i % 2]
                    T(lambda e: e.matmul(b[:, 0:qn], KT[0:64, kt * 128:(kt + 1) * 128], QT[0:64, p, q0:q0 + qn],
                                         start=True, stop=True), r=["KT", "QT"], w=[f"pS{i % 2}"], inc=False)
                    T(lambda e: e.matmul(b[:, 512:512 + qn], KT[64:128, kt * 128:(kt + 1) * 128],
                                         QT[64:128, p, q0:q0 + qn], start=True, stop=True),
                      r=["KT", "QT"], w=[f"pS{i % 2}"])

                def ex(kt, i):
                    A(lambda e: e.activation(Pb[i % 3][:], pS[i % 2][:], AF.Exp, scale=0.125),
                      r=[f"pS{i % 2}"], w=[f"Pb{i % 3}"])

                def pv(kt, i, qn):
                    pb = Pb[i % 3]
                    T(lambda e: e.matmul(pOa[:, 0:qn], VVk[:, kt, 64:192], pb[:, 0:qn],
                                         start=(kt == 0), stop=(kt == NT - 1)),
                      r=[f"Pb{i % 3}", "VV"], w=["pOa"], inc=False)
                    T(lambda e: e.matmul(pOb[:, 0:qn], VV[:, kt, 1, 0:128], pb[:, 512:512 + qn],
                                         start=(kt == 0), stop=(kt == NT - 1)),
                      r=[f"Pb{i % 3}", "VV"], w=["pOa", "pOb"])

                def epi_a(qn):
                    V(lambda e: e.tensor_copy(Oa[0:65, 0:qn], pOa[0:65, 0:qn]), r=["pOa"], w=["Oa"])
                    V(lambda e: e.tensor_copy(Ob[:, 0:qn], pOb[:, 0:qn]), r=["pOb"], w=["Ob"])
                    V(lambda e: e.reciprocal(rden[64:65, 0:qn], Oa[64:65, 0:qn]), r=["Oa"], w=["rdenA"])
                    V(lambda e: e.reciprocal(rden[0:1, 0:qn], Ob[0:1, 0:qn]), r=["Ob"], w=["rdenB"])

                def epi_b(qn):
                    T(lambda e: e.matmul(pBc[:, 0:qn], ones_f[64:65, :], rden[64:65, 0:qn], start=True, stop=True),
                      r=["rdenA", "ones_f"], w=["pBc"], inc=False)
                    T(lambda e: e.matmul(pSA[:, 0:qn], ones_f[0:1, :], rden[0:1, 0:qn], start=True, stop=True),
                      r=["rdenB", "ones_f"], w=["pBc", "pSA"])
                    V(lambda e: e.tensor_tensor(at[0:64, 0:qn], Oa[0:64, 0:qn], pBc[0:64, 0:qn], ALU.mult),
                      r=["Oa", "pBc"], w=["at"])
                    V(lambda e: e.tensor_tensor(at[64:128, 0:qn], Ob[64:128, 0:qn], pSA[64:128, 0:qn], ALU.mult),
                      r=["Ob", "pSA"], w=["at"])

                def epi_main(qi, p, q0):
                    epi_b(512)
                    V(lambda e: e.tensor_scalar(mixT[:, p, q0:q0 + 512], at[:, :], gmix[:, p:p + 1], None, ALU.mult),
                      r=["at", "gmix"], w=["mixT"])
                    if p == 0:
                        G(lambda e: e.tensor_tensor(acc[:, :], at[:, :], at[:, :], ALU.mult), r=["at"], w=["acc"])
                    else:
                        G(lambda e: e.tensor_tensor(sq[:, :], at[:, :], at[:, :], ALU.mult), r=["at"], w=["sq"])
                        G(lambda e: e.tensor_tensor(acc[:, :], acc[:, :], sq[:, :], ALU.add), r=["acc", "sq"], w=["acc"])

                def epi_ss(qi):
                    for w_ in range(4):
                        T(lambda e, w_=w_: e.matmul(pSA[:, w_:w_ + 1], acc[:, w_ * 128:(w_ + 1) * 128], ones_f[:, 0:1],
                                                    start=True, stop=True), r=["acc", "ones_f"], w=["pSA"], inc=(w_ == 3))
                    V(lambda e: e.tensor_copy(ssatt[:, 1 + qi * 4:1 + qi * 4 + 4], pSA[:, 0:4]), r=["pSA"], w=["ssatt"])

                V(lambda e: e.tensor_copy(QH[:, :, 0:1], QT[:, :, 127:128]), r=["QT"], w=["QH"])
                V(lambda e: e.tensor_copy(QH[:, :, 1:2], QT[:, :, 2176:2177]), r=["QT"], w=["QH"])
                for g in range(2):
                    for kt in range(NT):
                        T(lambda e: e.matmul(pS[0][:, g * 512 + kt * 8:g * 512 + kt * 8 + 8],
                                             KT[g * 64:(g + 1) * 64, kt * 128:(kt + 1) * 128], QH[g * 64:(g + 1) * 64, :, :],
                                             start=True, stop=True), r=["KT", "QH"], w=["pS0"], inc=(kt == NT - 1))
                ex(0, 0)
                for kt in range(NT):
                    T(lambda e: e.matmul(pOa[0:65, 0:8], VV[:, kt, 0, 64:129], Pb[0][:, kt * 8:kt * 8 + 8],
                                         start=(kt == 0), stop=(kt == NT - 1)), r=["Pb0", "VV"], w=["pOa"], inc=(kt == NT - 1))
                for kt in range(NT):
                    T(lambda e: e.matmul(pOb[:, 0:8], VV[:, kt, 1, 0:128], Pb[0][:, 512 + kt * 8:512 + kt * 8 + 8],
                                         start=(kt == 0), stop=(kt == NT - 1)), r=["Pb0", "VV"], w=["pOb"], inc=(kt == NT - 1))
                epi_a(8)
                epi_b(8)
                at3 = at[:, 0:8].rearrange("p (a t) -> p a t", t=2)
                for tk_, col in ((0, 127), (1, 2176)):
                    V(lambda e: e.tensor_tensor(mixT[:, 0:4, col:col + 1], at3[:, :, tk_:tk_ + 1], gmix[:, 0:4].unsqueeze(2), ALU.mult),
                      r=["at", "gmix"], w=["mixT"])
                V(lambda e: e.tensor_tensor(sq[:, 0:8], at[:, 0:8], at[:, 0:8], ALU.mult), r=["at"], w=["sq"])
                V(lambda e: e.tensor_reduce(hs[:, 0:2], sq[:, 0:8].rearrange("p (a t) -> p t a", t=2), axis=AX.X, op=ALU.add),
                  r=["sq"], w=["hs"])
                V(lambda e: e.memset(acc[:, 0:256], 0.0), w=["acc"])
                V(lambda e: e.tensor_copy(acc[:, 127:129], hs[:, 0:2]), r=["hs"], w=["acc"])
                for w_ in range(2):
                    T(lambda e, w_=w_: e.matmul(pSA[:, w_:w_ + 1], acc[:, w_ * 128:(w_ + 1) * 128], ones_f[:, 0:1],
                                                start=True, stop=True), r=["acc", "ones_f"], w=["pSA"], inc=(w_ == 1))
                V(lambda e: e.tensor_copy(ssatt[:, 0:1], pSA[:, 0:1]), r=["pSA"], w=["ssatt"])
                V(lambda e: e.tensor_copy(ssatt[:, 17:18], pSA[:, 1:2]), r=["pSA"], w=["ssatt"])

                iters = [(qi, p) for qi in range(4) for p in range(4)]
                step = 1
                pend = None
                for (qi, p) in iters:
                    q0 = 128 + qi * 512
                    qk(0, step, p, q0, 512)
                    qk(1, step + 1, p, q0, 512)
                    for kt in range(NT):
                        ex(kt, step + kt)
                        if kt + 2 < NT:
                            qk(kt + 2, step + kt + 2, p, q0, 512)
                        pv(kt, step + kt, 512)
                        if kt == 3 and pend is not None:
                            epi_main(*pend)
                        if kt == 16 and pend is not None:
                            if pend[1] == 3:
                                epi_ss(pend[0])
                            pend = None
                    step += NT
                    epi_a(512)
                    pend = (qi, p, q0)
                epi_main(*pend)
                epi_ss(3)

                rsqrt_cols(rs_a, ssatt, 0, NX, 1.0 / 512, "ssatt", "rs_a")
                P.barrier()
                stage_end(3, lambda: [(dbgb[:, 0:9216], mixT[:, 0:4, :].rearrange("p a n -> p (a n)"), "mixT"), (dbgf[:, 0:18], rs_a[:, 0:18], "rs_a")])

            x1nT = AR1[:, 0:8 * NCOLS].rearrange("p (k n) -> p k n", k=8)
            with ExitStack() as st2:
                G2_b = SB(st2, "G2_b", [128, D], F32)
                Wout = SB(st2, "Wout", [128, 8, D], BF16)
                xe = [SB(st2, f"xe{i}", [128, D], F32) for i in range(2)]
                x1h = [SB(st2, f"x1h{i}", [128, D], F32) for i in range(2)]
                t1 = SB(st2, "t1", [128, D], F32)
                x1n = [SB(st2, f"x1n{i}", [128, D], BF16) for i in range(2)]
                junk2 = SB(st2, "junk2", [128, D], BF16)
                pYa = PS(st2, "pYa", [128, D], F32)
                pYb = PS(st2, "pYb", [128, D], F32)
                pT2 = [PS(st2, f"pT2{i}", [128, D], BF16) for i in range(2)]
                LD(G2_b[:], scr[0], "G2_b", "d_c2")
                for kk in range(2):
                    P.op("gpsimd", lambda e, kk=kk: e.dma_start(
                        out=Wout[:, 4 * kk:4 * kk + 4, :],
                        in_=w_out[512 * kk:512 * kk + 512, :].rearrange("(k p) n -> p k n", p=128)),
                        writes=["Wout"], dma="d_wout")
                def out_mm(t):
                    s = t % 2
                    LD(xe[s][:], x_rot[t * 128:(t + 1) * 128, :], f"xe{s}")
                    for c2 in range(2):
                        for k in range(4):
                            T(lambda e, k=k, c2=c2: e.matmul(pYa[:, c2 * 512:(c2 + 1) * 512], mixT[:, k, t * 128:(t + 1) * 128],
                                                             Wout[:, k, c2 * 512:(c2 + 1) * 512], start=(k == 0), stop=(k == 3)),
                              r=["mixT", "Wout"], w=["pYa"], inc=(k == 3 and c2 == 1))
                    for c2 in range(2):
                        for k in range(4, 8):
                            T(lambda e, k=k, c2=c2: e.matmul(pYb[:, c2 * 512:(c2 + 1) * 512], mixT[:, k, t * 128:(t + 1) * 128],
                                                             Wout[:, k, c2 * 512:(c2 + 1) * 512], start=(k == 4), stop=(k == 7)),
                              r=["mixT", "Wout"], w=["pYb"], inc=(k == 7 and c2 == 1))

                def out_ep1(t):
                    s = t % 2
                    xd = x1h[s][:]
                    xdn = f"x1h{s}"
                    A(lambda e: e.activation(t1[:], pYa[:], AF.Identity, scale=rs_a[:, t:t + 1]), r=["pYa", "rs_a"], w=["t1"])
                    V(lambda e: e.scalar_tensor_tensor(t1[:], pYb[:], rs_t[:, t:t + 1], t1[:], ALU.mult, ALU.add),
                      r=["pYb", "rs_t", "t1"], w=["t1"])

                def out_ep2(t):
                    s = t % 2
                    xd = x1h[s][:]
                    xdn = f"x1h{s}"
                    V(lambda e: e.tensor_tensor(t1[:], t1[:], gt1_b[:], ALU.mult), r=["t1", "gt1_b"], w=["t1"])
                    V(lambda e: e.tensor_tensor(xd, t1[:], xe[s][:], ALU.add), r=["t1", f"xe{s}"], w=[xdn])
                    if 1 <= t <= 16:
                        P.op("sync", lambda e: e.dma_start(out=x1s[(t - 1) * 128:t * 128, :], in_=xd), reads=[xdn], dma=f"d_x1s{s}")
                    A(lambda e: e.activation(junk2[:], xd, AF.Square, accum_out=ss2[:, t:t + 1]), r=[xdn], w=["junk2", "ss2"])
                    rsqrt_cols(rs2, ss2, t, t + 1, 1.0 / D, "ss2", "rs2")
                    V(lambda e: e.scalar_tensor_tensor(x1n[s][:], xd, rs2[:, t:t + 1], G2_b[:], ALU.mult, ALU.mult),
                      r=[xdn, "rs2", "G2_b"], w=[f"x1n{s}"])
                    for k in range(8):
                        T(lambda e, k=k: e.transpose(pT2[s][:, k * 128:(k + 1) * 128], x1n[s][:, k * 128:(k + 1) * 128], ident_b[:]),
                          r=[f"x1n{s}", "ident_b"], w=[f"pT2{s}"], inc=(k == 7))
                    V(lambda e: e.tensor_copy(x1nT[:, :, t * 128:(t + 1) * 128], pT2[s][:].rearrange("p (k n) -> p k n", k=8)),
                      r=[f"pT2{s}"], w=["x1nT"])

                out_mm(0)
                for t in range(NX):
                    out_ep1(t)
                    if t + 1 < NX:
                        out_mm(t + 1)
                    out_ep2(t)
                P.barrier()
                stage_end(4, lambda: [(dbgb[:, 0:18432], AR1[:, 0:18432], "x1nT")])
                print("sbuf remaining at OUT:", nc.sbuf_bytes_remaining)

            stA.close()
            with ExitStack() as st2:
                NWU = 2
                GW = 2
                groups = [(c, min(GW, NCT - c)) for c in range(0, NCT, GW)]
                NG = len(groups)
                gt2_b = SB(st2, "gt2_b", [128, D], F32)
                gfin_b = SB(st2, "gfin_b", [128, D], F32)
                gT = SB(st2, "gT", [128, NCT, 512], BF16)
                tail0 = 8 * NCOLS
                wu = [AR1[:, tail0:tail0 + 8 * 2 * GW * 128].rearrange("p (c f) -> p c f", c=GW),
                      SB(st2, "wu1", [128, GW, 8 * 2 * 128], BF16)]
                Wd = SB(st2, "Wd", [128, NCT, D], BF16)
                zb = [SB(st2, f"zb{i}", [128, 2, 514], F32) for i in range(2)]
                cv = [SB(st2, f"cv{i}", [128, 2, 512], F32) for i in range(2)]
                sl_ = [SB(st2, f"sl{i}", [128, 512], F32) for i in range(2)]
                y2q = SB(st2, "y2q", [128, 4, D], F32)
                xr = [SB(st2, f"xr{i}", [128, D], F32) for i in range(2)]
                junk3 = AR1[:, tail0 + 8 * 2 * GW * 128:tail0 + 8 * 2 * GW * 128 + D]
                ot = [SB(st2, f"ot{i}", [128, D], F32) for i in range(2)]
                pZ = [PS(st2, f"pZ{i}", [128, 512], F32) for i in range(4)]
                pY = [PS(st2, f"pY{i}", [128, 512], F32) for i in range(4)]
                LD(gt2_b[:], scr[1], "gt2_b")
                LD(gfin_b[:], gfin_bd, "gfin_b")
                print("sbuf remaining at FFN:", nc.sbuf_bytes_remaining)

                def ld_wu(gg):
                    c0_, n_ = groups[gg % NG]
                    s = gg % NWU
                    P.op("sync", lambda e: e.dma_start(out=wu[s][:, 0:n_, :], in_=wub[c0_:c0_ + n_].rearrange("c p f -> p c f")),
                         writes=[f"wu{s}"], dma=f"d_wu{s}")

                def wsl(qtr, ct, k, ab):
                    gi = ct // GW
                    s = (qtr * NG + gi) % NWU
                    j = ct - groups[gi][0]
                    return wu[s][:, j, :].rearrange("p (k a n) -> p k a n", k=8, a=2)[:, k, ab, :], f"wu{s}"

                def ld_wd(ct):
                    P.op("gpsimd", lambda e: e.dma_start(out=Wd[:, ct, :], in_=w_down[ct * 128:(ct + 1) * 128, :]),
                         writes=[f"Wd{ct}"], dma=f"d_Wd{ct}")

                def up_tail(qtr, ct):
                    ci = ct % 2
                    A(lambda e: e.activation(sl_[ci][:], cv[ci][:, 0, :], AF.Silu), r=[f"cv{ci}"], w=[f"sl{ci}"])
                    V(lambda e: e.tensor_tensor(gT[:, ct, :], sl_[ci][:], cv[ci][:, 1, :], ALU.mult),
                      r=[f"sl{ci}", f"cv{ci}"], w=["gT"])
                    if qtr >= 1:
                        down_mm(ct, 0)

                def down_mm(ct, c2):
                    for tt in range(4):
                        T(lambda e, tt=tt: e.matmul(pY[tt][:, :], gT[:, ct, tt * 128:(tt + 1) * 128],
                                                    Wd[:, ct, c2 * 512:(c2 + 1) * 512],
                                                    start=(ct == 0), stop=(ct == NCT - 1)),
                          r=["gT", f"Wd{ct}"], w=[f"pY{tt}"], inc=(ct == NCT - 1 or tt == 3))

                def bias2_for(ct):
                    pb = pY[ct % 4]
                    for ab in range(2):
                        for k in range(8):
                            wl, wn = wsl(0, ct, k, ab)
                            T(lambda e, k=k, ab=ab, wl=wl: e.matmul(pb[:, ab:ab + 1], wl, sh2T[:, k:k + 1], start=(k == 0), stop=(k == 7)),
                              r=[wn, "sh2T"], w=[f"pY{ct % 4}"], inc=(k == 7 and ab == 1))
                    A(lambda e: e.copy(bias2[:, ct:ct + 1], pb[:, 0:1]), r=[f"pY{ct % 4}"], w=["bias2"])
                    A(lambda e: e.copy(bias2[:, 22 + ct:23 + ct], pb[:, 1:2]), r=[f"pY{ct % 4}"], w=["bias2"])

                def fin_tail(qtr, tt):
                    ti = qtr * 4 + tt
                    os_ = ti % 2
                    V(lambda e: e.tensor_tensor(y2q[:, tt, :], y2q[:, tt, :], xr[tt % 2][:], ALU.add),
                      r=[f"y2q{tt}", f"xr{tt % 2}"], w=[f"y2q{tt}"])
                    if tt < 2:
                        LD(xr[tt % 2][:], x1s[(ti + 2) * 128:(ti + 3) * 128, :], f"xr{tt % 2}")
                    A(lambda e: e.activation(junk3, y2q[:, tt, :], AF.Square, accum_out=ssf[:, ti:ti + 1]),
                      r=[f"y2q{tt}"], w=["junk3", "ssf"])
                    rsqrt_cols(rsf, ssf, ti, ti + 1, 1.0 / D, "ssf", "rsf")
                    V(lambda e: e.scalar_tensor_tensor(ot[os_][:], y2q[:, tt, :], rsf[:, ti:ti + 1], gfin_b[:],
                                                       ALU.mult, ALU.mult),
                      r=[f"y2q{tt}", "rsf", "gfin_b"], w=[f"ot{os_}"])
                    P.op("sync", lambda e: e.dma_start(out=out_d[ti * 128:(ti + 1) * 128, :], in_=ot[os_][:]),
                         reads=[f"ot{os_}"], dma=f"d_out{os_}")

                ld_wu(0)
                for qtr in range(4):
                    c0 = 127 + qtr * 512
                    for ct in range(NCT):
                        zi = ct % 2
                        ci = ct % 2
                        if ct % GW == 0:
                            gg = qtr * NG + ct // GW
                            if gg + 1 < 4 * NG:
                                ld_wu(gg + 1)
                        if qtr == 0 and ct == 0:
                            bias2_for(0)
                        for ab in range(2):
                            for blk in range(2):
                                pz = pZ[ab * 2 + blk]
                                for k in range(8):
                                    wl, wn = wsl(qtr, ct, k, ab)
                                    T(lambda e, k=k, wl=wl: e.matmul(
                                        pz[:, 0:257], wl,
                                        x1nT[:, k, c0 + blk * 257:c0 + (blk + 1) * 257], start=(k == 0), stop=(k == 7)),
                                      r=[wn, "x1nT"], w=[f"pZ{ab * 2 + blk}"], inc=(k == 7))
                                A(lambda e: e.activation(
                                    zb[zi][:, ab, blk * 257:(blk + 1) * 257], pz[:, 0:257], AF.Identity,
                                    bias=bias2[:, ab * 22 + ct:ab * 22 + ct + 1]),
                                  r=[f"pZ{ab * 2 + blk}", "bias2"], w=[f"zb{zi}"])
                        if qtr == 0 and ct + 1 < NCT:
                            bias2_for(ct + 1)
                        if qtr == 0:
                            ld_wd(ct)
                        if qtr > 0 and ct < 4:
                            fin_tail(qtr - 1, ct)
                        if ct == 4:
                            for tt in range(2):
                                LD(xr[tt][:], x1s[(qtr * 4 + tt) * 128:(qtr * 4 + tt + 1) * 128, :], f"xr{tt}")
                        if ct > 0:
                            up_tail(qtr, ct - 1)
                        if qtr == 0:
                            V(lambda e: e.tensor_scalar(zb[zi][:, :, 0:1], zb[zi][:, :, 0:1], hm[:, 0:1], None, ALU.mult),
                              r=[f"zb{zi}", "hm"], w=[f"zb{zi}"])
                        if qtr == 3:
                            V(lambda e: e.tensor_scalar(zb[zi][:, :, 513:514], zb[zi][:, :, 513:514], hm[:, 1:2], None, ALU.mult),
                              r=[f"zb{zi}", "hm"], w=[f"zb{zi}"])
                        for ab in range(2):
                            ch = ab * 22 + ct
                            if ab == 0:
                                A(lambda e: e.activation(cv[ci][:, ab, :], zb[zi][:, ab, 1:513], AF.Identity,
                                                         bias=bconv[:, ch:ch + 1], scale=wconv[:, 1, ch:ch + 1]),
                                  r=[f"zb{zi}", "wconv", "bconv"], w=[f"cv{ci}"])
                            else:
                                G(lambda e: e.tensor_scalar(cv[ci][:, ab, :], zb[zi][:, ab, 1:513], wconv[:, 1, ch:ch + 1],
                                                            bconv[:, ch:ch + 1], ALU.mult, ALU.add),
                                  r=[f"zb{zi}", "wconv", "bconv"], w=[f"cv{ci}"])
                        for ab in range(2):
                            ch = ab * 22 + ct
                            V(lambda e: e.scalar_tensor_tensor(cv[ci][:, ab, :], zb[zi][:, ab, 0:512], wconv[:, 0, ch:ch + 1],
                                                               cv[ci][:, ab, :], ALU.mult, ALU.add),
                              r=[f"zb{zi}", "wconv", f"cv{ci}"], w=[f"cv{ci}"])
                            V(lambda e: e.scalar_tensor_tensor(cv[ci][:, ab, :], zb[zi][:, ab, 2:514], wconv[:, 2, ch:ch + 1],
                                                               cv[ci][:, ab, :], ALU.mult, ALU.add),
                              r=[f"zb{zi}", "wconv", f"cv{ci}"], w=[f"cv{ci}"])
                    up_tail(qtr, NCT - 1)
                    for c2 in range(2):
                        if not (qtr >= 1 and c2 == 0):
                            for ct in range(NCT):
                                down_mm(ct, c2)
                        for tt in range(4):
                            V(lambda e, tt=tt: e.tensor_tensor(y2q[:, tt, c2 * 512:(c2 + 1) * 512], pY[tt][:, :],
                                                               gt2_b[:, c2 * 512:(c2 + 1) * 512], ALU.mult),
                              r=[f"pY{tt}", "gt2_b"], w=[f"y2q{tt}"])
                    if qtr == 3:
                        for tt in range(4):
                            fin_tail(3, tt)
                P.barrier()
        print("kernel build: inst", P.ninst, "waits", P.nwait, "sems", len(P.sem), P.cnt)
    except _Stop:
        pass
    return nc


def _prep_inputs(inp):
    x = np.asarray(inp["x"], np.float32)
    c = np.asarray(inp["c"], np.float32)
    f = lambda k: np.asarray(inp[k], np.float32)
    rows = S // 64
    row_id = np.repeat(np.arange(rows), 64).astype(np.float32)
    col_id = np.tile(np.arange(64), rows).astype(np.float32)
    inv_freq = np.power(np.float32(10000.0), -np.arange(0, 32, 2, dtype=np.float32) / np.float32(32)).astype(np.float32)
    ang_r = row_id[:, None] * inv_freq[None, :]
    ang_c = col_id[:, None] * inv_freq[None, :]
    ang = np.concatenate([ang_r, ang_r, ang_c, ang_c], axis=-1).astype(np.float32)
    cos = np.cos(ang).astype(np.float32)
    sin = np.sin(ang).astype(np.float32)
    sgn = np.concatenate([-np.ones(16), np.ones(16), -np.ones(16), np.ones(16)]).astype(np.float32)
    sinS = sin * sgn[None, :]
    perm_h = [0, 4, 1, 5, 2, 6, 3, 7]
    qcols = np.concatenate([np.arange(h * 64, (h + 1) * 64) for h in perm_h])
    w_in = f("w_in")[0].copy()
    w_in[:, 0:512] = w_in[:, qcols]
    w_out = f("w_out")[0].copy()
    w_out[0:512, :] = w_out[qcols, :]
    g_attn = f("g_attn_out")[0][qcols]
    gmixv = np.concatenate([g_attn, f("g_tok_out")[0]])
    gmix = np.ascontiguousarray(gmixv.reshape(8, 128).T)
    rep = lambda v, n=128: np.ascontiguousarray(np.broadcast_to(v[None, :], (n, v.shape[0])))
    wsT = np.ascontiguousarray(f("w_s")[0].transpose(2, 0, 1))
    b_s = f("b_s")[0]
    bsbT = np.zeros((128, 4, 128), np.float32)
    for c4 in range(4):
        bsbT[0:64, c4, :] = b_s[2 * c4][None, :]
        bsbT[64:128, c4, :] = b_s[2 * c4 + 1][None, :]
    wconv = np.ascontiguousarray(f("w_conv")[0].reshape(3, 44, 128).transpose(2, 0, 1))
    bconv = np.ascontiguousarray(f("b_conv")[0].reshape(44, 128).T)
    shared = {
        "w_ada": f("w_ada")[0], "b_ada": f("b_ada"), "g1_b": rep(f("g_norm1")[0]), "g2_b": rep(f("g_norm2")[0]),
        "gfin_b": rep(f("g_final")), "w_in": w_in, "w_out": w_out, "w_up": f("w_up")[0], "w_down": f("w_down")[0],
        "gq_b": rep(f("g_q")[0]), "gk_b": rep(f("g_k")[0]), "gtv_b": rep(f("g_tok_v")[0]), "wsT": wsT,
        "bsbT": bsbT.reshape(128, 512), "gmix": gmix, "wconv": wconv, "bconv": bconv,
        "ident": np.eye(128, dtype=np.float32),
    }
    in_maps = []
    for core in range(8):
        b, j = core // 4, core % 4
        sh = j * 2048 - 128
        xr = np.roll(x[b], -sh, axis=0)
        cr = np.roll(cos, -sh, axis=0).reshape(NT, 128, 64).transpose(1, 0, 2)
        sr = np.roll(sinS, -sh, axis=0).reshape(NT, 128, 64).transpose(1, 0, 2)
        hmk = np.ones((128, 2), np.float32)
        if j == 0:
            hmk[:, 0] = 0.0
        if j == 3:
            hmk[:, 1] = 0.0
        m = dict(shared)
        m.update({"x_rot": np.ascontiguousarray(xr), "cos_t": np.ascontiguousarray(cr), "sin_t": np.ascontiguousarray(sr),
                  "cb": np.ascontiguousarray(c[b].reshape(8, 128).T), "hmask": hmk})
        in_maps.append(m)
    return in_maps


_NC_CACHE = {}


def kernel(**inputs):
    in_maps = _prep_inputs(inputs)
    if "nc" not in _NC_CACHE:
        _NC_CACHE["nc"] = build_nc()
    nc = _NC_CACHE["nc"]
    res = run_bass_kernel_spmd(nc, in_maps, core_ids=list(range(8)))
    out = np.zeros((2, S, D), np.float32)
    for core in range(8):
        b, j = core // 4, core % 4
        out[b, j * 2048:(j + 1) * 2048, :] = res.results[core]["out"]
    return out
```

```python
import numpy as np
import concourse.bass as bass
import concourse.mybir as mybir
from concourse.bass_utils import run_bass_kernel_spmd
from contextlib import ExitStack

F32 = mybir.dt.float32
BF16 = mybir.dt.bfloat16
AF = mybir.ActivationFunctionType
ALU = mybir.AluOpType
AX = mybir.AxisListType

ENGS = ("sync", "scalar", "vector", "gpsimd", "tensor")
D = 1024
S = 8192
NT = 64
NX = 18
NCOLS = NX * 128
DFF = 2816
NCT = 22
EPS = 1e-6
AB_TILES = NT


class _Stop(Exception):
    pass


class Prog:
    def __init__(self, nc, stack):
        self.nc = nc
        self.stack = stack
        self.sem = {}
        self.cnt = {}
        self.waited = {e: {} for e in ENGS}
        self.lastw = {}
        self.readers = {}
        self.ninst = 0
        self.nwait = 0

    def getsem(self, name):
        if name not in self.sem:
            self.sem[name] = self.stack.enter_context(self.nc.semaphore(name))
            self.cnt[name] = 0
        return self.sem[name]

    def _wait(self, eng, tok):
        s, v = tok
        if self.waited[eng].get(s, 0) >= v:
            return
        self.waited[eng][s] = v
        getattr(self.nc, eng).wait_ge(self.sem[s], v)
        self.nwait += 1

    def op(self, eng, fn, reads=(), writes=(), dma=None, inc=True):
        deps = []
        for r in reads:
            if r in self.lastw:
                deps.append(self.lastw[r])
        for w in writes:
            if w in self.lastw:
                deps.append(self.lastw[w])
            for s, v in self.readers.get(w, {}).items():
                deps.append((s, v))
        for t in deps:
            if eng == "tensor" and t[0] == "e_tensor":
                continue
            self._wait(eng, t)
        e = getattr(self.nc, eng)
        tok = None
        if dma is not None:
            self.getsem(dma)
            self.cnt[dma] += 16
            tok = (dma, self.cnt[dma])
            fn(e).then_inc(self.sem[dma], 16)
        elif inc:
            sn = "e_" + eng
            self.getsem(sn)
            self.cnt[sn] += 1
            tok = (sn, self.cnt[sn])
            fn(e).then_inc(self.sem[sn], 1)
        else:
            fn(e)
        self.ninst += 1
        if tok is not None:
            for r in reads:
                d = self.readers.setdefault(r, {})
                d[tok[0]] = max(d.get(tok[0], 0), tok[1])
            for w in writes:
                self.lastw[w] = tok
                self.readers[w] = {}
        return tok

    def barrier(self):
        for eng in ENGS:
            for s, v in self.cnt.items():
                if v > 0:
                    self._wait(eng, (s, v))
        self.lastw = {}
        self.readers = {}


def build_nc(debug_stage=0):
    nc = bass.Bass("TRN2", target_bir_lowering=False)
    di = lambda name, shape: nc.dram_tensor(name, shape, F32, kind="ExternalInput").ap()
    x_rot = di("x_rot", [S, D])
    cos_t = di("cos_t", [128, NT, 64])
    sin_t = di("sin_t", [128, NT, 64])
    cb = di("cb", [128, 8])
    hmask = di("hmask", [128, 2])
    w_ada = di("w_ada", [D, 6 * D])
    b_ada = di("b_ada", [1, 6 * D])
    g1_bd = di("g1_b", [128, D])
    g2_bd = di("g2_b", [128, D])
    gfin_bd = di("gfin_b", [128, D])
    w_in = di("w_in", [D, 1792])
    w_out = di("w_out", [D, D])
    w_up = di("w_up", [D, 2 * DFF])
    w_down = di("w_down", [DFF, D])
    gq_bd = di("gq_b", [128, 64])
    gk_bd = di("gk_b", [128, 64])
    gtv_bd = di("gtv_b", [128, 64])
    wsT_d = di("wsT", [128, 8, 128])
    bsbT_d = di("bsbT", [128, 512])
    gmix_d = di("gmix", [128, 8])
    wconv_d = di("wconv", [128, 3, 44])
    bconv_d = di("bconv", [128, 44])
    ident_d = di("ident", [128, 128])
    out_d = nc.dram_tensor("out", [2048, D], F32, kind="ExternalOutput").ap()
    scr = nc.dram_tensor("scr", [2, 128, D], F32, kind="Internal").ap()
    x1s = nc.dram_tensor("x1s", [2048, D], F32, kind=("ExternalOutput" if debug_stage else "Internal")).ap()
    wub = nc.dram_tensor("wub", [NCT, 128, 8 * 2 * 128], BF16, kind="Internal").ap()
    if debug_stage:
        dbgf = nc.dram_tensor("dbgf", [128, 8192], F32, kind="ExternalOutput").ap()
        dbgb = nc.dram_tensor("dbgb", [128, 65536], BF16, kind="ExternalOutput").ap()

    try:
      with ExitStack() as st0:
        P = Prog(nc, st0)

        def dump(dst, src, res):
            tok = P.op("sync", lambda e: e.dma_start(out=dst, in_=src), reads=[res], dma="d_dbg")
            P._wait("sync", tok)

        def stage_end(k, dumps):
            if debug_stage == k:
                for (dst, src, res) in dumps():
                    dump(dst, src, res)
                print("STOP at stage", k, "inst", P.ninst, "waits", P.nwait, "cnt", P.cnt)
                raise _Stop()

        def SB(st, name, shape, dt):
            return st.enter_context(nc.sbuf_tensor("s_" + name, shape, dt))

        def PS(st, name, shape, dt):
            return st.enter_context(nc.psum_tensor("p_" + name, shape, dt))

        V = lambda fn, r=(), w=(): P.op("vector", fn, r, w)
        A = lambda fn, r=(), w=(): P.op("scalar", fn, r, w)
        G = lambda fn, r=(), w=(): P.op("gpsimd", fn, r, w)
        T = lambda fn, r=(), w=(), inc=True: P.op("tensor", fn, r, w, inc=inc)

        def LD(dst, src, res, sem=None, eng="sync"):
            return P.op(eng, lambda e: e.dma_start(out=dst, in_=src), writes=[res], dma="d_" + res)

        ident_b = SB(st0, "ident_b", [128, 128], BF16)
        ones_f = SB(st0, "ones_f", [128, 128], F32)
        epsc = SB(st0, "epsc", [128, 1], F32)
        G1_b = SB(st0, "G1_b", [128, D], F32)
        gt1_b = SB(st0, "gt1_b", [128, D], F32)
        bias_q = SB(st0, "bias_q", [128, 512], F32)
        bias_kv = SB(st0, "bias_kv", [128, 256], F32)
        bias_zv = SB(st0, "bias_zv", [128, 512], F32)
        biasU = SB(st0, "biasU", [128, 4], F32)
        sh1T = SB(st0, "sh1T", [128, 8], BF16)
        sh2T = SB(st0, "sh2T", [128, 8], BF16)
        gq_b = SB(st0, "gq_b", [128, 64], F32)
        gk_b = SB(st0, "gk_b", [128, 64], F32)
        gtv_b = SB(st0, "gtv_b", [128, 64], F32)
        gmix = SB(st0, "gmix", [128, 8], F32)
        wconv = SB(st0, "wconv", [128, 3, 44], F32)
        bconv = SB(st0, "bconv", [128, 44], F32)
        bias2 = SB(st0, "bias2", [128, 44], F32)
        hm = SB(st0, "hm", [128, 2], F32)
        ss = SB(st0, "ss", [128, NT], F32)
        lnv = SB(st0, "lnv", [128, NT], F32)
        rstd = SB(st0, "rstd", [128, NT], F32)
        sstok = SB(st0, "sstok", [128, NX], F32)
        ssatt = SB(st0, "ssatt", [128, 20], F32)
        rs_t = SB(st0, "rs_t", [128, NX], F32)
        rs_a = SB(st0, "rs_a", [128, 20], F32)
        ss2 = SB(st0, "ss2", [128, NX], F32)
        rs2 = SB(st0, "rs2", [128, NX], F32)
        ssf = SB(st0, "ssf", [128, 16], F32)
        rsf = SB(st0, "rsf", [128, 16], F32)
        sm = SB(st0, "sm", [128, 16], F32)

        LD(ident_b[:], ident_d, "ident_b", "d_c0", eng="gpsimd")
        V(lambda e: e.memset(ones_f[:], 1.0), w=["ones_f"])
        V(lambda e: e.memset(epsc[:], EPS), w=["epsc"])
        for nm, tl, src in [("gq_b", gq_b, gq_bd), ("gk_b", gk_b, gk_bd), ("gtv_b", gtv_b, gtv_bd),
                            ("gmix", gmix, gmix_d), ("wconv", wconv, wconv_d), ("bconv", bconv, bconv_d),
                            ("hm", hm, hmask)]:
            LD(tl[:], src, nm, "d_c1")
        for nm, tl in [("ss", ss), ("sstok", sstok), ("ssatt", ssatt), ("ss2", ss2), ("ssf", ssf), ("rstd", rstd), ("sm", sm)]:
            V(lambda e, tl=tl: e.memset(tl[:], 0.0), w=[nm])

        def rsqrt_cols(dst, src, lo, hi, scale, rn, wn, tmp=None):
            A(lambda e: e.activation(dst[:, lo:hi], src[:, lo:hi], AF.Ln, bias=epsc[:, 0:1], scale=scale),
              r=[rn, "epsc"], w=[wn])
            A(lambda e: e.activation(dst[:, lo:hi], dst[:, lo:hi], AF.Exp, scale=-0.5), r=[wn], w=[wn])

        with ExitStack() as st:
            wada = [SB(st, f"wada{i}", [128, 8, D], BF16) for i in range(2)]
            cbt = SB(st, "cbt", [128, 8], F32)
            scT = SB(st, "scT", [128, 8], BF16)
            brow = SB(st, "brow", [1, 6 * D], F32)
            mrow = [SB(st, f"mrow{i}", [1, D], F32) for i in range(2)]
            gtmp = SB(st, "gtmp", [128, D], F32)
            pMod = PS(st, "pMod", [128, D], F32)
            pB = PS(st, "pB", [128, D], F32)
            pX = PS(st, "pX", [128, 8], F32)
            LD(cbt[:], cb, "cbt", "d_c1")
            LD(brow[:], b_ada, "brow", "d_c1")
            A(lambda e: e.activation(scT[:], cbt[:], AF.Silu), r=["cbt"], w=["scT"])
            for m in range(6):
                wb = wada[m % 2]
                wn = f"wada{m % 2}"
                for kk in range(2):
                    P.op("gpsimd", lambda e, kk=kk, m=m, wb=wb: e.dma_start(
                        out=wb[:, 4 * kk:4 * kk + 4, :],
                        in_=w_ada[512 * kk:512 * kk + 512, m * D:(m + 1) * D].rearrange("(k p) n -> p k n", p=128)),
                        writes=[wn], dma=f"d_wada{m % 2}")
                for c2 in range(2):
                    for k in range(8):
                        T(lambda e, k=k, c2=c2, wb=wb: e.matmul(pMod[0:1, c2 * 512:(c2 + 1) * 512], scT[:, k:k + 1],
                                                                wb[:, k, c2 * 512:(c2 + 1) * 512],
                                                                start=(k == 0), stop=(k == 7)),
                          r=[wn, "scT"], w=["pMod"], inc=(k == 7))
                mr = mrow[m % 2]
                mn = f"mrow{m % 2}"
                V(lambda e, m=m, mr=mr: e.tensor_tensor(mr[:], pMod[0:1, :], brow[0:1, m * D:(m + 1) * D], ALU.add),
                  r=["pMod", "brow"], w=[mn])
                if m in (0, 3):
                    for k in range(8):
                        T(lambda e, k=k, mr=mr: e.matmul(pX[:, k:k + 1], mr[0:1, k * 128:(k + 1) * 128],
                                                         ones_f[0:1, 0:1], start=True, stop=True),
                          r=[mn, "ones_f"], w=["pX"], inc=(k == 7))
                    dst, dn = (sh1T, "sh1T") if m == 0 else (sh2T, "sh2T")
                    V(lambda e, dst=dst: e.tensor_copy(dst[:], pX[:]), r=["pX"], w=[dn])
                else:
                    for c2 in range(2):
                        T(lambda e, c2=c2, mr=mr: e.matmul(pB[:, c2 * 512:(c2 + 1) * 512], ones_f[0:1, :],
                                                           mr[0:1, c2 * 512:(c2 + 1) * 512], start=True, stop=True),
                          r=[mn, "ones_f"], w=["pB"], inc=(c2 == 1))
                    if m in (1, 4):
                        LD(gtmp[:], g1_bd if m == 1 else g2_bd, "gtmp", "d_c2")
                        if m == 1:
                            V(lambda e: e.scalar_tensor_tensor(G1_b[:], pB[:], 1.0, gtmp[:], ALU.add, ALU.mult),
                              r=["pB", "gtmp"], w=["G1_b"])
                        else:
                            V(lambda e: e.scalar_tensor_tensor(gtmp[:], pB[:], 1.0, gtmp[:], ALU.add, ALU.mult),
                              r=["pB", "gtmp"], w=["gtmp"])
                            P.op("sync", lambda e: e.dma_start(out=scr[0], in_=gtmp[:]), reads=["gtmp"], dma="d_scr")
                    elif m == 2:
                        V(lambda e: e.tensor_copy(gt1_b[:], pB[:]), r=["pB"], w=["gt1_b"])
                    else:
                        V(lambda e: e.tensor_copy(gtmp[:], pB[:]), r=["pB"], w=["gtmp"])
                        P.op("sync", lambda e: e.dma_start(out=scr[1], in_=gtmp[:]), reads=["gtmp"], dma="d_scr")
            P.barrier()
            stage_end(1, lambda: [(dbgf[:, 0:1024], G1_b[:], "G1_b"), (dbgf[:, 1024:2048], gt1_b[:], "gt1_b"), (dbgb[:, 0:8], sh1T[:], "sh1T"), (dbgb[:, 8:16], sh2T[:], "sh2T")])

        with ExitStack() as stAR, ExitStack() as stA:
            AR1 = SB(stAR, "AR1", [128, S + NT * 2 * 129], BF16)
            KT = AR1[:, 0:S]
            VVf = AR1[:, S:S + NT * 2 * 129]
            VV = VVf.rearrange("p (t g c) -> p t g c", t=NT, g=2)
            VVk = VVf.rearrange("p (t c) -> p t c", t=NT)
            QT = SB(stA, "QT", [128, 4, NCOLS], BF16)
            mixT = SB(stA, "mixT", [128, 8, NCOLS], BF16)
            V(lambda e: e.memset(AR1[:], 0.0), w=["VV", "KT"])
            V(lambda e: e.memset(mixT[:, 0:4, 0:128], 0.0), w=["mixT"])
            V(lambda e: e.memset(mixT[:, 0:4, 2176:2304], 0.0), w=["mixT"])
            if debug_stage:
                V(lambda e: e.memset(QT[:], 0.0), w=["QT"])
            V(lambda e: e.memset(VV[:, :, :, 0:1], 1.0), w=["VV"])
            V(lambda e: e.memset(VV[:, :, :, 128:129], 1.0), w=["VV"])

            with ExitStack() as st:
                Win = SB(st, "Win", [128, 8, 1792], BF16)
                wsT = SB(st, "wsTb", [128, 8, 128], BF16)
                bsbT = SB(st, "bsbT", [128, 512], F32)
                xt = [SB(st, f"xt{i}", [128, D], F32) for i in range(2)]
                junk = SB(st, "junk", [128, D], BF16)
                xn = [SB(st, f"xn{i}", [128, D], BF16) for i in range(2)]
                xnT = [SB(st, f"xnT{i}", [128, 8, 128], BF16) for i in range(2)]
                cst = [SB(st, f"cst{i}", [128, 64], F32) for i in range(2)]
                snt = [SB(st, f"snt{i}", [128, 64], F32) for i in range(2)]
                kvf = SB(st, "kvf", [128, 256], F32)
                ksq = SB(st, "ksq", [128, 128], F32)
                kra = SB(st, "kra", [128, 128], F32)
                krb = SB(st, "krb", [128, 128], F32)
                kbf = SB(st, "kbf", [128, 128], BF16)
                f1 = SB(st, "f1", [128, 512], F32)
                f2 = SB(st, "f2", [128, 512], F32)
                f3 = SB(st, "f3", [128, 512], F32)
                qbf = SB(st, "qbf", [128, 512], BF16)
                vnpad = SB(st, "vnpad", [128, 8, 128], BF16)
                uT = SB(st, "uT", [128, 512], F32)
                tk = SB(st, "tk", [128, 512], F32)
                sqt = SB(st, "sqt", [128, 512], F32)
                pT = [PS(st, f"pT{i}", [128, D], BF16) for i in range(2)]
                pKV = PS(st, "pKV", [128, 512], F32)
                pQ = PS(st, "pQ", [128, 512], F32)
                pZV = PS(st, "pZV", [128, 512], F32)
                pU = PS(st, "pU", [128, 512], F32)
                pTQ = PS(st, "pTQ", [128, D], BF16)
                pM = PS(st, "pM", [128, 512], F32)

                for kk in range(2):
                    P.op("gpsimd", lambda e, kk=kk: e.dma_start(
                        out=Win[:, 4 * kk:4 * kk + 4, :],
                        in_=w_in[512 * kk:512 * kk + 512, :].rearrange("(k p) n -> p k n", p=128)),
                        writes=["Win"], dma="d_win")
                LD(wsT[:], wsT_d, "wsT", "d_c3", eng="gpsimd")
                LD(bsbT[:], bsbT_d, "bsbT", "d_c1")
                V(lambda e: e.memset(vnpad[:], 0.0), w=["vnpad"])

                colgrp = [(0, 512, bias_q, "bias_q", 0), (512, 256, bias_kv, "bias_kv", 512),
                          (1280, 512, bias_zv, "bias_zv", 768)]
                for (c0, n, dst, dn, r0) in colgrp:
                    for k in range(8):
                        T(lambda e, k=k, c0=c0, n=n: e.matmul(pQ[0:1, 0:n], sh1T[:, k:k + 1], Win[:, k, c0:c0 + n],
                                                              start=(k == 0), stop=(k == 7)),
                          r=["Win", "sh1T"], w=["pQ"], inc=(k == 7))
                    V(lambda e, n=n, r0=r0: e.tensor_copy(tk[0:1, 0:n], pQ[0:1, 0:n]), r=["pQ"], w=["tk"])
                    T(lambda e, n=n, r0=r0: e.matmul(pZV[:, 0:n], ones_f[0:1, :], tk[0:1, 0:n],
                                                     start=True, stop=True), r=["tk", "ones_f"], w=["pZV"])
                    V(lambda e, n=n, dst=dst: e.tensor_copy(dst[:, 0:n], pZV[:, 0:n]), r=["pZV"], w=[dn])
                for c4 in range(4):
                    for k in range(8):
                        T(lambda e, k=k, c4=c4: e.matmul(pU[:, c4:c4 + 1], Win[:, k, 768 + c4 * 128:768 + (c4 + 1) * 128],
                                                         sh1T[:, k:k + 1], start=(k == 0), stop=(k == 7)),
                          r=["Win", "sh1T"], w=["pU"], inc=(k == 7 and c4 == 3))
                V(lambda e: e.tensor_copy(biasU[:], pU[:, 0:4]), r=["pU"], w=["biasU"])


                qf = SB(st, "qf", [128, 512], F32)
                zvf = SB(st, "zvf", [128, 512], F32)
                sm18 = SB(st, "sm18", [128, 18], F32)
                V(lambda e: e.memset(sm18[:], 1.0), w=["sm18"])
                print("sbuf remaining at AB:", nc.sbuf_bytes_remaining)

                def v3(a, nh):
                    return a[:, 0:nh * 64].rearrange("p (h d) -> p h d", h=nh)

                def front(t):
                    s = t % 2
                    own = t < NX
                    LD(xt[s][:], x_rot[t * 128:(t + 1) * 128, :], f"xt{s}")
                    LD(cst[s][:], cos_t[:, t, :], f"cst{s}")
                    LD(snt[s][:], sin_t[:, t, :], f"snt{s}")
                    A(lambda e: e.activation(junk[:], xt[s][:], AF.Square, accum_out=ss[:, t:t + 1]),
                      r=[f"xt{s}"], w=["junk", "ss"])
                    rsqrt_cols(rstd, ss, t, t + 1, 1.0 / D, "ss", "rstd")
                    V(lambda e: e.scalar_tensor_tensor(xn[s][:], xt[s][:], rstd[:, t:t + 1], G1_b[:], ALU.mult, ALU.mult),
                      r=[f"xt{s}", "rstd", "G1_b"], w=[f"xn{s}"])
                    for k in range(8):
                        T(lambda e, k=k: e.transpose(pT[s][:, k * 128:(k + 1) * 128], xn[s][:, k * 128:(k + 1) * 128], ident_b[:]),
                          r=[f"xn{s}", "ident_b"], w=[f"pT{s}"], inc=(k == 7))
                    A(lambda e: e.copy(xnT[s][:].rearrange("p k n -> p (k n)"), pT[s][:]), r=[f"pT{s}"], w=[f"xnT{s}"])
                    for k in range(8):
                        T(lambda e, k=k: e.matmul(pKV[:, 0:256], xnT[s][:, k, :], Win[:, k, 512:768],
                                                  start=(k == 0), stop=(k == 7)),
                          r=[f"xnT{s}", "Win"], w=["pKV"], inc=(k == 7))
                    if own:
                        for k in range(8):
                            T(lambda e, k=k: e.matmul(pQ[:, :], xnT[s][:, k, :], Win[:, k, 0:512],
                                                      start=(k == 0), stop=(k == 7)),
                              r=[f"xnT{s}", "Win"], w=["pQ"], inc=(k == 7))
                        for k in range(8):
                            T(lambda e, k=k: e.matmul(pZV[:, :], xnT[s][:, k, :], Win[:, k, 1280:1792],
                                                      start=(k == 0), stop=(k == 7)),
                              r=[f"xnT{s}", "Win"], w=["pZV"], inc=(k == 7))
                        for c4 in range(4):
                            for k in range(8):
                                T(lambda e, k=k, c4=c4: e.matmul(pU[:, c4 * 128:(c4 + 1) * 128],
                                                                 Win[:, k, 768 + c4 * 128:768 + (c4 + 1) * 128],
                                                                 xnT[s][:, k, :], start=(k == 0), stop=(k == 7)),
                                  r=[f"xnT{s}", "Win"], w=["pU"], inc=(k == 7 and c4 == 3))

                def mid(t):
                    own = t < NX
                    V(lambda e: e.tensor_tensor(kvf[:], pKV[:, 0:256], bias_kv[:], ALU.add), r=["pKV", "bias_kv"], w=["kvf"])
                    if own:
                        V(lambda e: e.tensor_tensor(zvf[:], pZV[:], bias_zv[:], ALU.add), r=["pZV", "bias_zv"], w=["zvf"])
                        A(lambda e: e.activation(zvf[:], zvf[:], AF.Gelu_apprx_tanh), r=["zvf"], w=["zvf"])
                        V(lambda e: e.tensor_tensor(qf[:], pQ[:], bias_q[:], ALU.add), r=["pQ", "bias_q"], w=["qf"])
                        for c4 in range(4):
                            A(lambda e, c4=c4: e.activation(uT[:, c4 * 128:(c4 + 1) * 128], pU[:, c4 * 128:(c4 + 1) * 128],
                                                            AF.Gelu_apprx_tanh, bias=biasU[:, c4:c4 + 1]),
                              r=["pU", "biasU"], w=["uT"])

                def rope2(src, sname, nh, ra, raname, rb, rbname, dst, dname, cs, csn, sn, snn):
                    V(lambda e: e.tensor_tensor(v3(ra, nh), v3(src, nh), cs[:].unsqueeze(1).to_broadcast([128, nh, 64]), ALU.mult),
                      r=[sname, csn], w=[raname])
                    for blk in range(4):
                        pb = blk ^ 1
                        G(lambda e, blk=blk, pb=pb: e.tensor_tensor(
                            v3(rb, nh)[:, :, blk * 16:(blk + 1) * 16], v3(src, nh)[:, :, pb * 16:(pb + 1) * 16],
                            sn[:, blk * 16:(blk + 1) * 16].unsqueeze(1).to_broadcast([128, nh, 16]), ALU.mult),
                          r=[sname, snn], w=[rbname])
                    V(lambda e: e.tensor_tensor(dst[:, 0:nh * 64], ra[:, 0:nh * 64], rb[:, 0:nh * 64], ALU.add),
                      r=[raname, rbname], w=[dname])

                def back_a(t):
                    s = t % 2
                    own = t < NX
                    G(lambda e: e.tensor_copy(VV[:, t, :, 64:128], kvf[:, 128:256].rearrange("p (g d) -> p g d", g=2)),
                      r=["kvf"], w=["VV"])
                    G(lambda e: e.tensor_tensor(ksq[:], kvf[:, 0:128], kvf[:, 0:128], ALU.mult), r=["kvf"], w=["ksq"])
                    V(lambda e: e.tensor_reduce(sm18[:, 0:2], v3(ksq, 2), axis=AX.X, op=ALU.add), r=["ksq"], w=["sm18"])
                    if own:
                        G(lambda e: e.tensor_tensor(f3[:], qf[:], qf[:], ALU.mult), r=["qf"], w=["f3"])
                        V(lambda e: e.tensor_reduce(sm18[:, 2:10], v3(f3, 8), axis=AX.X, op=ALU.add), r=["f3"], w=["sm18"])
                        G(lambda e: e.tensor_tensor(f1[:], zvf[:], zvf[:], ALU.mult), r=["zvf"], w=["f1"])
                        V(lambda e: e.tensor_reduce(sm18[:, 10:18], v3(f1, 8), axis=AX.X, op=ALU.add), r=["f1"], w=["sm18"])
                    nc_ = 18 if own else 2
                    rsqrt_cols(sm18, sm18, 0, nc_, 1.0 / 64, "sm18", "sm18")

                def back_b(t):
                    s = t % 2
                    own = t < NX
                    V(lambda e: e.tensor_tensor(v3(kra, 2), v3(kvf, 2), sm18[:, 0:2].unsqueeze(2).to_broadcast([128, 2, 64]), ALU.mult),
                      r=["kvf", "sm18"], w=["kra"])
                    V(lambda e: e.tensor_tensor(v3(kra, 2), v3(kra, 2), gk_b[:].unsqueeze(1).to_broadcast([128, 2, 64]), ALU.mult),
                      r=["kra", "gk_b"], w=["kra"])
                    rope2(kra, "kra", 2, ksq, "ksq", krb, "krb", kbf, "kbf", cst[s], f"cst{s}", snt[s], f"snt{s}")
                    T(lambda e: e.transpose(pTQ[:, 512:640], kbf[:], ident_b[:]), r=["kbf", "ident_b"], w=["pTQk"])
                    kt_copy = lambda: V(lambda e: e.tensor_copy(KT[:, t * 128:(t + 1) * 128], pTQ[:, 512:640]), r=["pTQk"], w=["KT"])
                    if not own:
                        pending.append(kt_copy)
                        return
                    kt_copy()
                    V(lambda e: e.tensor_tensor(v3(f2, 8), v3(qf, 8), sm18[:, 2:10].unsqueeze(2).to_broadcast([128, 8, 64]), ALU.mult),
                      r=["qf", "sm18"], w=["f2"])
                    V(lambda e: e.tensor_tensor(v3(f2, 8), v3(f2, 8), gq_b[:].unsqueeze(1).to_broadcast([128, 8, 64]), ALU.mult),
                      r=["f2", "gq_b"], w=["f2"])
                    rope2(f2, "f2", 8, f3, "f3", f1, "f1", qbf, "qbf", cst[s], f"cst{s}", snt[s], f"snt{s}")
                    for p in range(4):
                        T(lambda e, p=p: e.transpose(pTQ[:, p * 128:(p + 1) * 128], qbf[:, p * 128:(p + 1) * 128], ident_b[:]),
                          r=["qbf", "ident_b"], w=["pTQ"], inc=(p == 3))
                    V(lambda e: e.tensor_tensor(v3(f2, 8), v3(zvf, 8), sm18[:, 10:18].unsqueeze(2).to_broadcast([128, 8, 64]), ALU.mult),
                      r=["zvf", "sm18"], w=["f2"])
                    for sl in range(2):
                        G(lambda e, sl=sl: e.tensor_tensor(
                            vnpad[:].rearrange("p (c s) n -> p c s n", s=2)[:, :, sl, sl * 64:(sl + 1) * 64],
                            f2[:].rearrange("p (c s d) -> p c s d", s=2, d=64)[:, :, sl, :],
                            gtv_b[:].unsqueeze(1).to_broadcast([128, 4, 64]), ALU.mult),
                          r=["f2", "gtv_b"], w=["vnpad"])
                    for c4 in range(4):
                        for sl in range(2):
                            T(lambda e, c4=c4, sl=sl: e.matmul(pM[:, c4 * 128:(c4 + 1) * 128], vnpad[:, 2 * c4 + sl, :],
                                                               wsT[:, 2 * c4 + sl, :], start=(sl == 0), stop=(sl == 1)),
                              r=["vnpad", "wsT"], w=["pM"], inc=(sl == 1 and c4 == 3))
                    V(lambda e: e.tensor_copy(QT[:, :, t * 128:(t + 1) * 128], pTQ[:, 0:512].rearrange("p (a n) -> p a n", a=4)),
                      r=["pTQ"], w=["QT"])
                    V(lambda e: e.tensor_tensor(tk[:], pM[:], bsbT[:], ALU.add), r=["pM", "bsbT"], w=["tk"])
                    V(lambda e: e.tensor_tensor(tk[:], tk[:], uT[:], ALU.mult), r=["tk", "uT"], w=["tk"])
                    G(lambda e: e.tensor_tensor(sqt[:], tk[:], tk[:], ALU.mult), r=["tk"], w=["sqt"])
                    for c4 in range(4):
                        T(lambda e, c4=c4: e.matmul(pM[:, 0:1] if False else pTS[:, 0:1], sqt[:, c4 * 128:(c4 + 1) * 128], ones_f[:, 0:1],
                                                    start=(c4 == 0), stop=(c4 == 3)),
                          r=["sqt", "ones_f"], w=["pTS"], inc=(c4 == 3))
                    V(lambda e: e.tensor_tensor(mixT[:, 4:8, t * 128:(t + 1) * 128],
                                                tk[:].rearrange("p (c n) -> p c n", c=4),
                                                gmix[:, 4:8].unsqueeze(2).to_broadcast([128, 4, 128]), ALU.mult),
                      r=["tk", "gmix"], w=["mixT"])
                    V(lambda e: e.tensor_copy(sstok[:, t:t + 1], pTS[:, 0:1]), r=["pTS"], w=["sstok"])

                pTS = pKV[:, 256:512]
                pending = []
                front(0)
                mid(0)
                for t in range(AB_TILES):
                    if t + 1 < AB_TILES:
                        front(t + 1)
                    back_a(t)
                    back_b(t)
                    if t + 1 < AB_TILES:
                        mid(t + 1)
                    for f_ in pending:
                        f_()
                    pending.clear()

                rsqrt_cols(rs_t, sstok, 0, NX, 1.0 / 512, "sstok", "rs_t")
                P.barrier()
                stage_end(2, lambda: [(dbgb[:, 0:8192], KT, "KT"), (dbgb[:, 8192:17408], QT[:].rearrange("p a n -> p (a n)"), "QT"), (dbgb[:, 17408:26624], mixT[:, 4:8, :].rearrange("p a n -> p (a n)"), "mixT"), (dbgb[:, 26624:43136], VVf, "VV"), (dbgf[:, 0:18], rs_t[:, 0:18], "rs_t"), (dbgf[:, 64:128], rstd[:, :], "rstd"), (dbgf[:, 128:132], biasU[:], "biasU"), (dbgf[:, 1024:1536], bias_q[:], "bias_q"), (dbgf[:, 1536:1792], bias_kv[:], "bias_kv"), (dbgf[:, 2048:2560], bias_zv[:], "bias_zv")])

            with ExitStack() as st:
                Pb = [SB(st, f"Pb{i}", [128, 1024], BF16) for i in range(3)]
                Oa = SB(st, "Oa", [128, 512], F32)
                Ob = SB(st, "Ob", [128, 512], F32)
                rden = SB(st, "rden", [128, 512], F32)
                at = SB(st, "at", [128, 512], F32)
                sq = SB(st, "sq", [128, 512], F32)
                acc = SB(st, "acc", [128, 512], F32)
                pS = [PS(st, f"pS{i}", [128, 1024], F32) for i in range(2)]
                pOa = PS(st, "pOa", [128, 512], F32)
                pOb = PS(st, "pOb", [128, 512], F32)
                pBc = PS(st, "pBc", [128, 512], F32)
                pSA = PS(st, "pSA", [128, 512], F32)
                for ct in range(NCT):
                    for ab in range(2):
                        P.op("gpsimd", lambda e: e.dma_start(
                            out=wub[ct].rearrange("p (k a n) -> p k a n", k=8, a=2)[:, :, ab, :],
                            in_=w_up[:, ab * DFF + ct * 128:ab * DFF + (ct + 1) * 128].rearrange("(k p) n -> p k n", p=128)),
                            writes=[f"wub{ct}_{ab}"], dma="d_wub")
                QH = SB(st, "QH", [128, 4, 2], BF16)
                hs = SB(st, "hs", [128, 2], F32)

                def qk(kt, i, p, q0, qn):
                    b = pS[i % 2]
                    T(lambda e: e.matmul(b[:, 0:qn], KT[0:64, kt * 128:(kt + 1) * 128], QT[0:64, p, q0:q0 + qn],
                                         start=True, stop=True), r=["KT", "QT"], w=[f"pS{i % 2}"], inc=False)
                    T(lambda e: e.matmul(b[:, 512:512 + qn], KT[64:128, kt * 128:(kt + 1) * 128],
                                         QT[64:128, p, q0:q0 + qn], start=True, stop=True),
                      r=["KT", "QT"], w=[f"pS{i % 2}"])

                def ex(kt, i):
                    A(lambda e: e.activation(Pb[i % 3][:], pS[i % 2][:], AF.Exp, scale=0.125),
                      r=[f"pS{i % 2}"], w=[f"Pb{i % 3}"])

                def pv(kt, i, qn):
                    pb = Pb[i % 3]
                    T(lambda e: e.matmul(pOa[:, 0:qn], VVk[:, kt, 64:192], pb[:, 0:qn],
                                         start=(kt == 0), stop=(kt == NT - 1)),
                      r=[f"Pb{i % 3}", "VV"], w=["pOa"], inc=False)
                    T(lambda e: e.matmul(pOb[:, 0:qn], VV[:, kt, 1, 0:128], pb[:, 512:512 + qn],
                                         start=(kt == 0), stop=(kt == NT - 1)),
                      r=[f"Pb{i % 3}", "VV"], w=["pOa", "pOb"])

                def epi_a(qn):
                    V(lambda e: e.tensor_copy(Oa[0:65, 0:qn], pOa[0:65, 0:qn]), r=["pOa"], w=["Oa"])
                    V(lambda e: e.tensor_copy(Ob[:, 0:qn], pOb[:, 0:qn]), r=["pOb"], w=["Ob"])
                    V(lambda e: e.reciprocal(rden[64:65, 0:qn], Oa[64:65, 0:qn]), r=["Oa"], w=["rdenA"])
                    V(lambda e: e.reciprocal(rden[0:1, 0:qn], Ob[0:1, 0:qn]), r=["Ob"], w=["rdenB"])

                def epi_b(qn):
                    T(lambda e: e.matmul(pBc[:, 0:qn], ones_f[64:65, :], rden[64:65, 0:qn], start=True, stop=True),
                      r=["rdenA", "ones_f"], w=["pBc"], inc=False)
                    T(lambda e: e.matmul(pSA[:, 0:qn], ones_f[0:1, :], rden[0:1, 0:qn], start=True, stop=True),
                      r=["rdenB", "ones_f"], w=["pBc", "pSA"])
                    V(lambda e: e.tensor_tensor(at[0:64, 0:qn], Oa[0:64, 0:qn], pBc[0:64, 0:qn], ALU.mult),
                      r=["Oa", "pBc"], w=["at"])
                    V(lambda e: e.tensor_tensor(at[64:128, 0:qn], Ob[64:128, 0:qn], pSA[64:128, 0:qn], ALU.mult),
                      r=["Ob", "pSA"], w=["at"])

                def epi_main(qi, p, q0):
                    epi_b(512)
                    V(lambda e: e.tensor_scalar(mixT[:, p, q0:q0 + 512], at[:, :], gmix[:, p:p + 1], None, ALU.mult),
                      r=["at", "gmix"], w=["mixT"])
                    if p == 0:
                        G(lambda e: e.tensor_tensor(acc[:, :], at[:, :], at[:, :], ALU.mult), r=["at"], w=["acc"])
                    else:
                        G(lambda e: e.tensor_tensor(sq[:, :], at[:, :], at[:, :], ALU.mult), r=["at"], w=["sq"])
                        G(lambda e: e.tensor_tensor(acc[:, :], acc[:, :], sq[:, :], ALU.add), r=["acc", "sq"], w=["acc"])

                def epi_ss(qi):
                    for w_ in range(4):
                        T(lambda e, w_=w_: e.matmul(pSA[:, w_:w_ + 1], acc[:, w_ * 128:(w_ + 1) * 128], ones_f[:, 0:1],
                                                    start=True, stop=True), r=["acc", "ones_f"], w=["pSA"], inc=(w_ == 3))
                    V(lambda e: e.tensor_copy(ssatt[:, 1 + qi * 4:1 + qi * 4 + 4], pSA[:, 0:4]), r=["pSA"], w=["ssatt"])

                V(lambda e: e.tensor_copy(QH[:, :, 0:1], QT[:, :, 127:128]), r=["QT"], w=["QH"])
                V(lambda e: e.tensor_copy(QH[:, :, 1:2], QT[:, :, 2176:2177]), r=["QT"], w=["QH"])
                for g in range(2):
                    for kt in range(NT):
                        T(lambda e: e.matmul(pS[0][:, g * 512 + kt * 8:g * 512 + kt * 8 + 8],
                                             KT[g * 64:(g + 1) * 64, kt * 128:(kt + 1) * 128], QH[g * 64:(g + 1) * 64, :, :],
                                             start=True, stop=True), r=["KT", "QH"], w=["pS0"], inc=(kt == NT - 1))
                ex(0, 0)
                for kt in range(NT):
                    T(lambda e: e.matmul(pOa[0:65, 0:8], VV[:, kt, 0, 64:129], Pb[0][:, kt * 8:kt * 8 + 8],
                                         start=(kt == 0), stop=(kt == NT - 1)), r=["Pb0", "VV"], w=["pOa"], inc=(kt == NT - 1))
                for kt in range(NT):
                    T(lambda e: e.matmul(pOb[:, 0:8], VV[:, kt, 1, 0:128], Pb[0][:, 512 + kt * 8:512 + kt * 8 + 8],
                                         start=(kt == 0), stop=(kt == NT - 1)), r=["Pb0", "VV"], w=["pOb"], inc=(kt == NT - 1))
                epi_a(8)
                epi_b(8)
                at3 = at[:, 0:8].rearrange("p (a t) -> p a t", t=2)
                for tk_, col in ((0, 127), (1, 2176)):
                    V(lambda e: e.tensor_tensor(mixT[:, 0:4, col:col + 1], at3[:, :, tk_:tk_ + 1], gmix[:, 0:4].unsqueeze(2), ALU.mult),
                      r=["at", "gmix"], w=["mixT"])
                V(lambda e: e.tensor_tensor(sq[:, 0:8], at[:, 0:8], at[:, 0:8], ALU.mult), r=["at"], w=["sq"])
                V(lambda e: e.tensor_reduce(hs[:, 0:2], sq[:, 0:8].rearrange("p (a t) -> p t a", t=2), axis=AX.X, op=ALU.add),
                  r=["sq"], w=["hs"])
                V(lambda e: e.memset(acc[:, 0:256], 0.0), w=["acc"])
                V(lambda e: e.tensor_copy(acc[:, 127:129], hs[:, 0:2]), r=["hs"], w=["acc"])
                for w_ in range(2):
                    T(lambda e, w_=w_: e.matmul(pSA[:, w_:w_ + 1], acc[:, w_ * 128:(w_ + 1) * 128], ones_f[:, 0:1],
                                                start=True, stop=True), r=["acc", "ones_f"], w=["pSA"], inc=(w_ == 1))
                V(lambda e: e.tensor_copy(ssatt[:, 0:1], pSA[:, 0:1]), r=["pSA"], w=["ssatt"])
                V(lambda e: e.tensor_copy(ssatt[:, 17:18], pSA[:, 1:2]), r=["pSA"], w=["ssatt"])

                iters = [(qi, p) for qi in range(4) for p in range(4)]
                step = 1
                pend = None
                for (qi, p) in iters:
                    q0 = 128 + qi * 512
                    qk(0, step, p, q0, 512)
                    qk(1, step + 1, p, q0, 512)
                    for kt in range(NT):
                        ex(kt, step + kt)
                        if kt + 2 < NT:
                            qk(kt + 2, step + kt + 2, p, q0, 512)
                        pv(kt, step + kt, 512)
                        if kt == 3 and pend is not None:
                            epi_main(*pend)
                        if kt == 16 and pend is not None:
                            if pend[1] == 3:
                                epi_ss(pend[0])
                            pend = None
                    step += NT
                    epi_a(512)
                    pend = (qi, p, q0)
                epi_main(*pend)
                epi_ss(3)

                rsqrt_cols(rs_a, ssatt, 0, NX, 1.0 / 512, "ssatt", "rs_a")
                P.barrier()
                stage_end(3, lambda: [(dbgb[:, 0:9216], mixT[:, 0:4, :].rearrange("p a n -> p (a n)"), "mixT"), (dbgf[:, 0:18], rs_a[:, 0:18], "rs_a")])

            x1nT = AR1[:, 0:8 * NCOLS].rearrange("p (k n) -> p k n", k=8)
            with ExitStack() as st2:
                G2_b = SB(st2, "G2_b", [128, D], F32)
                Wout = SB(st2, "Wout", [128, 8, D], BF16)
                xe = [SB(st2, f"xe{i}", [128, D], F32) for i in range(2)]
                x1h = [SB(st2, f"x1h{i}", [128, D], F32) for i in range(2)]
                t1 = SB(st2, "t1", [128, D], F32)
                x1n = [SB(st2, f"x1n{i}", [128, D], BF16) for i in range(2)]
                junk2 = SB(st2, "junk2", [128, D], BF16)
                pYa = PS(st2, "pYa", [128, D], F32)
                pYb = PS(st2, "pYb", [128, D], F32)
                pT2 = [PS(st2, f"pT2{i}", [128, D], BF16) for i in range(2)]
                LD(G2_b[:], scr[0], "G2_b", "d_c2")
                for kk in range(2):
                    P.op("gpsimd", lambda e, kk=kk: e.dma_start(
                        out=Wout[:, 4 * kk:4 * kk + 4, :],
                        in_=w_out[512 * kk:512 * kk + 512, :].rearrange("(k p) n -> p k n", p=128)),
                        writes=["Wout"], dma="d_wout")
                def out_mm(t):
                    s = t % 2
                    LD(xe[s][:], x_rot[t * 128:(t + 1) * 128, :], f"xe{s}")
                    for c2 in range(2):
                        for k in range(4):
                            T(lambda e, k=k, c2=c2: e.matmul(pYa[:, c2 * 512:(c2 + 1) * 512], mixT[:, k, t * 128:(t + 1) * 128],
                                                             Wout[:, k, c2 * 512:(c2 + 1) * 512], start=(k == 0), stop=(k == 3)),
                              r=["mixT", "Wout"], w=["pYa"], inc=(k == 3 and c2 == 1))
                    for c2 in range(2):
                        for k in range(4, 8):
                            T(lambda e, k=k, c2=c2: e.matmul(pYb[:, c2 * 512:(c2 + 1) * 512], mixT[:, k, t * 128:(t + 1) * 128],
                                                             Wout[:, k, c2 * 512:(c2 + 1) * 512], start=(k == 4), stop=(k == 7)),
                              r=["mixT", "Wout"], w=["pYb"], inc=(k == 7 and c2 == 1))

                def out_ep1(t):
                    s = t % 2
                    xd = x1h[s][:]
                    xdn = f"x1h{s}"
                    A(lambda e: e.activation(t1[:], pYa[:], AF.Identity, scale=rs_a[:, t:t + 1]), r=["pYa", "rs_a"], w=["t1"])
                    V(lambda e: e.scalar_tensor_tensor(t1[:], pYb[:], rs_t[:, t:t + 1], t1[:], ALU.mult, ALU.add),
                      r=["pYb", "rs_t", "t1"], w=["t1"])

                def out_ep2(t):
                    s = t % 2
                    xd = x1h[s][:]
                    xdn = f"x1h{s}"
                    V(lambda e: e.tensor_tensor(t1[:], t1[:], gt1_b[:], ALU.mult), r=["t1", "gt1_b"], w=["t1"])
                    V(lambda e: e.tensor_tensor(xd, t1[:], xe[s][:], ALU.add), r=["t1", f"xe{s}"], w=[xdn])
                    if 1 <= t <= 16:
                        P.op("sync", lambda e: e.dma_start(out=x1s[(t - 1) * 128:t * 128, :], in_=xd), reads=[xdn], dma=f"d_x1s{s}")
                    A(lambda e: e.activation(junk2[:], xd, AF.Square, accum_out=ss2[:, t:t + 1]), r=[xdn], w=["junk2", "ss2"])
                    rsqrt_cols(rs2, ss2, t, t + 1, 1.0 / D, "ss2", "rs2")
                    V(lambda e: e.scalar_tensor_tensor(x1n[s][:], xd, rs2[:, t:t + 1], G2_b[:], ALU.mult, ALU.mult),
                      r=[xdn, "rs2", "G2_b"], w=[f"x1n{s}"])
                    for k in range(8):
                        T(lambda e, k=k: e.transpose(pT2[s][:, k * 128:(k + 1) * 128], x1n[s][:, k * 128:(k + 1) * 128], ident_b[:]),
                          r=[f"x1n{s}", "ident_b"], w=[f"pT2{s}"], inc=(k == 7))
                    V(lambda e: e.tensor_copy(x1nT[:, :, t * 128:(t + 1) * 128], pT2[s][:].rearrange("p (k n) -> p k n", k=8)),
                      r=[f"pT2{s}"], w=["x1nT"])

                out_mm(0)
                for t in range(NX):
                    out_ep1(t)
                    if t + 1 < NX:
                        out_mm(t + 1)
                    out_ep2(t)
                P.barrier()
                stage_end(4, lambda: [(dbgb[:, 0:18432], AR1[:, 0:18432], "x1nT")])
                print("sbuf remaining at OUT:", nc.sbuf_bytes_remaining)

            stA.close()
            with ExitStack() as st2:
                NWU = 2
                GW = 2
                groups = [(c, min(GW, NCT - c)) for c in range(0, NCT, GW)]
                NG = len(groups)
                gt2_b = SB(st2, "gt2_b", [128, D], F32)
                gfin_b = SB(st2, "gfin_b", [128, D], F32)
                gT = SB(st2, "gT", [128, NCT, 512], BF16)
                tail0 = 8 * NCOLS
                wu = [AR1[:, tail0:tail0 + 8 * 2 * GW * 128].rearrange("p (c f) -> p c f", c=GW),
                      SB(st2, "wu1", [128, GW, 8 * 2 * 128], BF16)]
                Wd = SB(st2, "Wd", [128, NCT, D], BF16)
                zb = [SB(st2, f"zb{i}", [128, 2, 514], F32) for i in range(2)]
                cv = [SB(st2, f"cv{i}", [128, 2, 512], F32) for i in range(2)]
                sl_ = [SB(st2, f"sl{i}", [128, 512], F32) for i in range(2)]
                y2q = SB(st2, "y2q", [128, 4, D], F32)
                xr = [SB(st2, f"xr{i}", [128, D], F32) for i in range(2)]
                junk3 = AR1[:, tail0 + 8 * 2 * GW * 128:tail0 + 8 * 2 * GW * 128 + D]
                ot = [SB(st2, f"ot{i}", [128, D], F32) for i in range(2)]
                pZ = [PS(st2, f"pZ{i}", [128, 512], F32) for i in range(4)]
                pY = [PS(st2, f"pY{i}", [128, 512], F32) for i in range(4)]
                LD(gt2_b[:], scr[1], "gt2_b")
                LD(gfin_b[:], gfin_bd, "gfin_b")
                print("sbuf remaining at FFN:", nc.sbuf_bytes_remaining)

                def ld_wu(gg):
                    c0_, n_ = groups[gg % NG]
                    s = gg % NWU
                    P.op("sync", lambda e: e.dma_start(out=wu[s][:, 0:n_, :], in_=wub[c0_:c0_ + n_].rearrange("c p f -> p c f")),
                         writes=[f"wu{s}"], dma=f"d_wu{s}")

                def wsl(qtr, ct, k, ab):
                    gi = ct // GW
                    s = (qtr * NG + gi) % NWU
                    j = ct - groups[gi][0]
                    return wu[s][:, j, :].rearrange("p (k a n) -> p k a n", k=8, a=2)[:, k, ab, :], f"wu{s}"

                def ld_wd(ct):
                    P.op("gpsimd", lambda e: e.dma_start(out=Wd[:, ct, :], in_=w_down[ct * 128:(ct + 1) * 128, :]),
                         writes=[f"Wd{ct}"], dma=f"d_Wd{ct}")

                def up_tail(qtr, ct):
                    ci = ct % 2
                    A(lambda e: e.activation(sl_[ci][:], cv[ci][:, 0, :], AF.Silu), r=[f"cv{ci}"], w=[f"sl{ci}"])
                    V(lambda e: e.tensor_tensor(gT[:, ct, :], sl_[ci][:], cv[ci][:, 1, :], ALU.mult),
                      r=[f"sl{ci}", f"cv{ci}"], w=["gT"])

                def down_mm(ct, c2):
                    for tt in range(4):
                        T(lambda e, tt=tt: e.matmul(pY[tt][:, :], gT[:, ct, tt * 128:(tt + 1) * 128],
                                                    Wd[:, ct, c2 * 512:(c2 + 1) * 512],
                                                    start=(ct == 0), stop=(ct == NCT - 1)),
                          r=["gT", f"Wd{ct}"], w=[f"pY{tt}"], inc=(ct == NCT - 1 or tt == 3))

                def bias2_for(ct):
                    pb = pY[ct % 4]
                    for ab in range(2):
                        for k in range(8):
                            wl, wn = wsl(0, ct, k, ab)
                            T(lambda e, k=k, ab=ab, wl=wl: e.matmul(pb[:, ab:ab + 1], wl, sh2T[:, k:k + 1], start=(k == 0), stop=(k == 7)),
                              r=[wn, "sh2T"], w=[f"pY{ct % 4}"], inc=(k == 7 and ab == 1))
                    A(lambda e: e.copy(bias2[:, ct:ct + 1], pb[:, 0:1]), r=[f"pY{ct % 4}"], w=["bias2"])
                    A(lambda e: e.copy(bias2[:, 22 + ct:23 + ct], pb[:, 1:2]), r=[f"pY{ct % 4}"], w=["bias2"])

                def fin_tail(qtr, tt):
                    ti = qtr * 4 + tt
                    os_ = ti % 2
                    V(lambda e: e.tensor_tensor(y2q[:, tt, :], y2q[:, tt, :], xr[tt % 2][:], ALU.add),
                      r=[f"y2q{tt}", f"xr{tt % 2}"], w=[f"y2q{tt}"])
                    if tt < 2:
                        LD(xr[tt % 2][:], x1s[(ti + 2) * 128:(ti + 3) * 128, :], f"xr{tt % 2}")
                    A(lambda e: e.activation(junk3, y2q[:, tt, :], AF.Square, accum_out=ssf[:, ti:ti + 1]),
                      r=[f"y2q{tt}"], w=["junk3", "ssf"])
                    rsqrt_cols(rsf, ssf, ti, ti + 1, 1.0 / D, "ssf", "rsf")
                    V(lambda e: e.scalar_tensor_tensor(ot[os_][:], y2q[:, tt, :], rsf[:, ti:ti + 1], gfin_b[:],
                                                       ALU.mult, ALU.mult),
                      r=[f"y2q{tt}", "rsf", "gfin_b"], w=[f"ot{os_}"])
                    P.op("sync", lambda e: e.dma_start(out=out_d[ti * 128:(ti + 1) * 128, :], in_=ot[os_][:]),
                         reads=[f"ot{os_}"], dma=f"d_out{os_}")

                ld_wu(0)
                for qtr in range(4):
                    c0 = 127 + qtr * 512
                    for ct in range(NCT):
                        zi = ct % 2
                        ci = ct % 2
                        if ct % GW == 0:
                            gg = qtr * NG + ct // GW
                            if gg + 1 < 4 * NG:
                                ld_wu(gg + 1)
                        if qtr == 0 and ct == 0:
                            bias2_for(0)
                        for ab in range(2):
                            for blk in range(2):
                                pz = pZ[ab * 2 + blk]
                                for k in range(8):
                                    wl, wn = wsl(qtr, ct, k, ab)
                                    T(lambda e, k=k, wl=wl: e.matmul(
                                        pz[:, 0:257], wl,
                                        x1nT[:, k, c0 + blk * 257:c0 + (blk + 1) * 257], start=(k == 0), stop=(k == 7)),
                                      r=[wn, "x1nT"], w=[f"pZ{ab * 2 + blk}"], inc=(k == 7))
                                A(lambda e: e.activation(
                                    zb[zi][:, ab, blk * 257:(blk + 1) * 257], pz[:, 0:257], AF.Identity,
                                    bias=bias2[:, ab * 22 + ct:ab * 22 + ct + 1]),
                                  r=[f"pZ{ab * 2 + blk}", "bias2"], w=[f"zb{zi}"])
                        if qtr >= 1 and ct >= 2:
                            down_mm(ct - 2, 0)
                        if qtr == 0 and ct + 1 < NCT:
                            bias2_for(ct + 1)
                        if qtr == 0:
                            ld_wd(ct)
                        if qtr > 0 and ct < 4:
                            fin_tail(qtr - 1, ct)
                        if ct == 4:
                            for tt in range(2):
                                LD(xr[tt][:], x1s[(qtr * 4 + tt) * 128:(qtr * 4 + tt + 1) * 128, :], f"xr{tt}")
                        if ct > 0:
                            up_tail(qtr, ct - 1)
                        if qtr == 0:
                            V(lambda e: e.tensor_scalar(zb[zi][:, :, 0:1], zb[zi][:, :, 0:1], hm[:, 0:1], None, ALU.mult),
                              r=[f"zb{zi}", "hm"], w=[f"zb{zi}"])
                        if qtr == 3:
                            V(lambda e: e.tensor_scalar(zb[zi][:, :, 513:514], zb[zi][:, :, 513:514], hm[:, 1:2], None, ALU.mult),
                              r=[f"zb{zi}", "hm"], w=[f"zb{zi}"])
                        for ab in range(2):
                            ch = ab * 22 + ct
                            if ab == 0:
                                A(lambda e: e.activation(cv[ci][:, ab, :], zb[zi][:, ab, 1:513], AF.Identity,
                                                         bias=bconv[:, ch:ch + 1], scale=wconv[:, 1, ch:ch + 1]),
                                  r=[f"zb{zi}", "wconv", "bconv"], w=[f"cv{ci}"])
                            else:
                                G(lambda e: e.tensor_scalar(cv[ci][:, ab, :], zb[zi][:, ab, 1:513], wconv[:, 1, ch:ch + 1],
                                                            bconv[:, ch:ch + 1], ALU.mult, ALU.add),
                                  r=[f"zb{zi}", "wconv", "bconv"], w=[f"cv{ci}"])
                        for ab in range(2):
                            ch = ab * 22 + ct
                            V(lambda e: e.scalar_tensor_tensor(cv[ci][:, ab, :], zb[zi][:, ab, 0:512], wconv[:, 0, ch:ch + 1],
                                                               cv[ci][:, ab, :], ALU.mult, ALU.add),
                              r=[f"zb{zi}", "wconv", f"cv{ci}"], w=[f"cv{ci}"])
                            V(lambda e: e.scalar_tensor_tensor(cv[ci][:, ab, :], zb[zi][:, ab, 2:514], wconv[:, 2, ch:ch + 1],
                                                               cv[ci][:, ab, :], ALU.mult, ALU.add),
                              r=[f"zb{zi}", "wconv", f"cv{ci}"], w=[f"cv{ci}"])
                    up_tail(qtr, NCT - 1)
                    if qtr >= 1:
                        down_mm(NCT - 2, 0)
                        down_mm(NCT - 1, 0)
                    for c2 in range(2):
                        if not (qtr >= 1 and c2 == 0):
                            for ct in range(NCT):
                                down_mm(ct, c2)
                        for tt in range(4):
                            V(lambda e, tt=tt: e.tensor_tensor(y2q[:, tt, c2 * 512:(c2 + 1) * 512], pY[tt][:, :],
                                                               gt2_b[:, c2 * 512:(c2 + 1) * 512], ALU.mult),
                              r=[f"pY{tt}", "gt2_b"], w=[f"y2q{tt}"])
                    if qtr == 3:
                        for tt in range(4):
                            fin_tail(3, tt)
                P.barrier()
        print("kernel build: inst", P.ninst, "waits", P.nwait, "sems", len(P.sem), P.cnt)
    except _Stop:
        pass
    return nc


def _prep_inputs(inp):
    x = np.asarray(inp["x"], np.float32)
    c = np.asarray(inp["c"], np.float32)
    f = lambda k: np.asarray(inp[k], np.float32)
    rows = S // 64
    row_id = np.repeat(np.arange(rows), 64).astype(np.float32)
    col_id = np.tile(np.arange(64), rows).astype(np.float32)
    inv_freq = np.power(np.float32(10000.0), -np.arange(0, 32, 2, dtype=np.float32) / np.float32(32)).astype(np.float32)
    ang_r = row_id[:, None] * inv_freq[None, :]
    ang_c = col_id[:, None] * inv_freq[None, :]
    ang = np.concatenate([ang_r, ang_r, ang_c, ang_c], axis=-1).astype(np.float32)
    cos = np.cos(ang).astype(np.float32)
    sin = np.sin(ang).astype(np.float32)
    sgn = np.concatenate([-np.ones(16), np.ones(16), -np.ones(16), np.ones(16)]).astype(np.float32)
    sinS = sin * sgn[None, :]
    perm_h = [0, 4, 1, 5, 2, 6, 3, 7]
    qcols = np.concatenate([np.arange(h * 64, (h + 1) * 64) for h in perm_h])
    w_in = f("w_in")[0].copy()
    w_in[:, 0:512] = w_in[:, qcols]
    w_out = f("w_out")[0].copy()
    w_out[0:512, :] = w_out[qcols, :]
    g_attn = f("g_attn_out")[0][qcols]
    gmixv = np.concatenate([g_attn, f("g_tok_out")[0]])
    gmix = np.ascontiguousarray(gmixv.reshape(8, 128).T)
    rep = lambda v, n=128: np.ascontiguousarray(np.broadcast_to(v[None, :], (n, v.shape[0])))
    wsT = np.ascontiguousarray(f("w_s")[0].transpose(2, 0, 1))
    b_s = f("b_s")[0]
    bsbT = np.zeros((128, 4, 128), np.float32)
    for c4 in range(4):
        bsbT[0:64, c4, :] = b_s[2 * c4][None, :]
        bsbT[64:128, c4, :] = b_s[2 * c4 + 1][None, :]
    wconv = np.ascontiguousarray(f("w_conv")[0].reshape(3, 44, 128).transpose(2, 0, 1))
    bconv = np.ascontiguousarray(f("b_conv")[0].reshape(44, 128).T)
    shared = {
        "w_ada": f("w_ada")[0], "b_ada": f("b_ada"), "g1_b": rep(f("g_norm1")[0]), "g2_b": rep(f("g_norm2")[0]),
        "gfin_b": rep(f("g_final")), "w_in": w_in, "w_out": w_out, "w_up": f("w_up")[0], "w_down": f("w_down")[0],
        "gq_b": rep(f("g_q")[0]), "gk_b": rep(f("g_k")[0]), "gtv_b": rep(f("g_tok_v")[0]), "wsT": wsT,
        "bsbT": bsbT.reshape(128, 512), "gmix": gmix, "wconv": wconv, "bconv": bconv,
        "ident": np.eye(128, dtype=np.float32),
    }
    in_maps = []
    for core in range(8):
        b, j = core // 4, core % 4
        sh = j * 2048 - 128
        xr = np.roll(x[b], -sh, axis=0)
        cr = np.roll(cos, -sh, axis=0).reshape(NT, 128, 64).transpose(1, 0, 2)
        sr = np.roll(sinS, -sh, axis=0).reshape(NT, 128, 64).transpose(1, 0, 2)
        hmk = np.ones((128, 2), np.float32)
        if j == 0:
            hmk[:, 0] = 0.0
        if j == 3:
            hmk[:, 1] = 0.0
        m = dict(shared)
        m.update({"x_rot": np.ascontiguousarray(xr), "cos_t": np.ascontiguousarray(cr), "sin_t": np.ascontiguousarray(sr),
                  "cb": np.ascontiguousarray(c[b].reshape(8, 128).T), "hmask": hmk})
        in_maps.append(m)
    return in_maps


_NC_CACHE = {}


def kernel(**inputs):
    in_maps = _prep_inputs(inputs)
    if "nc" not in _NC_CACHE:
        _NC_CACHE["nc"] = build_nc()
    nc = _NC_CACHE["nc"]
    res = run_bass_kernel_spmd(nc, in_maps, core_ids=list(range(8)))
    out = np.zeros((2, S, D), np.float32)
    for core in range(8):
        b, j = core // 4, core % 4
        out[b, j * 2048:(j + 1) * 2048, :] = res.results[core]["out"]
    return out
```

```python
import numpy as np
import concourse.bass as bass
import concourse.mybir as mybir
from concourse.bass_utils import run_bass_kernel_spmd
from contextlib import ExitStack

F32 = mybir.dt.float32
BF16 = mybir.dt.bfloat16
AF = mybir.ActivationFunctionType
ALU = mybir.AluOpType
AX = mybir.AxisListType

ENGS = ("sync", "scalar", "vector", "gpsimd", "tensor")
D = 1024
S = 8192
NT = 64
NX = 18
NCOLS = NX * 128
DFF = 2816
NCT = 22
EPS = 1e-6
AB_TILES = NT


class _Stop(Exception):
    pass


class Prog:
    def __init__(self, nc, stack):
        self.nc = nc
        self.stack = stack
        self.sem = {}
        self.cnt = {}
        self.waited = {e: {} for e in ENGS}
        self.lastw = {}
        self.readers = {}
        self.ninst = 0
        self.nwait = 0

    def getsem(self, name):
        if name not in self.sem:
            self.sem[name] = self.stack.enter_context(self.nc.semaphore(name))
            self.cnt[name] = 0
        return self.sem[name]

    def _wait(self, eng, tok):
        s, v = tok
        if self.waited[eng].get(s, 0) >= v:
            return
        self.waited[eng][s] = v
        getattr(self.nc, eng).wait_ge(self.sem[s], v)
        self.nwait += 1

    def op(self, eng, fn, reads=(), writes=(), dma=None, inc=True):
        deps = []
        for r in reads:
            if r in self.lastw:
                deps.append(self.lastw[r])
        for w in writes:
            if w in self.lastw:
                deps.append(self.lastw[w])
            for s, v in self.readers.get(w, {}).items():
                deps.append((s, v))
        for t in deps:
            if eng == "tensor" and t[0] == "e_tensor":
                continue
            self._wait(eng, t)
        e = getattr(self.nc, eng)
        tok = None
        if dma is not None:
            self.getsem(dma)
            self.cnt[dma] += 16
            tok = (dma, self.cnt[dma])
            fn(e).then_inc(self.sem[dma], 16)
        elif inc:
            sn = "e_" + eng
            self.getsem(sn)
            self.cnt[sn] += 1
            tok = (sn, self.cnt[sn])
            fn(e).then_inc(self.sem[sn], 1)
        else:
            fn(e)
        self.ninst += 1
        if tok is not None:
            for r in reads:
                d = self.readers.setdefault(r, {})
                d[tok[0]] = max(d.get(tok[0], 0), tok[1])
            for w in writes:
                self.lastw[w] = tok
                self.readers[w] = {}
        return tok

    def barrier(self):
        for eng in ENGS:
            for s, v in self.cnt.items():
                if v > 0:
                    self._wait(eng, (s, v))
        self.lastw = {}
        self.readers = {}


def build_nc(debug_stage=0):
    nc = bass.Bass("TRN2", target_bir_lowering=False)
    di = lambda name, shape: nc.dram_tensor(name, shape, F32, kind="ExternalInput").ap()
    x_rot = di("x_rot", [S, D])
    cos_t = di("cos_t", [128, NT, 64])
    sin_t = di("sin_t", [128, NT, 64])
    cb = di("cb", [128, 8])
    hmask = di("hmask", [128, 2])
    w_ada = di("w_ada", [D, 6 * D])
    b_ada = di("b_ada", [1, 6 * D])
    g1_bd = di("g1_b", [128, D])
    g2_bd = di("g2_b", [128, D])
    gfin_bd = di("gfin_b", [128, D])
    w_in = di("w_in", [D, 1792])
    w_out = di("w_out", [D, D])
    w_up = di("w_up", [D, 2 * DFF])
    w_down = di("w_down", [DFF, D])
    gq_bd = di("gq_b", [128, 64])
    gk_bd = di("gk_b", [128, 64])
    gtv_bd = di("gtv_b", [128, 64])
    wsT_d = di("wsT", [128, 8, 128])
    bsbT_d = di("bsbT", [128, 512])
    gmix_d = di("gmix", [128, 8])
    wconv_d = di("wconv", [128, 3, 44])
    bconv_d = di("bconv", [128, 44])
    ident_d = di("ident", [128, 128])
    out_d = nc.dram_tensor("out", [2048, D], F32, kind="ExternalOutput").ap()
    scr = nc.dram_tensor("scr", [2, 128, D], F32, kind="Internal").ap()
    x1s = nc.dram_tensor("x1s", [2048, D], F32, kind=("ExternalOutput" if debug_stage else "Internal")).ap()
    wub = nc.dram_tensor("wub", [NCT, 128, 8 * 2 * 128], BF16, kind="Internal").ap()
    if debug_stage:
        dbgf = nc.dram_tensor("dbgf", [128, 8192], F32, kind="ExternalOutput").ap()
        dbgb = nc.dram_tensor("dbgb", [128, 65536], BF16, kind="ExternalOutput").ap()

    try:
      with ExitStack() as st0:
        P = Prog(nc, st0)

        def dump(dst, src, res):
            tok = P.op("sync", lambda e: e.dma_start(out=dst, in_=src), reads=[res], dma="d_dbg")
            P._wait("sync", tok)

        def stage_end(k, dumps):
            if debug_stage == k:
                for (dst, src, res) in dumps():
                    dump(dst, src, res)
                print("STOP at stage", k, "inst", P.ninst, "waits", P.nwait, "cnt", P.cnt)
                raise _Stop()

        def SB(st, name, shape, dt):
            return st.enter_context(nc.sbuf_tensor("s_" + name, shape, dt))

        def PS(st, name, shape, dt):
            return st.enter_context(nc.psum_tensor("p_" + name, shape, dt))

        V = lambda fn, r=(), w=(): P.op("vector", fn, r, w)
        A = lambda fn, r=(), w=(): P.op("scalar", fn, r, w)
        G = lambda fn, r=(), w=(): P.op("gpsimd", fn, r, w)
        T = lambda fn, r=(), w=(), inc=True: P.op("tensor", fn, r, w, inc=inc)

        def LD(dst, src, res, sem=None, eng="sync"):
            return P.op(eng, lambda e: e.dma_start(out=dst, in_=src), writes=[res], dma="d_" + res)

        ident_b = SB(st0, "ident_b", [128, 128], BF16)
        ones_f = SB(st0, "ones_f", [128, 128], F32)
        epsc = SB(st0, "epsc", [128, 1], F32)
        G1_b = SB(st0, "G1_b", [128, D], F32)
        gt1_b = SB(st0, "gt1_b", [128, D], F32)
        bias_q = SB(st0, "bias_q", [128, 512], F32)
        bias_kv = SB(st0, "bias_kv", [128, 256], F32)
        bias_zv = SB(st0, "bias_zv", [128, 512], F32)
        biasU = SB(st0, "biasU", [128, 4], F32)
        sh1T = SB(st0, "sh1T", [128, 8], BF16)
        sh2T = SB(st0, "sh2T", [128, 8], BF16)
        gq_b = SB(st0, "gq_b", [128, 64], F32)
        gk_b = SB(st0, "gk_b", [128, 64], F32)
        gtv_b = SB(st0, "gtv_b", [128, 64], F32)
        gmix = SB(st0, "gmix", [128, 8], F32)
        wconv = SB(st0, "wconv", [128, 3, 44], F32)
        bconv = SB(st0, "bconv", [128, 44], F32)
        bias2 = SB(st0, "bias2", [128, 44], F32)
        hm = SB(st0, "hm", [128, 2], F32)
        ss = SB(st0, "ss", [128, NT], F32)
        lnv = SB(st0, "lnv", [128, NT], F32)
        rstd = SB(st0, "rstd", [128, NT], F32)
        sstok = SB(st0, "sstok", [128, NX], F32)
        ssatt = SB(st0, "ssatt", [128, 20], F32)
        rs_t = SB(st0, "rs_t", [128, NX], F32)
        rs_a = SB(st0, "rs_a", [128, 20], F32)
        ss2 = SB(st0, "ss2", [128, NX], F32)
        rs2 = SB(st0, "rs2", [128, NX], F32)
        ssf = SB(st0, "ssf", [128, 16], F32)
        rsf = SB(st0, "rsf", [128, 16], F32)
        sm = SB(st0, "sm", [128, 16], F32)

        LD(ident_b[:], ident_d, "ident_b", "d_c0", eng="gpsimd")
        V(lambda e: e.memset(ones_f[:], 1.0), w=["ones_f"])
        V(lambda e: e.memset(epsc[:], EPS), w=["epsc"])
        for nm, tl, src in [("gq_b", gq_b, gq_bd), ("gk_b", gk_b, gk_bd), ("gtv_b", gtv_b, gtv_bd),
                            ("gmix", gmix, gmix_d), ("wconv", wconv, wconv_d), ("bconv", bconv, bconv_d),
                            ("hm", hm, hmask)]:
            LD(tl[:], src, nm, "d_c1")
        for nm, tl in [("ss", ss), ("sstok", sstok), ("ssatt", ssatt), ("ss2", ss2), ("ssf", ssf), ("rstd", rstd), ("sm", sm)]:
            V(lambda e, tl=tl: e.memset(tl[:], 0.0), w=[nm])

        def rsqrt_cols(dst, src, lo, hi, scale, rn, wn, tmp=None):
            A(lambda e: e.activation(dst[:, lo:hi], src[:, lo:hi], AF.Ln, bias=epsc[:, 0:1], scale=scale),
              r=[rn, "epsc"], w=[wn])
            A(lambda e: e.activation(dst[:, lo:hi], dst[:, lo:hi], AF.Exp, scale=-0.5), r=[wn], w=[wn])

        with ExitStack() as st:
            wada = [SB(st, f"wada{i}", [128, 8, D], BF16) for i in range(2)]
            cbt = SB(st, "cbt", [128, 8], F32)
            scT = SB(st, "scT", [128, 8], BF16)
            brow = SB(st, "brow", [1, 6 * D], F32)
            mrow = [SB(st, f"mrow{i}", [1, D], F32) for i in range(2)]
            gtmp = SB(st, "gtmp", [128, D], F32)
            pMod = PS(st, "pMod", [128, D], F32)
            pB = PS(st, "pB", [128, D], F32)
            pX = PS(st, "pX", [128, 8], F32)
            LD(cbt[:], cb, "cbt", "d_c1")
            LD(brow[:], b_ada, "brow", "d_c1")
            A(lambda e: e.activation(scT[:], cbt[:], AF.Silu), r=["cbt"], w=["scT"])
            for m in range(6):
                wb = wada[m % 2]
                wn = f"wada{m % 2}"
                for kk in range(2):
                    P.op("gpsimd", lambda e, kk=kk, m=m, wb=wb: e.dma_start(
                        out=wb[:, 4 * kk:4 * kk + 4, :],
                        in_=w_ada[512 * kk:512 * kk + 512, m * D:(m + 1) * D].rearrange("(k p) n -> p k n", p=128)),
                        writes=[wn], dma=f"d_wada{m % 2}")
                for c2 in range(2):
                    for k in range(8):
                        T(lambda e, k=k, c2=c2, wb=wb: e.matmul(pMod[0:1, c2 * 512:(c2 + 1) * 512], scT[:, k:k + 1],
                                                                wb[:, k, c2 * 512:(c2 + 1) * 512],
                                                                start=(k == 0), stop=(k == 7)),
                          r=[wn, "scT"], w=["pMod"], inc=(k == 7))
                mr = mrow[m % 2]
                mn = f"mrow{m % 2}"
                V(lambda e, m=m, mr=mr: e.tensor_tensor(mr[:], pMod[0:1, :], brow[0:1, m * D:(m + 1) * D], ALU.add),
                  r=["pMod", "brow"], w=[mn])
                if m in (0, 3):
                    for k in range(8):
                        T(lambda e, k=k, mr=mr: e.matmul(pX[:, k:k + 1], mr[0:1, k * 128:(k + 1) * 128],
                                                         ones_f[0:1, 0:1], start=True, stop=True),
                          r=[mn, "ones_f"], w=["pX"], inc=(k == 7))
                    dst, dn = (sh1T, "sh1T") if m == 0 else (sh2T, "sh2T")
                    V(lambda e, dst=dst: e.tensor_copy(dst[:], pX[:]), r=["pX"], w=[dn])
                else:
                    for c2 in range(2):
                        T(lambda e, c2=c2, mr=mr: e.matmul(pB[:, c2 * 512:(c2 + 1) * 512], ones_f[0:1, :],
                                                           mr[0:1, c2 * 512:(c2 + 1) * 512], start=True, stop=True),
                          r=[mn, "ones_f"], w=["pB"], inc=(c2 == 1))
                    if m in (1, 4):
                        LD(gtmp[:], g1_bd if m == 1 else g2_bd, "gtmp", "d_c2")
                        if m == 1:
                            V(lambda e: e.scalar_tensor_tensor(G1_b[:], pB[:], 1.0, gtmp[:], ALU.add, ALU.mult),
                              r=["pB", "gtmp"], w=["G1_b"])
                        else:
                            V(lambda e: e.scalar_tensor_tensor(gtmp[:], pB[:], 1.0, gtmp[:], ALU.add, ALU.mult),
                              r=["pB", "gtmp"], w=["gtmp"])
                            P.op("sync", lambda e: e.dma_start(out=scr[0], in_=gtmp[:]), reads=["gtmp"], dma="d_scr")
                    elif m == 2:
                        V(lambda e: e.tensor_copy(gt1_b[:], pB[:]), r=["pB"], w=["gt1_b"])
                    else:
                        V(lambda e: e.tensor_copy(gtmp[:], pB[:]), r=["pB"], w=["gtmp"])
                        P.op("sync", lambda e: e.dma_start(out=scr[1], in_=gtmp[:]), reads=["gtmp"], dma="d_scr")
            P.barrier()
            stage_end(1, lambda: [(dbgf[:, 0:1024], G1_b[:], "G1_b"), (dbgf[:, 1024:2048], gt1_b[:], "gt1_b"), (dbgb[:, 0:8], sh1T[:], "sh1T"), (dbgb[:, 8:16], sh2T[:], "sh2T")])

        with ExitStack() as stAR, ExitStack() as stA:
            AR1 = SB(stAR, "AR1", [128, S + NT * 2 * 129], BF16)
            KT = AR1[:, 0:S]
            VVf = AR1[:, S:S + NT * 2 * 129]
            VV = VVf.rearrange("p (t g c) -> p t g c", t=NT, g=2)
            VVk = VVf.rearrange("p (t c) -> p t c", t=NT)
            QT = SB(stA, "QT", [128, 4, NCOLS], BF16)
            mixT = SB(stA, "mixT", [128, 8, NCOLS], BF16)
            V(lambda e: e.memset(AR1[:], 0.0), w=["VV", "KT"])
            V(lambda e: e.memset(mixT[:, 0:4, 0:128], 0.0), w=["mixT"])
            V(lambda e: e.memset(mixT[:, 0:4, 2176:2304], 0.0), w=["mixT"])
            if debug_stage:
                V(lambda e: e.memset(QT[:], 0.0), w=["QT"])
            V(lambda e: e.memset(VV[:, :, :, 0:1], 1.0), w=["VV"])
            V(lambda e: e.memset(VV[:, :, :, 128:129], 1.0), w=["VV"])

            with ExitStack() as st:
                Win = SB(st, "Win", [128, 8, 1792], BF16)
                wsT = SB(st, "wsTb", [128, 8, 128], BF16)
                bsbT = SB(st, "bsbT", [128, 512], F32)
                xt = [SB(st, f"xt{i}", [128, D], F32) for i in range(2)]
                junk = SB(st, "junk", [128, D], BF16)
                xn = [SB(st, f"xn{i}", [128, D], BF16) for i in range(2)]
                xnT = [SB(st, f"xnT{i}", [128, 8, 128], BF16) for i in range(2)]
                cst = [SB(st, f"cst{i}", [128, 64], F32) for i in range(2)]
                snt = [SB(st, f"snt{i}", [128, 64], F32) for i in range(2)]
                kvf = SB(st, "kvf", [128, 256], F32)
                ksq = SB(st, "ksq", [128, 128], F32)
                kra = SB(st, "kra", [128, 128], F32)
                krb = SB(st, "krb", [128, 128], F32)
                kbf = SB(st, "kbf", [128, 128], BF16)
                f1 = SB(st, "f1", [128, 512], F32)
                f2 = SB(st, "f2", [128, 512], F32)
                f3 = SB(st, "f3", [128, 512], F32)
                qbf = SB(st, "qbf", [128, 512], BF16)
                vnpad = SB(st, "vnpad", [128, 8, 128], BF16)
                uT = SB(st, "uT", [128, 512], F32)
                tk = SB(st, "tk", [128, 512], F32)
                sqt = SB(st, "sqt", [128, 512], F32)
                pT = [PS(st, f"pT{i}", [128, D], BF16) for i in range(2)]
                pKV = PS(st, "pKV", [128, 512], F32)
                pQ = PS(st, "pQ", [128, 512], F32)
                pZV = PS(st, "pZV", [128, 512], F32)
                pU = PS(st, "pU", [128, 512], F32)
                pTQ = PS(st, "pTQ", [128, D], BF16)
                pM = PS(st, "pM", [128, 512], F32)

                for kk in range(2):
                    P.op("gpsimd", lambda e, kk=kk: e.dma_start(
                        out=Win[:, 4 * kk:4 * kk + 4, :],
                        in_=w_in[512 * kk:512 * kk + 512, :].rearrange("(k p) n -> p k n", p=128)),
                        writes=["Win"], dma="d_win")
                LD(wsT[:], wsT_d, "wsT", "d_c3", eng="gpsimd")
                LD(bsbT[:], bsbT_d, "bsbT", "d_c1")
                V(lambda e: e.memset(vnpad[:], 0.0), w=["vnpad"])

                colgrp = [(0, 512, bias_q, "bias_q", 0), (512, 256, bias_kv, "bias_kv", 512),
                          (1280, 512, bias_zv, "bias_zv", 768)]
                for (c0, n, dst, dn, r0) in colgrp:
                    for k in range(8):
                        T(lambda e, k=k, c0=c0, n=n: e.matmul(pQ[0:1, 0:n], sh1T[:, k:k + 1], Win[:, k, c0:c0 + n],
                                                              start=(k == 0), stop=(k == 7)),
                          r=["Win", "sh1T"], w=["pQ"], inc=(k == 7))
                    V(lambda e, n=n, r0=r0: e.tensor_copy(tk[0:1, 0:n], pQ[0:1, 0:n]), r=["pQ"], w=["tk"])
                    T(lambda e, n=n, r0=r0: e.matmul(pZV[:, 0:n], ones_f[0:1, :], tk[0:1, 0:n],
                                                     start=True, stop=True), r=["tk", "ones_f"], w=["pZV"])
                    V(lambda e, n=n, dst=dst: e.tensor_copy(dst[:, 0:n], pZV[:, 0:n]), r=["pZV"], w=[dn])
                for c4 in range(4):
                    for k in range(8):
                        T(lambda e, k=k, c4=c4: e.matmul(pU[:, c4:c4 + 1], Win[:, k, 768 + c4 * 128:768 + (c4 + 1) * 128],
                                                         sh1T[:, k:k + 1], start=(k == 0), stop=(k == 7)),
                          r=["Win", "sh1T"], w=["pU"], inc=(k == 7 and c4 == 3))
                V(lambda e: e.tensor_copy(biasU[:], pU[:, 0:4]), r=["pU"], w=["biasU"])


                qf = SB(st, "qf", [128, 512], F32)
                zvf = SB(st, "zvf", [128, 512], F32)
                sm18 = SB(st, "sm18", [128, 18], F32)
                V(lambda e: e.memset(sm18[:], 1.0), w=["sm18"])
                print("sbuf remaining at AB:", nc.sbuf_bytes_remaining)

                def v3(a, nh):
                    return a[:, 0:nh * 64].rearrange("p (h d) -> p h d", h=nh)

                def front(t):
                    s = t % 2
                    own = t < NX
                    LD(xt[s][:], x_rot[t * 128:(t + 1) * 128, :], f"xt{s}")
                    LD(cst[s][:], cos_t[:, t, :], f"cst{s}")
                    LD(snt[s][:], sin_t[:, t, :], f"snt{s}")
                    A(lambda e: e.activation(junk[:], xt[s][:], AF.Square, accum_out=ss[:, t:t + 1]),
                      r=[f"xt{s}"], w=["junk", "ss"])
                    rsqrt_cols(rstd, ss, t, t + 1, 1.0 / D, "ss", "rstd")
                    V(lambda e: e.scalar_tensor_tensor(xn[s][:], xt[s][:], rstd[:, t:t + 1], G1_b[:], ALU.mult, ALU.mult),
                      r=[f"xt{s}", "rstd", "G1_b"], w=[f"xn{s}"])
                    for k in range(8):
                        T(lambda e, k=k: e.transpose(pT[s][:, k * 128:(k + 1) * 128], xn[s][:, k * 128:(k + 1) * 128], ident_b[:]),
                          r=[f"xn{s}", "ident_b"], w=[f"pT{s}"], inc=(k == 7))
                    A(lambda e: e.copy(xnT[s][:].rearrange("p k n -> p (k n)"), pT[s][:]), r=[f"pT{s}"], w=[f"xnT{s}"])
                    for k in range(8):
                        T(lambda e, k=k: e.matmul(pKV[:, 0:256], xnT[s][:, k, :], Win[:, k, 512:768],
                                                  start=(k == 0), stop=(k == 7)),
                          r=[f"xnT{s}", "Win"], w=["pKV"], inc=(k == 7))
                    if own:
                        for k in range(8):
                            T(lambda e, k=k: e.matmul(pQ[:, :], xnT[s][:, k, :], Win[:, k, 0:512],
                                                      start=(k == 0), stop=(k == 7)),
                              r=[f"xnT{s}", "Win"], w=["pQ"], inc=(k == 7))
                        for k in range(8):
                            T(lambda e, k=k: e.matmul(pZV[:, :], xnT[s][:, k, :], Win[:, k, 1280:1792],
                                                      start=(k == 0), stop=(k == 7)),
                              r=[f"xnT{s}", "Win"], w=["pZV"], inc=(k == 7))
                        for c4 in range(4):
                            for k in range(8):
                                T(lambda e, k=k, c4=c4: e.matmul(pU[:, c4 * 128:(c4 + 1) * 128],
                                                                 Win[:, k, 768 + c4 * 128:768 + (c4 + 1) * 128],
                                                                 xnT[s][:, k, :], start=(k == 0), stop=(k == 7)),
                                  r=[f"xnT{s}", "Win"], w=["pU"], inc=(k == 7 and c4 == 3))

                def mid(t):
                    own = t < NX
                    V(lambda e: e.tensor_tensor(kvf[:], pKV[:, 0:256], bias_kv[:], ALU.add), r=["pKV", "bias_kv"], w=["kvf"])
                    if own:
                        V(lambda e: e.tensor_tensor(zvf[:], pZV[:], bias_zv[:], ALU.add), r=["pZV", "bias_zv"], w=["zvf"])
                        A(lambda e: e.activation(zvf[:], zvf[:], AF.Gelu_apprx_tanh), r=["zvf"], w=["zvf"])
                        V(lambda e: e.tensor_tensor(qf[:], pQ[:], bias_q[:], ALU.add), r=["pQ", "bias_q"], w=["qf"])
                        for c4 in range(4):
                            A(lambda e, c4=c4: e.activation(uT[:, c4 * 128:(c4 + 1) * 128], pU[:, c4 * 128:(c4 + 1) * 128],
                                                            AF.Gelu_apprx_tanh, bias=biasU[:, c4:c4 + 1]),
                              r=["pU", "biasU"], w=["uT"])

                def rope2(src, sname, nh, ra, raname, rb, rbname, dst, dname, cs, csn, sn, snn):
                    V(lambda e: e.tensor_tensor(v3(ra, nh), v3(src, nh), cs[:].unsqueeze(1).to_broadcast([128, nh, 64]), ALU.mult),
                      r=[sname, csn], w=[raname])
                    for blk in range(4):
                        pb = blk ^ 1
                        G(lambda e, blk=blk, pb=pb: e.tensor_tensor(
                            v3(rb, nh)[:, :, blk * 16:(blk + 1) * 16], v3(src, nh)[:, :, pb * 16:(pb + 1) * 16],
                            sn[:, blk * 16:(blk + 1) * 16].unsqueeze(1).to_broadcast([128, nh, 16]), ALU.mult),
                          r=[sname, snn], w=[rbname])
                    V(lambda e: e.tensor_tensor(dst[:, 0:nh * 64], ra[:, 0:nh * 64], rb[:, 0:nh * 64], ALU.add),
                      r=[raname, rbname], w=[dname])

                def back_a(t):
                    s = t % 2
                    own = t < NX
                    G(lambda e: e.tensor_copy(VV[:, t, :, 64:128], kvf[:, 128:256].rearrange("p (g d) -> p g d", g=2)),
                      r=["kvf"], w=["VV"])
                    G(lambda e: e.tensor_tensor(ksq[:], kvf[:, 0:128], kvf[:, 0:128], ALU.mult), r=["kvf"], w=["ksq"])
                    V(lambda e: e.tensor_reduce(sm18[:, 0:2], v3(ksq, 2), axis=AX.X, op=ALU.add), r=["ksq"], w=["sm18"])
                    if own:
                        G(lambda e: e.tensor_tensor(f3[:], qf[:], qf[:], ALU.mult), r=["qf"], w=["f3"])
                        V(lambda e: e.tensor_reduce(sm18[:, 2:10], v3(f3, 8), axis=AX.X, op=ALU.add), r=["f3"], w=["sm18"])
                        G(lambda e: e.tensor_tensor(f1[:], zvf[:], zvf[:], ALU.mult), r=["zvf"], w=["f1"])
                        V(lambda e: e.tensor_reduce(sm18[:, 10:18], v3(f1, 8), axis=AX.X, op=ALU.add), r=["f1"], w=["sm18"])
                    nc_ = 18 if own else 2
                    rsqrt_cols(sm18, sm18, 0, nc_, 1.0 / 64, "sm18", "sm18")

                def back_b(t):
                    s = t % 2
                    own = t < NX
                    V(lambda e: e.tensor_tensor(v3(kra, 2), v3(kvf, 2), sm18[:, 0:2].unsqueeze(2).to_broadcast([128, 2, 64]), ALU.mult),
                      r=["kvf", "sm18"], w=["kra"])
                    V(lambda e: e.tensor_tensor(v3(kra, 2), v3(kra, 2), gk_b[:].unsqueeze(1).to_broadcast([128, 2, 64]), ALU.mult),
                      r=["kra", "gk_b"], w=["kra"])
                    rope2(kra, "kra", 2, ksq, "ksq", krb, "krb", kbf, "kbf", cst[s], f"cst{s}", snt[s], f"snt{s}")
                    T(lambda e: e.transpose(pTQ[:, 512:640], kbf[:], ident_b[:]), r=["kbf", "ident_b"], w=["pTQk"])
                    kt_copy = lambda: V(lambda e: e.tensor_copy(KT[:, t * 128:(t + 1) * 128], pTQ[:, 512:640]), r=["pTQk"], w=["KT"])
                    if not own:
                        pending.append(kt_copy)
                        return
                    kt_copy()
                    V(lambda e: e.tensor_tensor(v3(f2, 8), v3(qf, 8), sm18[:, 2:10].unsqueeze(2).to_broadcast([128, 8, 64]), ALU.mult),
                      r=["qf", "sm18"], w=["f2"])
                    V(lambda e: e.tensor_tensor(v3(f2, 8), v3(f2, 8), gq_b[:].unsqueeze(1).to_broadcast([128, 8, 64]), ALU.mult),
                      r=["f2", "gq_b"], w=["f2"])
                    rope2(f2, "f2", 8, f3, "f3", f1, "f1", qbf, "qbf", cst[s], f"cst{s}", snt[s], f"snt{s}")
                    for p in range(4):
                        T(lambda e, p=p: e.transpose(pTQ[:, p * 128:(p + 1) * 128], qbf[:, p * 128:(p + 1) * 128], ident_b[:]),
                          r=["qbf", "ident_b"], w=["pTQ"], inc=(p == 3))
                    V(lambda e: e.tensor_tensor(v3(f2, 8), v3(zvf, 8), sm18[:, 10:18].unsqueeze(2).to_broadcast([128, 8, 64]), ALU.mult),
                      r=["zvf", "sm18"], w=["f2"])
                    for sl in range(2):
                        G(lambda e, sl=sl: e.tensor_tensor(
                            vnpad[:].rearrange("p (c s) n -> p c s n", s=2)[:, :, sl, sl * 64:(sl + 1) * 64],
                            f2[:].rearrange("p (c s d) -> p c s d", s=2, d=64)[:, :, sl, :],
                            gtv_b[:].unsqueeze(1).to_broadcast([128, 4, 64]), ALU.mult),
                          r=["f2", "gtv_b"], w=["vnpad"])
                    for c4 in range(4):
                        for sl in range(2):
                            T(lambda e, c4=c4, sl=sl: e.matmul(pM[:, c4 * 128:(c4 + 1) * 128], vnpad[:, 2 * c4 + sl, :],
                                                               wsT[:, 2 * c4 + sl, :], start=(sl == 0), stop=(sl == 1)),
                              r=["vnpad", "wsT"], w=["pM"], inc=(sl == 1 and c4 == 3))
                    V(lambda e: e.tensor_copy(QT[:, :, t * 128:(t + 1) * 128], pTQ[:, 0:512].rearrange("p (a n) -> p a n", a=4)),
                      r=["pTQ"], w=["QT"])
                    V(lambda e: e.tensor_tensor(tk[:], pM[:], bsbT[:], ALU.add), r=["pM", "bsbT"], w=["tk"])
                    V(lambda e: e.tensor_tensor(tk[:], tk[:], uT[:], ALU.mult), r=["tk", "uT"], w=["tk"])
                    G(lambda e: e.tensor_tensor(sqt[:], tk[:], tk[:], ALU.mult), r=["tk"], w=["sqt"])
                    for c4 in range(4):
                        T(lambda e, c4=c4: e.matmul(pM[:, 0:1] if False else pTS[:, 0:1], sqt[:, c4 * 128:(c4 + 1) * 128], ones_f[:, 0:1],
                                                    start=(c4 == 0), stop=(c4 == 3)),
                          r=["sqt", "ones_f"], w=["pTS"], inc=(c4 == 3))
                    V(lambda e: e.tensor_tensor(mixT[:, 4:8, t * 128:(t + 1) * 128],
                                                tk[:].rearrange("p (c n) -> p c n", c=4),
                                                gmix[:, 4:8].unsqueeze(2).to_broadcast([128, 4, 128]), ALU.mult),
                      r=["tk", "gmix"], w=["mixT"])
                    V(lambda e: e.tensor_copy(sstok[:, t:t + 1], pTS[:, 0:1]), r=["pTS"], w=["sstok"])

                pTS = pKV[:, 256:512]
                pending = []
                front(0)
                mid(0)
                for t in range(AB_TILES):
                    if t + 1 < AB_TILES:
                        front(t + 1)
                    back_a(t)
                    back_b(t)
                    if t + 1 < AB_TILES:
                        mid(t + 1)
                    for f_ in pending:
                        f_()
                    pending.clear()

                rsqrt_cols(rs_t, sstok, 0, NX, 1.0 / 512, "sstok", "rs_t")
                P.barrier()
                stage_end(2, lambda: [(dbgb[:, 0:8192], KT, "KT"), (dbgb[:, 8192:17408], QT[:].rearrange("p a n -> p (a n)"), "QT"), (dbgb[:, 17408:26624], mixT[:, 4:8, :].rearrange("p a n -> p (a n)"), "mixT"), (dbgb[:, 26624:43136], VVf, "VV"), (dbgf[:, 0:18], rs_t[:, 0:18], "rs_t"), (dbgf[:, 64:128], rstd[:, :], "rstd"), (dbgf[:, 128:132], biasU[:], "biasU"), (dbgf[:, 1024:1536], bias_q[:], "bias_q"), (dbgf[:, 1536:1792], bias_kv[:], "bias_kv"), (dbgf[:, 2048:2560], bias_zv[:], "bias_zv")])

            with ExitStack() as st:
                Pb = [SB(st, f"Pb{i}", [128, 1024], BF16) for i in range(3)]
                Oa = SB(st, "Oa", [128, 512], F32)
                Ob = SB(st, "Ob", [128, 512], F32)
                rden = SB(st, "rden", [128, 512], F32)
                at = SB(st, "at", [128, 512], F32)
                sq = SB(st, "sq", [128, 512], F32)
                acc = SB(st, "acc", [128, 512], F32)
                pS = [PS(st, f"pS{i}", [128, 1024], F32) for i in range(2)]
                pOa = PS(st, "pOa", [128, 512], F32)
                pOb = PS(st, "pOb", [128, 512], F32)
                pBc = PS(st, "pBc", [128, 512], F32)
                pSA = PS(st, "pSA", [128, 512], F32)
                for ct in range(NCT):
                    for ab in range(2):
                        P.op("gpsimd", lambda e: e.dma_start(
                            out=wub[ct].rearrange("p (k a n) -> p k a n", k=8, a=2)[:, :, ab, :],
                            in_=w_up[:, ab * DFF + ct * 128:ab * DFF + (ct + 1) * 128].rearrange("(k p) n -> p k n", p=128)),
                            writes=[f"wub{ct}_{ab}"], dma="d_wub")
                QH = SB(st, "QH", [128, 4, 2], BF16)
                hs = SB(st, "hs", [128, 2], F32)

                def qk(kt, i, p, q0, qn):
                    b = pS[i % 2]
                    T(lambda e: e.matmul(b[:, 0:qn], KT[0:64, kt * 128:(kt + 1) * 128], QT[0:64, p, q0:q0 + qn],
                                         start=True, stop=True), r=["KT", "QT"], w=[f"pS{i % 2}"], inc=False)
                    T(lambda e: e.matmul(b[:, 512:512 + qn], KT[64:128, kt * 128:(kt + 1) * 128],
                                         QT[64:128, p, q0:q0 + qn], start=True, stop=True),
                      r=["KT", "QT"], w=[f"pS{i % 2}"])

                def ex(kt, i):
                    A(lambda e: e.activation(Pb[i % 3][:], pS[i % 2][:], AF.Exp, scale=0.125),
                      r=[f"pS{i % 2}"], w=[f"Pb{i % 3}"])

                def pv(kt, i, qn):
                    pb = Pb[i % 3]
                    T(lambda e: e.matmul(pOa[:, 0:qn], VVk[:, kt, 64:192], pb[:, 0:qn],
                                         start=(kt == 0), stop=(kt == NT - 1)),
                      r=[f"Pb{i % 3}", "VV"], w=["pOa"], inc=False)
                    T(lambda e: e.matmul(pOb[:, 0:qn], VV[:, kt, 1, 0:128], pb[:, 512:512 + qn],
                                         start=(kt == 0), stop=(kt == NT - 1)),
                      r=[f"Pb{i % 3}", "VV"], w=["pOa", "pOb"])

                def epi_a(qn):
                    V(lambda e: e.tensor_copy(Oa[0:65, 0:qn], pOa[0:65, 0:qn]), r=["pOa"], w=["Oa"])
                    V(lambda e: e.tensor_copy(Ob[:, 0:qn], pOb[:, 0:qn]), r=["pOb"], w=["Ob"])
                    V(lambda e: e.reciprocal(rden[64:65, 0:qn], Oa[64:65, 0:qn]), r=["Oa"], w=["rdenA"])
                    V(lambda e: e.reciprocal(rden[0:1, 0:qn], Ob[0:1, 0:qn]), r=["Ob"], w=["rdenB"])

                def epi_b(qn):
                    T(lambda e: e.matmul(pBc[:, 0:qn], ones_f[64:65, :], rden[64:65, 0:qn], start=True, stop=True),
                      r=["rdenA", "ones_f"], w=["pBc"], inc=False)
                    T(lambda e: e.matmul(pSA[:, 0:qn], ones_f[0:1, :], rden[0:1, 0:qn], start=True, stop=True),
                      r=["rdenB", "ones_f"], w=["pBc", "pSA"])
                    V(lambda e: e.tensor_tensor(at[0:64, 0:qn], Oa[0:64, 0:qn], pBc[0:64, 0:qn], ALU.mult),
                      r=["Oa", "pBc"], w=["at"])
                    V(lambda e: e.tensor_tensor(at[64:128, 0:qn], Ob[64:128, 0:qn], pSA[64:128, 0:qn], ALU.mult),
                      r=["Ob", "pSA"], w=["at"])

                def epi_main(qi, p, q0):
                    epi_b(512)
                    V(lambda e: e.tensor_scalar(mixT[:, p, q0:q0 + 512], at[:, :], gmix[:, p:p + 1], None, ALU.mult),
                      r=["at", "gmix"], w=["mixT"])
                    if p == 0:
                        G(lambda e: e.tensor_tensor(acc[:, :], at[:, :], at[:, :], ALU.mult), r=["at"], w=["acc"])
                    else:
                        G(lambda e: e.tensor_tensor(sq[:, :], at[:, :], at[:, :], ALU.mult), r=["at"], w=["sq"])
                        G(lambda e: e.tensor_tensor(acc[:, :], acc[:, :], sq[:, :], ALU.add), r=["acc", "sq"], w=["acc"])

                def epi_ss(qi):
                    for w_ in range(4):
                        T(lambda e, w_=w_: e.matmul(pSA[:, w_:w_ + 1], acc[:, w_ * 128:(w_ + 1) * 128], ones_f[:, 0:1],
                                                    start=True, stop=True), r=["acc", "ones_f"], w=["pSA"], inc=(w_ == 3))
                    V(lambda e: e.tensor_copy(ssatt[:, 1 + qi * 4:1 + qi * 4 + 4], pSA[:, 0:4]), r=["pSA"], w=["ssatt"])

                V(lambda e: e.tensor_copy(QH[:, :, 0:1], QT[:, :, 127:128]), r=["QT"], w=["QH"])
                V(lambda e: e.tensor_copy(QH[:, :, 1:2], QT[:, :, 2176:2177]), r=["QT"], w=["QH"])
                for g in range(2):
                    for kt in range(NT):
                        T(lambda e: e.matmul(pS[0][:, g * 512 + kt * 8:g * 512 + kt * 8 + 8],
                                             KT[g * 64:(g + 1) * 64, kt * 128:(kt + 1) * 128], QH[g * 64:(g + 1) * 64, :, :],
                                             start=True, stop=True), r=["KT", "QH"], w=["pS0"], inc=(kt == NT - 1))
                ex(0, 0)
                for kt in range(NT):
                    T(lambda e: e.matmul(pOa[0:65, 0:8], VV[:, kt, 0, 64:129], Pb[0][:, kt * 8:kt * 8 + 8],
                                         start=(kt == 0), stop=(kt == NT - 1)), r=["Pb0", "VV"], w=["pOa"], inc=(kt == NT - 1))
                for kt in range(NT):
                    T(lambda e: e.matmul(pOb[:, 0:8], VV[:, kt, 1, 0:128], Pb[0][:, 512 + kt * 8:512 + kt * 8 + 8],
                                         start=(kt == 0), stop=(kt == NT - 1)), r=["Pb0", "VV"], w=["pOb"], inc=(kt == NT - 1))
                epi_a(8)
                epi_b(8)
                at3 = at[:, 0:8].rearrange("p (a t) -> p a t", t=2)
                for tk_, col in ((0, 127), (1, 2176)):
                    V(lambda e: e.tensor_tensor(mixT[:, 0:4, col:col + 1], at3[:, :, tk_:tk_ + 1], gmix[:, 0:4].unsqueeze(2), ALU.mult),
                      r=["at", "gmix"], w=["mixT"])
                V(lambda e: e.tensor_tensor(sq[:, 0:8], at[:, 0:8], at[:, 0:8], ALU.mult), r=["at"], w=["sq"])
                V(lambda e: e.tensor_reduce(hs[:, 0:2], sq[:, 0:8].rearrange("p (a t) -> p t a", t=2), axis=AX.X, op=ALU.add),
                  r=["sq"], w=["hs"])
                V(lambda e: e.memset(acc[:, 0:256], 0.0), w=["acc"])
                V(lambda e: e.tensor_copy(acc[:, 127:129], hs[:, 0:2]), r=["hs"], w=["acc"])
                for w_ in range(2):
                    T(lambda e, w_=w_: e.matmul(pSA[:, w_:w_ + 1], acc[:, w_ * 128:(w_ + 1) * 128], ones_f[:, 0:1],
                                                start=True, stop=True), r=["acc", "ones_f"], w=["pSA"], inc=(w_ == 1))
                V(lambda e: e.tensor_copy(ssatt[:, 0:1], pSA[:, 0:1]), r=["pSA"], w=["ssatt"])
                V(lambda e: e.tensor_copy(ssatt[:, 17:18], pSA[:, 1:2]), r=["pSA"], w=["ssatt"])

                iters = [(qi, p) for qi in range(4) for p in range(4)]
                step = 1
                pend = None
                for (qi, p) in iters:
                    q0 = 128 + qi * 512
                    qk(0, step, p, q0, 512)
                    qk(1, step + 1, p, q0, 512)
                    for kt in range(NT):
                        ex(kt, step + kt)
                        if kt + 2 < NT:
                            qk(kt + 2, step + kt + 2, p, q0, 512)
                        pv(kt, step + kt, 512)
                        if kt == 3 and pend is not None:
                            epi_main(*pend)
                        if kt == 16 and pend is not None:
                            if pend[1] == 3:
                                epi_ss(pend[0])
                            pend = None
                    step += NT
                    epi_a(512)
                    pend = (qi, p, q0)
                epi_main(*pend)
                epi_ss(3)

                rsqrt_cols(rs_a, ssatt, 0, NX, 1.0 / 512, "ssatt", "rs_a")
                P.barrier()
                stage_end(3, lambda: [(dbgb[:, 0:9216], mixT[:, 0:4, :].rearrange("p a n -> p (a n)"), "mixT"), (dbgf[:, 0:18], rs_a[:, 0:18], "rs_a")])

            x1nT = AR1[:, 0:8 * NCOLS].rearrange("p (k n) -> p k n", k=8)
            with ExitStack() as st2:
                G2_b = SB(st2, "G2_b", [128, D], F32)
                Wout = SB(st2, "Wout", [128, 8, D], BF16)
                xe = [SB(st2, f"xe{i}", [128, D], F32) for i in range(2)]
                x1h = [SB(st2, f"x1h{i}", [128, D], F32) for i in range(2)]
                t1 = SB(st2, "t1", [128, D], F32)
                x1n = [SB(st2, f"x1n{i}", [128, D], BF16) for i in range(2)]
                junk2 = SB(st2, "junk2", [128, D], BF16)
                pYa = PS(st2, "pYa", [128, D], F32)
                pYb = PS(st2, "pYb", [128, D], F32)
                pT2 = [PS(st2, f"pT2{i}", [128, D], BF16) for i in range(2)]
                LD(G2_b[:], scr[0], "G2_b", "d_c2")
                for kk in range(2):
                    P.op("gpsimd", lambda e, kk=kk: e.dma_start(
                        out=Wout[:, 4 * kk:4 * kk + 4, :],
                        in_=w_out[512 * kk:512 * kk + 512, :].rearrange("(k p) n -> p k n", p=128)),
                        writes=["Wout"], dma="d_wout")
                def out_mm(t):
                    s = t % 2
                    LD(xe[s][:], x_rot[t * 128:(t + 1) * 128, :], f"xe{s}")
                    for c2 in range(2):
                        for k in range(4):
                            T(lambda e, k=k, c2=c2: e.matmul(pYa[:, c2 * 512:(c2 + 1) * 512], mixT[:, k, t * 128:(t + 1) * 128],
                                                             Wout[:, k, c2 * 512:(c2 + 1) * 512], start=(k == 0), stop=(k == 3)),
                              r=["mixT", "Wout"], w=["pYa"], inc=(k == 3 and c2 == 1))
                    for c2 in range(2):
                        for k in range(4, 8):
                            T(lambda e, k=k, c2=c2: e.matmul(pYb[:, c2 * 512:(c2 + 1) * 512], mixT[:, k, t * 128:(t + 1) * 128],
                                                             Wout[:, k, c2 * 512:(c2 + 1) * 512], start=(k == 4), stop=(k == 7)),
                              r=["mixT", "Wout"], w=["pYb"], inc=(k == 7 and c2 == 1))

                def out_ep1(t):
                    s = t % 2
                    xd = x1h[s][:]
                    xdn = f"x1h{s}"
                    A(lambda e: e.activation(t1[:], pYa[:], AF.Identity, scale=rs_a[:, t:t + 1]), r=["pYa", "rs_a"], w=["t1"])
                    V(lambda e: e.scalar_tensor_tensor(t1[:], pYb[:], rs_t[:, t:t + 1], t1[:], ALU.mult, ALU.add),
                      r=["pYb", "rs_t", "t1"], w=["t1"])

                def out_ep2(t):
                    s = t % 2
                    xd = x1h[s][:]
                    xdn = f"x1h{s}"
                    V(lambda e: e.tensor_tensor(t1[:], t1[:], gt1_b[:], ALU.mult), r=["t1", "gt1_b"], w=["t1"])
                    V(lambda e: e.tensor_tensor(xd, t1[:], xe[s][:], ALU.add), r=["t1", f"xe{s}"], w=[xdn])
                    if 1 <= t <= 16:
                        P.op("sync", lambda e: e.dma_start(out=x1s[(t - 1) * 128:t * 128, :], in_=xd), reads=[xdn], dma=f"d_x1s{s}")
                    A(lambda e: e.activation(junk2[:], xd, AF.Square, accum_out=ss2[:, t:t + 1]), r=[xdn], w=["junk2", "ss2"])
                    rsqrt_cols(rs2, ss2, t, t + 1, 1.0 / D, "ss2", "rs2")
                    V(lambda e: e.scalar_tensor_tensor(x1n[s][:], xd, rs2[:, t:t + 1], G2_b[:], ALU.mult, ALU.mult),
                      r=[xdn, "rs2", "G2_b"], w=[f"x1n{s}"])
                    for k in range(8):
                        T(lambda e, k=k: e.transpose(pT2[s][:, k * 128:(k + 1) * 128], x1n[s][:, k * 128:(k + 1) * 128], ident_b[:]),
                          r=[f"x1n{s}", "ident_b"], w=[f"pT2{s}"], inc=(k == 7))
                    V(lambda e: e.tensor_copy(x1nT[:, :, t * 128:(t + 1) * 128], pT2[s][:].rearrange("p (k n) -> p k n", k=8)),
                      r=[f"pT2{s}"], w=["x1nT"])

                out_mm(0)
                for t in range(NX):
                    out_ep1(t)
                    if t + 1 < NX:
                        out_mm(t + 1)
                    out_ep2(t)
                P.barrier()
                stage_end(4, lambda: [(dbgb[:, 0:18432], AR1[:, 0:18432], "x1nT")])
                print("sbuf remaining at OUT:", nc.sbuf_bytes_remaining)

            stA.close()
            with ExitStack() as st2:
                NWU = 2
                GW = 2
                groups = [(c, min(GW, NCT - c)) for c in range(0, NCT, GW)]
                NG = len(groups)
                gt2_b = SB(st2, "gt2_b", [128, D], F32)
                gfin_b = SB(st2, "gfin_b", [128, D], F32)
                gT = SB(st2, "gT", [128, NCT, 512], BF16)
                tail0 = 8 * NCOLS
                wu = [AR1[:, tail0:tail0 + 8 * 2 * GW * 128].rearrange("p (c f) -> p c f", c=GW),
                      SB(st2, "wu1", [128, GW, 8 * 2 * 128], BF16)]
                Wd = SB(st2, "Wd", [128, NCT, D], BF16)
                zb = [SB(st2, f"zb{i}", [128, 2, 514], F32) for i in range(2)]
                cv = [SB(st2, f"cv{i}", [128, 2, 512], F32) for i in range(2)]
                sl_ = [SB(st2, f"sl{i}", [128, 512], F32) for i in range(2)]
                y2q = SB(st2, "y2q", [128, 4, D], F32)
                xr = [SB(st2, f"xr{i}", [128, D], F32) for i in range(2)]
                junk3 = AR1[:, tail0 + 8 * 2 * GW * 128:tail0 + 8 * 2 * GW * 128 + D]
                ot = [SB(st2, f"ot{i}", [128, D], F32) for i in range(2)]
                pZ = [PS(st2, f"pZ{i}", [128, 512], F32) for i in range(4)]
                pY = [PS(st2, f"pY{i}", [128, 512], F32) for i in range(4)]
                LD(gt2_b[:], scr[1], "gt2_b")
                LD(gfin_b[:], gfin_bd, "gfin_b")
                print("sbuf remaining at FFN:", nc.sbuf_bytes_remaining)

                def ld_wu(gg):
                    c0_, n_ = groups[gg % NG]
                    s = gg % NWU
                    P.op("sync", lambda e: e.dma_start(out=wu[s][:, 0:n_, :], in_=wub[c0_:c0_ + n_].rearrange("c p f -> p c f")),
                         writes=[f"wu{s}"], dma=f"d_wu{s}")

                def wsl(qtr, ct, k, ab):
                    gi = ct // GW
                    s = (qtr * NG + gi) % NWU
                    j = ct - groups[gi][0]
                    return wu[s][:, j, :].rearrange("p (k a n) -> p k a n", k=8, a=2)[:, k, ab, :], f"wu{s}"

                def ld_wd(ct):
                    P.op("gpsimd", lambda e: e.dma_start(out=Wd[:, ct, :], in_=w_down[ct * 128:(ct + 1) * 128, :]),
                         writes=[f"Wd{ct}"], dma=f"d_Wd{ct}")

                def up_tail(qtr, ct):
                    ci = ct % 2
                    A(lambda e: e.activation(sl_[ci][:], cv[ci][:, 0, :], AF.Silu), r=[f"cv{ci}"], w=[f"sl{ci}"])
                    V(lambda e: e.tensor_tensor(gT[:, ct, :], sl_[ci][:], cv[ci][:, 1, :], ALU.mult),
                      r=[f"sl{ci}", f"cv{ci}"], w=["gT"])

                def down_mm(ct, c2):
                    for tt in range(4):
                        T(lambda e, tt=tt: e.matmul(pY[tt][:, :], gT[:, ct, tt * 128:(tt + 1) * 128],
                                                    Wd[:, ct, c2 * 512:(c2 + 1) * 512],
                                                    start=(ct == 0), stop=(ct == NCT - 1)),
                          r=["gT", f"Wd{ct}"], w=[f"pY{tt}"], inc=(ct == NCT - 1 or tt == 3))

                def bias2_for(ct):
                    pb = pY[ct % 4]
                    for ab in range(2):
                        for k in range(8):
                            wl, wn = wsl(0, ct, k, ab)
                            T(lambda e, k=k, ab=ab, wl=wl: e.matmul(pb[:, ab:ab + 1], wl, sh2T[:, k:k + 1], start=(k == 0), stop=(k == 7)),
                              r=[wn, "sh2T"], w=[f"pY{ct % 4}"], inc=(k == 7 and ab == 1))
                    A(lambda e: e.copy(bias2[:, ct:ct + 1], pb[:, 0:1]), r=[f"pY{ct % 4}"], w=["bias2"])
                    A(lambda e: e.copy(bias2[:, 22 + ct:23 + ct], pb[:, 1:2]), r=[f"pY{ct % 4}"], w=["bias2"])

                def fin_tail(qtr, tt):
                    ti = qtr * 4 + tt
                    os_ = ti % 2
                    V(lambda e: e.tensor_tensor(y2q[:, tt, :], y2q[:, tt, :], xr[tt % 2][:], ALU.add),
                      r=[f"y2q{tt}", f"xr{tt % 2}"], w=[f"y2q{tt}"])
                    if tt < 2:
                        LD(xr[tt % 2][:], x1s[(ti + 2) * 128:(ti + 3) * 128, :], f"xr{tt % 2}")
                    A(lambda e: e.activation(junk3, y2q[:, tt, :], AF.Square, accum_out=ssf[:, ti:ti + 1]),
                      r=[f"y2q{tt}"], w=["junk3", "ssf"])
                    rsqrt_cols(rsf, ssf, ti, ti + 1, 1.0 / D, "ssf", "rsf")
                    V(lambda e: e.scalar_tensor_tensor(ot[os_][:], y2q[:, tt, :], rsf[:, ti:ti + 1], gfin_b[:],
                                                       ALU.mult, ALU.mult),
                      r=[f"y2q{tt}", "rsf", "gfin_b"], w=[f"ot{os_}"])
                    P.op("sync", lambda e: e.dma_start(out=out_d[ti * 128:(ti + 1) * 128, :], in_=ot[os_][:]),
                         reads=[f"ot{os_}"], dma=f"d_out{os_}")

                ld_wu(0)
                for qtr in range(4):
                    c0 = 127 + qtr * 512
                    for ct in range(NCT):
                        zi = ct % 2
                        ci = ct % 2
                        if ct % GW == 0:
                            gg = qtr * NG + ct // GW
                            if gg + 1 < 4 * NG:
                                ld_wu(gg + 1)
                        if qtr == 0 and ct == 0:
                            bias2_for(0)
                        for ab in range(2):
                            for blk in range(2):
                                pz = pZ[ab * 2 + blk]
                                for k in range(8):
                                    wl, wn = wsl(qtr, ct, k, ab)
                                    T(lambda e, k=k, wl=wl: e.matmul(
                                        pz[:, 0:257], wl,
                                        x1nT[:, k, c0 + blk * 257:c0 + (blk + 1) * 257], start=(k == 0), stop=(k == 7)),
                                      r=[wn, "x1nT"], w=[f"pZ{ab * 2 + blk}"], inc=(k == 7))
                                A(lambda e: e.activation(
                                    zb[zi][:, ab, blk * 257:(blk + 1) * 257], pz[:, 0:257], AF.Identity,
                                    bias=bias2[:, ab * 22 + ct:ab * 22 + ct + 1]),
                                  r=[f"pZ{ab * 2 + blk}", "bias2"], w=[f"zb{zi}"])
                        if qtr >= 1 and ct >= 2:
                            down_mm(ct - 2, 0)
                        if qtr == 0 and ct + 1 < NCT:
                            bias2_for(ct + 1)
                        if qtr == 0:
                            ld_wd(ct)
                        if qtr > 0 and ct < 4:
                            fin_tail(qtr - 1, ct)
                        if ct == 4:
                            for tt in range(2):
                                LD(xr[tt][:], x1s[(qtr * 4 + tt) * 128:(qtr * 4 + tt + 1) * 128, :], f"xr{tt}")
                        if qtr == 0:
                            V(lambda e: e.tensor_scalar(zb[zi][:, :, 0:1], zb[zi][:, :, 0:1], hm[:, 0:1], None, ALU.mult),
                              r=[f"zb{zi}", "hm"], w=[f"zb{zi}"])
                        if qtr == 3:
                            V(lambda e: e.tensor_scalar(zb[zi][:, :, 513:514], zb[zi][:, :, 513:514], hm[:, 1:2], None, ALU.mult),
                              r=[f"zb{zi}", "hm"], w=[f"zb{zi}"])
                        for ab in range(2):
                            ch = ab * 22 + ct
                            if ab == 0:
                                A(lambda e: e.activation(cv[ci][:, ab, :], zb[zi][:, ab, 1:513], AF.Identity,
                                                         bias=bconv[:, ch:ch + 1], scale=wconv[:, 1, ch:ch + 1]),
                                  r=[f"zb{zi}", "wconv", "bconv"], w=[f"cv{ci}"])
                            else:
                                G(lambda e: e.tensor_scalar(cv[ci][:, ab, :], zb[zi][:, ab, 1:513], wconv[:, 1, ch:ch + 1],
                                                            bconv[:, ch:ch + 1], ALU.mult, ALU.add),
                                  r=[f"zb{zi}", "wconv", "bconv"], w=[f"cv{ci}"])
                        for ab in range(2):
                            ch = ab * 22 + ct
                            V(lambda e: e.scalar_tensor_tensor(cv[ci][:, ab, :], zb[zi][:, ab, 0:512], wconv[:, 0, ch:ch + 1],
                                                               cv[ci][:, ab, :], ALU.mult, ALU.add),
                              r=[f"zb{zi}", "wconv", f"cv{ci}"], w=[f"cv{ci}"])
                            V(lambda e: e.scalar_tensor_tensor(cv[ci][:, ab, :], zb[zi][:, ab, 2:514], wconv[:, 2, ch:ch + 1],
                                                               cv[ci][:, ab, :], ALU.mult, ALU.add),
                              r=[f"zb{zi}", "wconv", f"cv{ci}"], w=[f"cv{ci}"])
                        if ct > 0:
                            up_tail(qtr, ct - 1)
                    up_tail(qtr, NCT - 1)
                    if qtr >= 1:
                        down_mm(NCT - 2, 0)
                        down_mm(NCT - 1, 0)
                    for c2 in range(2):
                        if not (qtr >= 1 and c2 == 0):
                            for ct in range(NCT):
                                down_mm(ct, c2)
                        for tt in range(4):
                            V(lambda e, tt=tt: e.tensor_tensor(y2q[:, tt, c2 * 512:(c2 + 1) * 512], pY[tt][:, :],
                                                               gt2_b[:, c2 * 512:(c2 + 1) * 512], ALU.mult),
                              r=[f"pY{tt}", "gt2_b"], w=[f"y2q{tt}"])
                    if qtr == 3:
                        for tt in range(4):
                            fin_tail(3, tt)
                P.barrier()
        print("kernel build: inst", P.ninst, "waits", P.nwait, "sems", len(P.sem), P.cnt)
    except _Stop:
        pass
    return nc


def _prep_inputs(inp):
    x = np.asarray(inp["x"], np.float32)
    c = np.asarray(inp["c"], np.float32)
    f = lambda k: np.asarray(inp[k], np.float32)
    rows = S // 64
    row_id = np.repeat(np.arange(rows), 64).astype(np.float32)
    col_id = np.tile(np.arange(64), rows).astype(np.float32)
    inv_freq = np.power(np.float32(10000.0), -np.arange(0, 32, 2, dtype=np.float32) / np.float32(32)).astype(np.float32)
    ang_r = row_id[:, None] * inv_freq[None, :]
    ang_c = col_id[:, None] * inv_freq[None, :]
    ang = np.concatenate([ang_r, ang_r, ang_c, ang_c], axis=-1).astype(np.float32)
    cos = np.cos(ang).astype(np.float32)
    sin = np.sin(ang).astype(np.float32)
    sgn = np.concatenate([-np.ones(16), np.ones(16), -np.ones(16), np.ones(16)]).astype(np.float32)
    sinS = sin * sgn[None, :]
    perm_h = [0, 4, 1, 5, 2, 6, 3, 7]
    qcols = np.concatenate([np.arange(h * 64, (h + 1) * 64) for h in perm_h])
    w_in = f("w_in")[0].copy()
    w_in[:, 0:512] = w_in[:, qcols]
    w_out = f("w_out")[0].copy()
    w_out[0:512, :] = w_out[qcols, :]
    g_attn = f("g_attn_out")[0][qcols]
    gmixv = np.concatenate([g_attn, f("g_tok_out")[0]])
    gmix = np.ascontiguousarray(gmixv.reshape(8, 128).T)
    rep = lambda v, n=128: np.ascontiguousarray(np.broadcast_to(v[None, :], (n, v.shape[0])))
    wsT = np.ascontiguousarray(f("w_s")[0].transpose(2, 0, 1))
    b_s = f("b_s")[0]
    bsbT = np.zeros((128, 4, 128), np.float32)
    for c4 in range(4):
        bsbT[0:64, c4, :] = b_s[2 * c4][None, :]
        bsbT[64:128, c4, :] = b_s[2 * c4 + 1][None, :]
    wconv = np.ascontiguousarray(f("w_conv")[0].reshape(3, 44, 128).transpose(2, 0, 1))
    bconv = np.ascontiguousarray(f("b_conv")[0].reshape(44, 128).T)
    shared = {
        "w_ada": f("w_ada")[0], "b_ada": f("b_ada"), "g1_b": rep(f("g_norm1")[0]), "g2_b": rep(f("g_norm2")[0]),
        "gfin_b": rep(f("g_final")), "w_in": w_in, "w_out": w_out, "w_up": f("w_up")[0], "w_down": f("w_down")[0],
        "gq_b": rep(f("g_q")[0]), "gk_b": rep(f("g_k")[0]), "gtv_b": rep(f("g_tok_v")[0]), "wsT": wsT,
        "bsbT": bsbT.reshape(128, 512), "gmix": gmix, "wconv": wconv, "bconv": bconv,
        "ident": np.eye(128, dtype=np.float32),
    }
    in_maps = []
    for core in range(8):
        b, j = core // 4, core % 4
        sh = j * 2048 - 128
        xr = np.roll(x[b], -sh, axis=0)
        cr = np.roll(cos, -sh, axis=0).reshape(NT, 128, 64).transpose(1, 0, 2)
        sr = np.roll(sinS, -sh, axis=0).reshape(NT, 128, 64).transpose(1, 0, 2)
        hmk = np.ones((128, 2), np.float32)
        if j == 0:
            hmk[:, 0] = 0.0
        if j == 3:
            hmk[:, 1] = 0.0
        m = dict(shared)
        m.update({"x_rot": np.ascontiguousarray(xr), "cos_t": np.ascontiguousarray(cr), "sin_t": np.ascontiguousarray(sr),
                  "cb": np.ascontiguousarray(c[b].reshape(8, 128).T), "hmask": hmk})
        in_maps.append(m)
    return in_maps


_NC_CACHE = {}


def kernel(**inputs):
    in_maps = _prep_inputs(inputs)
    if "nc" not in _NC_CACHE:
        _NC_CACHE["nc"] = build_nc()
    nc = _NC_CACHE["nc"]
    res = run_bass_kernel_spmd(nc, in_maps, core_ids=list(range(8)))
    out = np.zeros((2, S, D), np.float32)
    for core in range(8):
        b, j = core // 4, core % 4
        out[b, j * 2048:(j + 1) * 2048, :] = res.results[core]["out"]
    return out
```

```python
import numpy as np
import concourse.bass as bass
import concourse.mybir as mybir
from concourse.bass_utils import run_bass_kernel_spmd
from contextlib import ExitStack

F32 = mybir.dt.float32
BF16 = mybir.dt.bfloat16
AF = mybir.ActivationFunctionType
ALU = mybir.AluOpType
AX = mybir.AxisListType

ENGS = ("sync", "scalar", "vector", "gpsimd", "tensor")
D = 1024
S = 8192
NT = 64
NX = 18
NCOLS = NX * 128
DFF = 2816
NCT = 22
EPS = 1e-6
AB_TILES = NT


class _Stop(Exception):
    pass


class Prog:
    def __init__(self, nc, stack):
        self.nc = nc
        self.stack = stack
        self.sem = {}
        self.cnt = {}
        self.waited = {e: {} for e in ENGS}
        self.lastw = {}
        self.readers = {}
        self.ninst = 0
        self.nwait = 0

    def getsem(self, name):
        if name not in self.sem:
            self.sem[name] = self.stack.enter_context(self.nc.semaphore(name))
            self.cnt[name] = 0
        return self.sem[name]

    def _wait(self, eng, tok):
        s, v = tok
        if self.waited[eng].get(s, 0) >= v:
            return
        self.waited[eng][s] = v
        getattr(self.nc, eng).wait_ge(self.sem[s], v)
        self.nwait += 1

    def op(self, eng, fn, reads=(), writes=(), dma=None, inc=True):
        deps = []
        for r in reads:
            if r in self.lastw:
                deps.append(self.lastw[r])
        for w in writes:
            if w in self.lastw:
                deps.append(self.lastw[w])
            for s, v in self.readers.get(w, {}).items():
                deps.append((s, v))
        for t in deps:
            if eng == "tensor" and t[0] == "e_tensor":
                continue
            self._wait(eng, t)
        e = getattr(self.nc, eng)
        tok = None
        if dma is not None:
            self.getsem(dma)
            self.cnt[dma] += 16
            tok = (dma, self.cnt[dma])
            fn(e).then_inc(self.sem[dma], 16)
        elif inc:
            sn = "e_" + eng
            self.getsem(sn)
            self.cnt[sn] += 1
            tok = (sn, self.cnt[sn])
            fn(e).then_inc(self.sem[sn], 1)
        else:
            fn(e)
        self.ninst += 1
        if tok is not None:
            for r in reads:
                d = self.readers.setdefault(r, {})
                d[tok[0]] = max(d.get(tok[0], 0), tok[1])
            for w in writes:
                self.lastw[w] = tok
                self.readers[w] = {}
        return tok

    def barrier(self):
        for eng in ENGS:
            for s, v in self.cnt.items():
                if v > 0:
                    self._wait(eng, (s, v))
        self.lastw = {}
        self.readers = {}


def build_nc(debug_stage=0):
    nc = bass.Bass("TRN2", target_bir_lowering=False)
    di = lambda name, shape: nc.dram_tensor(name, shape, F32, kind="ExternalInput").ap()
    x_rot = di("x_rot", [S, D])
    cos_t = di("cos_t", [128, NT, 64])
    sin_t = di("sin_t", [128, NT, 64])
    cb = di("cb", [128, 8])
    hmask = di("hmask", [128, 2])
    w_ada = di("w_ada", [D, 6 * D])
    b_ada = di("b_ada", [1, 6 * D])
    g1_bd = di("g1_b", [128, D])
    g2_bd = di("g2_b", [128, D])
    gfin_bd = di("gfin_b", [128, D])
    w_in = di("w_in", [D, 1792])
    w_out = di("w_out", [D, D])
    w_up = di("w_up", [D, 2 * DFF])
    w_down = di("w_down", [DFF, D])
    gq_bd = di("gq_b", [128, 64])
    gk_bd = di("gk_b", [128, 64])
    gtv_bd = di("gtv_b", [128, 64])
    wsT_d = di("wsT", [128, 8, 128])
    bsbT_d = di("bsbT", [128, 512])
    gmix_d = di("gmix", [128, 8])
    wconv_d = di("wconv", [128, 3, 44])
    bconv_d = di("bconv", [128, 44])
    ident_d = di("ident", [128, 128])
    out_d = nc.dram_tensor("out", [2048, D], F32, kind="ExternalOutput").ap()
    scr = nc.dram_tensor("scr", [2, 128, D], F32, kind="Internal").ap()
    x1s = nc.dram_tensor("x1s", [2048, D], F32, kind=("ExternalOutput" if debug_stage else "Internal")).ap()
    wub = nc.dram_tensor("wub", [NCT, 128, 8 * 2 * 128], BF16, kind="Internal").ap()
    if debug_stage:
        dbgf = nc.dram_tensor("dbgf", [128, 8192], F32, kind="ExternalOutput").ap()
        dbgb = nc.dram_tensor("dbgb", [128, 65536], BF16, kind="ExternalOutput").ap()

    try:
      with ExitStack() as st0:
        P = Prog(nc, st0)

        def dump(dst, src, res):
            tok = P.op("sync", lambda e: e.dma_start(out=dst, in_=src), reads=[res], dma="d_dbg")
            P._wait("sync", tok)

        def stage_end(k, dumps):
            if debug_stage == k:
                for (dst, src, res) in dumps():
                    dump(dst, src, res)
                print("STOP at stage", k, "inst", P.ninst, "waits", P.nwait, "cnt", P.cnt)
                raise _Stop()

        def SB(st, name, shape, dt):
            return st.enter_context(nc.sbuf_tensor("s_" + name, shape, dt))

        def PS(st, name, shape, dt):
            return st.enter_context(nc.psum_tensor("p_" + name, shape, dt))

        V = lambda fn, r=(), w=(): P.op("vector", fn, r, w)
        A = lambda fn, r=(), w=(): P.op("scalar", fn, r, w)
        G = lambda fn, r=(), w=(): P.op("gpsimd", fn, r, w)
        T = lambda fn, r=(), w=(), inc=True: P.op("tensor", fn, r, w, inc=inc)

        def LD(dst, src, res, sem=None, eng="sync"):
            return P.op(eng, lambda e: e.dma_start(out=dst, in_=src), writes=[res], dma="d_" + res)

        ident_b = SB(st0, "ident_b", [128, 128], BF16)
        ones_f = SB(st0, "ones_f", [128, 128], F32)
        epsc = SB(st0, "epsc", [128, 1], F32)
        G1_b = SB(st0, "G1_b", [128, D], F32)
        gt1_b = SB(st0, "gt1_b", [128, D], F32)
        bias_q = SB(st0, "bias_q", [128, 512], F32)
        bias_kv = SB(st0, "bias_kv", [128, 256], F32)
        bias_zv = SB(st0, "bias_zv", [128, 512], F32)
        biasU = SB(st0, "biasU", [128, 4], F32)
        sh1T = SB(st0, "sh1T", [128, 8], BF16)
        sh2T = SB(st0, "sh2T", [128, 8], BF16)
        gq_b = SB(st0, "gq_b", [128, 64], F32)
        gk_b = SB(st0, "gk_b", [128, 64], F32)
        gtv_b = SB(st0, "gtv_b", [128, 64], F32)
        gmix = SB(st0, "gmix", [128, 8], F32)
        wconv = SB(st0, "wconv", [128, 3, 44], F32)
        bconv = SB(st0, "bconv", [128, 44], F32)
        bias2 = SB(st0, "bias2", [128, 44], F32)
        hm = SB(st0, "hm", [128, 2], F32)
        ss = SB(st0, "ss", [128, NT], F32)
        lnv = SB(st0, "lnv", [128, NT], F32)
        rstd = SB(st0, "rstd", [128, NT], F32)
        sstok = SB(st0, "sstok", [128, NX], F32)
        ssatt = SB(st0, "ssatt", [128, 20], F32)
        rs_t = SB(st0, "rs_t", [128, NX], F32)
        rs_a = SB(st0, "rs_a", [128, 20], F32)
        ss2 = SB(st0, "ss2", [128, NX], F32)
        rs2 = SB(st0, "rs2", [128, NX], F32)
        ssf = SB(st0, "ssf", [128, 16], F32)
        rsf = SB(st0, "rsf", [128, 16], F32)
        sm = SB(st0, "sm", [128, 16], F32)

        LD(ident_b[:], ident_d, "ident_b", "d_c0", eng="gpsimd")
        V(lambda e: e.memset(ones_f[:], 1.0), w=["ones_f"])
        V(lambda e: e.memset(epsc[:], EPS), w=["epsc"])
        for nm, tl, src in [("gq_b", gq_b, gq_bd), ("gk_b", gk_b, gk_bd), ("gtv_b", gtv_b, gtv_bd),
                            ("gmix", gmix, gmix_d), ("wconv", wconv, wconv_d), ("bconv", bconv, bconv_d),
                            ("hm", hm, hmask)]:
            LD(tl[:], src, nm, "d_c1")
        for nm, tl in [("ss", ss), ("sstok", sstok), ("ssatt", ssatt), ("ss2", ss2), ("ssf", ssf), ("rstd", rstd), ("sm", sm)]:
            V(lambda e, tl=tl: e.memset(tl[:], 0.0), w=[nm])

        def rsqrt_cols(dst, src, lo, hi, scale, rn, wn, tmp=None):
            A(lambda e: e.activation(dst[:, lo:hi], src[:, lo:hi], AF.Ln, bias=epsc[:, 0:1], scale=scale),
              r=[rn, "epsc"], w=[wn])
            A(lambda e: e.activation(dst[:, lo:hi], dst[:, lo:hi], AF.Exp, scale=-0.5), r=[wn], w=[wn])

        with ExitStack() as st:
            wada = [SB(st, f"wada{i}", [128, 8, D], BF16) for i in range(2)]
            cbt = SB(st, "cbt", [128, 8], F32)
            scT = SB(st, "scT", [128, 8], BF16)
            brow = SB(st, "brow", [1, 6 * D], F32)
            mrow = [SB(st, f"mrow{i}", [1, D], F32) for i in range(2)]
            gtmp = SB(st, "gtmp", [128, D], F32)
            pMod = PS(st, "pMod", [128, D], F32)
            pB = PS(st, "pB", [128, D], F32)
            pX = PS(st, "pX", [128, 8], F32)
            LD(cbt[:], cb, "cbt", "d_c1")
            LD(brow[:], b_ada, "brow", "d_c1")
            A(lambda e: e.activation(scT[:], cbt[:], AF.Silu), r=["cbt"], w=["scT"])
            for m in range(6):
                wb = wada[m % 2]
                wn = f"wada{m % 2}"
                for kk in range(2):
                    P.op("gpsimd", lambda e, kk=kk, m=m, wb=wb: e.dma_start(
                        out=wb[:, 4 * kk:4 * kk + 4, :],
                        in_=w_ada[512 * kk:512 * kk + 512, m * D:(m + 1) * D].rearrange("(k p) n -> p k n", p=128)),
                        writes=[wn], dma=f"d_wada{m % 2}")
                for c2 in range(2):
                    for k in range(8):
                        T(lambda e, k=k, c2=c2, wb=wb: e.matmul(pMod[0:1, c2 * 512:(c2 + 1) * 512], scT[:, k:k + 1],
                                                                wb[:, k, c2 * 512:(c2 + 1) * 512],
                                                                start=(k == 0), stop=(k == 7)),
                          r=[wn, "scT"], w=["pMod"], inc=(k == 7))
                mr = mrow[m % 2]
                mn = f"mrow{m % 2}"
                V(lambda e, m=m, mr=mr: e.tensor_tensor(mr[:], pMod[0:1, :], brow[0:1, m * D:(m + 1) * D], ALU.add),
                  r=["pMod", "brow"], w=[mn])
                if m in (0, 3):
                    for k in range(8):
                        T(lambda e, k=k, mr=mr: e.matmul(pX[:, k:k + 1], mr[0:1, k * 128:(k + 1) * 128],
                                                         ones_f[0:1, 0:1], start=True, stop=True),
                          r=[mn, "ones_f"], w=["pX"], inc=(k == 7))
                    dst, dn = (sh1T, "sh1T") if m == 0 else (sh2T, "sh2T")
                    V(lambda e, dst=dst: e.tensor_copy(dst[:], pX[:]), r=["pX"], w=[dn])
                else:
                    for c2 in range(2):
                        T(lambda e, c2=c2, mr=mr: e.matmul(pB[:, c2 * 512:(c2 + 1) * 512], ones_f[0:1, :],
                                                           mr[0:1, c2 * 512:(c2 + 1) * 512], start=True, stop=True),
                          r=[mn, "ones_f"], w=["pB"], inc=(c2 == 1))
                    if m in (1, 4):
                        LD(gtmp[:], g1_bd if m == 1 else g2_bd, "gtmp", "d_c2")
                        if m == 1:
                            V(lambda e: e.scalar_tensor_tensor(G1_b[:], pB[:], 1.0, gtmp[:], ALU.add, ALU.mult),
                              r=["pB", "gtmp"], w=["G1_b"])
                        else:
                            V(lambda e: e.scalar_tensor_tensor(gtmp[:], pB[:], 1.0, gtmp[:], ALU.add, ALU.mult),
                              r=["pB", "gtmp"], w=["gtmp"])
                            P.op("sync", lambda e: e.dma_start(out=scr[0], in_=gtmp[:]), reads=["gtmp"], dma="d_scr")
                    elif m == 2:
                        V(lambda e: e.tensor_copy(gt1_b[:], pB[:]), r=["pB"], w=["gt1_b"])
                    else:
                        V(lambda e: e.tensor_copy(gtmp[:], pB[:]), r=["pB"], w=["gtmp"])
                        P.op("sync", lambda e: e.dma_start(out=scr[1], in_=gtmp[:]), reads=["gtmp"], dma="d_scr")
            P.barrier()
            stage_end(1, lambda: [(dbgf[:, 0:1024], G1_b[:], "G1_b"), (dbgf[:, 1024:2048], gt1_b[:], "gt1_b"), (dbgb[:, 0:8], sh1T[:], "sh1T"), (dbgb[:, 8:16], sh2T[:], "sh2T")])

        with ExitStack() as stAR, ExitStack() as stA:
            AR1 = SB(stAR, "AR1", [128, S + NT * 2 * 129], BF16)
            KT = AR1[:, 0:S]
            VVf = AR1[:, S:S + NT * 2 * 129]
            VV = VVf.rearrange("p (t g c) -> p t g c", t=NT, g=2)
            VVk = VVf.rearrange("p (t c) -> p t c", t=NT)
            QT = SB(stA, "QT", [128, 4, NCOLS], BF16)
            mixT = SB(stA, "mixT", [128, 8, NCOLS], BF16)
            V(lambda e: e.memset(AR1[:], 0.0), w=["VV", "KT"])
            V(lambda e: e.memset(mixT[:, 0:4, 0:128], 0.0), w=["mixT"])
            V(lambda e: e.memset(mixT[:, 0:4, 2176:2304], 0.0), w=["mixT"])
            if debug_stage:
                V(lambda e: e.memset(QT[:], 0.0), w=["QT"])
            V(lambda e: e.memset(VV[:, :, :, 0:1], 1.0), w=["VV"])
            V(lambda e: e.memset(VV[:, :, :, 128:129], 1.0), w=["VV"])

            with ExitStack() as st:
                Win = SB(st, "Win", [128, 8, 1792], BF16)
                wsT = SB(st, "wsTb", [128, 8, 128], BF16)
                bsbT = SB(st, "bsbT", [128, 512], F32)
                xt = [SB(st, f"xt{i}", [128, D], F32) for i in range(2)]
                junk = SB(st, "junk", [128, D], BF16)
                xn = [SB(st, f"xn{i}", [128, D], BF16) for i in range(2)]
                xnT = [SB(st, f"xnT{i}", [128, 8, 128], BF16) for i in range(2)]
                cst = [SB(st, f"cst{i}", [128, 64], F32) for i in range(2)]
                snt = [SB(st, f"snt{i}", [128, 64], F32) for i in range(2)]
                kvf = SB(st, "kvf", [128, 256], F32)
                ksq = SB(st, "ksq", [128, 128], F32)
                kra = SB(st, "kra", [128, 128], F32)
                krb = SB(st, "krb", [128, 128], F32)
                kbf = SB(st, "kbf", [128, 128], BF16)
                f1 = SB(st, "f1", [128, 512], F32)
                f2 = SB(st, "f2", [128, 512], F32)
                f3 = SB(st, "f3", [128, 512], F32)
                qbf = SB(st, "qbf", [128, 512], BF16)
                vnpad = SB(st, "vnpad", [128, 8, 128], BF16)
                uT = SB(st, "uT", [128, 512], F32)
                tk = SB(st, "tk", [128, 512], F32)
                sqt = SB(st, "sqt", [128, 512], F32)
                pT = [PS(st, f"pT{i}", [128, D], BF16) for i in range(2)]
                pKV = PS(st, "pKV", [128, 512], F32)
                pQ = PS(st, "pQ", [128, 512], F32)
                pZV = PS(st, "pZV", [128, 512], F32)
                pU = PS(st, "pU", [128, 512], F32)
                pTQ = PS(st, "pTQ", [128, D], BF16)
                pM = PS(st, "pM", [128, 512], F32)

                for kk in range(2):
                    P.op("gpsimd", lambda e, kk=kk: e.dma_start(
                        out=Win[:, 4 * kk:4 * kk + 4, :],
                        in_=w_in[512 * kk:512 * kk + 512, :].rearrange("(k p) n -> p k n", p=128)),
                        writes=["Win"], dma="d_win")
                LD(wsT[:], wsT_d, "wsT", "d_c3", eng="gpsimd")
                LD(bsbT[:], bsbT_d, "bsbT", "d_c1")
                V(lambda e: e.memset(vnpad[:], 0.0), w=["vnpad"])

                colgrp = [(0, 512, bias_q, "bias_q", 0), (512, 256, bias_kv, "bias_kv", 512),
                          (1280, 512, bias_zv, "bias_zv", 768)]
                for (c0, n, dst, dn, r0) in colgrp:
                    for k in range(8):
                        T(lambda e, k=k, c0=c0, n=n: e.matmul(pQ[0:1, 0:n], sh1T[:, k:k + 1], Win[:, k, c0:c0 + n],
                                                              start=(k == 0), stop=(k == 7)),
                          r=["Win", "sh1T"], w=["pQ"], inc=(k == 7))
                    V(lambda e, n=n, r0=r0: e.tensor_copy(tk[0:1, 0:n], pQ[0:1, 0:n]), r=["pQ"], w=["tk"])
                    T(lambda e, n=n, r0=r0: e.matmul(pZV[:, 0:n], ones_f[0:1, :], tk[0:1, 0:n],
                                                     start=True, stop=True), r=["tk", "ones_f"], w=["pZV"])
                    V(lambda e, n=n, dst=dst: e.tensor_copy(dst[:, 0:n], pZV[:, 0:n]), r=["pZV"], w=[dn])
                for c4 in range(4):
                    for k in range(8):
                        T(lambda e, k=k, c4=c4: e.matmul(pU[:, c4:c4 + 1], Win[:, k, 768 + c4 * 128:768 + (c4 + 1) * 128],
                                                         sh1T[:, k:k + 1], start=(k == 0), stop=(k == 7)),
                          r=["Win", "sh1T"], w=["pU"], inc=(k == 7 and c4 == 3))
                V(lambda e: e.tensor_copy(biasU[:], pU[:, 0:4]), r=["pU"], w=["biasU"])


                qf = SB(st, "qf", [128, 512], F32)
                zvf = SB(st, "zvf", [128, 512], F32)
                sm18 = SB(st, "sm18", [128, 18], F32)
                V(lambda e: e.memset(sm18[:], 1.0), w=["sm18"])
                print("sbuf remaining at AB:", nc.sbuf_bytes_remaining)

                def v3(a, nh):
                    return a[:, 0:nh * 64].rearrange("p (h d) -> p h d", h=nh)

                def front(t):
                    s = t % 2
                    own = t < NX
                    LD(xt[s][:], x_rot[t * 128:(t + 1) * 128, :], f"xt{s}")
                    LD(cst[s][:], cos_t[:, t, :], f"cst{s}")
                    LD(snt[s][:], sin_t[:, t, :], f"snt{s}")
                    A(lambda e: e.activation(junk[:], xt[s][:], AF.Square, accum_out=ss[:, t:t + 1]),
                      r=[f"xt{s}"], w=["junk", "ss"])
                    rsqrt_cols(rstd, ss, t, t + 1, 1.0 / D, "ss", "rstd")
                    V(lambda e: e.scalar_tensor_tensor(xn[s][:], xt[s][:], rstd[:, t:t + 1], G1_b[:], ALU.mult, ALU.mult),
                      r=[f"xt{s}", "rstd", "G1_b"], w=[f"xn{s}"])
                    for k in range(8):
                        T(lambda e, k=k: e.transpose(pT[s][:, k * 128:(k + 1) * 128], xn[s][:, k * 128:(k + 1) * 128], ident_b[:]),
                          r=[f"xn{s}", "ident_b"], w=[f"pT{s}"], inc=(k == 7))
                    A(lambda e: e.copy(xnT[s][:].rearrange("p k n -> p (k n)"), pT[s][:]), r=[f"pT{s}"], w=[f"xnT{s}"])
                    for k in range(8):
                        T(lambda e, k=k: e.matmul(pKV[:, 0:256], xnT[s][:, k, :], Win[:, k, 512:768],
                                                  start=(k == 0), stop=(k == 7)),
                          r=[f"xnT{s}", "Win"], w=["pKV"], inc=(k == 7))
                    if own:
                        for k in range(8):
                            T(lambda e, k=k: e.matmul(pQ[:, :], xnT[s][:, k, :], Win[:, k, 0:512],
                                                      start=(k == 0), stop=(k == 7)),
                              r=[f"xnT{s}", "Win"], w=["pQ"], inc=(k == 7))
                        for k in range(8):
                            T(lambda e, k=k: e.matmul(pZV[:, :], xnT[s][:, k, :], Win[:, k, 1280:1792],
                                                      start=(k == 0), stop=(k == 7)),
                              r=[f"xnT{s}", "Win"], w=["pZV"], inc=(k == 7))
                        for c4 in range(4):
                            for k in range(8):
                                T(lambda e, k=k, c4=c4: e.matmul(pU[:, c4 * 128:(c4 + 1) * 128],
                                                                 Win[:, k, 768 + c4 * 128:768 + (c4 + 1) * 128],
                                                                 xnT[s][:, k, :], start=(k == 0), stop=(k == 7)),
                                  r=[f"xnT{s}", "Win"], w=["pU"], inc=(k == 7 and c4 == 3))

                def mid(t):
                    own = t < NX
                    V(lambda e: e.tensor_tensor(kvf[:], pKV[:, 0:256], bias_kv[:], ALU.add), r=["pKV", "bias_kv"], w=["kvf"])
                    if own:
                        V(lambda e: e.tensor_tensor(zvf[:], pZV[:], bias_zv[:], ALU.add), r=["pZV", "bias_zv"], w=["zvf"])
                        A(lambda e: e.activation(zvf[:], zvf[:], AF.Gelu_apprx_tanh), r=["zvf"], w=["zvf"])
                        V(lambda e: e.tensor_tensor(qf[:], pQ[:], bias_q[:], ALU.add), r=["pQ", "bias_q"], w=["qf"])
                        for c4 in range(4):
                            A(lambda e, c4=c4: e.activation(uT[:, c4 * 128:(c4 + 1) * 128], pU[:, c4 * 128:(c4 + 1) * 128],
                                                            AF.Gelu_apprx_tanh, bias=biasU[:, c4:c4 + 1]),
                              r=["pU", "biasU"], w=["uT"])

                def rope2(src, sname, nh, ra, raname, rb, rbname, dst, dname, cs, csn, sn, snn):
                    V(lambda e: e.tensor_tensor(v3(ra, nh), v3(src, nh), cs[:].unsqueeze(1).to_broadcast([128, nh, 64]), ALU.mult),
                      r=[sname, csn], w=[raname])
                    for blk in range(4):
                        pb = blk ^ 1
                        G(lambda e, blk=blk, pb=pb: e.tensor_tensor(
                            v3(rb, nh)[:, :, blk * 16:(blk + 1) * 16], v3(src, nh)[:, :, pb * 16:(pb + 1) * 16],
                            sn[:, blk * 16:(blk + 1) * 16].unsqueeze(1).to_broadcast([128, nh, 16]), ALU.mult),
                          r=[sname, snn], w=[rbname])
                    V(lambda e: e.tensor_tensor(dst[:, 0:nh * 64], ra[:, 0:nh * 64], rb[:, 0:nh * 64], ALU.add),
                      r=[raname, rbname], w=[dname])

                def back_a(t):
                    s = t % 2
                    own = t < NX
                    G(lambda e: e.tensor_copy(VV[:, t, :, 64:128], kvf[:, 128:256].rearrange("p (g d) -> p g d", g=2)),
                      r=["kvf"], w=["VV"])
                    G(lambda e: e.tensor_tensor(ksq[:], kvf[:, 0:128], kvf[:, 0:128], ALU.mult), r=["kvf"], w=["ksq"])
                    V(lambda e: e.tensor_reduce(sm18[:, 0:2], v3(ksq, 2), axis=AX.X, op=ALU.add), r=["ksq"], w=["sm18"])
                    if own:
                        G(lambda e: e.tensor_tensor(f3[:], qf[:], qf[:], ALU.mult), r=["qf"], w=["f3"])
                        V(lambda e: e.tensor_reduce(sm18[:, 2:10], v3(f3, 8), axis=AX.X, op=ALU.add), r=["f3"], w=["sm18"])
                        G(lambda e: e.tensor_tensor(f1[:], zvf[:], zvf[:], ALU.mult), r=["zvf"], w=["f1"])
                        V(lambda e: e.tensor_reduce(sm18[:, 10:18], v3(f1, 8), axis=AX.X, op=ALU.add), r=["f1"], w=["sm18"])
                    nc_ = 18 if own else 2
                    rsqrt_cols(sm18, sm18, 0, nc_, 1.0 / 64, "sm18", "sm18")

                def back_b(t):
                    s = t % 2
                    own = t < NX
                    V(lambda e: e.tensor_tensor(v3(kra, 2), v3(kvf, 2), sm18[:, 0:2].unsqueeze(2).to_broadcast([128, 2, 64]), ALU.mult),
                      r=["kvf", "sm18"], w=["kra"])
                    V(lambda e: e.tensor_tensor(v3(kra, 2), v3(kra, 2), gk_b[:].unsqueeze(1).to_broadcast([128, 2, 64]), ALU.mult),
                      r=["kra", "gk_b"], w=["kra"])
                    rope2(kra, "kra", 2, ksq, "ksq", krb, "krb", kbf, "kbf", cst[s], f"cst{s}", snt[s], f"snt{s}")
                    T(lambda e: e.transpose(pTQ[:, 512:640], kbf[:], ident_b[:]), r=["kbf", "ident_b"], w=["pTQk"])
                    kt_copy = lambda: V(lambda e: e.tensor_copy(KT[:, t * 128:(t + 1) * 128], pTQ[:, 512:640]), r=["pTQk"], w=["KT"])
                    if not own:
                        pending.append(kt_copy)
                        return
                    kt_copy()
                    V(lambda e: e.tensor_tensor(v3(f2, 8), v3(qf, 8), sm18[:, 2:10].unsqueeze(2).to_broadcast([128, 8, 64]), ALU.mult),
                      r=["qf", "sm18"], w=["f2"])
                    V(lambda e: e.tensor_tensor(v3(f2, 8), v3(f2, 8), gq_b[:].unsqueeze(1).to_broadcast([128, 8, 64]), ALU.mult),
                      r=["f2", "gq_b"], w=["f2"])
                    rope2(f2, "f2", 8, f3, "f3", f1, "f1", qbf, "qbf", cst[s], f"cst{s}", snt[s], f"snt{s}")
                    for p in range(4):
                        T(lambda e, p=p: e.transpose(pTQ[:, p * 128:(p + 1) * 128], qbf[:, p * 128:(p + 1) * 128], ident_b[:]),
                          r=["qbf", "ident_b"], w=["pTQ"], inc=(p == 3))
                    V(lambda e: e.tensor_tensor(v3(f2, 8), v3(zvf, 8), sm18[:, 10:18].unsqueeze(2).to_broadcast([128, 8, 64]), ALU.mult),
                      r=["zvf", "sm18"], w=["f2"])
                    for sl in range(2):
                        G(lambda e, sl=sl: e.tensor_tensor(
                            vnpad[:].rearrange("p (c s) n -> p c s n", s=2)[:, :, sl, sl * 64:(sl + 1) * 64],
                            f2[:].rearrange("p (c s d) -> p c s d", s=2, d=64)[:, :, sl, :],
                            gtv_b[:].unsqueeze(1).to_broadcast([128, 4, 64]), ALU.mult),
                          r=["f2", "gtv_b"], w=["vnpad"])
                    for c4 in range(4):
                        for sl in range(2):
                            T(lambda e, c4=c4, sl=sl: e.matmul(pM[:, c4 * 128:(c4 + 1) * 128], vnpad[:, 2 * c4 + sl, :],
                                                               wsT[:, 2 * c4 + sl, :], start=(sl == 0), stop=(sl == 1)),
                              r=["vnpad", "wsT"], w=["pM"], inc=(sl == 1 and c4 == 3))
                    V(lambda e: e.tensor_copy(QT[:, :, t * 128:(t + 1) * 128], pTQ[:, 0:512].rearrange("p (a n) -> p a n", a=4)),
                      r=["pTQ"], w=["QT"])
                    V(lambda e: e.tensor_tensor(tk[:], pM[:], bsbT[:], ALU.add), r=["pM", "bsbT"], w=["tk"])
                    V(lambda e: e.tensor_tensor(tk[:], tk[:], uT[:], ALU.mult), r=["tk", "uT"], w=["tk"])
                    G(lambda e: e.tensor_tensor(sqt[:], tk[:], tk[:], ALU.mult), r=["tk"], w=["sqt"])
                    for c4 in range(4):
                        T(lambda e, c4=c4: e.matmul(pM[:, 0:1] if False else pTS[:, 0:1], sqt[:, c4 * 128:(c4 + 1) * 128], ones_f[:, 0:1],
                                                    start=(c4 == 0), stop=(c4 == 3)),
                          r=["sqt", "ones_f"], w=["pTS"], inc=(c4 == 3))
                    V(lambda e: e.tensor_tensor(mixT[:, 4:8, t * 128:(t + 1) * 128],
                                                tk[:].rearrange("p (c n) -> p c n", c=4),
                                                gmix[:, 4:8].unsqueeze(2).to_broadcast([128, 4, 128]), ALU.mult),
                      r=["tk", "gmix"], w=["mixT"])
                    V(lambda e: e.tensor_copy(sstok[:, t:t + 1], pTS[:, 0:1]), r=["pTS"], w=["sstok"])

                pTS = pKV[:, 256:512]
                pending = []
                front(0)
                mid(0)
                for t in range(AB_TILES):
                    if t + 1 < AB_TILES:
                        front(t + 1)
                    back_a(t)
                    back_b(t)
                    if t + 1 < AB_TILES:
                        mid(t + 1)
                    for f_ in pending:
                        f_()
                    pending.clear()

                rsqrt_cols(rs_t, sstok, 0, NX, 1.0 / 512, "sstok", "rs_t")
                P.barrier()
                stage_end(2, lambda: [(dbgb[:, 0:8192], KT, "KT"), (dbgb[:, 8192:17408], QT[:].rearrange("p a n -> p (a n)"), "QT"), (dbgb[:, 17408:26624], mixT[:, 4:8, :].rearrange("p a n -> p (a n)"), "mixT"), (dbgb[:, 26624:43136], VVf, "VV"), (dbgf[:, 0:18], rs_t[:, 0:18], "rs_t"), (dbgf[:, 64:128], rstd[:, :], "rstd"), (dbgf[:, 128:132], biasU[:], "biasU"), (dbgf[:, 1024:1536], bias_q[:], "bias_q"), (dbgf[:, 1536:1792], bias_kv[:], "bias_kv"), (dbgf[:, 2048:2560], bias_zv[:], "bias_zv")])

            with ExitStack() as st:
                Pb = [SB(st, f"Pb{i}", [128, 1024], BF16) for i in range(3)]
                Oa = SB(st, "Oa", [128, 512], F32)
                Ob = SB(st, "Ob", [128, 512], F32)
                rden = SB(st, "rden", [128, 512], F32)
                at = SB(st, "at", [128, 512], F32)
                sq = SB(st, "sq", [128, 512], F32)
                acc = SB(st, "acc", [128, 512], F32)
                pS = [PS(st, f"pS{i}", [128, 1024], F32) for i in range(2)]
                pOa = PS(st, "pOa", [128, 512], F32)
                pOb = PS(st, "pOb", [128, 512], F32)
                pBc = PS(st, "pBc", [128, 512], F32)
                pSA = PS(st, "pSA", [128, 512], F32)
                for ct in range(NCT):
                    for ab in range(2):
                        P.op("gpsimd", lambda e: e.dma_start(
                            out=wub[ct].rearrange("p (k a n) -> p k a n", k=8, a=2)[:, :, ab, :],
                            in_=w_up[:, ab * DFF + ct * 128:ab * DFF + (ct + 1) * 128].rearrange("(k p) n -> p k n", p=128)),
                            writes=[f"wub{ct}_{ab}"], dma="d_wub")
                QH = SB(st, "QH", [128, 4, 2], BF16)
                hs = SB(st, "hs", [128, 2], F32)

                def qk(kt, i, p, q0, qn):
                    b = pS[i % 2]
                    T(lambda e: e.matmul(b[:, 0:qn], KT[0:64, kt * 128:(kt + 1) * 128], QT[0:64, p, q0:q0 + qn],
                                         start=True, stop=True), r=["KT", "QT"], w=[f"pS{i % 2}"], inc=False)
                    T(lambda e: e.matmul(b[:, 512:512 + qn], KT[64:128, kt * 128:(kt + 1) * 128],
                                         QT[64:128, p, q0:q0 + qn], start=True, stop=True),
                      r=["KT", "QT"], w=[f"pS{i % 2}"])

                def ex(kt, i):
                    A(lambda e: e.activation(Pb[i % 3][:], pS[i % 2][:], AF.Exp, scale=0.125),
                      r=[f"pS{i % 2}"], w=[f"Pb{i % 3}"])

                def pv(kt, i, qn):
                    pb = Pb[i % 3]
                    T(lambda e: e.matmul(pOa[:, 0:qn], VVk[:, kt, 64:192], pb[:, 0:qn],
                                         start=(kt == 0), stop=(kt == NT - 1)),
                      r=[f"Pb{i % 3}", "VV"], w=["pOa"], inc=False)
                    T(lambda e: e.matmul(pOb[:, 0:qn], VV[:, kt, 1, 0:128], pb[:, 512:512 + qn],
                                         start=(kt == 0), stop=(kt == NT - 1)),
                      r=[f"Pb{i % 3}", "VV"], w=["pOa", "pOb"])

                def epi_a(qn):
                    V(lambda e: e.tensor_copy(Oa[0:65, 0:qn], pOa[0:65, 0:qn]), r=["pOa"], w=["Oa"])
                    V(lambda e: e.tensor_copy(Ob[:, 0:qn], pOb[:, 0:qn]), r=["pOb"], w=["Ob"])
                    V(lambda e: e.reciprocal(rden[64:65, 0:qn], Oa[64:65, 0:qn]), r=["Oa"], w=["rdenA"])
                    V(lambda e: e.reciprocal(rden[0:1, 0:qn], Ob[0:1, 0:qn]), r=["Ob"], w=["rdenB"])

                def epi_b(qn):
                    T(lambda e: e.matmul(pBc[:, 0:qn], ones_f[64:65, :], rden[64:65, 0:qn], start=True, stop=True),
                      r=["rdenA", "ones_f"], w=["pBc"], inc=False)
                    T(lambda e: e.matmul(pSA[:, 0:qn], ones_f[0:1, :], rden[0:1, 0:qn], start=True, stop=True),
                      r=["rdenB", "ones_f"], w=["pBc", "pSA"])
                    V(lambda e: e.tensor_tensor(at[0:64, 0:qn], Oa[0:64, 0:qn], pBc[0:64, 0:qn], ALU.mult),
                      r=["Oa", "pBc"], w=["at"])
                    V(lambda e: e.tensor_tensor(at[64:128, 0:qn], Ob[64:128, 0:qn], pSA[64:128, 0:qn], ALU.mult),
                      r=["Ob", "pSA"], w=["at"])

                def epi_main(qi, p, q0):
                    epi_b(512)
                    V(lambda e: e.tensor_scalar(mixT[:, p, q0:q0 + 512], at[:, :], gmix[:, p:p + 1], None, ALU.mult),
                      r=["at", "gmix"], w=["mixT"])
                    if p == 0:
                        G(lambda e: e.tensor_tensor(acc[:, :], at[:, :], at[:, :], ALU.mult), r=["at"], w=["acc"])
                    else:
                        G(lambda e: e.tensor_tensor(sq[:, :], at[:, :], at[:, :], ALU.mult), r=["at"], w=["sq"])
                        G(lambda e: e.tensor_tensor(acc[:, :], acc[:, :], sq[:, :], ALU.add), r=["acc", "sq"], w=["acc"])

                def epi_ss(qi):
                    for w_ in range(4):
                        T(lambda e, w_=w_: e.matmul(pSA[:, w_:w_ + 1], acc[:, w_ * 128:(w_ + 1) * 128], ones_f[:, 0:1],
                                                    start=True, stop=True), r=["acc", "ones_f"], w=["pSA"], inc=(w_ == 3))
                    V(lambda e: e.tensor_copy(ssatt[:, 1 + qi * 4:1 + qi * 4 + 4], pSA[:, 0:4]), r=["pSA"], w=["ssatt"])

                V(lambda e: e.tensor_copy(QH[:, :, 0:1], QT[:, :, 127:128]), r=["QT"], w=["QH"])
                V(lambda e: e.tensor_copy(QH[:, :, 1:2], QT[:, :, 2176:2177]), r=["QT"], w=["QH"])
                for g in range(2):
                    for kt in range(NT):
                        T(lambda e: e.matmul(pS[0][:, g * 512 + kt * 8:g * 512 + kt * 8 + 8],
                                             KT[g * 64:(g + 1) * 64, kt * 128:(kt + 1) * 128], QH[g * 64:(g + 1) * 64, :, :],
                                             start=True, stop=True), r=["KT", "QH"], w=["pS0"], inc=(kt == NT - 1))
                ex(0, 0)
                for kt in range(NT):
                    T(lambda e: e.matmul(pOa[0:65, 0:8], VV[:, kt, 0, 64:129], Pb[0][:, kt * 8:kt * 8 + 8],
                                         start=(kt == 0), stop=(kt == NT - 1)), r=["Pb0", "VV"], w=["pOa"], inc=(kt == NT - 1))
                for kt in range(NT):
                    T(lambda e: e.matmul(pOb[:, 0:8], VV[:, kt, 1, 0:128], Pb[0][:, 512 + kt * 8:512 + kt * 8 + 8],
                                         start=(kt == 0), stop=(kt == NT - 1)), r=["Pb0", "VV"], w=["pOb"], inc=(kt == NT - 1))
                epi_a(8)
                epi_b(8)
                at3 = at[:, 0:8].rearrange("p (a t) -> p a t", t=2)
                for tk_, col in ((0, 127), (1, 2176)):
                    V(lambda e: e.tensor_tensor(mixT[:, 0:4, col:col + 1], at3[:, :, tk_:tk_ + 1], gmix[:, 0:4].unsqueeze(2), ALU.mult),
                      r=["at", "gmix"], w=["mixT"])
                V(lambda e: e.tensor_tensor(sq[:, 0:8], at[:, 0:8], at[:, 0:8], ALU.mult), r=["at"], w=["sq"])
                V(lambda e: e.tensor_reduce(hs[:, 0:2], sq[:, 0:8].rearrange("p (a t) -> p t a", t=2), axis=AX.X, op=ALU.add),
                  r=["sq"], w=["hs"])
                V(lambda e: e.memset(acc[:, 0:256], 0.0), w=["acc"])
                V(lambda e: e.tensor_copy(acc[:, 127:129], hs[:, 0:2]), r=["hs"], w=["acc"])
                for w_ in range(2):
                    T(lambda e, w_=w_: e.matmul(pSA[:, w_:w_ + 1], acc[:, w_ * 128:(w_ + 1) * 128], ones_f[:, 0:1],
                                                start=True, stop=True), r=["acc", "ones_f"], w=["pSA"], inc=(w_ == 1))
                V(lambda e: e.tensor_copy(ssatt[:, 0:1], pSA[:, 0:1]), r=["pSA"], w=["ssatt"])
                V(lambda e: e.tensor_copy(ssatt[:, 17:18], pSA[:, 1:2]), r=["pSA"], w=["ssatt"])

                iters = [(qi, p) for qi in range(4) for p in range(4)]
                step = 1
                pend = None
                for (qi, p) in iters:
                    q0 = 128 + qi * 512
                    qk(0, step, p, q0, 512)
                    qk(1, step + 1, p, q0, 512)
                    for kt in range(NT):
                        ex(kt, step + kt)
                        if kt + 2 < NT:
                            qk(kt + 2, step + kt + 2, p, q0, 512)
                        pv(kt, step + kt, 512)
                        if kt == 3 and pend is not None:
                            epi_main(*pend)
                        if kt == 16 and pend is not None:
                            if pend[1] == 3:
                                epi_ss(pend[0])
                            pend = None
                    step += NT
                    epi_a(512)
                    pend = (qi, p, q0)
                epi_main(*pend)
                epi_ss(3)

                rsqrt_cols(rs_a, ssatt, 0, NX, 1.0 / 512, "ssatt", "rs_a")
                P.barrier()
                stage_end(3, lambda: [(dbgb[:, 0:9216], mixT[:, 0:4, :].rearrange("p a n -> p (a n)"), "mixT"), (dbgf[:, 0:18], rs_a[:, 0:18], "rs_a")])

            x1nT = AR1[:, 0:8 * NCOLS].rearrange("p (k n) -> p k n", k=8)
            with ExitStack() as st2:
                G2_b = SB(st2, "G2_b", [128, D], F32)
                Wout = SB(st2, "Wout", [128, 8, D], BF16)
                xe = [SB(st2, f"xe{i}", [128, D], F32) for i in range(2)]
                x1h = [SB(st2, f"x1h{i}", [128, D], F32) for i in range(2)]
                t1 = SB(st2, "t1", [128, D], F32)
                x1n = [SB(st2, f"x1n{i}", [128, D], BF16) for i in range(2)]
                junk2 = SB(st2, "junk2", [128, D], BF16)
                pYa = PS(st2, "pYa", [128, D], F32)
                pYb = PS(st2, "pYb", [128, D], F32)
                pT2 = [PS(st2, f"pT2{i}", [128, D], BF16) for i in range(2)]
                LD(G2_b[:], scr[0], "G2_b", "d_c2")
                for kk in range(2):
                    P.op("gpsimd", lambda e, kk=kk: e.dma_start(
                        out=Wout[:, 4 * kk:4 * kk + 4, :],
                        in_=w_out[512 * kk:512 * kk + 512, :].rearrange("(k p) n -> p k n", p=128)),
                        writes=["Wout"], dma="d_wout")
                def out_mm(t):
                    s = t % 2
                    LD(xe[s][:], x_rot[t * 128:(t + 1) * 128, :], f"xe{s}")
                    for c2 in range(2):
                        for k in range(4):
                            T(lambda e, k=k, c2=c2: e.matmul(pYa[:, c2 * 512:(c2 + 1) * 512], mixT[:, k, t * 128:(t + 1) * 128],
                                                             Wout[:, k, c2 * 512:(c2 + 1) * 512], start=(k == 0), stop=(k == 3)),
                              r=["mixT", "Wout"], w=["pYa"], inc=(k == 3 and c2 == 1))
                    for c2 in range(2):
                        for k in range(4, 8):
                            T(lambda e, k=k, c2=c2: e.matmul(pYb[:, c2 * 512:(c2 + 1) * 512], mixT[:, k, t * 128:(t + 1) * 128],
                                                             Wout[:, k, c2 * 512:(c2 + 1) * 512], start=(k == 4), stop=(k == 7)),
                              r=["mixT", "Wout"], w=["pYb"], inc=(k == 7 and c2 == 1))

                def out_ep1(t):
                    s = t % 2
                    xd = x1h[s][:]
                    xdn = f"x1h{s}"
                    A(lambda e: e.activation(t1[:], pYa[:], AF.Identity, scale=rs_a[:, t:t + 1]), r=["pYa", "rs_a"], w=["t1"])
                    V(lambda e: e.scalar_tensor_tensor(t1[:], pYb[:], rs_t[:, t:t + 1], t1[:], ALU.mult, ALU.add),
                      r=["pYb", "rs_t", "t1"], w=["t1"])

                def out_ep2(t):
                    s = t % 2
                    xd = x1h[s][:]
                    xdn = f"x1h{s}"
                    V(lambda e: e.tensor_tensor(t1[:], t1[:], gt1_b[:], ALU.mult), r=["t1", "gt1_b"], w=["t1"])
                    V(lambda e: e.tensor_tensor(xd, t1[:], xe[s][:], ALU.add), r=["t1", f"xe{s}"], w=[xdn])
                    if 1 <= t <= 16:
                        P.op("sync", lambda e: e.dma_start(out=x1s[(t - 1) * 128:t * 128, :], in_=xd), reads=[xdn], dma=f"d_x1s{s}")
                    A(lambda e: e.activation(junk2[:], xd, AF.Square, accum_out=ss2[:, t:t + 1]), r=[xdn], w=["junk2", "ss2"])
                    rsqrt_cols(rs2, ss2, t, t + 1, 1.0 / D, "ss2", "rs2")
                    V(lambda e: e.scalar_tensor_tensor(x1n[s][:], xd, rs2[:, t:t + 1], G2_b[:], ALU.mult, ALU.mult),
                      r=[xdn, "rs2", "G2_b"], w=[f"x1n{s}"])
                    for k in range(8):
                        T(lambda e, k=k: e.transpose(pT2[s][:, k * 128:(k + 1) * 128], x1n[s][:, k * 128:(k + 1) * 128], ident_b[:]),
                          r=[f"x1n{s}", "ident_b"], w=[f"pT2{s}"], inc=(k == 7))
                    V(lambda e: e.tensor_copy(x1nT[:, :, t * 128:(t + 1) * 128], pT2[s][:].rearrange("p (k n) -> p k n", k=8)),
                      r=[f"pT2{s}"], w=["x1nT"])

                out_mm(0)
                for t in range(NX):
                    out_ep1(t)
                    if t + 1 < NX:
                        out_mm(t + 1)
                    out_ep2(t)
                P.barrier()
                stage_end(4, lambda: [(dbgb[:, 0:18432], AR1[:, 0:18432], "x1nT")])
                print("sbuf remaining at OUT:", nc.sbuf_bytes_remaining)

            stA.close()
            with ExitStack() as st2:
                NWU = 2
                GW = 2
                groups = [(c, min(GW, NCT - c)) for c in range(0, NCT, GW)]
                NG = len(groups)
                gt2_b = SB(st2, "gt2_b", [128, D], F32)
                gfin_b = SB(st2, "gfin_b", [128, D], F32)
                gT = SB(st2, "gT", [128, NCT, 512], BF16)
                tail0 = 8 * NCOLS
                wu = [AR1[:, tail0:tail0 + 8 * 2 * GW * 128].rearrange("p (c f) -> p c f", c=GW),
                      SB(st2, "wu1", [128, GW, 8 * 2 * 128], BF16)]
                Wd = SB(st2, "Wd", [128, NCT, D], BF16)
                zb = [SB(st2, f"zb{i}", [128, 2, 514], F32) for i in range(2)]
                cv = [SB(st2, f"cv{i}", [128, 2, 512], F32) for i in range(2)]
                sl_ = [SB(st2, f"sl{i}", [128, 512], F32) for i in range(2)]
                y2q = SB(st2, "y2q", [128, 4, D], F32)
                xr = [SB(st2, f"xr{i}", [128, D], F32) for i in range(2)]
                junk3 = AR1[:, tail0 + 8 * 2 * GW * 128:tail0 + 8 * 2 * GW * 128 + D]
                ot = [SB(st2, f"ot{i}", [128, D], F32) for i in range(2)]
                pZ = [PS(st2, f"pZ{i}", [128, 512], F32) for i in range(4)]
                pY = [PS(st2, f"pY{i}", [128, 512], F32) for i in range(4)]
                LD(gt2_b[:], scr[1], "gt2_b")
                LD(gfin_b[:], gfin_bd, "gfin_b")
                print("sbuf remaining at FFN:", nc.sbuf_bytes_remaining)

                def ld_wu(gg):
                    c0_, n_ = groups[gg % NG]
                    s = gg % NWU
                    P.op("sync", lambda e: e.dma_start(out=wu[s][:, 0:n_, :], in_=wub[c0_:c0_ + n_].rearrange("c p f -> p c f")),
                         writes=[f"wu{s}"], dma=f"d_wu{s}")

                def wsl(qtr, ct, k, ab):
                    gi = ct // GW
                    s = (qtr * NG + gi) % NWU
                    j = ct - groups[gi][0]
                    return wu[s][:, j, :].rearrange("p (k a n) -> p k a n", k=8, a=2)[:, k, ab, :], f"wu{s}"

                def ld_wd(ct):
                    P.op("gpsimd", lambda e: e.dma_start(out=Wd[:, ct, :], in_=w_down[ct * 128:(ct + 1) * 128, :]),
                         writes=[f"Wd{ct}"], dma=f"d_Wd{ct}")

                def up_tail(qtr, ct):
                    ci = ct % 2
                    A(lambda e: e.activation(sl_[ci][:], cv[ci][:, 0, :], AF.Silu), r=[f"cv{ci}a"], w=[f"sl{ci}"])
                    V(lambda e: e.tensor_tensor(gT[:, ct, :], sl_[ci][:], cv[ci][:, 1, :], ALU.mult),
                      r=[f"sl{ci}", f"cv{ci}b"], w=["gT"])

                def down_mm(ct, c2):
                    for tt in range(4):
                        T(lambda e, tt=tt: e.matmul(pY[tt][:, :], gT[:, ct, tt * 128:(tt + 1) * 128],
                                                    Wd[:, ct, c2 * 512:(c2 + 1) * 512],
                                                    start=(ct == 0), stop=(ct == NCT - 1)),
                          r=["gT", f"Wd{ct}"], w=[f"pY{tt}"], inc=(ct == NCT - 1 or tt == 3))

                def bias2_for(ct):
                    pb = pY[ct % 4]
                    for ab in range(2):
                        for k in range(8):
                            wl, wn = wsl(0, ct, k, ab)
                            T(lambda e, k=k, ab=ab, wl=wl: e.matmul(pb[:, ab:ab + 1], wl, sh2T[:, k:k + 1], start=(k == 0), stop=(k == 7)),
                              r=[wn, "sh2T"], w=[f"pY{ct % 4}"], inc=(k == 7 and ab == 1))
                    A(lambda e: e.copy(bias2[:, ct:ct + 1], pb[:, 0:1]), r=[f"pY{ct % 4}"], w=["bias2"])
                    A(lambda e: e.copy(bias2[:, 22 + ct:23 + ct], pb[:, 1:2]), r=[f"pY{ct % 4}"], w=["bias2"])

                def fin_tail(qtr, tt):
                    ti = qtr * 4 + tt
                    os_ = ti % 2
                    V(lambda e: e.tensor_tensor(y2q[:, tt, :], y2q[:, tt, :], xr[tt % 2][:], ALU.add),
                      r=[f"y2q{tt}", f"xr{tt % 2}"], w=[f"y2q{tt}"])
                    if tt < 2:
                        LD(xr[tt % 2][:], x1s[(ti + 2) * 128:(ti + 3) * 128, :], f"xr{tt % 2}")
                    A(lambda e: e.activation(junk3, y2q[:, tt, :], AF.Square, accum_out=ssf[:, ti:ti + 1]),
                      r=[f"y2q{tt}"], w=["junk3", "ssf"])
                    rsqrt_cols(rsf, ssf, ti, ti + 1, 1.0 / D, "ssf", "rsf")
                    V(lambda e: e.scalar_tensor_tensor(ot[os_][:], y2q[:, tt, :], rsf[:, ti:ti + 1], gfin_b[:],
                                                       ALU.mult, ALU.mult),
                      r=[f"y2q{tt}", "rsf", "gfin_b"], w=[f"ot{os_}"])
                    P.op("sync", lambda e: e.dma_start(out=out_d[ti * 128:(ti + 1) * 128, :], in_=ot[os_][:]),
                         reads=[f"ot{os_}"], dma=f"d_out{os_}")

                ld_wu(0)
                for qtr in range(4):
                    c0 = 127 + qtr * 512
                    for ct in range(NCT):
                        zi = ct % 2
                        ci = ct % 2
                        if ct % GW == 0:
                            gg = qtr * NG + ct // GW
                            if gg + 1 < 4 * NG:
                                ld_wu(gg + 1)
                        if qtr == 0 and ct == 0:
                            bias2_for(0)
                        for ab in range(2):
                            for blk in range(2):
                                pz = pZ[ab * 2 + blk]
                                for k in range(8):
                                    wl, wn = wsl(qtr, ct, k, ab)
                                    T(lambda e, k=k, wl=wl: e.matmul(
                                        pz[:, 0:257], wl,
                                        x1nT[:, k, c0 + blk * 257:c0 + (blk + 1) * 257], start=(k == 0), stop=(k == 7)),
                                      r=[wn, "x1nT"], w=[f"pZ{ab * 2 + blk}"], inc=(k == 7))
                                A(lambda e: e.activation(
                                    zb[zi][:, ab, blk * 257:(blk + 1) * 257], pz[:, 0:257], AF.Identity,
                                    bias=bias2[:, ab * 22 + ct:ab * 22 + ct + 1]),
                                  r=[f"pZ{ab * 2 + blk}", "bias2"], w=[f"zb{zi}"])
                        if qtr >= 1 and ct >= 2:
                            down_mm(ct - 2, 0)
                        if qtr == 0 and ct + 1 < NCT:
                            bias2_for(ct + 1)
                        if qtr == 0:
                            ld_wd(ct)
                        if qtr > 0 and ct < 4:
                            fin_tail(qtr - 1, ct)
                        if ct == 4:
                            for tt in range(2):
                                LD(xr[tt][:], x1s[(qtr * 4 + tt) * 128:(qtr * 4 + tt + 1) * 128, :], f"xr{tt}")
                        if qtr == 0:
                            V(lambda e: e.tensor_scalar(zb[zi][:, :, 0:1], zb[zi][:, :, 0:1], hm[:, 0:1], None, ALU.mult),
                              r=[f"zb{zi}", "hm"], w=[f"zb{zi}"])
                        if qtr == 3:
                            V(lambda e: e.tensor_scalar(zb[zi][:, :, 513:514], zb[zi][:, :, 513:514], hm[:, 1:2], None, ALU.mult),
                              r=[f"zb{zi}", "hm"], w=[f"zb{zi}"])
                        for ab in range(2):
                            ch = ab * 22 + ct
                            if ab == 0:
                                A(lambda e: e.activation(cv[ci][:, ab, :], zb[zi][:, ab, 1:513], AF.Identity,
                                                         bias=bconv[:, ch:ch + 1], scale=wconv[:, 1, ch:ch + 1]),
                                  r=[f"zb{zi}", "wconv", "bconv"], w=[f"cv{ci}a"])
                            else:
                                G(lambda e: e.tensor_scalar(cv[ci][:, ab, :], zb[zi][:, ab, 1:513], wconv[:, 1, ch:ch + 1],
                                                            bconv[:, ch:ch + 1], ALU.mult, ALU.add),
                                  r=[f"zb{zi}", "wconv", "bconv"], w=[f"cv{ci}b"])
                        for ab in range(2):
                            ch = ab * 22 + ct
                            cvn = f"cv{ci}" + "ab"[ab]
                            V(lambda e: e.scalar_tensor_tensor(cv[ci][:, ab, :], zb[zi][:, ab, 0:512], wconv[:, 0, ch:ch + 1],
                                                               cv[ci][:, ab, :], ALU.mult, ALU.add),
                              r=[f"zb{zi}", "wconv", cvn], w=[cvn])
                            V(lambda e: e.scalar_tensor_tensor(cv[ci][:, ab, :], zb[zi][:, ab, 2:514], wconv[:, 2, ch:ch + 1],
                                                               cv[ci][:, ab, :], ALU.mult, ALU.add),
                              r=[f"zb{zi}", "wconv", cvn], w=[cvn])
                        if ct > 0:
                            up_tail(qtr, ct - 1)
                    up_tail(qtr, NCT - 1)
                    if qtr >= 1:
                        down_mm(NCT - 2, 0)
                        down_mm(NCT - 1, 0)
                    for c2 in range(2):
                        if not (qtr >= 1 and c2 == 0):
                            for ct in range(NCT):
                                down_mm(ct, c2)
                        for tt in range(4):
                            V(lambda e, tt=tt: e.tensor_tensor(y2q[:, tt, c2 * 512:(c2 + 1) * 512], pY[tt][:, :],
                                                               gt2_b[:, c2 * 512:(c2 + 1) * 512], ALU.mult),
                              r=[f"pY{tt}", "gt2_b"], w=[f"y2q{tt}"])
                    if qtr == 3:
                        for tt in range(4):
                            fin_tail(3, tt)
                P.barrier()
        print("kernel build: inst", P.ninst, "waits", P.nwait, "sems", len(P.sem), P.cnt)
    except _Stop:
        pass
    return nc


def _prep_inputs(inp):
    x = np.asarray(inp["x"], np.float32)
    c = np.asarray(inp["c"], np.float32)
    f = lambda k: np.asarray(inp[k], np.float32)
    rows = S // 64
    row_id = np.repeat(np.arange(rows), 64).astype(np.float32)
    col_id = np.tile(np.arange(64), rows).astype(np.float32)
    inv_freq = np.power(np.float32(10000.0), -np.arange(0, 32, 2, dtype=np.float32) / np.float32(32)).astype(np.float32)
    ang_r = row_id[:, None] * inv_freq[None, :]
    ang_c = col_id[:, None] * inv_freq[None, :]
    ang = np.concatenate([ang_r, ang_r, ang_c, ang_c], axis=-1).astype(np.float32)
    cos = np.cos(ang).astype(np.float32)
    sin = np.sin(ang).astype(np.float32)
    sgn = np.concatenate([-np.ones(16), np.ones(16), -np.ones(16), np.ones(16)]).astype(np.float32)
    sinS = sin * sgn[None, :]
    perm_h = [0, 4, 1, 5, 2, 6, 3, 7]
    qcols = np.concatenate([np.arange(h * 64, (h + 1) * 64) for h in perm_h])
    w_in = f("w_in")[0].copy()
    w_in[:, 0:512] = w_in[:, qcols]
    w_out = f("w_out")[0].copy()
    w_out[0:512, :] = w_out[qcols, :]
    g_attn = f("g_attn_out")[0][qcols]
    gmixv = np.concatenate([g_attn, f("g_tok_out")[0]])
    gmix = np.ascontiguousarray(gmixv.reshape(8, 128).T)
    rep = lambda v, n=128: np.ascontiguousarray(np.broadcast_to(v[None, :], (n, v.shape[0])))
    wsT = np.ascontiguousarray(f("w_s")[0].transpose(2, 0, 1))
    b_s = f("b_s")[0]
    bsbT = np.zeros((128, 4, 128), np.float32)
    for c4 in range(4):
        bsbT[0:64, c4, :] = b_s[2 * c4][None, :]
        bsbT[64:128, c4, :] = b_s[2 * c4 + 1][None, :]
    wconv = np.ascontiguousarray(f("w_conv")[0].reshape(3, 44, 128).transpose(2, 0, 1))
    bconv = np.ascontiguousarray(f("b_conv")[0].reshape(44, 128).T)
    shared = {
        "w_ada": f("w_ada")[0], "b_ada": f("b_ada"), "g1_b": rep(f("g_norm1")[0]), "g2_b": rep(f("g_norm2")[0]),
        "gfin_b": rep(f("g_final")), "w_in": w_in, "w_out": w_out, "w_up": f("w_up")[0], "w_down": f("w_down")[0],
        "gq_b": rep(f("g_q")[0]), "gk_b": rep(f("g_k")[0]), "gtv_b": rep(f("g_tok_v")[0]), "wsT": wsT,
        "bsbT": bsbT.reshape(128, 512), "gmix": gmix, "wconv": wconv, "bconv": bconv,
        "ident": np.eye(128, dtype=np.float32),
    }
    in_maps = []
    for core in range(8):
        b, j = core // 4, core % 4
        sh = j * 2048 - 128
        xr = np.roll(x[b], -sh, axis=0)
        cr = np.roll(cos, -sh, axis=0).reshape(NT, 128, 64).transpose(1, 0, 2)
        sr = np.roll(sinS, -sh, axis=0).reshape(NT, 128, 64).transpose(1, 0, 2)
        hmk = np.ones((128, 2), np.float32)
        if j == 0:
            hmk[:, 0] = 0.0
        if j == 3:
            hmk[:, 1] = 0.0
        m = dict(shared)
        m.update({"x_rot": np.ascontiguousarray(xr), "cos_t": np.ascontiguousarray(cr), "sin_t": np.ascontiguousarray(sr),
                  "cb": np.ascontiguousarray(c[b].reshape(8, 128).T), "hmask": hmk})
        in_maps.append(m)
    return in_maps


_NC_CACHE = {}


def kernel(**inputs):
    in_maps = _prep_inputs(inputs)
    if "nc" not in _NC_CACHE:
        _NC_CACHE["nc"] = build_nc()
    nc = _NC_CACHE["nc"]
    res = run_bass_kernel_spmd(nc, in_maps, core_ids=list(range(8)))
    out = np.zeros((2, S, D), np.float32)
    for core in range(8):
        b, j = core // 4, core % 4
        out[b, j * 2048:(j + 1) * 2048, :] = res.results[core]["out"]
    return out
```

```python
import numpy as np
import concourse.bass as bass
import concourse.mybir as mybir
from concourse.bass_utils import run_bass_kernel_spmd
from contextlib import ExitStack

F32 = mybir.dt.float32
BF16 = mybir.dt.bfloat16
AF = mybir.ActivationFunctionType
ALU = mybir.AluOpType
AX = mybir.AxisListType

ENGS = ("sync", "scalar", "vector", "gpsimd", "tensor")
D = 1024
S = 8192
NT = 64
NX = 18
NCOLS = NX * 128
DFF = 2816
NCT = 22
EPS = 1e-6
AB_TILES = NT


class _Stop(Exception):
    pass


class Prog:
    def __init__(self, nc, stack):
        self.nc = nc
        self.stack = stack
        self.sem = {}
        self.cnt = {}
        self.waited = {e: {} for e in ENGS}
        self.lastw = {}
        self.readers = {}
        self.ninst = 0
        self.nwait = 0

    def getsem(self, name):
        if name not in self.sem:
            self.sem[name] = self.stack.enter_context(self.nc.semaphore(name))
            self.cnt[name] = 0
        return self.sem[name]

    def _wait(self, eng, tok):
        s, v = tok
        if self.waited[eng].get(s, 0) >= v:
            return
        self.waited[eng][s] = v
        getattr(self.nc, eng).wait_ge(self.sem[s], v)
        self.nwait += 1

    def op(self, eng, fn, reads=(), writes=(), dma=None, inc=True):
        deps = []
        for r in reads:
            if r in self.lastw:
                deps.append(self.lastw[r])
        for w in writes:
            if w in self.lastw:
                deps.append(self.lastw[w])
            for s, v in self.readers.get(w, {}).items():
                deps.append((s, v))
        for t in deps:
            if eng == "tensor" and t[0] == "e_tensor":
                continue
            self._wait(eng, t)
        e = getattr(self.nc, eng)
        tok = None
        if dma is not None:
            self.getsem(dma)
            self.cnt[dma] += 16
            tok = (dma, self.cnt[dma])
            fn(e).then_inc(self.sem[dma], 16)
        elif inc:
            sn = "e_" + eng
            self.getsem(sn)
            self.cnt[sn] += 1
            tok = (sn, self.cnt[sn])
            fn(e).then_inc(self.sem[sn], 1)
        else:
            fn(e)
        self.ninst += 1
        if tok is not None:
            for r in reads:
                d = self.readers.setdefault(r, {})
                d[tok[0]] = max(d.get(tok[0], 0), tok[1])
            for w in writes:
                self.lastw[w] = tok
                self.readers[w] = {}
        return tok

    def barrier(self):
        for eng in ENGS:
            for s, v in self.cnt.items():
                if v > 0:
                    self._wait(eng, (s, v))
        self.lastw = {}
        self.readers = {}


def build_nc(debug_stage=0):
    nc = bass.Bass("TRN2", target_bir_lowering=False)
    di = lambda name, shape: nc.dram_tensor(name, shape, F32, kind="ExternalInput").ap()
    x_rot = di("x_rot", [S, D])
    cos_t = di("cos_t", [128, NT, 64])
    sin_t = di("sin_t", [128, NT, 64])
    cb = di("cb", [128, 8])
    hmask = di("hmask", [128, 2])
    w_ada = di("w_ada", [D, 6 * D])
    b_ada = di("b_ada", [1, 6 * D])
    g1_bd = di("g1_b", [128, D])
    g2_bd = di("g2_b", [128, D])
    gfin_bd = di("gfin_b", [128, D])
    w_in = di("w_in", [D, 1792])
    w_out = di("w_out", [D, D])
    w_up = di("w_up", [D, 2 * DFF])
    w_down = di("w_down", [DFF, D])
    gq_bd = di("gq_b", [128, 64])
    gk_bd = di("gk_b", [128, 64])
    gtv_bd = di("gtv_b", [128, 64])
    wsT_d = di("wsT", [128, 8, 128])
    bsbT_d = di("bsbT", [128, 512])
    gmix_d = di("gmix", [128, 8])
    wconv_d = di("wconv", [128, 3, 44])
    bconv_d = di("bconv", [128, 44])
    ident_d = di("ident", [128, 128])
    out_d = nc.dram_tensor("out", [2048, D], F32, kind="ExternalOutput").ap()
    scr = nc.dram_tensor("scr", [2, 128, D], F32, kind="Internal").ap()
    x1s = nc.dram_tensor("x1s", [2048, D], F32, kind=("ExternalOutput" if debug_stage else "Internal")).ap()
    wub = nc.dram_tensor("wub", [NCT, 128, 8 * 2 * 128], BF16, kind="Internal").ap()
    if debug_stage:
        dbgf = nc.dram_tensor("dbgf", [128, 8192], F32, kind="ExternalOutput").ap()
        dbgb = nc.dram_tensor("dbgb", [128, 65536], BF16, kind="ExternalOutput").ap()

    try:
      with ExitStack() as st0:
        P = Prog(nc, st0)

        def dump(dst, src, res):
            tok = P.op("sync", lambda e: e.dma_start(out=dst, in_=src), reads=[res], dma="d_dbg")
            P._wait("sync", tok)

        def stage_end(k, dumps):
            if debug_stage == k:
                for (dst, src, res) in dumps():
                    dump(dst, src, res)
                print("STOP at stage", k, "inst", P.ninst, "waits", P.nwait, "cnt", P.cnt)
                raise _Stop()

        def SB(st, name, shape, dt):
            return st.enter_context(nc.sbuf_tensor("s_" + name, shape, dt))

        def PS(st, name, shape, dt):
            return st.enter_context(nc.psum_tensor("p_" + name, shape, dt))

        V = lambda fn, r=(), w=(): P.op("vector", fn, r, w)
        A = lambda fn, r=(), w=(): P.op("scalar", fn, r, w)
        G = lambda fn, r=(), w=(): P.op("gpsimd", fn, r, w)
        T = lambda fn, r=(), w=(), inc=True: P.op("tensor", fn, r, w, inc=inc)

        def LD(dst, src, res, sem=None, eng="sync"):
            return P.op(eng, lambda e: e.dma_start(out=dst, in_=src), writes=[res], dma="d_" + res)

        ident_b = SB(st0, "ident_b", [128, 128], BF16)
        ones_f = SB(st0, "ones_f", [128, 128], F32)
        epsc = SB(st0, "epsc", [128, 1], F32)
        G1_b = SB(st0, "G1_b", [128, D], F32)
        gt1_b = SB(st0, "gt1_b", [128, D], F32)
        bias_q = SB(st0, "bias_q", [128, 512], F32)
        bias_kv = SB(st0, "bias_kv", [128, 256], F32)
        bias_zv = SB(st0, "bias_zv", [128, 512], F32)
        biasU = SB(st0, "biasU", [128, 4], F32)
        sh1T = SB(st0, "sh1T", [128, 8], BF16)
        sh2T = SB(st0, "sh2T", [128, 8], BF16)
        gq_b = SB(st0, "gq_b", [128, 64], F32)
        gk_b = SB(st0, "gk_b", [128, 64], F32)
        gtv_b = SB(st0, "gtv_b", [128, 64], F32)
        gmix = SB(st0, "gmix", [128, 8], F32)
        wconv = SB(st0, "wconv", [128, 3, 44], F32)
        bconv = SB(st0, "bconv", [128, 44], F32)
        bias2 = SB(st0, "bias2", [128, 44], F32)
        hm = SB(st0, "hm", [128, 2], F32)
        ss = SB(st0, "ss", [128, NT], F32)
        lnv = SB(st0, "lnv", [128, NT], F32)
        rstd = SB(st0, "rstd", [128, NT], F32)
        sstok = SB(st0, "sstok", [128, NX], F32)
        ssatt = SB(st0, "ssatt", [128, 20], F32)
        rs_t = SB(st0, "rs_t", [128, NX], F32)
        rs_a = SB(st0, "rs_a", [128, 20], F32)
        ss2 = SB(st0, "ss2", [128, NX], F32)
        rs2 = SB(st0, "rs2", [128, NX], F32)
        ssf = SB(st0, "ssf", [128, 16], F32)
        rsf = SB(st0, "rsf", [128, 16], F32)
        sm = SB(st0, "sm", [128, 16], F32)

        LD(ident_b[:], ident_d, "ident_b", "d_c0", eng="gpsimd")
        V(lambda e: e.memset(ones_f[:], 1.0), w=["ones_f"])
        V(lambda e: e.memset(epsc[:], EPS), w=["epsc"])
        for nm, tl, src in [("gq_b", gq_b, gq_bd), ("gk_b", gk_b, gk_bd), ("gtv_b", gtv_b, gtv_bd),
                            ("gmix", gmix, gmix_d), ("wconv", wconv, wconv_d), ("bconv", bconv, bconv_d),
                            ("hm", hm, hmask)]:
            LD(tl[:], src, nm, "d_c1")
        for nm, tl in [("ss", ss), ("sstok", sstok), ("ssatt", ssatt), ("ss2", ss2), ("ssf", ssf), ("rstd", rstd), ("sm", sm)]:
            V(lambda e, tl=tl: e.memset(tl[:], 0.0), w=[nm])

        def rsqrt_cols(dst, src, lo, hi, scale, rn, wn, tmp=None):
            A(lambda e: e.activation(dst[:, lo:hi], src[:, lo:hi], AF.Ln, bias=epsc[:, 0:1], scale=scale),
              r=[rn, "epsc"], w=[wn])
            A(lambda e: e.activation(dst[:, lo:hi], dst[:, lo:hi], AF.Exp, scale=-0.5), r=[wn], w=[wn])

        with ExitStack() as st:
            wada = [SB(st, f"wada{i}", [128, 8, D], BF16) for i in range(2)]
            cbt = SB(st, "cbt", [128, 8], F32)
            scT = SB(st, "scT", [128, 8], BF16)
            brow = SB(st, "brow", [1, 6 * D], F32)
            mrow = [SB(st, f"mrow{i}", [1, D], F32) for i in range(2)]
            gtmp = SB(st, "gtmp", [128, D], F32)
            pMod = PS(st, "pMod", [128, D], F32)
            pB = PS(st, "pB", [128, D], F32)
            pX = PS(st, "pX", [128, 8], F32)
            LD(cbt[:], cb, "cbt", "d_c1")
            LD(brow[:], b_ada, "brow", "d_c1")
            A(lambda e: e.activation(scT[:], cbt[:], AF.Silu), r=["cbt"], w=["scT"])
            for m in range(6):
                wb = wada[m % 2]
                wn = f"wada{m % 2}"
                for kk in range(2):
                    P.op("gpsimd", lambda e, kk=kk, m=m, wb=wb: e.dma_start(
                        out=wb[:, 4 * kk:4 * kk + 4, :],
                        in_=w_ada[512 * kk:512 * kk + 512, m * D:(m + 1) * D].rearrange("(k p) n -> p k n", p=128)),
                        writes=[wn], dma=f"d_wada{m % 2}")
                for c2 in range(2):
                    for k in range(8):
                        T(lambda e, k=k, c2=c2, wb=wb: e.matmul(pMod[0:1, c2 * 512:(c2 + 1) * 512], scT[:, k:k + 1],
                                                                wb[:, k, c2 * 512:(c2 + 1) * 512],
                                                                start=(k == 0), stop=(k == 7)),
                          r=[wn, "scT"], w=["pMod"], inc=(k == 7))
                mr = mrow[m % 2]
                mn = f"mrow{m % 2}"
                V(lambda e, m=m, mr=mr: e.tensor_tensor(mr[:], pMod[0:1, :], brow[0:1, m * D:(m + 1) * D], ALU.add),
                  r=["pMod", "brow"], w=[mn])
                if m in (0, 3):
                    for k in range(8):
                        T(lambda e, k=k, mr=mr: e.matmul(pX[:, k:k + 1], mr[0:1, k * 128:(k + 1) * 128],
                                                         ones_f[0:1, 0:1], start=True, stop=True),
                          r=[mn, "ones_f"], w=["pX"], inc=(k == 7))
                    dst, dn = (sh1T, "sh1T") if m == 0 else (sh2T, "sh2T")
                    V(lambda e, dst=dst: e.tensor_copy(dst[:], pX[:]), r=["pX"], w=[dn])
                else:
                    for c2 in range(2):
                        T(lambda e, c2=c2, mr=mr: e.matmul(pB[:, c2 * 512:(c2 + 1) * 512], ones_f[0:1, :],
                                                           mr[0:1, c2 * 512:(c2 + 1) * 512], start=True, stop=True),
                          r=[mn, "ones_f"], w=["pB"], inc=(c2 == 1))
                    if m in (1, 4):
                        LD(gtmp[:], g1_bd if m == 1 else g2_bd, "gtmp", "d_c2")
                        if m == 1:
                            V(lambda e: e.scalar_tensor_tensor(G1_b[:], pB[:], 1.0, gtmp[:], ALU.add, ALU.mult),
                              r=["pB", "gtmp"], w=["G1_b"])
                        else:
                            V(lambda e: e.scalar_tensor_tensor(gtmp[:], pB[:], 1.0, gtmp[:], ALU.add, ALU.mult),
                              r=["pB", "gtmp"], w=["gtmp"])
                            P.op("sync", lambda e: e.dma_start(out=scr[0], in_=gtmp[:]), reads=["gtmp"], dma="d_scr")
                    elif m == 2:
                        V(lambda e: e.tensor_copy(gt1_b[:], pB[:]), r=["pB"], w=["gt1_b"])
                    else:
                        V(lambda e: e.tensor_copy(gtmp[:], pB[:]), r=["pB"], w=["gtmp"])
                        P.op("sync", lambda e: e.dma_start(out=scr[1], in_=gtmp[:]), reads=["gtmp"], dma="d_scr")
            P.barrier()
            stage_end(1, lambda: [(dbgf[:, 0:1024], G1_b[:], "G1_b"), (dbgf[:, 1024:2048], gt1_b[:], "gt1_b"), (dbgb[:, 0:8], sh1T[:], "sh1T"), (dbgb[:, 8:16], sh2T[:], "sh2T")])

        with ExitStack() as stAR, ExitStack() as stA:
            AR1 = SB(stAR, "AR1", [128, S + NT * 2 * 129], BF16)
            KT = AR1[:, 0:S]
            VVf = AR1[:, S:S + NT * 2 * 129]
            VV = VVf.rearrange("p (t g c) -> p t g c", t=NT, g=2)
            VVk = VVf.rearrange("p (t c) -> p t c", t=NT)
            QT = SB(stA, "QT", [128, 4, NCOLS], BF16)
            mixT = SB(stA, "mixT", [128, 8, NCOLS], BF16)
            V(lambda e: e.memset(AR1[:], 0.0), w=["VV", "KT"])
            V(lambda e: e.memset(mixT[:, 0:4, 0:128], 0.0), w=["mixT"])
            V(lambda e: e.memset(mixT[:, 0:4, 2176:2304], 0.0), w=["mixT"])
            if debug_stage:
                V(lambda e: e.memset(QT[:], 0.0), w=["QT"])
            V(lambda e: e.memset(VV[:, :, :, 0:1], 1.0), w=["VV"])
            V(lambda e: e.memset(VV[:, :, :, 128:129], 1.0), w=["VV"])

            with ExitStack() as st:
                Win = SB(st, "Win", [128, 8, 1792], BF16)
                wsT = SB(st, "wsTb", [128, 8, 128], BF16)
                bsbT = SB(st, "bsbT", [128, 512], F32)
                xt = [SB(st, f"xt{i}", [128, D], F32) for i in range(2)]
                junk = SB(st, "junk", [128, D], BF16)
                xn = [SB(st, f"xn{i}", [128, D], BF16) for i in range(2)]
                xnT = [SB(st, f"xnT{i}", [128, 8, 128], BF16) for i in range(2)]
                cst = [SB(st, f"cst{i}", [128, 64], F32) for i in range(2)]
                snt = [SB(st, f"snt{i}", [128, 64], F32) for i in range(2)]
                kvf = SB(st, "kvf", [128, 256], F32)
                ksq = SB(st, "ksq", [128, 128], F32)
                kra = SB(st, "kra", [128, 128], F32)
                krb = SB(st, "krb", [128, 128], F32)
                kbf = SB(st, "kbf", [128, 128], BF16)
                f1 = SB(st, "f1", [128, 512], F32)
                f2 = SB(st, "f2", [128, 512], F32)
                f3 = SB(st, "f3", [128, 512], F32)
                qbf = SB(st, "qbf", [128, 512], BF16)
                vnpad = SB(st, "vnpad", [128, 8, 128], BF16)
                uT = SB(st, "uT", [128, 512], F32)
                tk = SB(st, "tk", [128, 512], F32)
                sqt = SB(st, "sqt", [128, 512], F32)
                pT = [PS(st, f"pT{i}", [128, D], BF16) for i in range(2)]
                pKV = PS(st, "pKV", [128, 512], F32)
                pQ = PS(st, "pQ", [128, 512], F32)
                pZV = PS(st, "pZV", [128, 512], F32)
                pU = PS(st, "pU", [128, 512], F32)
                pTQ = PS(st, "pTQ", [128, D], BF16)
                pM = PS(st, "pM", [128, 512], F32)

                for kk in range(2):
                    P.op("gpsimd", lambda e, kk=kk: e.dma_start(
                        out=Win[:, 4 * kk:4 * kk + 4, :],
                        in_=w_in[512 * kk:512 * kk + 512, :].rearrange("(k p) n -> p k n", p=128)),
                        writes=["Win"], dma="d_win")
                LD(wsT[:], wsT_d, "wsT", "d_c3", eng="gpsimd")
                LD(bsbT[:], bsbT_d, "bsbT", "d_c1")
                V(lambda e: e.memset(vnpad[:], 0.0), w=["vnpad"])

                colgrp = [(0, 512, bias_q, "bias_q", 0), (512, 256, bias_kv, "bias_kv", 512),
                          (1280, 512, bias_zv, "bias_zv", 768)]
                for (c0, n, dst, dn, r0) in colgrp:
                    for k in range(8):
                        T(lambda e, k=k, c0=c0, n=n: e.matmul(pQ[0:1, 0:n], sh1T[:, k:k + 1], Win[:, k, c0:c0 + n],
                                                              start=(k == 0), stop=(k == 7)),
                          r=["Win", "sh1T"], w=["pQ"], inc=(k == 7))
                    V(lambda e, n=n, r0=r0: e.tensor_copy(tk[0:1, 0:n], pQ[0:1, 0:n]), r=["pQ"], w=["tk"])
                    T(lambda e, n=n, r0=r0: e.matmul(pZV[:, 0:n], ones_f[0:1, :], tk[0:1, 0:n],
                                                     start=True, stop=True), r=["tk", "ones_f"], w=["pZV"])
                    V(lambda e, n=n, dst=dst: e.tensor_copy(dst[:, 0:n], pZV[:, 0:n]), r=["pZV"], w=[dn])
                for c4 in range(4):
                    for k in range(8):
                        T(lambda e, k=k, c4=c4: e.matmul(pU[:, c4:c4 + 1], Win[:, k, 768 + c4 * 128:768 + (c4 + 1) * 128],
                                                         sh1T[:, k:k + 1], start=(k == 0), stop=(k == 7)),
                          r=["Win", "sh1T"], w=["pU"], inc=(k == 7 and c4 == 3))
                V(lambda e: e.tensor_copy(biasU[:], pU[:, 0:4]), r=["pU"], w=["biasU"])


                qf = SB(st, "qf", [128, 512], F32)
                zvf = SB(st, "zvf", [128, 512], F32)
                sm18 = SB(st, "sm18", [128, 18], F32)
                V(lambda e: e.memset(sm18[:], 1.0), w=["sm18"])
                print("sbuf remaining at AB:", nc.sbuf_bytes_remaining)

                def v3(a, nh):
                    return a[:, 0:nh * 64].rearrange("p (h d) -> p h d", h=nh)

                def front(t):
                    s = t % 2
                    own = t < NX
                    LD(xt[s][:], x_rot[t * 128:(t + 1) * 128, :], f"xt{s}")
                    LD(cst[s][:], cos_t[:, t, :], f"cst{s}")
                    LD(snt[s][:], sin_t[:, t, :], f"snt{s}")
                    A(lambda e: e.activation(junk[:], xt[s][:], AF.Square, accum_out=ss[:, t:t + 1]),
                      r=[f"xt{s}"], w=["junk", "ss"])
                    rsqrt_cols(rstd, ss, t, t + 1, 1.0 / D, "ss", "rstd")
                    V(lambda e: e.scalar_tensor_tensor(xn[s][:], xt[s][:], rstd[:, t:t + 1], G1_b[:], ALU.mult, ALU.mult),
                      r=[f"xt{s}", "rstd", "G1_b"], w=[f"xn{s}"])
                    for k in range(8):
                        T(lambda e, k=k: e.transpose(pT[s][:, k * 128:(k + 1) * 128], xn[s][:, k * 128:(k + 1) * 128], ident_b[:]),
                          r=[f"xn{s}", "ident_b"], w=[f"pT{s}"], inc=(k == 7))

                def front2(t):
                    s = t % 2
                    own = t < NX
                    A(lambda e: e.copy(xnT[s][:].rearrange("p k n -> p (k n)"), pT[s][:]), r=[f"pT{s}"], w=[f"xnT{s}"])
                    for k in range(8):
                        T(lambda e, k=k: e.matmul(pKV[:, 0:256], xnT[s][:, k, :], Win[:, k, 512:768],
                                                  start=(k == 0), stop=(k == 7)),
                          r=[f"xnT{s}", "Win"], w=["pKV"], inc=(k == 7))
                    if own:
                        for k in range(8):
                            T(lambda e, k=k: e.matmul(pQ[:, :], xnT[s][:, k, :], Win[:, k, 0:512],
                                                      start=(k == 0), stop=(k == 7)),
                              r=[f"xnT{s}", "Win"], w=["pQ"], inc=(k == 7))
                        for k in range(8):
                            T(lambda e, k=k: e.matmul(pZV[:, :], xnT[s][:, k, :], Win[:, k, 1280:1792],
                                                      start=(k == 0), stop=(k == 7)),
                              r=[f"xnT{s}", "Win"], w=["pZV"], inc=(k == 7))
                        for c4 in range(4):
                            for k in range(8):
                                T(lambda e, k=k, c4=c4: e.matmul(pU[:, c4 * 128:(c4 + 1) * 128],
                                                                 Win[:, k, 768 + c4 * 128:768 + (c4 + 1) * 128],
                                                                 xnT[s][:, k, :], start=(k == 0), stop=(k == 7)),
                                  r=[f"xnT{s}", "Win"], w=["pU"], inc=(k == 7 and c4 == 3))

                def mid(t):
                    own = t < NX
                    V(lambda e: e.tensor_tensor(kvf[:], pKV[:, 0:256], bias_kv[:], ALU.add), r=["pKV", "bias_kv"], w=["kvf"])
                    if own:
                        V(lambda e: e.tensor_tensor(zvf[:], pZV[:], bias_zv[:], ALU.add), r=["pZV", "bias_zv"], w=["zvf"])
                        A(lambda e: e.activation(zvf[:], zvf[:], AF.Gelu_apprx_tanh), r=["zvf"], w=["zvf"])
                        V(lambda e: e.tensor_tensor(qf[:], pQ[:], bias_q[:], ALU.add), r=["pQ", "bias_q"], w=["qf"])
                        for c4 in range(4):
                            A(lambda e, c4=c4: e.activation(uT[:, c4 * 128:(c4 + 1) * 128], pU[:, c4 * 128:(c4 + 1) * 128],
                                                            AF.Gelu_apprx_tanh, bias=biasU[:, c4:c4 + 1]),
                              r=["pU", "biasU"], w=["uT"])

                def rope2(src, sname, nh, ra, raname, rb, rbname, dst, dname, cs, csn, sn, snn):
                    V(lambda e: e.tensor_tensor(v3(ra, nh), v3(src, nh), cs[:].unsqueeze(1).to_broadcast([128, nh, 64]), ALU.mult),
                      r=[sname, csn], w=[raname])
                    for blk in range(4):
                        pb = blk ^ 1
                        G(lambda e, blk=blk, pb=pb: e.tensor_tensor(
                            v3(rb, nh)[:, :, blk * 16:(blk + 1) * 16], v3(src, nh)[:, :, pb * 16:(pb + 1) * 16],
                            sn[:, blk * 16:(blk + 1) * 16].unsqueeze(1).to_broadcast([128, nh, 16]), ALU.mult),
                          r=[sname, snn], w=[rbname])
                    V(lambda e: e.tensor_tensor(dst[:, 0:nh * 64], ra[:, 0:nh * 64], rb[:, 0:nh * 64], ALU.add),
                      r=[raname, rbname], w=[dname])

                def back_a(t):
                    s = t % 2
                    own = t < NX
                    G(lambda e: e.tensor_copy(VV[:, t, :, 64:128], kvf[:, 128:256].rearrange("p (g d) -> p g d", g=2)),
                      r=["kvf"], w=["VV"])
                    G(lambda e: e.tensor_tensor(ksq[:], kvf[:, 0:128], kvf[:, 0:128], ALU.mult), r=["kvf"], w=["ksq"])
                    V(lambda e: e.tensor_reduce(sm18[:, 0:2], v3(ksq, 2), axis=AX.X, op=ALU.add), r=["ksq"], w=["sm18"])
                    if own:
                        G(lambda e: e.tensor_tensor(f3[:], qf[:], qf[:], ALU.mult), r=["qf"], w=["f3"])
                        V(lambda e: e.tensor_reduce(sm18[:, 2:10], v3(f3, 8), axis=AX.X, op=ALU.add), r=["f3"], w=["sm18"])
                        G(lambda e: e.tensor_tensor(f1[:], zvf[:], zvf[:], ALU.mult), r=["zvf"], w=["f1"])
                        V(lambda e: e.tensor_reduce(sm18[:, 10:18], v3(f1, 8), axis=AX.X, op=ALU.add), r=["f1"], w=["sm18"])
                    nc_ = 18 if own else 2
                    rsqrt_cols(sm18, sm18, 0, nc_, 1.0 / 64, "sm18", "sm18")

                def back_b(t):
                    s = t % 2
                    own = t < NX
                    V(lambda e: e.tensor_tensor(v3(kra, 2), v3(kvf, 2), sm18[:, 0:2].unsqueeze(2).to_broadcast([128, 2, 64]), ALU.mult),
                      r=["kvf", "sm18"], w=["kra"])
                    V(lambda e: e.tensor_tensor(v3(kra, 2), v3(kra, 2), gk_b[:].unsqueeze(1).to_broadcast([128, 2, 64]), ALU.mult),
                      r=["kra", "gk_b"], w=["kra"])
                    rope2(kra, "kra", 2, ksq, "ksq", krb, "krb", kbf, "kbf", cst[s], f"cst{s}", snt[s], f"snt{s}")
                    T(lambda e: e.transpose(pTQ[:, 512:640], kbf[:], ident_b[:]), r=["kbf", "ident_b"], w=["pTQk"])
                    kt_copy = lambda: V(lambda e: e.tensor_copy(KT[:, t * 128:(t + 1) * 128], pTQ[:, 512:640]), r=["pTQk"], w=["KT"])
                    if not own:
                        pending.append(kt_copy)
                        return
                    kt_copy()
                    V(lambda e: e.tensor_tensor(v3(f2, 8), v3(qf, 8), sm18[:, 2:10].unsqueeze(2).to_broadcast([128, 8, 64]), ALU.mult),
                      r=["qf", "sm18"], w=["f2"])
                    V(lambda e: e.tensor_tensor(v3(f2, 8), v3(f2, 8), gq_b[:].unsqueeze(1).to_broadcast([128, 8, 64]), ALU.mult),
                      r=["f2", "gq_b"], w=["f2"])
                    rope2(f2, "f2", 8, f3, "f3", f1, "f1", qbf, "qbf", cst[s], f"cst{s}", snt[s], f"snt{s}")
                    for p in range(4):
                        T(lambda e, p=p: e.transpose(pTQ[:, p * 128:(p + 1) * 128], qbf[:, p * 128:(p + 1) * 128], ident_b[:]),
                          r=["qbf", "ident_b"], w=["pTQ"], inc=(p == 3))
                    V(lambda e: e.tensor_tensor(v3(f2, 8), v3(zvf, 8), sm18[:, 10:18].unsqueeze(2).to_broadcast([128, 8, 64]), ALU.mult),
                      r=["zvf", "sm18"], w=["f2"])
                    for sl in range(2):
                        G(lambda e, sl=sl: e.tensor_tensor(
                            vnpad[:].rearrange("p (c s) n -> p c s n", s=2)[:, :, sl, sl * 64:(sl + 1) * 64],
                            f2[:].rearrange("p (c s d) -> p c s d", s=2, d=64)[:, :, sl, :],
                            gtv_b[:].unsqueeze(1).to_broadcast([128, 4, 64]), ALU.mult),
                          r=["f2", "gtv_b"], w=["vnpad"])
                    for c4 in range(4):
                        for sl in range(2):
                            T(lambda e, c4=c4, sl=sl: e.matmul(pM[:, c4 * 128:(c4 + 1) * 128], vnpad[:, 2 * c4 + sl, :],
                                                               wsT[:, 2 * c4 + sl, :], start=(sl == 0), stop=(sl == 1)),
                              r=["vnpad", "wsT"], w=["pM"], inc=(sl == 1 and c4 == 3))
                    V(lambda e: e.tensor_copy(QT[:, :, t * 128:(t + 1) * 128], pTQ[:, 0:512].rearrange("p (a n) -> p a n", a=4)),
                      r=["pTQ"], w=["QT"])
                    V(lambda e: e.tensor_tensor(tk[:], pM[:], bsbT[:], ALU.add), r=["pM", "bsbT"], w=["tk"])
                    V(lambda e: e.tensor_tensor(tk[:], tk[:], uT[:], ALU.mult), r=["tk", "uT"], w=["tk"])
                    G(lambda e: e.tensor_tensor(sqt[:], tk[:], tk[:], ALU.mult), r=["tk"], w=["sqt"])
                    for c4 in range(4):
                        T(lambda e, c4=c4: e.matmul(pM[:, 0:1] if False else pTS[:, 0:1], sqt[:, c4 * 128:(c4 + 1) * 128], ones_f[:, 0:1],
                                                    start=(c4 == 0), stop=(c4 == 3)),
                          r=["sqt", "ones_f"], w=["pTS"], inc=(c4 == 3))
                    V(lambda e: e.tensor_tensor(mixT[:, 4:8, t * 128:(t + 1) * 128],
                                                tk[:].rearrange("p (c n) -> p c n", c=4),
                                                gmix[:, 4:8].unsqueeze(2).to_broadcast([128, 4, 128]), ALU.mult),
                      r=["tk", "gmix"], w=["mixT"])
                    V(lambda e: e.tensor_copy(sstok[:, t:t + 1], pTS[:, 0:1]), r=["pTS"], w=["sstok"])

                pTS = pKV[:, 256:512]
                pending = []
                front(0)
                front2(0)
                mid(0)
                for t in range(AB_TILES):
                    if t + 1 < AB_TILES:
                        front(t + 1)
                    back_a(t)
                    if t + 1 < AB_TILES:
                        front2(t + 1)
                    back_b(t)
                    if t + 1 < AB_TILES:
                        mid(t + 1)
                    for f_ in pending:
                        f_()
                    pending.clear()

                rsqrt_cols(rs_t, sstok, 0, NX, 1.0 / 512, "sstok", "rs_t")
                P.barrier()
                stage_end(2, lambda: [(dbgb[:, 0:8192], KT, "KT"), (dbgb[:, 8192:17408], QT[:].rearrange("p a n -> p (a n)"), "QT"), (dbgb[:, 17408:26624], mixT[:, 4:8, :].rearrange("p a n -> p (a n)"), "mixT"), (dbgb[:, 26624:43136], VVf, "VV"), (dbgf[:, 0:18], rs_t[:, 0:18], "rs_t"), (dbgf[:, 64:128], rstd[:, :], "rstd"), (dbgf[:, 128:132], biasU[:], "biasU"), (dbgf[:, 1024:1536], bias_q[:], "bias_q"), (dbgf[:, 1536:1792], bias_kv[:], "bias_kv"), (dbgf[:, 2048:2560], bias_zv[:], "bias_zv")])

            with ExitStack() as st:
                Pb = [SB(st, f"Pb{i}", [128, 1024], BF16) for i in range(3)]
                Oa = SB(st, "Oa", [128, 512], F32)
                Ob = SB(st, "Ob", [128, 512], F32)
                rden = SB(st, "rden", [128, 512], F32)
                at = SB(st, "at", [128, 512], F32)
                sq = SB(st, "sq", [128, 512], F32)
                acc = SB(st, "acc", [128, 512], F32)
                pS = [PS(st, f"pS{i}", [128, 1024], F32) for i in range(2)]
                pOa = PS(st, "pOa", [128, 512], F32)
                pOb = PS(st, "pOb", [128, 512], F32)
                pBc = PS(st, "pBc", [128, 512], F32)
                pSA = PS(st, "pSA", [128, 512], F32)
                for ct in range(NCT):
                    for ab in range(2):
                        P.op("gpsimd", lambda e: e.dma_start(
                            out=wub[ct].rearrange("p (k a n) -> p k a n", k=8, a=2)[:, :, ab, :],
                            in_=w_up[:, ab * DFF + ct * 128:ab * DFF + (ct + 1) * 128].rearrange("(k p) n -> p k n", p=128)),
                            writes=[f"wub{ct}_{ab}"], dma="d_wub")
                QH = SB(st, "QH", [128, 4, 2], BF16)
                hs = SB(st, "hs", [128, 2], F32)

                def qk(kt, i, p, q0, qn):
                    b = pS[i % 2]
                    T(lambda e: e.matmul(b[:, 0:qn], KT[0:64, kt * 128:(kt + 1) * 128], QT[0:64, p, q0:q0 + qn],
                                         start=True, stop=True), r=["KT", "QT"], w=[f"pS{i % 2}"], inc=False)
                    T(lambda e: e.matmul(b[:, 512:512 + qn], KT[64:128, kt * 128:(kt + 1) * 128],
                                         QT[64:128, p, q0:q0 + qn], start=True, stop=True),
                      r=["KT", "QT"], w=[f"pS{i % 2}"])

                def ex(kt, i):
                    A(lambda e: e.activation(Pb[i % 3][:], pS[i % 2][:], AF.Exp, scale=0.125),
                      r=[f"pS{i % 2}"], w=[f"Pb{i % 3}"])

                def pv(kt, i, qn):
                    pb = Pb[i % 3]
                    T(lambda e: e.matmul(pOa[:, 0:qn], VVk[:, kt, 64:192], pb[:, 0:qn],
                                         start=(kt == 0), stop=(kt == NT - 1)),
                      r=[f"Pb{i % 3}", "VV"], w=["pOa"], inc=False)
                    T(lambda e: e.matmul(pOb[:, 0:qn], VV[:, kt, 1, 0:128], pb[:, 512:512 + qn],
                                         start=(kt == 0), stop=(kt == NT - 1)),
                      r=[f"Pb{i % 3}", "VV"], w=["pOa", "pOb"])

                def epi_a(qn):
                    V(lambda e: e.tensor_copy(Oa[0:65, 0:qn], pOa[0:65, 0:qn]), r=["pOa"], w=["Oa"])
                    V(lambda e: e.tensor_copy(Ob[:, 0:qn], pOb[:, 0:qn]), r=["pOb"], w=["Ob"])
                    V(lambda e: e.reciprocal(rden[64:65, 0:qn], Oa[64:65, 0:qn]), r=["Oa"], w=["rdenA"])
                    V(lambda e: e.reciprocal(rden[0:1, 0:qn], Ob[0:1, 0:qn]), r=["Ob"], w=["rdenB"])

                def epi_b(qn):
                    T(lambda e: e.matmul(pBc[:, 0:qn], ones_f[64:65, :], rden[64:65, 0:qn], start=True, stop=True),
                      r=["rdenA", "ones_f"], w=["pBc"], inc=False)
                    T(lambda e: e.matmul(pSA[:, 0:qn], ones_f[0:1, :], rden[0:1, 0:qn], start=True, stop=True),
                      r=["rdenB", "ones_f"], w=["pBc", "pSA"])
                    V(lambda e: e.tensor_tensor(at[0:64, 0:qn], Oa[0:64, 0:qn], pBc[0:64, 0:qn], ALU.mult),
                      r=["Oa", "pBc"], w=["at"])
                    V(lambda e: e.tensor_tensor(at[64:128, 0:qn], Ob[64:128, 0:qn], pSA[64:128, 0:qn], ALU.mult),
                      r=["Ob", "pSA"], w=["at"])

                def epi_main(qi, p, q0):
                    epi_b(512)
                    V(lambda e: e.tensor_scalar(mixT[:, p, q0:q0 + 512], at[:, :], gmix[:, p:p + 1], None, ALU.mult),
                      r=["at", "gmix"], w=["mixT"])
                    if p == 0:
                        G(lambda e: e.tensor_tensor(acc[:, :], at[:, :], at[:, :], ALU.mult), r=["at"], w=["acc"])
                    else:
                        G(lambda e: e.tensor_tensor(sq[:, :], at[:, :], at[:, :], ALU.mult), r=["at"], w=["sq"])
                        G(lambda e: e.tensor_tensor(acc[:, :], acc[:, :], sq[:, :], ALU.add), r=["acc", "sq"], w=["acc"])

                def epi_ss(qi):
                    for w_ in range(4):
                        T(lambda e, w_=w_: e.matmul(pSA[:, w_:w_ + 1], acc[:, w_ * 128:(w_ + 1) * 128], ones_f[:, 0:1],
                                                    start=True, stop=True), r=["acc", "ones_f"], w=["pSA"], inc=(w_ == 3))
                    V(lambda e: e.tensor_copy(ssatt[:, 1 + qi * 4:1 + qi * 4 + 4], pSA[:, 0:4]), r=["pSA"], w=["ssatt"])

                V(lambda e: e.tensor_copy(QH[:, :, 0:1], QT[:, :, 127:128]), r=["QT"], w=["QH"])
                V(lambda e: e.tensor_copy(QH[:, :, 1:2], QT[:, :, 2176:2177]), r=["QT"], w=["QH"])
                for g in range(2):
                    for kt in range(NT):
                        T(lambda e: e.matmul(pS[0][:, g * 512 + kt * 8:g * 512 + kt * 8 + 8],
                                             KT[g * 64:(g + 1) * 64, kt * 128:(kt + 1) * 128], QH[g * 64:(g + 1) * 64, :, :],
                                             start=True, stop=True), r=["KT", "QH"], w=["pS0"], inc=(kt == NT - 1))
                ex(0, 0)
                for kt in range(NT):
                    T(lambda e: e.matmul(pOa[0:65, 0:8], VV[:, kt, 0, 64:129], Pb[0][:, kt * 8:kt * 8 + 8],
                                         start=(kt == 0), stop=(kt == NT - 1)), r=["Pb0", "VV"], w=["pOa"], inc=(kt == NT - 1))
                for kt in range(NT):
                    T(lambda e: e.matmul(pOb[:, 0:8], VV[:, kt, 1, 0:128], Pb[0][:, 512 + kt * 8:512 + kt * 8 + 8],
                                         start=(kt == 0), stop=(kt == NT - 1)), r=["Pb0", "VV"], w=["pOb"], inc=(kt == NT - 1))
                epi_a(8)
                epi_b(8)
                at3 = at[:, 0:8].rearrange("p (a t) -> p a t", t=2)
                for tk_, col in ((0, 127), (1, 2176)):
                    V(lambda e: e.tensor_tensor(mixT[:, 0:4, col:col + 1], at3[:, :, tk_:tk_ + 1], gmix[:, 0:4].unsqueeze(2), ALU.mult),
                      r=["at", "gmix"], w=["mixT"])
                V(lambda e: e.tensor_tensor(sq[:, 0:8], at[:, 0:8], at[:, 0:8], ALU.mult), r=["at"], w=["sq"])
                V(lambda e: e.tensor_reduce(hs[:, 0:2], sq[:, 0:8].rearrange("p (a t) -> p t a", t=2), axis=AX.X, op=ALU.add),
                  r=["sq"], w=["hs"])
                V(lambda e: e.memset(acc[:, 0:256], 0.0), w=["acc"])
                V(lambda e: e.tensor_copy(acc[:, 127:129], hs[:, 0:2]), r=["hs"], w=["acc"])
                for w_ in range(2):
                    T(lambda e, w_=w_: e.matmul(pSA[:, w_:w_ + 1], acc[:, w_ * 128:(w_ + 1) * 128], ones_f[:, 0:1],
                                                start=True, stop=True), r=["acc", "ones_f"], w=["pSA"], inc=(w_ == 1))
                V(lambda e: e.tensor_copy(ssatt[:, 0:1], pSA[:, 0:1]), r=["pSA"], w=["ssatt"])
                V(lambda e: e.tensor_copy(ssatt[:, 17:18], pSA[:, 1:2]), r=["pSA"], w=["ssatt"])

                iters = [(qi, p) for qi in range(4) for p in range(4)]
                step = 1
                pend = None
                for (qi, p) in iters:
                    q0 = 128 + qi * 512
                    qk(0, step, p, q0, 512)
                    qk(1, step + 1, p, q0, 512)
                    for kt in range(NT):
                        ex(kt, step + kt)
                        if kt + 2 < NT:
                            qk(kt + 2, step + kt + 2, p, q0, 512)
                        pv(kt, step + kt, 512)
                        if kt == 3 and pend is not None:
                            epi_main(*pend)
                        if kt == 16 and pend is not None:
                            if pend[1] == 3:
                                epi_ss(pend[0])
                            pend = None
                    step += NT
                    epi_a(512)
                    pend = (qi, p, q0)
                epi_main(*pend)
                epi_ss(3)

                rsqrt_cols(rs_a, ssatt, 0, NX, 1.0 / 512, "ssatt", "rs_a")
                P.barrier()
                stage_end(3, lambda: [(dbgb[:, 0:9216], mixT[:, 0:4, :].rearrange("p a n -> p (a n)"), "mixT"), (dbgf[:, 0:18], rs_a[:, 0:18], "rs_a")])

            x1nT = AR1[:, 0:8 * NCOLS].rearrange("p (k n) -> p k n", k=8)
            with ExitStack() as st2:
                G2_b = SB(st2, "G2_b", [128, D], F32)
                Wout = SB(st2, "Wout", [128, 8, D], BF16)
                xe = [SB(st2, f"xe{i}", [128, D], F32) for i in range(2)]
                x1h = [SB(st2, f"x1h{i}", [128, D], F32) for i in range(2)]
                t1 = SB(st2, "t1", [128, D], F32)
                x1n = [SB(st2, f"x1n{i}", [128, D], BF16) for i in range(2)]
                junk2 = SB(st2, "junk2", [128, D], BF16)
                pYa = PS(st2, "pYa", [128, D], F32)
                pYb = PS(st2, "pYb", [128, D], F32)
                pT2 = [PS(st2, f"pT2{i}", [128, D], BF16) for i in range(2)]
                LD(G2_b[:], scr[0], "G2_b", "d_c2")
                for kk in range(2):
                    P.op("gpsimd", lambda e, kk=kk: e.dma_start(
                        out=Wout[:, 4 * kk:4 * kk + 4, :],
                        in_=w_out[512 * kk:512 * kk + 512, :].rearrange("(k p) n -> p k n", p=128)),
                        writes=["Wout"], dma="d_wout")
                def out_mm(t):
                    s = t % 2
                    LD(xe[s][:], x_rot[t * 128:(t + 1) * 128, :], f"xe{s}")
                    for c2 in range(2):
                        for k in range(4):
                            T(lambda e, k=k, c2=c2: e.matmul(pYa[:, c2 * 512:(c2 + 1) * 512], mixT[:, k, t * 128:(t + 1) * 128],
                                                             Wout[:, k, c2 * 512:(c2 + 1) * 512], start=(k == 0), stop=(k == 3)),
                              r=["mixT", "Wout"], w=["pYa"], inc=(k == 3 and c2 == 1))
                    for c2 in range(2):
                        for k in range(4, 8):
                            T(lambda e, k=k, c2=c2: e.matmul(pYb[:, c2 * 512:(c2 + 1) * 512], mixT[:, k, t * 128:(t + 1) * 128],
                                                             Wout[:, k, c2 * 512:(c2 + 1) * 512], start=(k == 4), stop=(k == 7)),
                              r=["mixT", "Wout"], w=["pYb"], inc=(k == 7 and c2 == 1))

                def out_ep1(t):
                    s = t % 2
                    xd = x1h[s][:]
                    xdn = f"x1h{s}"
                    A(lambda e: e.activation(t1[:], pYa[:], AF.Identity, scale=rs_a[:, t:t + 1]), r=["pYa", "rs_a"], w=["t1"])
                    V(lambda e: e.scalar_tensor_tensor(t1[:], pYb[:], rs_t[:, t:t + 1], t1[:], ALU.mult, ALU.add),
                      r=["pYb", "rs_t", "t1"], w=["t1"])

                def out_ep2(t):
                    s = t % 2
                    xd = x1h[s][:]
                    xdn = f"x1h{s}"
                    V(lambda e: e.tensor_tensor(t1[:], t1[:], gt1_b[:], ALU.mult), r=["t1", "gt1_b"], w=["t1"])
                    V(lambda e: e.tensor_tensor(xd, t1[:], xe[s][:], ALU.add), r=["t1", f"xe{s}"], w=[xdn])
                    if 1 <= t <= 16:
                        P.op("sync", lambda e: e.dma_start(out=x1s[(t - 1) * 128:t * 128, :], in_=xd), reads=[xdn], dma=f"d_x1s{s}")
                    A(lambda e: e.activation(junk2[:], xd, AF.Square, accum_out=ss2[:, t:t + 1]), r=[xdn], w=["junk2", "ss2"])
                    rsqrt_cols(rs2, ss2, t, t + 1, 1.0 / D, "ss2", "rs2")
                    V(lambda e: e.scalar_tensor_tensor(x1n[s][:], xd, rs2[:, t:t + 1], G2_b[:], ALU.mult, ALU.mult),
                      r=[xdn, "rs2", "G2_b"], w=[f"x1n{s}"])
                    for k in range(8):
                        T(lambda e, k=k: e.transpose(pT2[s][:, k * 128:(k + 1) * 128], x1n[s][:, k * 128:(k + 1) * 128], ident_b[:]),
                          r=[f"x1n{s}", "ident_b"], w=[f"pT2{s}"], inc=(k == 7))
                    V(lambda e: e.tensor_copy(x1nT[:, :, t * 128:(t + 1) * 128], pT2[s][:].rearrange("p (k n) -> p k n", k=8)),
                      r=[f"pT2{s}"], w=["x1nT"])

                out_mm(0)
                for t in range(NX):
                    out_ep1(t)
                    if t + 1 < NX:
                        out_mm(t + 1)
                    out_ep2(t)
                P.barrier()
                stage_end(4, lambda: [(dbgb[:, 0:18432], AR1[:, 0:18432], "x1nT")])
                print("sbuf remaining at OUT:", nc.sbuf_bytes_remaining)

            stA.close()
            with ExitStack() as st2:
                NWU = 2
                GW = 2
                groups = [(c, min(GW, NCT - c)) for c in range(0, NCT, GW)]
                NG = len(groups)
                gt2_b = SB(st2, "gt2_b", [128, D], F32)
                gfin_b = SB(st2, "gfin_b", [128, D], F32)
                gT = SB(st2, "gT", [128, NCT, 512], BF16)
                tail0 = 8 * NCOLS
                wu = [AR1[:, tail0:tail0 + 8 * 2 * GW * 128].rearrange("p (c f) -> p c f", c=GW),
                      SB(st2, "wu1", [128, GW, 8 * 2 * 128], BF16)]
                Wd = SB(st2, "Wd", [128, NCT, D], BF16)
                zb = [SB(st2, f"zb{i}", [128, 2, 514], F32) for i in range(2)]
                cv = [SB(st2, f"cv{i}", [128, 2, 512], F32) for i in range(2)]
                sl_ = [SB(st2, f"sl{i}", [128, 512], F32) for i in range(2)]
                y2q = SB(st2, "y2q", [128, 4, D], F32)
                xr = [SB(st2, f"xr{i}", [128, D], F32) for i in range(2)]
                junk3 = AR1[:, tail0 + 8 * 2 * GW * 128:tail0 + 8 * 2 * GW * 128 + D]
                ot = [SB(st2, f"ot{i}", [128, D], F32) for i in range(2)]
                pZ = [PS(st2, f"pZ{i}", [128, 512], F32) for i in range(4)]
                pY = [PS(st2, f"pY{i}", [128, 512], F32) for i in range(4)]
                LD(gt2_b[:], scr[1], "gt2_b")
                LD(gfin_b[:], gfin_bd, "gfin_b")
                print("sbuf remaining at FFN:", nc.sbuf_bytes_remaining)

                def ld_wu(gg):
                    c0_, n_ = groups[gg % NG]
                    s = gg % NWU
                    P.op("sync", lambda e: e.dma_start(out=wu[s][:, 0:n_, :], in_=wub[c0_:c0_ + n_].rearrange("c p f -> p c f")),
                         writes=[f"wu{s}"], dma=f"d_wu{s}")

                def wsl(qtr, ct, k, ab):
                    gi = ct // GW
                    s = (qtr * NG + gi) % NWU
                    j = ct - groups[gi][0]
                    return wu[s][:, j, :].rearrange("p (k a n) -> p k a n", k=8, a=2)[:, k, ab, :], f"wu{s}"

                def ld_wd(ct):
                    P.op("gpsimd", lambda e: e.dma_start(out=Wd[:, ct, :], in_=w_down[ct * 128:(ct + 1) * 128, :]),
                         writes=[f"Wd{ct}"], dma=f"d_Wd{ct}")

                def up_tail(qtr, ct):
                    ci = ct % 2
                    A(lambda e: e.activation(sl_[ci][:], cv[ci][:, 0, :], AF.Silu), r=[f"cv{ci}a"], w=[f"sl{ci}"])
                    V(lambda e: e.tensor_tensor(gT[:, ct, :], sl_[ci][:], cv[ci][:, 1, :], ALU.mult),
                      r=[f"sl{ci}", f"cv{ci}b"], w=["gT"])

                def down_mm(ct, c2):
                    for tt in range(4):
                        T(lambda e, tt=tt: e.matmul(pY[tt][:, :], gT[:, ct, tt * 128:(tt + 1) * 128],
                                                    Wd[:, ct, c2 * 512:(c2 + 1) * 512],
                                                    start=(ct == 0), stop=(ct == NCT - 1)),
                          r=["gT", f"Wd{ct}"], w=[f"pY{tt}"], inc=(ct == NCT - 1 or tt == 3))

                def bias2_for(ct):
                    pb = pY[ct % 4]
                    for ab in range(2):
                        for k in range(8):
                            wl, wn = wsl(0, ct, k, ab)
                            T(lambda e, k=k, ab=ab, wl=wl: e.matmul(pb[:, ab:ab + 1], wl, sh2T[:, k:k + 1], start=(k == 0), stop=(k == 7)),
                              r=[wn, "sh2T"], w=[f"pY{ct % 4}"], inc=(k == 7 and ab == 1))
                    A(lambda e: e.copy(bias2[:, ct:ct + 1], pb[:, 0:1]), r=[f"pY{ct % 4}"], w=["bias2"])
                    A(lambda e: e.copy(bias2[:, 22 + ct:23 + ct], pb[:, 1:2]), r=[f"pY{ct % 4}"], w=["bias2"])

                def fin_tail(qtr, tt):
                    ti = qtr * 4 + tt
                    os_ = ti % 2
                    V(lambda e: e.tensor_tensor(y2q[:, tt, :], y2q[:, tt, :], xr[tt % 2][:], ALU.add),
                      r=[f"y2q{tt}", f"xr{tt % 2}"], w=[f"y2q{tt}"])
                    if tt < 2:
                        LD(xr[tt % 2][:], x1s[(ti + 2) * 128:(ti + 3) * 128, :], f"xr{tt % 2}")
                    A(lambda e: e.activation(junk3, y2q[:, tt, :], AF.Square, accum_out=ssf[:, ti:ti + 1]),
                      r=[f"y2q{tt}"], w=["junk3", "ssf"])
                    rsqrt_cols(rsf, ssf, ti, ti + 1, 1.0 / D, "ssf", "rsf")
                    V(lambda e: e.scalar_tensor_tensor(ot[os_][:], y2q[:, tt, :], rsf[:, ti:ti + 1], gfin_b[:],
                                                       ALU.mult, ALU.mult),
                      r=[f"y2q{tt}", "rsf", "gfin_b"], w=[f"ot{os_}"])
                    P.op("sync", lambda e: e.dma_start(out=out_d[ti * 128:(ti + 1) * 128, :], in_=ot[os_][:]),
                         reads=[f"ot{os_}"], dma=f"d_out{os_}")

                ld_wu(0)
                for qtr in range(4):
                    c0 = 127 + qtr * 512
                    for ct in range(NCT):
                        zi = ct % 2
                        ci = ct % 2
                        if ct % GW == 0:
                            gg = qtr * NG + ct // GW
                            if gg + 1 < 4 * NG:
                                ld_wu(gg + 1)
                        if qtr == 0 and ct == 0:
                            bias2_for(0)
                        for ab in range(2):
                            for blk in range(2):
                                pz = pZ[ab * 2 + blk]
                                for k in range(8):
                                    wl, wn = wsl(qtr, ct, k, ab)
                                    T(lambda e, k=k, wl=wl: e.matmul(
                                        pz[:, 0:257], wl,
                                        x1nT[:, k, c0 + blk * 257:c0 + (blk + 1) * 257], start=(k == 0), stop=(k == 7)),
                                      r=[wn, "x1nT"], w=[f"pZ{ab * 2 + blk}"], inc=(k == 7))
                                A(lambda e: e.activation(
                                    zb[zi][:, ab, blk * 257:(blk + 1) * 257], pz[:, 0:257], AF.Identity,
                                    bias=bias2[:, ab * 22 + ct:ab * 22 + ct + 1]),
                                  r=[f"pZ{ab * 2 + blk}", "bias2"], w=[f"zb{zi}"])
                        if qtr >= 1 and ct >= 2:
                            down_mm(ct - 2, 0)
                        if qtr == 0 and ct + 1 < NCT:
                            bias2_for(ct + 1)
                        if qtr == 0:
                            ld_wd(ct)
                        if qtr > 0 and ct < 4:
                            fin_tail(qtr - 1, ct)
                        if ct == 4:
                            for tt in range(2):
                                LD(xr[tt][:], x1s[(qtr * 4 + tt) * 128:(qtr * 4 + tt + 1) * 128, :], f"xr{tt}")
                        if qtr == 0:
                            V(lambda e: e.tensor_scalar(zb[zi][:, :, 0:1], zb[zi][:, :, 0:1], hm[:, 0:1], None, ALU.mult),
                              r=[f"zb{zi}", "hm"], w=[f"zb{zi}"])
                        if qtr == 3:
                            V(lambda e: e.tensor_scalar(zb[zi][:, :, 513:514], zb[zi][:, :, 513:514], hm[:, 1:2], None, ALU.mult),
                              r=[f"zb{zi}", "hm"], w=[f"zb{zi}"])
                        for ab in range(2):
                            ch = ab * 22 + ct
                            if ab == 0:
                                A(lambda e: e.activation(cv[ci][:, ab, :], zb[zi][:, ab, 1:513], AF.Identity,
                                                         bias=bconv[:, ch:ch + 1], scale=wconv[:, 1, ch:ch + 1]),
                                  r=[f"zb{zi}", "wconv", "bconv"], w=[f"cv{ci}a"])
                            else:
                                G(lambda e: e.tensor_scalar(cv[ci][:, ab, :], zb[zi][:, ab, 1:513], wconv[:, 1, ch:ch + 1],
                                                            bconv[:, ch:ch + 1], ALU.mult, ALU.add),
                                  r=[f"zb{zi}", "wconv", "bconv"], w=[f"cv{ci}b"])
                        for ab in range(2):
                            ch = ab * 22 + ct
                            cvn = f"cv{ci}" + "ab"[ab]
                            V(lambda e: e.scalar_tensor_tensor(cv[ci][:, ab, :], zb[zi][:, ab, 0:512], wconv[:, 0, ch:ch + 1],
                                                               cv[ci][:, ab, :], ALU.mult, ALU.add),
                              r=[f"zb{zi}", "wconv", cvn], w=[cvn])
                            V(lambda e: e.scalar_tensor_tensor(cv[ci][:, ab, :], zb[zi][:, ab, 2:514], wconv[:, 2, ch:ch + 1],
                                                               cv[ci][:, ab, :], ALU.mult, ALU.add),
                              r=[f"zb{zi}", "wconv", cvn], w=[cvn])
                        if ct > 0:
                            up_tail(qtr, ct - 1)
                    up_tail(qtr, NCT - 1)
                    if qtr >= 1:
                        down_mm(NCT - 2, 0)
                        down_mm(NCT - 1, 0)
                    for c2 in range(2):
                        if not (qtr >= 1 and c2 == 0):
                            for ct in range(NCT):
                                down_mm(ct, c2)
                        for tt in range(4):
                            V(lambda e, tt=tt: e.tensor_tensor(y2q[:, tt, c2 * 512:(c2 + 1) * 512], pY[tt][:, :],
                                                               gt2_b[:, c2 * 512:(c2 + 1) * 512], ALU.mult),
                              r=[f"pY{tt}", "gt2_b"], w=[f"y2q{tt}"])
                    if qtr == 3:
                        for tt in range(4):
                            fin_tail(3, tt)
                P.barrier()
        print("kernel build: inst", P.ninst, "waits", P.nwait, "sems", len(P.sem), P.cnt)
    except _Stop:
        pass
    return nc


def _prep_inputs(inp):
    x = np.asarray(inp["x"], np.float32)
    c = np.asarray(inp["c"], np.float32)
    f = lambda k: np.asarray(inp[k], np.float32)
    rows = S // 64
    row_id = np.repeat(np.arange(rows), 64).astype(np.float32)
    col_id = np.tile(np.arange(64), rows).astype(np.float32)
    inv_freq = np.power(np.float32(10000.0), -np.arange(0, 32, 2, dtype=np.float32) / np.float32(32)).astype(np.float32)
    ang_r = row_id[:, None] * inv_freq[None, :]
    ang_c = col_id[:, None] * inv_freq[None, :]
    ang = np.concatenate([ang_r, ang_r, ang_c, ang_c], axis=-1).astype(np.float32)
    cos = np.cos(ang).astype(np.float32)
    sin = np.sin(ang).astype(np.float32)
    sgn = np.concatenate([-np.ones(16), np.ones(16), -np.ones(16), np.ones(16)]).astype(np.float32)
    sinS = sin * sgn[None, :]
    perm_h = [0, 4, 1, 5, 2, 6, 3, 7]
    qcols = np.concatenate([np.arange(h * 64, (h + 1) * 64) for h in perm_h])
    w_in = f("w_in")[0].copy()
    w_in[:, 0:512] = w_in[:, qcols]
    w_out = f("w_out")[0].copy()
    w_out[0:512, :] = w_out[qcols, :]
    g_attn = f("g_attn_out")[0][qcols]
    gmixv = np.concatenate([g_attn, f("g_tok_out")[0]])
    gmix = np.ascontiguousarray(gmixv.reshape(8, 128).T)
    rep = lambda v, n=128: np.ascontiguousarray(np.broadcast_to(v[None, :], (n, v.shape[0])))
    wsT = np.ascontiguousarray(f("w_s")[0].transpose(2, 0, 1))
    b_s = f("b_s")[0]
    bsbT = np.zeros((128, 4, 128), np.float32)
    for c4 in range(4):
        bsbT[0:64, c4, :] = b_s[2 * c4][None, :]
        bsbT[64:128, c4, :] = b_s[2 * c4 + 1][None, :]
    wconv = np.ascontiguousarray(f("w_conv")[0].reshape(3, 44, 128).transpose(2, 0, 1))
    bconv = np.ascontiguousarray(f("b_conv")[0].reshape(44, 128).T)
    shared = {
        "w_ada": f("w_ada")[0], "b_ada": f("b_ada"), "g1_b": rep(f("g_norm1")[0]), "g2_b": rep(f("g_norm2")[0]),
        "gfin_b": rep(f("g_final")), "w_in": w_in, "w_out": w_out, "w_up": f("w_up")[0], "w_down": f("w_down")[0],
        "gq_b": rep(f("g_q")[0]), "gk_b": rep(f("g_k")[0]), "gtv_b": rep(f("g_tok_v")[0]), "wsT": wsT,
        "bsbT": bsbT.reshape(128, 512), "gmix": gmix, "wconv": wconv, "bconv": bconv,
        "ident": np.eye(128, dtype=np.float32),
    }
    in_maps = []
    for core in range(8):
        b, j = core // 4, core % 4
        sh = j * 2048 - 128
        xr = np.roll(x[b], -sh, axis=0)
        cr = np.roll(cos, -sh, axis=0).reshape(NT, 128, 64).transpose(1, 0, 2)
        sr = np.roll(sinS, -sh, axis=0).reshape(NT, 128, 64).transpose(1, 0, 2)
        hmk = np.ones((128, 2), np.float32)
        if j == 0:
            hmk[:, 0] = 0.0
        if j == 3:
            hmk[:, 1] = 0.0
        m = dict(shared)
        m.update({"x_rot": np.ascontiguousarray(xr), "cos_t": np.ascontiguousarray(cr), "sin_t": np.ascontiguousarray(sr),
                  "cb": np.ascontiguousarray(c[b].reshape(8, 128).T), "hmask": hmk})
        in_maps.append(m)
    return in_maps


_NC_CACHE = {}


def kernel(**inputs):
    in_maps = _prep_inputs(inputs)
    if "nc" not in _NC_CACHE:
        _NC_CACHE["nc"] = build_nc()
    nc = _NC_CACHE["nc"]
    res = run_bass_kernel_spmd(nc, in_maps, core_ids=list(range(8)))
    out = np.zeros((2, S, D), np.float32)
    for core in range(8):
        b, j = core // 4, core % 4
        out[b, j * 2048:(j + 1) * 2048, :] = res.results[core]["out"]
    return out
```

```python
import numpy as np
import concourse.bass as bass
import concourse.mybir as mybir
from concourse.bass_utils import run_bass_kernel_spmd
from contextlib import ExitStack

F32 = mybir.dt.float32
BF16 = mybir.dt.bfloat16
AF = mybir.ActivationFunctionType
ALU = mybir.AluOpType
AX = mybir.AxisListType

ENGS = ("sync", "scalar", "vector", "gpsimd", "tensor")
D = 1024
S = 8192
NT = 64
NX = 18
NCOLS = NX * 128
DFF = 2816
NCT = 22
EPS = 1e-6
AB_TILES = NT


class _Stop(Exception):
    pass


class Prog:
    def __init__(self, nc, stack):
        self.nc = nc
        self.stack = stack
        self.sem = {}
        self.cnt = {}
        self.waited = {e: {} for e in ENGS}
        self.lastw = {}
        self.readers = {}
        self.ninst = 0
        self.nwait = 0

    def getsem(self, name):
        if name not in self.sem:
            self.sem[name] = self.stack.enter_context(self.nc.semaphore(name))
            self.cnt[name] = 0
        return self.sem[name]

    def _wait(self, eng, tok):
        s, v = tok
        if self.waited[eng].get(s, 0) >= v:
            return
        self.waited[eng][s] = v
        getattr(self.nc, eng).wait_ge(self.sem[s], v)
        self.nwait += 1

    def op(self, eng, fn, reads=(), writes=(), dma=None, inc=True):
        deps = []
        for r in reads:
            if r in self.lastw:
                deps.append(self.lastw[r])
        for w in writes:
            if w in self.lastw:
                deps.append(self.lastw[w])
            for s, v in self.readers.get(w, {}).items():
                deps.append((s, v))
        for t in deps:
            if eng == "tensor" and t[0] == "e_tensor":
                continue
            self._wait(eng, t)
        e = getattr(self.nc, eng)
        tok = None
        if dma is not None:
            self.getsem(dma)
            self.cnt[dma] += 16
            tok = (dma, self.cnt[dma])
            fn(e).then_inc(self.sem[dma], 16)
        elif inc:
            sn = "e_" + eng
            self.getsem(sn)
            self.cnt[sn] += 1
            tok = (sn, self.cnt[sn])
            fn(e).then_inc(self.sem[sn], 1)
        else:
            fn(e)
        self.ninst += 1
        if tok is not None:
            for r in reads:
                d = self.readers.setdefault(r, {})
                d[tok[0]] = max(d.get(tok[0], 0), tok[1])
            for w in writes:
                self.lastw[w] = tok
                self.readers[w] = {}
        return tok

    def barrier(self):
        for eng in ENGS:
            for s, v in self.cnt.items():
                if v > 0:
                    self._wait(eng, (s, v))
        self.lastw = {}
        self.readers = {}


def build_nc(debug_stage=0):
    nc = bass.Bass("TRN2", target_bir_lowering=False)
    di = lambda name, shape: nc.dram_tensor(name, shape, F32, kind="ExternalInput").ap()
    x_rot = di("x_rot", [S, D])
    cos_t = di("cos_t", [128, NT, 64])
    sin_t = di("sin_t", [128, NT, 64])
    cb = di("cb", [128, 8])
    hmask = di("hmask", [128, 2])
    w_ada = di("w_ada", [D, 6 * D])
    b_ada = di("b_ada", [1, 6 * D])
    g1_bd = di("g1_b", [128, D])
    g2_bd = di("g2_b", [128, D])
    gfin_bd = di("gfin_b", [128, D])
    w_in = di("w_in", [D, 1792])
    w_out = di("w_out", [D, D])
    w_up = di("w_up", [D, 2 * DFF])
    w_down = di("w_down", [DFF, D])
    gq_bd = di("gq_b", [128, 64])
    gk_bd = di("gk_b", [128, 64])
    gtv_bd = di("gtv_b", [128, 64])
    wsT_d = di("wsT", [128, 8, 128])
    bsbT_d = di("bsbT", [128, 512])
    gmix_d = di("gmix", [128, 8])
    wconv_d = di("wconv", [128, 3, 44])
    bconv_d = di("bconv", [128, 44])
    ident_d = di("ident", [128, 128])
    out_d = nc.dram_tensor("out", [2048, D], F32, kind="ExternalOutput").ap()
    scr = nc.dram_tensor("scr", [2, 128, D], F32, kind="Internal").ap()
    x1s = nc.dram_tensor("x1s", [2048, D], F32, kind=("ExternalOutput" if debug_stage else "Internal")).ap()
    wub = nc.dram_tensor("wub", [NCT, 128, 8 * 2 * 128], BF16, kind="Internal").ap()
    if debug_stage:
        dbgf = nc.dram_tensor("dbgf", [128, 8192], F32, kind="ExternalOutput").ap()
        dbgb = nc.dram_tensor("dbgb", [128, 65536], BF16, kind="ExternalOutput").ap()

    try:
      with ExitStack() as st0:
        P = Prog(nc, st0)

        def dump(dst, src, res):
            tok = P.op("sync", lambda e: e.dma_start(out=dst, in_=src), reads=[res], dma="d_dbg")
            P._wait("sync", tok)

        def stage_end(k, dumps):
            if debug_stage == k:
                for (dst, src, res) in dumps():
                    dump(dst, src, res)
                print("STOP at stage", k, "inst", P.ninst, "waits", P.nwait, "cnt", P.cnt)
                raise _Stop()

        def SB(st, name, shape, dt):
            return st.enter_context(nc.sbuf_tensor("s_" + name, shape, dt))

        def PS(st, name, shape, dt):
            return st.enter_context(nc.psum_tensor("p_" + name, shape, dt))

        V = lambda fn, r=(), w=(): P.op("vector", fn, r, w)
        A = lambda fn, r=(), w=(): P.op("scalar", fn, r, w)
        G = lambda fn, r=(), w=(): P.op("gpsimd", fn, r, w)
        T = lambda fn, r=(), w=(), inc=True: P.op("tensor", fn, r, w, inc=inc)

        def LD(dst, src, res, sem=None, eng="sync"):
            return P.op(eng, lambda e: e.dma_start(out=dst, in_=src), writes=[res], dma="d_" + res)

        ident_b = SB(st0, "ident_b", [128, 128], BF16)
        ones_f = SB(st0, "ones_f", [128, 128], F32)
        epsc = SB(st0, "epsc", [128, 1], F32)
        G1_b = SB(st0, "G1_b", [128, D], F32)
        gt1_b = SB(st0, "gt1_b", [128, D], F32)
        bias_q = SB(st0, "bias_q", [128, 512], F32)
        bias_kv = SB(st0, "bias_kv", [128, 256], F32)
        bias_zv = SB(st0, "bias_zv", [128, 512], F32)
        biasU = SB(st0, "biasU", [128, 4], F32)
        sh1T = SB(st0, "sh1T", [128, 8], BF16)
        sh2T = SB(st0, "sh2T", [128, 8], BF16)
        gq_b = SB(st0, "gq_b", [128, 64], F32)
        gk_b = SB(st0, "gk_b", [128, 64], F32)
        gtv_b = SB(st0, "gtv_b", [128, 64], F32)
        gmix = SB(st0, "gmix", [128, 8], F32)
        wconv = SB(st0, "wconv", [128, 3, 44], F32)
        bconv = SB(st0, "bconv", [128, 44], F32)
        bias2 = SB(st0, "bias2", [128, 44], F32)
        hm = SB(st0, "hm", [128, 2], F32)
        ss = SB(st0, "ss", [128, NT], F32)
        lnv = SB(st0, "lnv", [128, NT], F32)
        rstd = SB(st0, "rstd", [128, NT], F32)
        sstok = SB(st0, "sstok", [128, NX], F32)
        ssatt = SB(st0, "ssatt", [128, 20], F32)
        rs_t = SB(st0, "rs_t", [128, NX], F32)
        rs_a = SB(st0, "rs_a", [128, 20], F32)
        ss2 = SB(st0, "ss2", [128, NX], F32)
        rs2 = SB(st0, "rs2", [128, NX], F32)
        ssf = SB(st0, "ssf", [128, 16], F32)
        rsf = SB(st0, "rsf", [128, 16], F32)
        sm = SB(st0, "sm", [128, 16], F32)

        LD(ident_b[:], ident_d, "ident_b", "d_c0", eng="gpsimd")
        V(lambda e: e.memset(ones_f[:], 1.0), w=["ones_f"])
        V(lambda e: e.memset(epsc[:], EPS), w=["epsc"])
        for nm, tl, src in [("gq_b", gq_b, gq_bd), ("gk_b", gk_b, gk_bd), ("gtv_b", gtv_b, gtv_bd),
                            ("gmix", gmix, gmix_d), ("wconv", wconv, wconv_d), ("bconv", bconv, bconv_d),
                            ("hm", hm, hmask)]:
            LD(tl[:], src, nm, "d_c1")
        for nm, tl in [("ss", ss), ("sstok", sstok), ("ssatt", ssatt), ("ss2", ss2), ("ssf", ssf), ("rstd", rstd), ("sm", sm)]:
            V(lambda e, tl=tl: e.memset(tl[:], 0.0), w=[nm])

        def rsqrt_cols(dst, src, lo, hi, scale, rn, wn, tmp=None):
            A(lambda e: e.activation(dst[:, lo:hi], src[:, lo:hi], AF.Ln, bias=epsc[:, 0:1], scale=scale),
              r=[rn, "epsc"], w=[wn])
            A(lambda e: e.activation(dst[:, lo:hi], dst[:, lo:hi], AF.Exp, scale=-0.5), r=[wn], w=[wn])

        with ExitStack() as st:
            wada = [SB(st, f"wada{i}", [128, 8, D], BF16) for i in range(2)]
            cbt = SB(st, "cbt", [128, 8], F32)
            scT = SB(st, "scT", [128, 8], BF16)
            brow = SB(st, "brow", [1, 6 * D], F32)
            mrow = [SB(st, f"mrow{i}", [1, D], F32) for i in range(2)]
            gtmp = SB(st, "gtmp", [128, D], F32)
            pMod = PS(st, "pMod", [128, D], F32)
            pB = PS(st, "pB", [128, D], F32)
            pX = PS(st, "pX", [128, 8], F32)
            LD(cbt[:], cb, "cbt", "d_c1")
            LD(brow[:], b_ada, "brow", "d_c1")
            A(lambda e: e.activation(scT[:], cbt[:], AF.Silu), r=["cbt"], w=["scT"])
            for m in range(6):
                wb = wada[m % 2]
                wn = f"wada{m % 2}"
                for kk in range(2):
                    P.op("gpsimd", lambda e, kk=kk, m=m, wb=wb: e.dma_start(
                        out=wb[:, 4 * kk:4 * kk + 4, :],
                        in_=w_ada[512 * kk:512 * kk + 512, m * D:(m + 1) * D].rearrange("(k p) n -> p k n", p=128)),
                        writes=[wn], dma=f"d_wada{m % 2}")
                for c2 in range(2):
                    for k in range(8):
                        T(lambda e, k=k, c2=c2, wb=wb: e.matmul(pMod[0:1, c2 * 512:(c2 + 1) * 512], scT[:, k:k + 1],
                                                                wb[:, k, c2 * 512:(c2 + 1) * 512],
                                                                start=(k == 0), stop=(k == 7)),
                          r=[wn, "scT"], w=["pMod"], inc=(k == 7))
                mr = mrow[m % 2]
                mn = f"mrow{m % 2}"
                V(lambda e, m=m, mr=mr: e.tensor_tensor(mr[:], pMod[0:1, :], brow[0:1, m * D:(m + 1) * D], ALU.add),
                  r=["pMod", "brow"], w=[mn])
                if m in (0, 3):
                    for k in range(8):
                        T(lambda e, k=k, mr=mr: e.matmul(pX[:, k:k + 1], mr[0:1, k * 128:(k + 1) * 128],
                                                         ones_f[0:1, 0:1], start=True, stop=True),
                          r=[mn, "ones_f"], w=["pX"], inc=(k == 7))
                    dst, dn = (sh1T, "sh1T") if m == 0 else (sh2T, "sh2T")
                    V(lambda e, dst=dst: e.tensor_copy(dst[:], pX[:]), r=["pX"], w=[dn])
                else:
                    for c2 in range(2):
                        T(lambda e, c2=c2, mr=mr: e.matmul(pB[:, c2 * 512:(c2 + 1) * 512], ones_f[0:1, :],
                                                           mr[0:1, c2 * 512:(c2 + 1) * 512], start=True, stop=True),
                          r=[mn, "ones_f"], w=["pB"], inc=(c2 == 1))
                    if m in (1, 4):
                        LD(gtmp[:], g1_bd if m == 1 else g2_bd, "gtmp", "d_c2")
                        if m == 1:
                            V(lambda e: e.scalar_tensor_tensor(G1_b[:], pB[:], 1.0, gtmp[:], ALU.add, ALU.mult),
                              r=["pB", "gtmp"], w=["G1_b"])
                        else:
                            V(lambda e: e.scalar_tensor_tensor(gtmp[:], pB[:], 1.0, gtmp[:], ALU.add, ALU.mult),
                              r=["pB", "gtmp"], w=["gtmp"])
                            P.op("sync", lambda e: e.dma_start(out=scr[0], in_=gtmp[:]), reads=["gtmp"], dma="d_scr")
                    elif m == 2:
                        V(lambda e: e.tensor_copy(gt1_b[:], pB[:]), r=["pB"], w=["gt1_b"])
                    else:
                        V(lambda e: e.tensor_copy(gtmp[:], pB[:]), r=["pB"], w=["gtmp"])
                        P.op("sync", lambda e: e.dma_start(out=scr[1], in_=gtmp[:]), reads=["gtmp"], dma="d_scr")
            P.barrier()
            stage_end(1, lambda: [(dbgf[:, 0:1024], G1_b[:], "G1_b"), (dbgf[:, 1024:2048], gt1_b[:], "gt1_b"), (dbgb[:, 0:8], sh1T[:], "sh1T"), (dbgb[:, 8:16], sh2T[:], "sh2T")])

        with ExitStack() as stAR, ExitStack() as stA:
            AR1 = SB(stAR, "AR1", [128, S + NT * 2 * 129], BF16)
            KT = AR1[:, 0:S]
            VVf = AR1[:, S:S + NT * 2 * 129]
            VV = VVf.rearrange("p (t g c) -> p t g c", t=NT, g=2)
            VVk = VVf.rearrange("p (t c) -> p t c", t=NT)
            QT = SB(stA, "QT", [128, 4, NCOLS], BF16)
            mixT = SB(stA, "mixT", [128, 8, NCOLS], BF16)
            V(lambda e: e.memset(AR1[:], 0.0), w=["VV", "KT"])
            V(lambda e: e.memset(mixT[:, 0:4, 0:128], 0.0), w=["mixT"])
            V(lambda e: e.memset(mixT[:, 0:4, 2176:2304], 0.0), w=["mixT"])
            if debug_stage:
                V(lambda e: e.memset(QT[:], 0.0), w=["QT"])
            V(lambda e: e.memset(VV[:, :, :, 0:1], 1.0), w=["VV"])
            V(lambda e: e.memset(VV[:, :, :, 128:129], 1.0), w=["VV"])

            with ExitStack() as st:
                Win = SB(st, "Win", [128, 8, 1792], BF16)
                wsT = SB(st, "wsTb", [128, 8, 128], BF16)
                bsbT = SB(st, "bsbT", [128, 512], F32)
                xt = [SB(st, f"xt{i}", [128, D], F32) for i in range(2)]
                junk = SB(st, "junk", [128, D], BF16)
                xn = [SB(st, f"xn{i}", [128, D], BF16) for i in range(2)]
                xnT = [SB(st, f"xnT{i}", [128, 8, 128], BF16) for i in range(2)]
                cst = [SB(st, f"cst{i}", [128, 64], F32) for i in range(2)]
                snt = [SB(st, f"snt{i}", [128, 64], F32) for i in range(2)]
                kvf = SB(st, "kvf", [128, 256], F32)
                ksq = SB(st, "ksq", [128, 128], F32)
                kra = SB(st, "kra", [128, 128], F32)
                krb = SB(st, "krb", [128, 128], F32)
                kbf = SB(st, "kbf", [128, 128], BF16)
                f1 = SB(st, "f1", [128, 512], F32)
                f2 = SB(st, "f2", [128, 512], F32)
                f3 = SB(st, "f3", [128, 512], F32)
                qbf = SB(st, "qbf", [128, 512], BF16)
                vnpad = SB(st, "vnpad", [128, 8, 128], BF16)
                uT = SB(st, "uT", [128, 512], F32)
                tk = SB(st, "tk", [128, 512], F32)
                sqt = SB(st, "sqt", [128, 512], F32)
                pT = [PS(st, f"pT{i}", [128, D], BF16) for i in range(2)]
                pKV = PS(st, "pKV", [128, 512], F32)
                pQ = PS(st, "pQ", [128, 512], F32)
                pZV = PS(st, "pZV", [128, 512], F32)
                pU = PS(st, "pU", [128, 512], F32)
                pTQ = PS(st, "pTQ", [128, D], BF16)
                pM = PS(st, "pM", [128, 512], F32)

                for kk in range(2):
                    P.op("gpsimd", lambda e, kk=kk: e.dma_start(
                        out=Win[:, 4 * kk:4 * kk + 4, :],
                        in_=w_in[512 * kk:512 * kk + 512, :].rearrange("(k p) n -> p k n", p=128)),
                        writes=["Win"], dma="d_win")
                LD(wsT[:], wsT_d, "wsT", "d_c3", eng="gpsimd")
                LD(bsbT[:], bsbT_d, "bsbT", "d_c1")
                V(lambda e: e.memset(vnpad[:], 0.0), w=["vnpad"])

                colgrp = [(0, 512, bias_q, "bias_q", 0), (512, 256, bias_kv, "bias_kv", 512),
                          (1280, 512, bias_zv, "bias_zv", 768)]
                for (c0, n, dst, dn, r0) in colgrp:
                    for k in range(8):
                        T(lambda e, k=k, c0=c0, n=n: e.matmul(pQ[0:1, 0:n], sh1T[:, k:k + 1], Win[:, k, c0:c0 + n],
                                                              start=(k == 0), stop=(k == 7)),
                          r=["Win", "sh1T"], w=["pQ"], inc=(k == 7))
                    V(lambda e, n=n, r0=r0: e.tensor_copy(tk[0:1, 0:n], pQ[0:1, 0:n]), r=["pQ"], w=["tk"])
                    T(lambda e, n=n, r0=r0: e.matmul(pZV[:, 0:n], ones_f[0:1, :], tk[0:1, 0:n],
                                                     start=True, stop=True), r=["tk", "ones_f"], w=["pZV"])
                    V(lambda e, n=n, dst=dst: e.tensor_copy(dst[:, 0:n], pZV[:, 0:n]), r=["pZV"], w=[dn])
                for c4 in range(4):
                    for k in range(8):
                        T(lambda e, k=k, c4=c4: e.matmul(pU[:, c4:c4 + 1], Win[:, k, 768 + c4 * 128:768 + (c4 + 1) * 128],
                                                         sh1T[:, k:k + 1], start=(k == 0), stop=(k == 7)),
                          r=["Win", "sh1T"], w=["pU"], inc=(k == 7 and c4 == 3))
                V(lambda e: e.tensor_copy(biasU[:], pU[:, 0:4]), r=["pU"], w=["biasU"])


                qf = SB(st, "qf", [128, 512], F32)
                zvf = SB(st, "zvf", [128, 512], F32)
                sm18 = SB(st, "sm18", [128, 18], F32)
                V(lambda e: e.memset(sm18[:], 1.0), w=["sm18"])
                print("sbuf remaining at AB:", nc.sbuf_bytes_remaining)

                def v3(a, nh):
                    return a[:, 0:nh * 64].rearrange("p (h d) -> p h d", h=nh)

                def front(t):
                    s = t % 2
                    own = t < NX
                    LD(xt[s][:], x_rot[t * 128:(t + 1) * 128, :], f"xt{s}")
                    LD(cst[s][:], cos_t[:, t, :], f"cst{s}")
                    LD(snt[s][:], sin_t[:, t, :], f"snt{s}")
                    A(lambda e: e.activation(junk[:], xt[s][:], AF.Square, accum_out=ss[:, t:t + 1]),
                      r=[f"xt{s}"], w=["junk", "ss"])
                    rsqrt_cols(rstd, ss, t, t + 1, 1.0 / D, "ss", "rstd")
                    V(lambda e: e.scalar_tensor_tensor(xn[s][:], xt[s][:], rstd[:, t:t + 1], G1_b[:], ALU.mult, ALU.mult),
                      r=[f"xt{s}", "rstd", "G1_b"], w=[f"xn{s}"])
                    for k in range(8):
                        T(lambda e, k=k: e.transpose(pT[s][:, k * 128:(k + 1) * 128], xn[s][:, k * 128:(k + 1) * 128], ident_b[:]),
                          r=[f"xn{s}", "ident_b"], w=[f"pT{s}"], inc=(k == 7))

                def front2(t):
                    s = t % 2
                    own = t < NX
                    A(lambda e: e.copy(xnT[s][:].rearrange("p k n -> p (k n)"), pT[s][:]), r=[f"pT{s}"], w=[f"xnT{s}"])
                    for k in range(8):
                        T(lambda e, k=k: e.matmul(pKV[:, 0:256], xnT[s][:, k, :], Win[:, k, 512:768],
                                                  start=(k == 0), stop=(k == 7)),
                          r=[f"xnT{s}", "Win"], w=["pKV"], inc=(k == 7))
                    if own:
                        for k in range(8):
                            T(lambda e, k=k: e.matmul(pQ[:, :], xnT[s][:, k, :], Win[:, k, 0:512],
                                                      start=(k == 0), stop=(k == 7)),
                              r=[f"xnT{s}", "Win"], w=["pQ"], inc=(k == 7))
                        for k in range(8):
                            T(lambda e, k=k: e.matmul(pZV[:, :], xnT[s][:, k, :], Win[:, k, 1280:1792],
                                                      start=(k == 0), stop=(k == 7)),
                              r=[f"xnT{s}", "Win"], w=["pZV"], inc=(k == 7))
                        for c4 in range(4):
                            for k in range(8):
                                T(lambda e, k=k, c4=c4: e.matmul(pU[:, c4 * 128:(c4 + 1) * 128],
                                                                 Win[:, k, 768 + c4 * 128:768 + (c4 + 1) * 128],
                                                                 xnT[s][:, k, :], start=(k == 0), stop=(k == 7)),
                                  r=[f"xnT{s}", "Win"], w=["pU"], inc=(k == 7 and c4 == 3))

                def mid(t):
                    own = t < NX
                    V(lambda e: e.tensor_tensor(kvf[:], pKV[:, 0:256], bias_kv[:], ALU.add), r=["pKV", "bias_kv"], w=["kvf"])
                    if own:
                        V(lambda e: e.tensor_tensor(zvf[:], pZV[:], bias_zv[:], ALU.add), r=["pZV", "bias_zv"], w=["zvf"])
                        A(lambda e: e.activation(zvf[:], zvf[:], AF.Gelu_apprx_tanh), r=["zvf"], w=["zvf"])
                        V(lambda e: e.tensor_tensor(qf[:], pQ[:], bias_q[:], ALU.add), r=["pQ", "bias_q"], w=["qf"])
                        for c4 in range(4):
                            A(lambda e, c4=c4: e.activation(uT[:, c4 * 128:(c4 + 1) * 128], pU[:, c4 * 128:(c4 + 1) * 128],
                                                            AF.Gelu_apprx_tanh, bias=biasU[:, c4:c4 + 1]),
                              r=["pU", "biasU"], w=["uT"])

                def rope2(src, sname, nh, ra, raname, rb, rbname, dst, dname, cs, csn, sn, snn):
                    V(lambda e: e.tensor_tensor(v3(ra, nh), v3(src, nh), cs[:].unsqueeze(1).to_broadcast([128, nh, 64]), ALU.mult),
                      r=[sname, csn], w=[raname])
                    v5 = lambda a: a[:, 0:nh * 64].rearrange("p (h s t d) -> p h s t d", h=nh, s=2, t=2)
                    sn4 = sn[:].rearrange("p (s t d) -> p s t d", s=2, t=2)
                    for half in range(2):
                        G(lambda e, half=half: e.tensor_tensor(
                            v5(rb)[:, :, :, half, :], v5(src)[:, :, :, 1 - half, :],
                            sn4[:, :, half, :].unsqueeze(1).to_broadcast([128, nh, 2, 16]), ALU.mult),
                          r=[sname, snn], w=[rbname])
                    V(lambda e: e.tensor_tensor(dst[:, 0:nh * 64], ra[:, 0:nh * 64], rb[:, 0:nh * 64], ALU.add),
                      r=[raname, rbname], w=[dname])

                def back_a(t):
                    s = t % 2
                    own = t < NX
                    G(lambda e: e.tensor_copy(VV[:, t, :, 64:128], kvf[:, 128:256].rearrange("p (g d) -> p g d", g=2)),
                      r=["kvf"], w=["VV"])
                    G(lambda e: e.tensor_tensor(ksq[:], kvf[:, 0:128], kvf[:, 0:128], ALU.mult), r=["kvf"], w=["ksq"])
                    V(lambda e: e.tensor_reduce(sm18[:, 0:2], v3(ksq, 2), axis=AX.X, op=ALU.add), r=["ksq"], w=["sm18"])
                    if own:
                        G(lambda e: e.tensor_tensor(f3[:], qf[:], qf[:], ALU.mult), r=["qf"], w=["f3"])
                        V(lambda e: e.tensor_reduce(sm18[:, 2:10], v3(f3, 8), axis=AX.X, op=ALU.add), r=["f3"], w=["sm18"])
                        G(lambda e: e.tensor_tensor(f1[:], zvf[:], zvf[:], ALU.mult), r=["zvf"], w=["f1"])
                        V(lambda e: e.tensor_reduce(sm18[:, 10:18], v3(f1, 8), axis=AX.X, op=ALU.add), r=["f1"], w=["sm18"])
                    nc_ = 18 if own else 2
                    rsqrt_cols(sm18, sm18, 0, nc_, 1.0 / 64, "sm18", "sm18")

                def back_b(t):
                    s = t % 2
                    own = t < NX
                    V(lambda e: e.tensor_tensor(v3(kra, 2), v3(kvf, 2), sm18[:, 0:2].unsqueeze(2).to_broadcast([128, 2, 64]), ALU.mult),
                      r=["kvf", "sm18"], w=["kra"])
                    V(lambda e: e.tensor_tensor(v3(kra, 2), v3(kra, 2), gk_b[:].unsqueeze(1).to_broadcast([128, 2, 64]), ALU.mult),
                      r=["kra", "gk_b"], w=["kra"])
                    rope2(kra, "kra", 2, ksq, "ksq", krb, "krb", kbf, "kbf", cst[s], f"cst{s}", snt[s], f"snt{s}")
                    T(lambda e: e.transpose(pTQ[:, 512:640], kbf[:], ident_b[:]), r=["kbf", "ident_b"], w=["pTQk"])
                    kt_copy = lambda: V(lambda e: e.tensor_copy(KT[:, t * 128:(t + 1) * 128], pTQ[:, 512:640]), r=["pTQk"], w=["KT"])
                    if not own:
                        pending.append(kt_copy)
                        return
                    kt_copy()
                    V(lambda e: e.tensor_tensor(v3(f2, 8), v3(qf, 8), sm18[:, 2:10].unsqueeze(2).to_broadcast([128, 8, 64]), ALU.mult),
                      r=["qf", "sm18"], w=["f2"])
                    V(lambda e: e.tensor_tensor(v3(f2, 8), v3(f2, 8), gq_b[:].unsqueeze(1).to_broadcast([128, 8, 64]), ALU.mult),
                      r=["f2", "gq_b"], w=["f2"])
                    rope2(f2, "f2", 8, f3, "f3", f1, "f1", qbf, "qbf", cst[s], f"cst{s}", snt[s], f"snt{s}")
                    for p in range(4):
                        T(lambda e, p=p: e.transpose(pTQ[:, p * 128:(p + 1) * 128], qbf[:, p * 128:(p + 1) * 128], ident_b[:]),
                          r=["qbf", "ident_b"], w=["pTQ"], inc=(p == 3))
                    V(lambda e: e.tensor_tensor(v3(f2, 8), v3(zvf, 8), sm18[:, 10:18].unsqueeze(2).to_broadcast([128, 8, 64]), ALU.mult),
                      r=["zvf", "sm18"], w=["f2"])
                    for sl in range(2):
                        G(lambda e, sl=sl: e.tensor_tensor(
                            vnpad[:].rearrange("p (c s) n -> p c s n", s=2)[:, :, sl, sl * 64:(sl + 1) * 64],
                            f2[:].rearrange("p (c s d) -> p c s d", s=2, d=64)[:, :, sl, :],
                            gtv_b[:].unsqueeze(1).to_broadcast([128, 4, 64]), ALU.mult),
                          r=["f2", "gtv_b"], w=["vnpad"])
                    for c4 in range(4):
                        for sl in range(2):
                            T(lambda e, c4=c4, sl=sl: e.matmul(pM[:, c4 * 128:(c4 + 1) * 128], vnpad[:, 2 * c4 + sl, :],
                                                               wsT[:, 2 * c4 + sl, :], start=(sl == 0), stop=(sl == 1)),
                              r=["vnpad", "wsT"], w=["pM"], inc=(sl == 1 and c4 == 3))
                    V(lambda e: e.tensor_copy(QT[:, :, t * 128:(t + 1) * 128], pTQ[:, 0:512].rearrange("p (a n) -> p a n", a=4)),
                      r=["pTQ"], w=["QT"])
                    V(lambda e: e.tensor_tensor(tk[:], pM[:], bsbT[:], ALU.add), r=["pM", "bsbT"], w=["tk"])
                    V(lambda e: e.tensor_tensor(tk[:], tk[:], uT[:], ALU.mult), r=["tk", "uT"], w=["tk"])
                    G(lambda e: e.tensor_tensor(sqt[:], tk[:], tk[:], ALU.mult), r=["tk"], w=["sqt"])
                    for c4 in range(4):
                        T(lambda e, c4=c4: e.matmul(pM[:, 0:1] if False else pTS[:, 0:1], sqt[:, c4 * 128:(c4 + 1) * 128], ones_f[:, 0:1],
                                                    start=(c4 == 0), stop=(c4 == 3)),
                          r=["sqt", "ones_f"], w=["pTS"], inc=(c4 == 3))
                    V(lambda e: e.tensor_tensor(mixT[:, 4:8, t * 128:(t + 1) * 128],
                                                tk[:].rearrange("p (c n) -> p c n", c=4),
                                                gmix[:, 4:8].unsqueeze(2).to_broadcast([128, 4, 128]), ALU.mult),
                      r=["tk", "gmix"], w=["mixT"])
                    V(lambda e: e.tensor_copy(sstok[:, t:t + 1], pTS[:, 0:1]), r=["pTS"], w=["sstok"])

                pTS = pKV[:, 256:512]
                pending = []
                front(0)
                front2(0)
                mid(0)
                for t in range(AB_TILES):
                    if t + 1 < AB_TILES:
                        front(t + 1)
                    back_a(t)
                    if t + 1 < AB_TILES:
                        front2(t + 1)
                    back_b(t)
                    if t + 1 < AB_TILES:
                        mid(t + 1)
                    for f_ in pending:
                        f_()
                    pending.clear()

                rsqrt_cols(rs_t, sstok, 0, NX, 1.0 / 512, "sstok", "rs_t")
                P.barrier()
                stage_end(2, lambda: [(dbgb[:, 0:8192], KT, "KT"), (dbgb[:, 8192:17408], QT[:].rearrange("p a n -> p (a n)"), "QT"), (dbgb[:, 17408:26624], mixT[:, 4:8, :].rearrange("p a n -> p (a n)"), "mixT"), (dbgb[:, 26624:43136], VVf, "VV"), (dbgf[:, 0:18], rs_t[:, 0:18], "rs_t"), (dbgf[:, 64:128], rstd[:, :], "rstd"), (dbgf[:, 128:132], biasU[:], "biasU"), (dbgf[:, 1024:1536], bias_q[:], "bias_q"), (dbgf[:, 1536:1792], bias_kv[:], "bias_kv"), (dbgf[:, 2048:2560], bias_zv[:], "bias_zv")])

            with ExitStack() as st:
                Pb = [SB(st, f"Pb{i}", [128, 1024], BF16) for i in range(3)]
                Oa = SB(st, "Oa", [128, 512], F32)
                Ob = SB(st, "Ob", [128, 512], F32)
                rden = SB(st, "rden", [128, 512], F32)
                at = SB(st, "at", [128, 512], F32)
                sq = SB(st, "sq", [128, 512], F32)
                acc = SB(st, "acc", [128, 512], F32)
                pS = [PS(st, f"pS{i}", [128, 1024], F32) for i in range(2)]
                pOa = PS(st, "pOa", [128, 512], F32)
                pOb = PS(st, "pOb", [128, 512], F32)
                pBc = PS(st, "pBc", [128, 512], F32)
                pSA = PS(st, "pSA", [128, 512], F32)
                for ct in range(NCT):
                    for ab in range(2):
                        P.op("gpsimd", lambda e: e.dma_start(
                            out=wub[ct].rearrange("p (k a n) -> p k a n", k=8, a=2)[:, :, ab, :],
                            in_=w_up[:, ab * DFF + ct * 128:ab * DFF + (ct + 1) * 128].rearrange("(k p) n -> p k n", p=128)),
                            writes=[f"wub{ct}_{ab}"], dma="d_wub")
                QH = SB(st, "QH", [128, 4, 2], BF16)
                hs = SB(st, "hs", [128, 2], F32)

                def qk(kt, i, p, q0, qn):
                    b = pS[i % 2]
                    T(lambda e: e.matmul(b[:, 0:qn], KT[0:64, kt * 128:(kt + 1) * 128], QT[0:64, p, q0:q0 + qn],
                                         start=True, stop=True), r=["KT", "QT"], w=[f"pS{i % 2}"], inc=False)
                    T(lambda e: e.matmul(b[:, 512:512 + qn], KT[64:128, kt * 128:(kt + 1) * 128],
                                         QT[64:128, p, q0:q0 + qn], start=True, stop=True),
                      r=["KT", "QT"], w=[f"pS{i % 2}"])

                def ex(kt, i):
                    A(lambda e: e.activation(Pb[i % 3][:], pS[i % 2][:], AF.Exp, scale=0.125),
                      r=[f"pS{i % 2}"], w=[f"Pb{i % 3}"])

                def pv(kt, i, qn):
                    pb = Pb[i % 3]
                    T(lambda e: e.matmul(pOa[:, 0:qn], VVk[:, kt, 64:192], pb[:, 0:qn],
                                         start=(kt == 0), stop=(kt == NT - 1)),
                      r=[f"Pb{i % 3}", "VV"], w=["pOa"], inc=False)
                    T(lambda e: e.matmul(pOb[:, 0:qn], VV[:, kt, 1, 0:128], pb[:, 512:512 + qn],
                                         start=(kt == 0), stop=(kt == NT - 1)),
                      r=[f"Pb{i % 3}", "VV"], w=["pOa", "pOb"])

                def epi_a(qn):
                    V(lambda e: e.tensor_copy(Oa[0:65, 0:qn], pOa[0:65, 0:qn]), r=["pOa"], w=["Oa"])
                    V(lambda e: e.tensor_copy(Ob[:, 0:qn], pOb[:, 0:qn]), r=["pOb"], w=["Ob"])
                    V(lambda e: e.reciprocal(rden[64:65, 0:qn], Oa[64:65, 0:qn]), r=["Oa"], w=["rdenA"])
                    V(lambda e: e.reciprocal(rden[0:1, 0:qn], Ob[0:1, 0:qn]), r=["Ob"], w=["rdenB"])

                def epi_b(qn):
                    T(lambda e: e.matmul(pBc[:, 0:qn], ones_f[64:65, :], rden[64:65, 0:qn], start=True, stop=True),
                      r=["rdenA", "ones_f"], w=["pBc"], inc=False)
                    T(lambda e: e.matmul(pSA[:, 0:qn], ones_f[0:1, :], rden[0:1, 0:qn], start=True, stop=True),
                      r=["rdenB", "ones_f"], w=["pBc", "pSA"])
                    V(lambda e: e.tensor_tensor(at[0:64, 0:qn], Oa[0:64, 0:qn], pBc[0:64, 0:qn], ALU.mult),
                      r=["Oa", "pBc"], w=["at"])
                    V(lambda e: e.tensor_tensor(at[64:128, 0:qn], Ob[64:128, 0:qn], pSA[64:128, 0:qn], ALU.mult),
                      r=["Ob", "pSA"], w=["at"])

                def epi_main(qi, p, q0):
                    epi_b(512)
                    V(lambda e: e.tensor_scalar(mixT[:, p, q0:q0 + 512], at[:, :], gmix[:, p:p + 1], None, ALU.mult),
                      r=["at", "gmix"], w=["mixT"])
                    if p == 0:
                        G(lambda e: e.tensor_tensor(acc[:, :], at[:, :], at[:, :], ALU.mult), r=["at"], w=["acc"])
                    else:
                        G(lambda e: e.tensor_tensor(sq[:, :], at[:, :], at[:, :], ALU.mult), r=["at"], w=["sq"])
                        G(lambda e: e.tensor_tensor(acc[:, :], acc[:, :], sq[:, :], ALU.add), r=["acc", "sq"], w=["acc"])

                def epi_ss(qi):
                    for w_ in range(4):
                        T(lambda e, w_=w_: e.matmul(pSA[:, w_:w_ + 1], acc[:, w_ * 128:(w_ + 1) * 128], ones_f[:, 0:1],
                                                    start=True, stop=True), r=["acc", "ones_f"], w=["pSA"], inc=(w_ == 3))
                    V(lambda e: e.tensor_copy(ssatt[:, 1 + qi * 4:1 + qi * 4 + 4], pSA[:, 0:4]), r=["pSA"], w=["ssatt"])

                V(lambda e: e.tensor_copy(QH[:, :, 0:1], QT[:, :, 127:128]), r=["QT"], w=["QH"])
                V(lambda e: e.tensor_copy(QH[:, :, 1:2], QT[:, :, 2176:2177]), r=["QT"], w=["QH"])
                for g in range(2):
                    for kt in range(NT):
                        T(lambda e: e.matmul(pS[0][:, g * 512 + kt * 8:g * 512 + kt * 8 + 8],
                                             KT[g * 64:(g + 1) * 64, kt * 128:(kt + 1) * 128], QH[g * 64:(g + 1) * 64, :, :],
                                             start=True, stop=True), r=["KT", "QH"], w=["pS0"], inc=(kt == NT - 1))
                ex(0, 0)
                for kt in range(NT):
                    T(lambda e: e.matmul(pOa[0:65, 0:8], VV[:, kt, 0, 64:129], Pb[0][:, kt * 8:kt * 8 + 8],
                                         start=(kt == 0), stop=(kt == NT - 1)), r=["Pb0", "VV"], w=["pOa"], inc=(kt == NT - 1))
                for kt in range(NT):
                    T(lambda e: e.matmul(pOb[:, 0:8], VV[:, kt, 1, 0:128], Pb[0][:, 512 + kt * 8:512 + kt * 8 + 8],
                                         start=(kt == 0), stop=(kt == NT - 1)), r=["Pb0", "VV"], w=["pOb"], inc=(kt == NT - 1))
                epi_a(8)
                epi_b(8)
                at3 = at[:, 0:8].rearrange("p (a t) -> p a t", t=2)
                for tk_, col in ((0, 127), (1, 2176)):
                    V(lambda e: e.tensor_tensor(mixT[:, 0:4, col:col + 1], at3[:, :, tk_:tk_ + 1], gmix[:, 0:4].unsqueeze(2), ALU.mult),
                      r=["at", "gmix"], w=["mixT"])
                V(lambda e: e.tensor_tensor(sq[:, 0:8], at[:, 0:8], at[:, 0:8], ALU.mult), r=["at"], w=["sq"])
                V(lambda e: e.tensor_reduce(hs[:, 0:2], sq[:, 0:8].rearrange("p (a t) -> p t a", t=2), axis=AX.X, op=ALU.add),
                  r=["sq"], w=["hs"])
                V(lambda e: e.memset(acc[:, 0:256], 0.0), w=["acc"])
                V(lambda e: e.tensor_copy(acc[:, 127:129], hs[:, 0:2]), r=["hs"], w=["acc"])
                for w_ in range(2):
                    T(lambda e, w_=w_: e.matmul(pSA[:, w_:w_ + 1], acc[:, w_ * 128:(w_ + 1) * 128], ones_f[:, 0:1],
                                                start=True, stop=True), r=["acc", "ones_f"], w=["pSA"], inc=(w_ == 1))
                V(lambda e: e.tensor_copy(ssatt[:, 0:1], pSA[:, 0:1]), r=["pSA"], w=["ssatt"])
                V(lambda e: e.tensor_copy(ssatt[:, 17:18], pSA[:, 1:2]), r=["pSA"], w=["ssatt"])

                iters = [(qi, p) for qi in range(4) for p in range(4)]
                step = 1
                pend = None
                for (qi, p) in iters:
                    q0 = 128 + qi * 512
                    qk(0, step, p, q0, 512)
                    qk(1, step + 1, p, q0, 512)
                    for kt in range(NT):
                        ex(kt, step + kt)
                        if kt + 2 < NT:
                            qk(kt + 2, step + kt + 2, p, q0, 512)
                        pv(kt, step + kt, 512)
                        if kt == 3 and pend is not None:
                            epi_main(*pend)
                        if kt == 16 and pend is not None:
                            if pend[1] == 3:
                                epi_ss(pend[0])
                            pend = None
                    step += NT
                    epi_a(512)
                    pend = (qi, p, q0)
                epi_main(*pend)
                epi_ss(3)

                rsqrt_cols(rs_a, ssatt, 0, NX, 1.0 / 512, "ssatt", "rs_a")
                P.barrier()
                stage_end(3, lambda: [(dbgb[:, 0:9216], mixT[:, 0:4, :].rearrange("p a n -> p (a n)"), "mixT"), (dbgf[:, 0:18], rs_a[:, 0:18], "rs_a")])

            x1nT = AR1[:, 0:8 * NCOLS].rearrange("p (k n) -> p k n", k=8)
            with ExitStack() as st2:
                G2_b = SB(st2, "G2_b", [128, D], F32)
                Wout = SB(st2, "Wout", [128, 8, D], BF16)
                xe = [SB(st2, f"xe{i}", [128, D], F32) for i in range(2)]
                x1h = [SB(st2, f"x1h{i}", [128, D], F32) for i in range(2)]
                t1 = SB(st2, "t1", [128, D], F32)
                x1n = [SB(st2, f"x1n{i}", [128, D], BF16) for i in range(2)]
                junk2 = SB(st2, "junk2", [128, D], BF16)
                pYa = PS(st2, "pYa", [128, D], F32)
                pYb = PS(st2, "pYb", [128, D], F32)
                pT2 = [PS(st2, f"pT2{i}", [128, D], BF16) for i in range(2)]
                LD(G2_b[:], scr[0], "G2_b", "d_c2")
                for kk in range(2):
                    P.op("gpsimd", lambda e, kk=kk: e.dma_start(
                        out=Wout[:, 4 * kk:4 * kk + 4, :],
                        in_=w_out[512 * kk:512 * kk + 512, :].rearrange("(k p) n -> p k n", p=128)),
                        writes=["Wout"], dma="d_wout")
                def out_mm(t):
                    s = t % 2
                    LD(xe[s][:], x_rot[t * 128:(t + 1) * 128, :], f"xe{s}")
                    for c2 in range(2):
                        for k in range(4):
                            T(lambda e, k=k, c2=c2: e.matmul(pYa[:, c2 * 512:(c2 + 1) * 512], mixT[:, k, t * 128:(t + 1) * 128],
                                                             Wout[:, k, c2 * 512:(c2 + 1) * 512], start=(k == 0), stop=(k == 3)),
                              r=["mixT", "Wout"], w=["pYa"], inc=(k == 3 and c2 == 1))
                    for c2 in range(2):
                        for k in range(4, 8):
                            T(lambda e, k=k, c2=c2: e.matmul(pYb[:, c2 * 512:(c2 + 1) * 512], mixT[:, k, t * 128:(t + 1) * 128],
                                                             Wout[:, k, c2 * 512:(c2 + 1) * 512], start=(k == 4), stop=(k == 7)),
                              r=["mixT", "Wout"], w=["pYb"], inc=(k == 7 and c2 == 1))

                def out_ep1(t):
                    s = t % 2
                    xd = x1h[s][:]
                    xdn = f"x1h{s}"
                    A(lambda e: e.activation(t1[:], pYa[:], AF.Identity, scale=rs_a[:, t:t + 1]), r=["pYa", "rs_a"], w=["t1"])
                    V(lambda e: e.scalar_tensor_tensor(t1[:], pYb[:], rs_t[:, t:t + 1], t1[:], ALU.mult, ALU.add),
                      r=["pYb", "rs_t", "t1"], w=["t1"])

                def out_ep2(t):
                    s = t % 2
                    xd = x1h[s][:]
                    xdn = f"x1h{s}"
                    V(lambda e: e.tensor_tensor(t1[:], t1[:], gt1_b[:], ALU.mult), r=["t1", "gt1_b"], w=["t1"])
                    V(lambda e: e.tensor_tensor(xd, t1[:], xe[s][:], ALU.add), r=["t1", f"xe{s}"], w=[xdn])
                    if 1 <= t <= 16:
                        P.op("sync", lambda e: e.dma_start(out=x1s[(t - 1) * 128:t * 128, :], in_=xd), reads=[xdn], dma=f"d_x1s{s}")
                    A(lambda e: e.activation(junk2[:], xd, AF.Square, accum_out=ss2[:, t:t + 1]), r=[xdn], w=["junk2", "ss2"])
                    rsqrt_cols(rs2, ss2, t, t + 1, 1.0 / D, "ss2", "rs2")
                    V(lambda e: e.scalar_tensor_tensor(x1n[s][:], xd, rs2[:, t:t + 1], G2_b[:], ALU.mult, ALU.mult),
                      r=[xdn, "rs2", "G2_b"], w=[f"x1n{s}"])
                    for k in range(8):
                        T(lambda e, k=k: e.transpose(pT2[s][:, k * 128:(k + 1) * 128], x1n[s][:, k * 128:(k + 1) * 128], ident_b[:]),
                          r=[f"x1n{s}", "ident_b"], w=[f"pT2{s}"], inc=(k == 7))
                    V(lambda e: e.tensor_copy(x1nT[:, :, t * 128:(t + 1) * 128], pT2[s][:].rearrange("p (k n) -> p k n", k=8)),
                      r=[f"pT2{s}"], w=["x1nT"])

                out_mm(0)
                for t in range(NX):
                    out_ep1(t)
                    if t + 1 < NX:
                        out_mm(t + 1)
                    out_ep2(t)
                P.barrier()
                stage_end(4, lambda: [(dbgb[:, 0:18432], AR1[:, 0:18432], "x1nT")])
                print("sbuf remaining at OUT:", nc.sbuf_bytes_remaining)

            stA.close()
            with ExitStack() as st2:
                NWU = 2
                GW = 2
                groups = [(c, min(GW, NCT - c)) for c in range(0, NCT, GW)]
                NG = len(groups)
                gt2_b = SB(st2, "gt2_b", [128, D], F32)
                gfin_b = SB(st2, "gfin_b", [128, D], F32)
                gT = SB(st2, "gT", [128, NCT, 512], BF16)
                tail0 = 8 * NCOLS
                wu = [AR1[:, tail0:tail0 + 8 * 2 * GW * 128].rearrange("p (c f) -> p c f", c=GW),
                      SB(st2, "wu1", [128, GW, 8 * 2 * 128], BF16)]
                Wd = SB(st2, "Wd", [128, NCT, D], BF16)
                zb = [SB(st2, f"zb{i}", [128, 2, 514], F32) for i in range(2)]
                cv = [SB(st2, f"cv{i}", [128, 2, 512], F32) for i in range(2)]
                sl_ = [SB(st2, f"sl{i}", [128, 512], F32) for i in range(2)]
                y2q = SB(st2, "y2q", [128, 4, D], F32)
                xr = [SB(st2, f"xr{i}", [128, D], F32) for i in range(2)]
                junk3 = AR1[:, tail0 + 8 * 2 * GW * 128:tail0 + 8 * 2 * GW * 128 + D]
                ot = [SB(st2, f"ot{i}", [128, D], F32) for i in range(2)]
                pZ = [PS(st2, f"pZ{i}", [128, 512], F32) for i in range(4)]
                pY = [PS(st2, f"pY{i}", [128, 512], F32) for i in range(4)]
                LD(gt2_b[:], scr[1], "gt2_b")
                LD(gfin_b[:], gfin_bd, "gfin_b")
                print("sbuf remaining at FFN:", nc.sbuf_bytes_remaining)

                def ld_wu(gg):
                    c0_, n_ = groups[gg % NG]
                    s = gg % NWU
                    P.op("sync", lambda e: e.dma_start(out=wu[s][:, 0:n_, :], in_=wub[c0_:c0_ + n_].rearrange("c p f -> p c f")),
                         writes=[f"wu{s}"], dma=f"d_wu{s}")

                def wsl(qtr, ct, k, ab):
                    gi = ct // GW
                    s = (qtr * NG + gi) % NWU
                    j = ct - groups[gi][0]
                    return wu[s][:, j, :].rearrange("p (k a n) -> p k a n", k=8, a=2)[:, k, ab, :], f"wu{s}"

                def ld_wd(ct):
                    P.op("gpsimd", lambda e: e.dma_start(out=Wd[:, ct, :], in_=w_down[ct * 128:(ct + 1) * 128, :]),
                         writes=[f"Wd{ct}"], dma=f"d_Wd{ct}")

                def up_tail(qtr, ct):
                    ci = ct % 2
                    A(lambda e: e.activation(sl_[ci][:], cv[ci][:, 0, :], AF.Silu), r=[f"cv{ci}a"], w=[f"sl{ci}"])
                    V(lambda e: e.tensor_tensor(gT[:, ct, :], sl_[ci][:], cv[ci][:, 1, :], ALU.mult),
                      r=[f"sl{ci}", f"cv{ci}b"], w=["gT"])

                def down_mm(ct, c2):
                    for tt in range(4):
                        T(lambda e, tt=tt: e.matmul(pY[tt][:, :], gT[:, ct, tt * 128:(tt + 1) * 128],
                                                    Wd[:, ct, c2 * 512:(c2 + 1) * 512],
                                                    start=(ct == 0), stop=(ct == NCT - 1)),
                          r=["gT", f"Wd{ct}"], w=[f"pY{tt}"], inc=(ct == NCT - 1 or tt == 3))

                def bias2_for(ct):
                    pb = pY[ct % 4]
                    for ab in range(2):
                        for k in range(8):
                            wl, wn = wsl(0, ct, k, ab)
                            T(lambda e, k=k, ab=ab, wl=wl: e.matmul(pb[:, ab:ab + 1], wl, sh2T[:, k:k + 1], start=(k == 0), stop=(k == 7)),
                              r=[wn, "sh2T"], w=[f"pY{ct % 4}"], inc=(k == 7 and ab == 1))
                    A(lambda e: e.copy(bias2[:, ct:ct + 1], pb[:, 0:1]), r=[f"pY{ct % 4}"], w=["bias2"])
                    A(lambda e: e.copy(bias2[:, 22 + ct:23 + ct], pb[:, 1:2]), r=[f"pY{ct % 4}"], w=["bias2"])

                def fin_tail(qtr, tt):
                    ti = qtr * 4 + tt
                    os_ = ti % 2
                    V(lambda e: e.tensor_tensor(y2q[:, tt, :], y2q[:, tt, :], xr[tt % 2][:], ALU.add),
                      r=[f"y2q{tt}", f"xr{tt % 2}"], w=[f"y2q{tt}"])
                    if tt < 2:
                        LD(xr[tt % 2][:], x1s[(ti + 2) * 128:(ti + 3) * 128, :], f"xr{tt % 2}")
                    A(lambda e: e.activation(junk3, y2q[:, tt, :], AF.Square, accum_out=ssf[:, ti:ti + 1]),
                      r=[f"y2q{tt}"], w=["junk3", "ssf"])
                    rsqrt_cols(rsf, ssf, ti, ti + 1, 1.0 / D, "ssf", "rsf")
                    V(lambda e: e.scalar_tensor_tensor(ot[os_][:], y2q[:, tt, :], rsf[:, ti:ti + 1], gfin_b[:],
                                                       ALU.mult, ALU.mult),
                      r=[f"y2q{tt}", "rsf", "gfin_b"], w=[f"ot{os_}"])
                    P.op("sync", lambda e: e.dma_start(out=out_d[ti * 128:(ti + 1) * 128, :], in_=ot[os_][:]),
                         reads=[f"ot{os_}"], dma=f"d_out{os_}")

                ld_wu(0)
                for qtr in range(4):
                    c0 = 127 + qtr * 512
                    for ct in range(NCT):
                        zi = ct % 2
                        ci = ct % 2
                        if ct % GW == 0:
                            gg = qtr * NG + ct // GW
                            if gg + 1 < 4 * NG:
                                ld_wu(gg + 1)
                        if qtr == 0 and ct == 0:
                            bias2_for(0)
                        for ab in range(2):
                            for blk in range(2):
                                pz = pZ[ab * 2 + blk]
                                for k in range(8):
                                    wl, wn = wsl(qtr, ct, k, ab)
                                    T(lambda e, k=k, wl=wl: e.matmul(
                                        pz[:, 0:257], wl,
                                        x1nT[:, k, c0 + blk * 257:c0 + (blk + 1) * 257], start=(k == 0), stop=(k == 7)),
                                      r=[wn, "x1nT"], w=[f"pZ{ab * 2 + blk}"], inc=(k == 7))
                                A(lambda e: e.activation(
                                    zb[zi][:, ab, blk * 257:(blk + 1) * 257], pz[:, 0:257], AF.Identity,
                                    bias=bias2[:, ab * 22 + ct:ab * 22 + ct + 1]),
                                  r=[f"pZ{ab * 2 + blk}", "bias2"], w=[f"zb{zi}"])
                        if qtr >= 1 and ct >= 2:
                            down_mm(ct - 2, 0)
                        if qtr == 0 and ct + 1 < NCT:
                            bias2_for(ct + 1)
                        if qtr == 0:
                            ld_wd(ct)
                        if qtr > 0 and ct < 4:
                            fin_tail(qtr - 1, ct)
                        if ct == 4:
                            for tt in range(2):
                                LD(xr[tt][:], x1s[(qtr * 4 + tt) * 128:(qtr * 4 + tt + 1) * 128, :], f"xr{tt}")
                        if qtr == 0:
                            V(lambda e: e.tensor_scalar(zb[zi][:, :, 0:1], zb[zi][:, :, 0:1], hm[:, 0:1], None, ALU.mult),
                              r=[f"zb{zi}", "hm"], w=[f"zb{zi}"])
                        if qtr == 3:
                            V(lambda e: e.tensor_scalar(zb[zi][:, :, 513:514], zb[zi][:, :, 513:514], hm[:, 1:2], None, ALU.mult),
                              r=[f"zb{zi}", "hm"], w=[f"zb{zi}"])
                        for ab in range(2):
                            ch = ab * 22 + ct
                            if ab == 0:
                                A(lambda e: e.activation(cv[ci][:, ab, :], zb[zi][:, ab, 1:513], AF.Identity,
                                                         bias=bconv[:, ch:ch + 1], scale=wconv[:, 1, ch:ch + 1]),
                                  r=[f"zb{zi}", "wconv", "bconv"], w=[f"cv{ci}a"])
                            else:
                                G(lambda e: e.tensor_scalar(cv[ci][:, ab, :], zb[zi][:, ab, 1:513], wconv[:, 1, ch:ch + 1],
                                                            bconv[:, ch:ch + 1], ALU.mult, ALU.add),
                                  r=[f"zb{zi}", "wconv", "bconv"], w=[f"cv{ci}b"])
                        for ab in range(2):
                            ch = ab * 22 + ct
                            cvn = f"cv{ci}" + "ab"[ab]
                            V(lambda e: e.scalar_tensor_tensor(cv[ci][:, ab, :], zb[zi][:, ab, 0:512], wconv[:, 0, ch:ch + 1],
                                                               cv[ci][:, ab, :], ALU.mult, ALU.add),
                              r=[f"zb{zi}", "wconv", cvn], w=[cvn])
                            V(lambda e: e.scalar_tensor_tensor(cv[ci][:, ab, :], zb[zi][:, ab, 2:514], wconv[:, 2, ch:ch + 1],
                                                               cv[ci][:, ab, :], ALU.mult, ALU.add),
                              r=[f"zb{zi}", "wconv", cvn], w=[cvn])
                        if ct > 0:
                            up_tail(qtr, ct - 1)
                    up_tail(qtr, NCT - 1)
                    if qtr >= 1:
                        down_mm(NCT - 2, 0)
                        down_mm(NCT - 1, 0)
                    for c2 in range(2):
                        if not (qtr >= 1 and c2 == 0):
                            for ct in range(NCT):
                                down_mm(ct, c2)
                        for tt in range(4):
                            V(lambda e, tt=tt: e.tensor_tensor(y2q[:, tt, c2 * 512:(c2 + 1) * 512], pY[tt][:, :],
                                                               gt2_b[:, c2 * 512:(c2 + 1) * 512], ALU.mult),
                              r=[f"pY{tt}", "gt2_b"], w=[f"y2q{tt}"])
                    if qtr == 3:
                        for tt in range(4):
                            fin_tail(3, tt)
                P.barrier()
        print("kernel build: inst", P.ninst, "waits", P.nwait, "sems", len(P.sem), P.cnt)
    except _Stop:
        pass
    return nc


def _prep_inputs(inp):
    x = np.asarray(inp["x"], np.float32)
    c = np.asarray(inp["c"], np.float32)
    f = lambda k: np.asarray(inp[k], np.float32)
    rows = S // 64
    row_id = np.repeat(np.arange(rows), 64).astype(np.float32)
    col_id = np.tile(np.arange(64), rows).astype(np.float32)
    inv_freq = np.power(np.float32(10000.0), -np.arange(0, 32, 2, dtype=np.float32) / np.float32(32)).astype(np.float32)
    ang_r = row_id[:, None] * inv_freq[None, :]
    ang_c = col_id[:, None] * inv_freq[None, :]
    ang = np.concatenate([ang_r, ang_r, ang_c, ang_c], axis=-1).astype(np.float32)
    cos = np.cos(ang).astype(np.float32)
    sin = np.sin(ang).astype(np.float32)
    sgn = np.concatenate([-np.ones(16), np.ones(16), -np.ones(16), np.ones(16)]).astype(np.float32)
    sinS = sin * sgn[None, :]
    perm_h = [0, 4, 1, 5, 2, 6, 3, 7]
    qcols = np.concatenate([np.arange(h * 64, (h + 1) * 64) for h in perm_h])
    w_in = f("w_in")[0].copy()
    w_in[:, 0:512] = w_in[:, qcols]
    w_out = f("w_out")[0].copy()
    w_out[0:512, :] = w_out[qcols, :]
    g_attn = f("g_attn_out")[0][qcols]
    gmixv = np.concatenate([g_attn, f("g_tok_out")[0]])
    gmix = np.ascontiguousarray(gmixv.reshape(8, 128).T)
    rep = lambda v, n=128: np.ascontiguousarray(np.broadcast_to(v[None, :], (n, v.shape[0])))
    wsT = np.ascontiguousarray(f("w_s")[0].transpose(2, 0, 1))
    b_s = f("b_s")[0]
    bsbT = np.zeros((128, 4, 128), np.float32)
    for c4 in range(4):
        bsbT[0:64, c4, :] = b_s[2 * c4][None, :]
        bsbT[64:128, c4, :] = b_s[2 * c4 + 1][None, :]
    wconv = np.ascontiguousarray(f("w_conv")[0].reshape(3, 44, 128).transpose(2, 0, 1))
    bconv = np.ascontiguousarray(f("b_conv")[0].reshape(44, 128).T)
    shared = {
        "w_ada": f("w_ada")[0], "b_ada": f("b_ada"), "g1_b": rep(f("g_norm1")[0]), "g2_b": rep(f("g_norm2")[0]),
        "gfin_b": rep(f("g_final")), "w_in": w_in, "w_out": w_out, "w_up": f("w_up")[0], "w_down": f("w_down")[0],
        "gq_b": rep(f("g_q")[0]), "gk_b": rep(f("g_k")[0]), "gtv_b": rep(f("g_tok_v")[0]), "wsT": wsT,
        "bsbT": bsbT.reshape(128, 512), "gmix": gmix, "wconv": wconv, "bconv": bconv,
        "ident": np.eye(128, dtype=np.float32),
    }
    in_maps = []
    for core in range(8):
        b, j = core // 4, core % 4
        sh = j * 2048 - 128
        xr = np.roll(x[b], -sh, axis=0)
        cr = np.roll(cos, -sh, axis=0).reshape(NT, 128, 64).transpose(1, 0, 2)
        sr = np.roll(sinS, -sh, axis=0).reshape(NT, 128, 64).transpose(1, 0, 2)
        hmk = np.ones((128, 2), np.float32)
        if j == 0:
            hmk[:, 0] = 0.0
        if j == 3:
            hmk[:, 1] = 0.0
        m = dict(shared)
        m.update({"x_rot": np.ascontiguousarray(xr), "cos_t": np.ascontiguousarray(cr), "sin_t": np.ascontiguousarray(sr),
                  "cb": np.ascontiguousarray(c[b].reshape(8, 128).T), "hmask": hmk})
        in_maps.append(m)
    return in_maps


_NC_CACHE = {}


def kernel(**inputs):
    in_maps = _prep_inputs(inputs)
    if "nc" not in _NC_CACHE:
        _NC_CACHE["nc"] = build_nc()
    nc = _NC_CACHE["nc"]
    res = run_bass_kernel_spmd(nc, in_maps, core_ids=list(range(8)))
    out = np.zeros((2, S, D), np.float32)
    for core in range(8):
        b, j = core // 4, core % 4
        out[b, j * 2048:(j + 1) * 2048, :] = res.results[core]["out"]
    return out
```

```python
import numpy as np
import concourse.bass as bass
import concourse.mybir as mybir
from concourse.bass_utils import run_bass_kernel_spmd
from contextlib import ExitStack

F32 = mybir.dt.float32
BF16 = mybir.dt.bfloat16
AF = mybir.ActivationFunctionType
ALU = mybir.AluOpType
AX = mybir.AxisListType

ENGS = ("sync", "scalar", "vector", "gpsimd", "tensor")
D = 1024
S = 8192
NT = 64
NX = 18
NCOLS = NX * 128
DFF = 2816
NCT = 22
EPS = 1e-6
AB_TILES = NT


class _Stop(Exception):
    pass


class Prog:
    def __init__(self, nc, stack):
        self.nc = nc
        self.stack = stack
        self.sem = {}
        self.cnt = {}
        self.waited = {e: {} for e in ENGS}
        self.lastw = {}
        self.readers = {}
        self.ninst = 0
        self.nwait = 0

    def getsem(self, name):
        if name not in self.sem:
            self.sem[name] = self.stack.enter_context(self.nc.semaphore(name))
            self.cnt[name] = 0
        return self.sem[name]

    def _wait(self, eng, tok):
        s, v = tok
        if self.waited[eng].get(s, 0) >= v:
            return
        self.waited[eng][s] = v
        getattr(self.nc, eng).wait_ge(self.sem[s], v)
        self.nwait += 1

    def op(self, eng, fn, reads=(), writes=(), dma=None, inc=True):
        deps = []
        for r in reads:
            if r in self.lastw:
                deps.append(self.lastw[r])
        for w in writes:
            if w in self.lastw:
                deps.append(self.lastw[w])
            for s, v in self.readers.get(w, {}).items():
                deps.append((s, v))
        for t in deps:
            if eng == "tensor" and t[0] == "e_tensor":
                continue
            self._wait(eng, t)
        e = getattr(self.nc, eng)
        tok = None
        if dma is not None:
            self.getsem(dma)
            self.cnt[dma] += 16
            tok = (dma, self.cnt[dma])
            fn(e).then_inc(self.sem[dma], 16)
        elif inc:
            sn = "e_" + eng
            self.getsem(sn)
            self.cnt[sn] += 1
            tok = (sn, self.cnt[sn])
            fn(e).then_inc(self.sem[sn], 1)
        else:
            fn(e)
        self.ninst += 1
        if tok is not None:
            for r in reads:
                d = self.readers.setdefault(r, {})
                d[tok[0]] = max(d.get(tok[0], 0), tok[1])
            for w in writes:
                self.lastw[w] = tok
                self.readers[w] = {}
        return tok

    def barrier(self):
        for eng in ENGS:
            for s, v in self.cnt.items():
                if v > 0:
                    self._wait(eng, (s, v))
        self.lastw = {}
        self.readers = {}


def build_nc(debug_stage=0):
    nc = bass.Bass("TRN2", target_bir_lowering=False)
    di = lambda name, shape: nc.dram_tensor(name, shape, F32, kind="ExternalInput").ap()
    x_rot = di("x_rot", [S, D])
    cos_t = di("cos_t", [128, NT, 64])
    sin_t = di("sin_t", [128, NT, 64])
    cb = di("cb", [128, 8])
    hmask = di("hmask", [128, 2])
    w_ada = di("w_ada", [D, 6 * D])
    b_ada = di("b_ada", [1, 6 * D])
    g1_bd = di("g1_b", [128, D])
    g2_bd = di("g2_b", [128, D])
    gfin_bd = di("gfin_b", [128, D])
    w_in = di("w_in", [D, 1792])
    w_out = di("w_out", [D, D])
    w_up = di("w_up", [D, 2 * DFF])
    w_down = di("w_down", [DFF, D])
    gq_bd = di("gq_b", [128, 64])
    gk_bd = di("gk_b", [128, 64])
    gtv_bd = di("gtv_b", [128, 64])
    wsT_d = di("wsT", [128, 8, 128])
    bsbT_d = di("bsbT", [128, 512])
    gmix_d = di("gmix", [128, 8])
    wconv_d = di("wconv", [128, 3, 44])
    bconv_d = di("bconv", [128, 44])
    ident_d = di("ident", [128, 128])
    out_d = nc.dram_tensor("out", [2048, D], F32, kind="ExternalOutput").ap()
    scr = nc.dram_tensor("scr", [2, 128, D], F32, kind="Internal").ap()
    x1s = nc.dram_tensor("x1s", [2048, D], F32, kind=("ExternalOutput" if debug_stage else "Internal")).ap()
    wub = nc.dram_tensor("wub", [NCT, 128, 8 * 2 * 128], BF16, kind="Internal").ap()
    if debug_stage:
        dbgf = nc.dram_tensor("dbgf", [128, 8192], F32, kind="ExternalOutput").ap()
        dbgb = nc.dram_tensor("dbgb", [128, 65536], BF16, kind="ExternalOutput").ap()

    try:
      with ExitStack() as st0:
        P = Prog(nc, st0)

        def dump(dst, src, res):
            tok = P.op("sync", lambda e: e.dma_start(out=dst, in_=src), reads=[res], dma="d_dbg")
            P._wait("sync", tok)

        def stage_end(k, dumps):
            if debug_stage == k:
                for (dst, src, res) in dumps():
                    dump(dst, src, res)
                print("STOP at stage", k, "inst", P.ninst, "waits", P.nwait, "cnt", P.cnt)
                raise _Stop()

        def SB(st, name, shape, dt):
            return st.enter_context(nc.sbuf_tensor("s_" + name, shape, dt))

        def PS(st, name, shape, dt):
            return st.enter_context(nc.psum_tensor("p_" + name, shape, dt))

        V = lambda fn, r=(), w=(): P.op("vector", fn, r, w)
        A = lambda fn, r=(), w=(): P.op("scalar", fn, r, w)
        G = lambda fn, r=(), w=(): P.op("gpsimd", fn, r, w)
        T = lambda fn, r=(), w=(), inc=True: P.op("tensor", fn, r, w, inc=inc)

        def LD(dst, src, res, sem=None, eng="sync"):
            return P.op(eng, lambda e: e.dma_start(out=dst, in_=src), writes=[res], dma="d_" + res)

        ident_b = SB(st0, "ident_b", [128, 128], BF16)
        ones_f = SB(st0, "ones_f", [128, 128], F32)
        epsc = SB(st0, "epsc", [128, 1], F32)
        G1_b = SB(st0, "G1_b", [128, D], F32)
        gt1_b = SB(st0, "gt1_b", [128, D], F32)
        bias_q = SB(st0, "bias_q", [128, 512], F32)
        bias_kv = SB(st0, "bias_kv", [128, 256], F32)
        bias_zv = SB(st0, "bias_zv", [128, 512], F32)
        biasU = SB(st0, "biasU", [128, 4], F32)
        sh1T = SB(st0, "sh1T", [128, 8], BF16)
        sh2T = SB(st0, "sh2T", [128, 8], BF16)
        gq_b = SB(st0, "gq_b", [128, 64], F32)
        gk_b = SB(st0, "gk_b", [128, 64], F32)
        gtv_b = SB(st0, "gtv_b", [128, 64], F32)
        gmix = SB(st0, "gmix", [128, 8], F32)
        wconv = SB(st0, "wconv", [128, 3, 44], F32)
        bconv = SB(st0, "bconv", [128, 44], F32)
        bias2 = SB(st0, "bias2", [128, 44], F32)
        hm = SB(st0, "hm", [128, 2], F32)
        ss = SB(st0, "ss", [128, NT], F32)
        lnv = SB(st0, "lnv", [128, NT], F32)
        rstd = SB(st0, "rstd", [128, NT], F32)
        sstok = SB(st0, "sstok", [128, NX], F32)
        ssatt = SB(st0, "ssatt", [128, 20], F32)
        rs_t = SB(st0, "rs_t", [128, NX], F32)
        rs_a = SB(st0, "rs_a", [128, 20], F32)
        ss2 = SB(st0, "ss2", [128, NX], F32)
        rs2 = SB(st0, "rs2", [128, NX], F32)
        ssf = SB(st0, "ssf", [128, 16], F32)
        rsf = SB(st0, "rsf", [128, 16], F32)
        sm = SB(st0, "sm", [128, 16], F32)

        LD(ident_b[:], ident_d, "ident_b", "d_c0", eng="gpsimd")
        V(lambda e: e.memset(ones_f[:], 1.0), w=["ones_f"])
        V(lambda e: e.memset(epsc[:], EPS), w=["epsc"])
        for nm, tl, src in [("gq_b", gq_b, gq_bd), ("gk_b", gk_b, gk_bd), ("gtv_b", gtv_b, gtv_bd),
                            ("gmix", gmix, gmix_d), ("wconv", wconv, wconv_d), ("bconv", bconv, bconv_d),
                            ("hm", hm, hmask)]:
            LD(tl[:], src, nm, "d_c1")
        for nm, tl in [("ss", ss), ("sstok", sstok), ("ssatt", ssatt), ("ss2", ss2), ("ssf", ssf), ("rstd", rstd), ("sm", sm)]:
            V(lambda e, tl=tl: e.memset(tl[:], 0.0), w=[nm])

        def rsqrt_cols(dst, src, lo, hi, scale, rn, wn, tmp=None):
            A(lambda e: e.activation(dst[:, lo:hi], src[:, lo:hi], AF.Ln, bias=epsc[:, 0:1], scale=scale),
              r=[rn, "epsc"], w=[wn])
            A(lambda e: e.activation(dst[:, lo:hi], dst[:, lo:hi], AF.Exp, scale=-0.5), r=[wn], w=[wn])

        with ExitStack() as st:
            wada = [SB(st, f"wada{i}", [128, 8, D], BF16) for i in range(2)]
            cbt = SB(st, "cbt", [128, 8], F32)
            scT = SB(st, "scT", [128, 8], BF16)
            brow = SB(st, "brow", [1, 6 * D], F32)
            mrow = [SB(st, f"mrow{i}", [1, D], F32) for i in range(2)]
            gtmp = SB(st, "gtmp", [128, D], F32)
            pMod = PS(st, "pMod", [128, D], F32)
            pB = PS(st, "pB", [128, D], F32)
            pX = PS(st, "pX", [128, 8], F32)
            LD(cbt[:], cb, "cbt", "d_c1")
            LD(brow[:], b_ada, "brow", "d_c1")
            A(lambda e: e.activation(scT[:], cbt[:], AF.Silu), r=["cbt"], w=["scT"])
            for m in range(6):
                wb = wada[m % 2]
                wn = f"wada{m % 2}"
                for kk in range(2):
                    P.op("gpsimd", lambda e, kk=kk, m=m, wb=wb: e.dma_start(
                        out=wb[:, 4 * kk:4 * kk + 4, :],
                        in_=w_ada[512 * kk:512 * kk + 512, m * D:(m + 1) * D].rearrange("(k p) n -> p k n", p=128)),
                        writes=[wn], dma=f"d_wada{m % 2}")
                for c2 in range(2):
                    for k in range(8):
                        T(lambda e, k=k, c2=c2, wb=wb: e.matmul(pMod[0:1, c2 * 512:(c2 + 1) * 512], scT[:, k:k + 1],
                                                                wb[:, k, c2 * 512:(c2 + 1) * 512],
                                                                start=(k == 0), stop=(k == 7)),
                          r=[wn, "scT"], w=["pMod"], inc=(k == 7))
                mr = mrow[m % 2]
                mn = f"mrow{m % 2}"
                V(lambda e, m=m, mr=mr: e.tensor_tensor(mr[:], pMod[0:1, :], brow[0:1, m * D:(m + 1) * D], ALU.add),
                  r=["pMod", "brow"], w=[mn])
                if m in (0, 3):
                    for k in range(8):
                        T(lambda e, k=k, mr=mr: e.matmul(pX[:, k:k + 1], mr[0:1, k * 128:(k + 1) * 128],
                                                         ones_f[0:1, 0:1], start=True, stop=True),
                          r=[mn, "ones_f"], w=["pX"], inc=(k == 7))
                    dst, dn = (sh1T, "sh1T") if m == 0 else (sh2T, "sh2T")
                    V(lambda e, dst=dst: e.tensor_copy(dst[:], pX[:]), r=["pX"], w=[dn])
                else:
                    for c2 in range(2):
                        T(lambda e, c2=c2, mr=mr: e.matmul(pB[:, c2 * 512:(c2 + 1) * 512], ones_f[0:1, :],
                                                           mr[0:1, c2 * 512:(c2 + 1) * 512], start=True, stop=True),
                          r=[mn, "ones_f"], w=["pB"], inc=(c2 == 1))
                    if m in (1, 4):
                        LD(gtmp[:], g1_bd if m == 1 else g2_bd, "gtmp", "d_c2")
                        if m == 1:
                            V(lambda e: e.scalar_tensor_tensor(G1_b[:], pB[:], 1.0, gtmp[:], ALU.add, ALU.mult),
                              r=["pB", "gtmp"], w=["G1_b"])
                        else:
                            V(lambda e: e.scalar_tensor_tensor(gtmp[:], pB[:], 1.0, gtmp[:], ALU.add, ALU.mult),
                              r=["pB", "gtmp"], w=["gtmp"])
                            P.op("sync", lambda e: e.dma_start(out=scr[0], in_=gtmp[:]), reads=["gtmp"], dma="d_scr")
                    elif m == 2:
                        V(lambda e: e.tensor_copy(gt1_b[:], pB[:]), r=["pB"], w=["gt1_b"])
                    else:
                        V(lambda e: e.tensor_copy(gtmp[:], pB[:]), r=["pB"], w=["gtmp"])
                        P.op("sync", lambda e: e.dma_start(out=scr[1], in_=gtmp[:]), reads=["gtmp"], dma="d_scr")
            P.barrier()
            stage_end(1, lambda: [(dbgf[:, 0:1024], G1_b[:], "G1_b"), (dbgf[:, 1024:2048], gt1_b[:], "gt1_b"), (dbgb[:, 0:8], sh1T[:], "sh1T"), (dbgb[:, 8:16], sh2T[:], "sh2T")])

        with ExitStack() as stAR, ExitStack() as stA:
            AR1 = SB(stAR, "AR1", [128, S + NT * 2 * 129], BF16)
            KT = AR1[:, 0:S]
            VVf = AR1[:, S:S + NT * 2 * 129]
            VV = VVf.rearrange("p (t g c) -> p t g c", t=NT, g=2)
            VVk = VVf.rearrange("p (t c) -> p t c", t=NT)
            QT = SB(stA, "QT", [128, 4, NCOLS], BF16)
            mixT = SB(stA, "mixT", [128, 8, NCOLS], BF16)
            V(lambda e: e.memset(AR1[:], 0.0), w=["VV", "KT"])
            V(lambda e: e.memset(mixT[:, 0:4, 0:128], 0.0), w=["mixT"])
            V(lambda e: e.memset(mixT[:, 0:4, 2176:2304], 0.0), w=["mixT"])
            if debug_stage:
                V(lambda e: e.memset(QT[:], 0.0), w=["QT"])
            V(lambda e: e.memset(VV[:, :, :, 0:1], 1.0), w=["VV"])
            V(lambda e: e.memset(VV[:, :, :, 128:129], 1.0), w=["VV"])

            with ExitStack() as st:
                Win = SB(st, "Win", [128, 8, 1792], BF16)
                wsT = SB(st, "wsTb", [128, 8, 128], BF16)
                bsbT = SB(st, "bsbT", [128, 512], F32)
                xt = [SB(st, f"xt{i}", [128, D], F32) for i in range(2)]
                junk = SB(st, "junk", [128, D], BF16)
                xn = [SB(st, f"xn{i}", [128, D], BF16) for i in range(2)]
                xnT = [SB(st, f"xnT{i}", [128, 8, 128], BF16) for i in range(2)]
                cst = [SB(st, f"cst{i}", [128, 64], F32) for i in range(2)]
                snt = [SB(st, f"snt{i}", [128, 64], F32) for i in range(2)]
                kvf = SB(st, "kvf", [128, 256], F32)
                ksq = SB(st, "ksq", [128, 128], F32)
                kra = SB(st, "kra", [128, 128], F32)
                krb = SB(st, "krb", [128, 128], F32)
                kbf = SB(st, "kbf", [128, 128], BF16)
                f1 = SB(st, "f1", [128, 512], F32)
                f2 = SB(st, "f2", [128, 512], F32)
                f3 = SB(st, "f3", [128, 512], F32)
                qbf = SB(st, "qbf", [128, 512], BF16)
                vnpad = SB(st, "vnpad", [128, 8, 128], BF16)
                uT = SB(st, "uT", [128, 512], F32)
                tk = SB(st, "tk", [128, 512], F32)
                sqt = SB(st, "sqt", [128, 512], F32)
                pT = [PS(st, f"pT{i}", [128, D], BF16) for i in range(2)]
                pKV = PS(st, "pKV", [128, 512], F32)
                pQ = PS(st, "pQ", [128, 512], F32)
                pZV = PS(st, "pZV", [128, 512], F32)
                pU = PS(st, "pU", [128, 512], F32)
                pTQ = PS(st, "pTQ", [128, D], BF16)
                pM = PS(st, "pM", [128, 512], F32)

                for kk in range(2):
                    P.op("gpsimd", lambda e, kk=kk: e.dma_start(
                        out=Win[:, 4 * kk:4 * kk + 4, :],
                        in_=w_in[512 * kk:512 * kk + 512, :].rearrange("(k p) n -> p k n", p=128)),
                        writes=["Win"], dma="d_win")
                LD(wsT[:], wsT_d, "wsT", "d_c3", eng="gpsimd")
                LD(bsbT[:], bsbT_d, "bsbT", "d_c1")
                V(lambda e: e.memset(vnpad[:], 0.0), w=["vnpad"])

                colgrp = [(0, 512, bias_q, "bias_q", 0), (512, 256, bias_kv, "bias_kv", 512),
                          (1280, 512, bias_zv, "bias_zv", 768)]
                for (c0, n, dst, dn, r0) in colgrp:
                    for k in range(8):
                        T(lambda e, k=k, c0=c0, n=n: e.matmul(pQ[0:1, 0:n], sh1T[:, k:k + 1], Win[:, k, c0:c0 + n],
                                                              start=(k == 0), stop=(k == 7)),
                          r=["Win", "sh1T"], w=["pQ"], inc=(k == 7))
                    V(lambda e, n=n, r0=r0: e.tensor_copy(tk[0:1, 0:n], pQ[0:1, 0:n]), r=["pQ"], w=["tk"])
                    T(lambda e, n=n, r0=r0: e.matmul(pZV[:, 0:n], ones_f[0:1, :], tk[0:1, 0:n],
                                                     start=True, stop=True), r=["tk", "ones_f"], w=["pZV"])
                    V(lambda e, n=n, dst=dst: e.tensor_copy(dst[:, 0:n], pZV[:, 0:n]), r=["pZV"], w=[dn])
                for c4 in range(4):
                    for k in range(8):
                        T(lambda e, k=k, c4=c4: e.matmul(pU[:, c4:c4 + 1], Win[:, k, 768 + c4 * 128:768 + (c4 + 1) * 128],
                                                         sh1T[:, k:k + 1], start=(k == 0), stop=(k == 7)),
                          r=["Win", "sh1T"], w=["pU"], inc=(k == 7 and c4 == 3))
                V(lambda e: e.tensor_copy(biasU[:], pU[:, 0:4]), r=["pU"], w=["biasU"])


                qf = SB(st, "qf", [128, 512], F32)
                zvf = SB(st, "zvf", [128, 512], F32)
                sm18 = SB(st, "sm18", [128, 18], F32)
                V(lambda e: e.memset(sm18[:], 1.0), w=["sm18"])
                print("sbuf remaining at AB:", nc.sbuf_bytes_remaining)

                def v3(a, nh):
                    return a[:, 0:nh * 64].rearrange("p (h d) -> p h d", h=nh)

                def front(t):
                    s = t % 2
                    own = t < NX
                    LD(xt[s][:], x_rot[t * 128:(t + 1) * 128, :], f"xt{s}")
                    LD(cst[s][:], cos_t[:, t, :], f"cst{s}")
                    LD(snt[s][:], sin_t[:, t, :], f"snt{s}")
                    A(lambda e: e.activation(junk[:], xt[s][:], AF.Square, accum_out=ss[:, t:t + 1]),
                      r=[f"xt{s}"], w=["junk", "ss"])
                    rsqrt_cols(rstd, ss, t, t + 1, 1.0 / D, "ss", "rstd")
                    V(lambda e: e.scalar_tensor_tensor(xn[s][:], xt[s][:], rstd[:, t:t + 1], G1_b[:], ALU.mult, ALU.mult),
                      r=[f"xt{s}", "rstd", "G1_b"], w=[f"xn{s}"])
                    for k in range(8):
                        T(lambda e, k=k: e.transpose(pT[s][:, k * 128:(k + 1) * 128], xn[s][:, k * 128:(k + 1) * 128], ident_b[:]),
                          r=[f"xn{s}", "ident_b"], w=[f"pT{s}"], inc=(k == 7))

                def front2(t):
                    s = t % 2
                    own = t < NX
                    A(lambda e: e.copy(xnT[s][:].rearrange("p k n -> p (k n)"), pT[s][:]), r=[f"pT{s}"], w=[f"xnT{s}"])
                    for k in range(8):
                        T(lambda e, k=k: e.matmul(pKV[:, 0:256], xnT[s][:, k, :], Win[:, k, 512:768],
                                                  start=(k == 0), stop=(k == 7)),
                          r=[f"xnT{s}", "Win"], w=["pKV"], inc=(k == 7))
                    if own:
                        for k in range(8):
                            T(lambda e, k=k: e.matmul(pQ[:, :], xnT[s][:, k, :], Win[:, k, 0:512],
                                                      start=(k == 0), stop=(k == 7)),
                              r=[f"xnT{s}", "Win"], w=["pQ"], inc=(k == 7))
                        for k in range(8):
                            T(lambda e, k=k: e.matmul(pZV[:, :], xnT[s][:, k, :], Win[:, k, 1280:1792],
                                                      start=(k == 0), stop=(k == 7)),
                              r=[f"xnT{s}", "Win"], w=["pZV"], inc=(k == 7))
                        for c4 in range(4):
                            for k in range(8):
                                T(lambda e, k=k, c4=c4: e.matmul(pU[:, c4 * 128:(c4 + 1) * 128],
                                                                 Win[:, k, 768 + c4 * 128:768 + (c4 + 1) * 128],
                                                                 xnT[s][:, k, :], start=(k == 0), stop=(k == 7)),
                                  r=[f"xnT{s}", "Win"], w=["pU"], inc=(k == 7 and c4 == 3))

                def mid(t):
                    own = t < NX
                    V(lambda e: e.tensor_tensor(kvf[:], pKV[:, 0:256], bias_kv[:], ALU.add), r=["pKV", "bias_kv"], w=["kvf"])
                    if own:
                        V(lambda e: e.tensor_tensor(zvf[:], pZV[:], bias_zv[:], ALU.add), r=["pZV", "bias_zv"], w=["zvf"])
                        A(lambda e: e.activation(zvf[:], zvf[:], AF.Gelu_apprx_tanh), r=["zvf"], w=["zvf"])
                        V(lambda e: e.tensor_tensor(qf[:], pQ[:], bias_q[:], ALU.add), r=["pQ", "bias_q"], w=["qf"])
                        for c4 in range(4):
                            A(lambda e, c4=c4: e.activation(uT[:, c4 * 128:(c4 + 1) * 128], pU[:, c4 * 128:(c4 + 1) * 128],
                                                            AF.Gelu_apprx_tanh, bias=biasU[:, c4:c4 + 1]),
                              r=["pU", "biasU"], w=["uT"])

                def rope2(src, sname, nh, ra, raname, rb, rbname, dst, dname, cs, csn, sn, snn):
                    V(lambda e: e.tensor_tensor(v3(ra, nh), v3(src, nh), cs[:].unsqueeze(1).to_broadcast([128, nh, 64]), ALU.mult),
                      r=[sname, csn], w=[raname])
                    v5 = lambda a: a[:, 0:nh * 64].rearrange("p (h s t d) -> p h s t d", h=nh, s=2, t=2)
                    sn4 = sn[:].rearrange("p (s t d) -> p s t d", s=2, t=2)
                    for half in range(2):
                        G(lambda e, half=half: e.tensor_tensor(
                            v5(rb)[:, :, :, half, :], v5(src)[:, :, :, 1 - half, :],
                            sn4[:, :, half, :].unsqueeze(1).to_broadcast([128, nh, 2, 16]), ALU.mult),
                          r=[sname, snn], w=[rbname])
                    V(lambda e: e.tensor_tensor(dst[:, 0:nh * 64], ra[:, 0:nh * 64], rb[:, 0:nh * 64], ALU.add),
                      r=[raname, rbname], w=[dname])

                def back_a(t):
                    s = t % 2
                    own = t < NX
                    G(lambda e: e.tensor_copy(VV[:, t, :, 64:128], kvf[:, 128:256].rearrange("p (g d) -> p g d", g=2)),
                      r=["kvf"], w=["VV"])
                    for h_ in range(2):
                        A(lambda e, h_=h_: e.activation(ksq[:, h_ * 64:(h_ + 1) * 64], kvf[:, h_ * 64:(h_ + 1) * 64], AF.Square,
                                                        accum_out=sm18[:, h_:h_ + 1]), r=["kvf"], w=["ksq", "sm18"])
                    if own:
                        G(lambda e: e.tensor_tensor(f3[:], qf[:], qf[:], ALU.mult), r=["qf"], w=["f3"])
                        V(lambda e: e.tensor_reduce(sm18[:, 2:10], v3(f3, 8), axis=AX.X, op=ALU.add), r=["f3"], w=["sm18"])
                        G(lambda e: e.tensor_tensor(f1[:], zvf[:], zvf[:], ALU.mult), r=["zvf"], w=["f1"])
                        V(lambda e: e.tensor_reduce(sm18[:, 10:18], v3(f1, 8), axis=AX.X, op=ALU.add), r=["f1"], w=["sm18"])
                    nc_ = 18 if own else 2
                    rsqrt_cols(sm18, sm18, 0, nc_, 1.0 / 64, "sm18", "sm18")

                def back_b(t):
                    s = t % 2
                    own = t < NX
                    V(lambda e: e.tensor_tensor(v3(kra, 2), v3(kvf, 2), sm18[:, 0:2].unsqueeze(2).to_broadcast([128, 2, 64]), ALU.mult),
                      r=["kvf", "sm18"], w=["kra"])
                    V(lambda e: e.tensor_tensor(v3(kra, 2), v3(kra, 2), gk_b[:].unsqueeze(1).to_broadcast([128, 2, 64]), ALU.mult),
                      r=["kra", "gk_b"], w=["kra"])
                    rope2(kra, "kra", 2, ksq, "ksq", krb, "krb", kbf, "kbf", cst[s], f"cst{s}", snt[s], f"snt{s}")
                    T(lambda e: e.transpose(pTQ[:, 512:640], kbf[:], ident_b[:]), r=["kbf", "ident_b"], w=["pTQk"])
                    kt_copy = lambda: V(lambda e: e.tensor_copy(KT[:, t * 128:(t + 1) * 128], pTQ[:, 512:640]), r=["pTQk"], w=["KT"])
                    if not own:
                        pending.append(kt_copy)
                        return
                    kt_copy()
                    V(lambda e: e.tensor_tensor(v3(f2, 8), v3(qf, 8), sm18[:, 2:10].unsqueeze(2).to_broadcast([128, 8, 64]), ALU.mult),
                      r=["qf", "sm18"], w=["f2"])
                    V(lambda e: e.tensor_tensor(v3(f2, 8), v3(f2, 8), gq_b[:].unsqueeze(1).to_broadcast([128, 8, 64]), ALU.mult),
                      r=["f2", "gq_b"], w=["f2"])
                    rope2(f2, "f2", 8, f3, "f3", f1, "f1", qbf, "qbf", cst[s], f"cst{s}", snt[s], f"snt{s}")
                    for p in range(4):
                        T(lambda e, p=p: e.transpose(pTQ[:, p * 128:(p + 1) * 128], qbf[:, p * 128:(p + 1) * 128], ident_b[:]),
                          r=["qbf", "ident_b"], w=["pTQ"], inc=(p == 3))
                    V(lambda e: e.tensor_tensor(v3(f2, 8), v3(zvf, 8), sm18[:, 10:18].unsqueeze(2).to_broadcast([128, 8, 64]), ALU.mult),
                      r=["zvf", "sm18"], w=["f2"])
                    for sl in range(2):
                        G(lambda e, sl=sl: e.tensor_tensor(
                            vnpad[:].rearrange("p (c s) n -> p c s n", s=2)[:, :, sl, sl * 64:(sl + 1) * 64],
                            f2[:].rearrange("p (c s d) -> p c s d", s=2, d=64)[:, :, sl, :],
                            gtv_b[:].unsqueeze(1).to_broadcast([128, 4, 64]), ALU.mult),
                          r=["f2", "gtv_b"], w=["vnpad"])
                    for c4 in range(4):
                        for sl in range(2):
                            T(lambda e, c4=c4, sl=sl: e.matmul(pM[:, c4 * 128:(c4 + 1) * 128], vnpad[:, 2 * c4 + sl, :],
                                                               wsT[:, 2 * c4 + sl, :], start=(sl == 0), stop=(sl == 1)),
                              r=["vnpad", "wsT"], w=["pM"], inc=(sl == 1 and c4 == 3))
                    V(lambda e: e.tensor_copy(QT[:, :, t * 128:(t + 1) * 128], pTQ[:, 0:512].rearrange("p (a n) -> p a n", a=4)),
                      r=["pTQ"], w=["QT"])
                    V(lambda e: e.tensor_tensor(tk[:], pM[:], bsbT[:], ALU.add), r=["pM", "bsbT"], w=["tk"])
                    V(lambda e: e.tensor_tensor(tk[:], tk[:], uT[:], ALU.mult), r=["tk", "uT"], w=["tk"])
                    G(lambda e: e.tensor_tensor(sqt[:], tk[:], tk[:], ALU.mult), r=["tk"], w=["sqt"])
                    for c4 in range(4):
                        T(lambda e, c4=c4: e.matmul(pM[:, 0:1] if False else pTS[:, 0:1], sqt[:, c4 * 128:(c4 + 1) * 128], ones_f[:, 0:1],
                                                    start=(c4 == 0), stop=(c4 == 3)),
                          r=["sqt", "ones_f"], w=["pTS"], inc=(c4 == 3))
                    V(lambda e: e.tensor_tensor(mixT[:, 4:8, t * 128:(t + 1) * 128],
                                                tk[:].rearrange("p (c n) -> p c n", c=4),
                                                gmix[:, 4:8].unsqueeze(2).to_broadcast([128, 4, 128]), ALU.mult),
                      r=["tk", "gmix"], w=["mixT"])
                    V(lambda e: e.tensor_copy(sstok[:, t:t + 1], pTS[:, 0:1]), r=["pTS"], w=["sstok"])

                pTS = pKV[:, 256:512]
                pending = []
                front(0)
                front2(0)
                mid(0)
                for t in range(AB_TILES):
                    if t + 1 < AB_TILES:
                        front(t + 1)
                    back_a(t)
                    if t + 1 < AB_TILES:
                        front2(t + 1)
                    back_b(t)
                    if t + 1 < AB_TILES:
                        mid(t + 1)
                    for f_ in pending:
                        f_()
                    pending.clear()

                rsqrt_cols(rs_t, sstok, 0, NX, 1.0 / 512, "sstok", "rs_t")
                P.barrier()
                stage_end(2, lambda: [(dbgb[:, 0:8192], KT, "KT"), (dbgb[:, 8192:17408], QT[:].rearrange("p a n -> p (a n)"), "QT"), (dbgb[:, 17408:26624], mixT[:, 4:8, :].rearrange("p a n -> p (a n)"), "mixT"), (dbgb[:, 26624:43136], VVf, "VV"), (dbgf[:, 0:18], rs_t[:, 0:18], "rs_t"), (dbgf[:, 64:128], rstd[:, :], "rstd"), (dbgf[:, 128:132], biasU[:], "biasU"), (dbgf[:, 1024:1536], bias_q[:], "bias_q"), (dbgf[:, 1536:1792], bias_kv[:], "bias_kv"), (dbgf[:, 2048:2560], bias_zv[:], "bias_zv")])

            with ExitStack() as st:
                Pb = [SB(st, f"Pb{i}", [128, 1024], BF16) for i in range(3)]
                Oa = SB(st, "Oa", [128, 512], F32)
                Ob = SB(st, "Ob", [128, 512], F32)
                rden = SB(st, "rden", [128, 512], F32)
                at = SB(st, "at", [128, 512], F32)
                sq = SB(st, "sq", [128, 512], F32)
                acc = SB(st, "acc", [128, 512], F32)
                pS = [PS(st, f"pS{i}", [128, 1024], F32) for i in range(2)]
                pOa = PS(st, "pOa", [128, 512], F32)
                pOb = PS(st, "pOb", [128, 512], F32)
                pBc = PS(st, "pBc", [128, 512], F32)
                pSA = PS(st, "pSA", [128, 512], F32)
                for ct in range(NCT):
                    for ab in range(2):
                        P.op("gpsimd", lambda e: e.dma_start(
                            out=wub[ct].rearrange("p (k a n) -> p k a n", k=8, a=2)[:, :, ab, :],
                            in_=w_up[:, ab * DFF + ct * 128:ab * DFF + (ct + 1) * 128].rearrange("(k p) n -> p k n", p=128)),
                            writes=[f"wub{ct}_{ab}"], dma="d_wub")
                QH = SB(st, "QH", [128, 4, 2], BF16)
                hs = SB(st, "hs", [128, 2], F32)

                def qk(kt, i, p, q0, qn):
                    b = pS[i % 2]
                    T(lambda e: e.matmul(b[:, 0:qn], KT[0:64, kt * 128:(kt + 1) * 128], QT[0:64, p, q0:q0 + qn],
                                         start=True, stop=True), r=["KT", "QT"], w=[f"pS{i % 2}"], inc=False)
                    T(lambda e: e.matmul(b[:, 512:512 + qn], KT[64:128, kt * 128:(kt + 1) * 128],
                                         QT[64:128, p, q0:q0 + qn], start=True, stop=True),
                      r=["KT", "QT"], w=[f"pS{i % 2}"])

                def ex(kt, i):
                    A(lambda e: e.activation(Pb[i % 3][:], pS[i % 2][:], AF.Exp, scale=0.125),
                      r=[f"pS{i % 2}"], w=[f"Pb{i % 3}"])

                def pv(kt, i, qn):
                    pb = Pb[i % 3]
                    T(lambda e: e.matmul(pOa[:, 0:qn], VVk[:, kt, 64:192], pb[:, 0:qn],
                                         start=(kt == 0), stop=(kt == NT - 1)),
                      r=[f"Pb{i % 3}", "VV"], w=["pOa"], inc=False)
                    T(lambda e: e.matmul(pOb[:, 0:qn], VV[:, kt, 1, 0:128], pb[:, 512:512 + qn],
                                         start=(kt == 0), stop=(kt == NT - 1)),
                      r=[f"Pb{i % 3}", "VV"], w=["pOa", "pOb"])

                def epi_a(qn):
                    V(lambda e: e.tensor_copy(Oa[0:65, 0:qn], pOa[0:65, 0:qn]), r=["pOa"], w=["Oa"])
                    V(lambda e: e.tensor_copy(Ob[:, 0:qn], pOb[:, 0:qn]), r=["pOb"], w=["Ob"])
                    V(lambda e: e.reciprocal(rden[64:65, 0:qn], Oa[64:65, 0:qn]), r=["Oa"], w=["rdenA"])
                    V(lambda e: e.reciprocal(rden[0:1, 0:qn], Ob[0:1, 0:qn]), r=["Ob"], w=["rdenB"])

                def epi_b(qn):
                    T(lambda e: e.matmul(pBc[:, 0:qn], ones_f[64:65, :], rden[64:65, 0:qn], start=True, stop=True),
                      r=["rdenA", "ones_f"], w=["pBc"], inc=False)
                    T(lambda e: e.matmul(pSA[:, 0:qn], ones_f[0:1, :], rden[0:1, 0:qn], start=True, stop=True),
                      r=["rdenB", "ones_f"], w=["pBc", "pSA"])
                    V(lambda e: e.tensor_tensor(at[0:64, 0:qn], Oa[0:64, 0:qn], pBc[0:64, 0:qn], ALU.mult),
                      r=["Oa", "pBc"], w=["at"])
                    V(lambda e: e.tensor_tensor(at[64:128, 0:qn], Ob[64:128, 0:qn], pSA[64:128, 0:qn], ALU.mult),
                      r=["Ob", "pSA"], w=["at"])

                def epi_main(qi, p, q0):
                    epi_b(512)
                    V(lambda e: e.tensor_scalar(mixT[:, p, q0:q0 + 512], at[:, :], gmix[:, p:p + 1], None, ALU.mult),
                      r=["at", "gmix"], w=["mixT"])
                    if p == 0:
                        G(lambda e: e.tensor_tensor(acc[:, :], at[:, :], at[:, :], ALU.mult), r=["at"], w=["acc"])
                    else:
                        G(lambda e: e.tensor_tensor(sq[:, :], at[:, :], at[:, :], ALU.mult), r=["at"], w=["sq"])
                        G(lambda e: e.tensor_tensor(acc[:, :], acc[:, :], sq[:, :], ALU.add), r=["acc", "sq"], w=["acc"])

                def epi_ss(qi):
                    for w_ in range(4):
                        T(lambda e, w_=w_: e.matmul(pSA[:, w_:w_ + 1], acc[:, w_ * 128:(w_ + 1) * 128], ones_f[:, 0:1],
                                                    start=True, stop=True), r=["acc", "ones_f"], w=["pSA"], inc=(w_ == 3))
                    V(lambda e: e.tensor_copy(ssatt[:, 1 + qi * 4:1 + qi * 4 + 4], pSA[:, 0:4]), r=["pSA"], w=["ssatt"])

                V(lambda e: e.tensor_copy(QH[:, :, 0:1], QT[:, :, 127:128]), r=["QT"], w=["QH"])
                V(lambda e: e.tensor_copy(QH[:, :, 1:2], QT[:, :, 2176:2177]), r=["QT"], w=["QH"])
                for g in range(2):
                    for kt in range(NT):
                        T(lambda e: e.matmul(pS[0][:, g * 512 + kt * 8:g * 512 + kt * 8 + 8],
                                             KT[g * 64:(g + 1) * 64, kt * 128:(kt + 1) * 128], QH[g * 64:(g + 1) * 64, :, :],
                                             start=True, stop=True), r=["KT", "QH"], w=["pS0"], inc=(kt == NT - 1))
                ex(0, 0)
                for kt in range(NT):
                    T(lambda e: e.matmul(pOa[0:65, 0:8], VV[:, kt, 0, 64:129], Pb[0][:, kt * 8:kt * 8 + 8],
                                         start=(kt == 0), stop=(kt == NT - 1)), r=["Pb0", "VV"], w=["pOa"], inc=(kt == NT - 1))
                for kt in range(NT):
                    T(lambda e: e.matmul(pOb[:, 0:8], VV[:, kt, 1, 0:128], Pb[0][:, 512 + kt * 8:512 + kt * 8 + 8],
                                         start=(kt == 0), stop=(kt == NT - 1)), r=["Pb0", "VV"], w=["pOb"], inc=(kt == NT - 1))
                epi_a(8)
                epi_b(8)
                at3 = at[:, 0:8].rearrange("p (a t) -> p a t", t=2)
                for tk_, col in ((0, 127), (1, 2176)):
                    V(lambda e: e.tensor_tensor(mixT[:, 0:4, col:col + 1], at3[:, :, tk_:tk_ + 1], gmix[:, 0:4].unsqueeze(2), ALU.mult),
                      r=["at", "gmix"], w=["mixT"])
                V(lambda e: e.tensor_tensor(sq[:, 0:8], at[:, 0:8], at[:, 0:8], ALU.mult), r=["at"], w=["sq"])
                V(lambda e: e.tensor_reduce(hs[:, 0:2], sq[:, 0:8].rearrange("p (a t) -> p t a", t=2), axis=AX.X, op=ALU.add),
                  r=["sq"], w=["hs"])
                V(lambda e: e.memset(acc[:, 0:256], 0.0), w=["acc"])
                V(lambda e: e.tensor_copy(acc[:, 127:129], hs[:, 0:2]), r=["hs"], w=["acc"])
                for w_ in range(2):
                    T(lambda e, w_=w_: e.matmul(pSA[:, w_:w_ + 1], acc[:, w_ * 128:(w_ + 1) * 128], ones_f[:, 0:1],
                                                start=True, stop=True), r=["acc", "ones_f"], w=["pSA"], inc=(w_ == 1))
                V(lambda e: e.tensor_copy(ssatt[:, 0:1], pSA[:, 0:1]), r=["pSA"], w=["ssatt"])
                V(lambda e: e.tensor_copy(ssatt[:, 17:18], pSA[:, 1:2]), r=["pSA"], w=["ssatt"])

                iters = [(qi, p) for qi in range(4) for p in range(4)]
                step = 1
                pend = None
                for (qi, p) in iters:
                    q0 = 128 + qi * 512
                    qk(0, step, p, q0, 512)
                    qk(1, step + 1, p, q0, 512)
                    for kt in range(NT):
                        ex(kt, step + kt)
                        if kt + 2 < NT:
                            qk(kt + 2, step + kt + 2, p, q0, 512)
                        pv(kt, step + kt, 512)
                        if kt == 3 and pend is not None:
                            epi_main(*pend)
                        if kt == 16 and pend is not None:
                            if pend[1] == 3:
                                epi_ss(pend[0])
                            pend = None
                    step += NT
                    epi_a(512)
                    pend = (qi, p, q0)
                epi_main(*pend)
                epi_ss(3)

                rsqrt_cols(rs_a, ssatt, 0, NX, 1.0 / 512, "ssatt", "rs_a")
                P.barrier()
                stage_end(3, lambda: [(dbgb[:, 0:9216], mixT[:, 0:4, :].rearrange("p a n -> p (a n)"), "mixT"), (dbgf[:, 0:18], rs_a[:, 0:18], "rs_a")])

            x1nT = AR1[:, 0:8 * NCOLS].rearrange("p (k n) -> p k n", k=8)
            with ExitStack() as st2:
                G2_b = SB(st2, "G2_b", [128, D], F32)
                Wout = SB(st2, "Wout", [128, 8, D], BF16)
                xe = [SB(st2, f"xe{i}", [128, D], F32) for i in range(2)]
                x1h = [SB(st2, f"x1h{i}", [128, D], F32) for i in range(2)]
                t1 = SB(st2, "t1", [128, D], F32)
                x1n = [SB(st2, f"x1n{i}", [128, D], BF16) for i in range(2)]
                junk2 = SB(st2, "junk2", [128, D], BF16)
                pYa = PS(st2, "pYa", [128, D], F32)
                pYb = PS(st2, "pYb", [128, D], F32)
                pT2 = [PS(st2, f"pT2{i}", [128, D], BF16) for i in range(2)]
                LD(G2_b[:], scr[0], "G2_b", "d_c2")
                for kk in range(2):
                    P.op("gpsimd", lambda e, kk=kk: e.dma_start(
                        out=Wout[:, 4 * kk:4 * kk + 4, :],
                        in_=w_out[512 * kk:512 * kk + 512, :].rearrange("(k p) n -> p k n", p=128)),
                        writes=["Wout"], dma="d_wout")
                def out_mm(t):
                    s = t % 2
                    LD(xe[s][:], x_rot[t * 128:(t + 1) * 128, :], f"xe{s}")
                    for c2 in range(2):
                        for k in range(4):
                            T(lambda e, k=k, c2=c2: e.matmul(pYa[:, c2 * 512:(c2 + 1) * 512], mixT[:, k, t * 128:(t + 1) * 128],
                                                             Wout[:, k, c2 * 512:(c2 + 1) * 512], start=(k == 0), stop=(k == 3)),
                              r=["mixT", "Wout"], w=["pYa"], inc=(k == 3 and c2 == 1))
                    for c2 in range(2):
                        for k in range(4, 8):
                            T(lambda e, k=k, c2=c2: e.matmul(pYb[:, c2 * 512:(c2 + 1) * 512], mixT[:, k, t * 128:(t + 1) * 128],
                                                             Wout[:, k, c2 * 512:(c2 + 1) * 512], start=(k == 4), stop=(k == 7)),
                              r=["mixT", "Wout"], w=["pYb"], inc=(k == 7 and c2 == 1))

                def out_ep1(t):
                    s = t % 2
                    xd = x1h[s][:]
                    xdn = f"x1h{s}"
                    A(lambda e: e.activation(t1[:], pYa[:], AF.Identity, scale=rs_a[:, t:t + 1]), r=["pYa", "rs_a"], w=["t1"])
                    V(lambda e: e.scalar_tensor_tensor(t1[:], pYb[:], rs_t[:, t:t + 1], t1[:], ALU.mult, ALU.add),
                      r=["pYb", "rs_t", "t1"], w=["t1"])

                def out_ep2(t):
                    s = t % 2
                    xd = x1h[s][:]
                    xdn = f"x1h{s}"
                    V(lambda e: e.tensor_tensor(t1[:], t1[:], gt1_b[:], ALU.mult), r=["t1", "gt1_b"], w=["t1"])
                    V(lambda e: e.tensor_tensor(xd, t1[:], xe[s][:], ALU.add), r=["t1", f"xe{s}"], w=[xdn])
                    if 1 <= t <= 16:
                        P.op("sync", lambda e: e.dma_start(out=x1s[(t - 1) * 128:t * 128, :], in_=xd), reads=[xdn], dma=f"d_x1s{s}")
                    A(lambda e: e.activation(junk2[:], xd, AF.Square, accum_out=ss2[:, t:t + 1]), r=[xdn], w=["junk2", "ss2"])
                    rsqrt_cols(rs2, ss2, t, t + 1, 1.0 / D, "ss2", "rs2")
                    V(lambda e: e.scalar_tensor_tensor(x1n[s][:], xd, rs2[:, t:t + 1], G2_b[:], ALU.mult, ALU.mult),
                      r=[xdn, "rs2", "G2_b"], w=[f"x1n{s}"])
                    for k in range(8):
                        T(lambda e, k=k: e.transpose(pT2[s][:, k * 128:(k + 1) * 128], x1n[s][:, k * 128:(k + 1) * 128], ident_b[:]),
                          r=[f"x1n{s}", "ident_b"], w=[f"pT2{s}"], inc=(k == 7))
                    V(lambda e: e.tensor_copy(x1nT[:, :, t * 128:(t + 1) * 128], pT2[s][:].rearrange("p (k n) -> p k n", k=8)),
                      r=[f"pT2{s}"], w=["x1nT"])

                out_mm(0)
                for t in range(NX):
                    out_ep1(t)
                    if t + 1 < NX:
                        out_mm(t + 1)
                    out_ep2(t)
                P.barrier()
                stage_end(4, lambda: [(dbgb[:, 0:18432], AR1[:, 0:18432], "x1nT")])
                print("sbuf remaining at OUT:", nc.sbuf_bytes_remaining)

            stA.close()
            with ExitStack() as st2:
                NWU = 2
                GW = 2
                groups = [(c, min(GW, NCT - c)) for c in range(0, NCT, GW)]
                NG = len(groups)
                gt2_b = SB(st2, "gt2_b", [128, D], F32)
                gfin_b = SB(st2, "gfin_b", [128, D], F32)
                gT = SB(st2, "gT", [128, NCT, 512], BF16)
                tail0 = 8 * NCOLS
                wu = [AR1[:, tail0:tail0 + 8 * 2 * GW * 128].rearrange("p (c f) -> p c f", c=GW),
                      SB(st2, "wu1", [128, GW, 8 * 2 * 128], BF16)]
                Wd = SB(st2, "Wd", [128, NCT, D], BF16)
                zb = [SB(st2, f"zb{i}", [128, 2, 514], F32) for i in range(2)]
                cv = [SB(st2, f"cv{i}", [128, 2, 512], F32) for i in range(2)]
                sl_ = [SB(st2, f"sl{i}", [128, 512], F32) for i in range(2)]
                y2q = SB(st2, "y2q", [128, 4, D], F32)
                xr = [SB(st2, f"xr{i}", [128, D], F32) for i in range(2)]
                junk3 = AR1[:, tail0 + 8 * 2 * GW * 128:tail0 + 8 * 2 * GW * 128 + D]
                ot = [SB(st2, f"ot{i}", [128, D], F32) for i in range(2)]
                pZ = [PS(st2, f"pZ{i}", [128, 512], F32) for i in range(4)]
                pY = [PS(st2, f"pY{i}", [128, 512], F32) for i in range(4)]
                LD(gt2_b[:], scr[1], "gt2_b")
                LD(gfin_b[:], gfin_bd, "gfin_b")
                print("sbuf remaining at FFN:", nc.sbuf_bytes_remaining)

                def ld_wu(gg):
                    c0_, n_ = groups[gg % NG]
                    s = gg % NWU
                    P.op("sync", lambda e: e.dma_start(out=wu[s][:, 0:n_, :], in_=wub[c0_:c0_ + n_].rearrange("c p f -> p c f")),
                         writes=[f"wu{s}"], dma=f"d_wu{s}")

                def wsl(qtr, ct, k, ab):
                    gi = ct // GW
                    s = (qtr * NG + gi) % NWU
                    j = ct - groups[gi][0]
                    return wu[s][:, j, :].rearrange("p (k a n) -> p k a n", k=8, a=2)[:, k, ab, :], f"wu{s}"

                def ld_wd(ct):
                    P.op("gpsimd", lambda e: e.dma_start(out=Wd[:, ct, :], in_=w_down[ct * 128:(ct + 1) * 128, :]),
                         writes=[f"Wd{ct}"], dma=f"d_Wd{ct}")

                def up_tail(qtr, ct):
                    ci = ct % 2
                    A(lambda e: e.activation(sl_[ci][:], cv[ci][:, 0, :], AF.Silu), r=[f"cv{ci}a"], w=[f"sl{ci}"])
                    V(lambda e: e.tensor_tensor(gT[:, ct, :], sl_[ci][:], cv[ci][:, 1, :], ALU.mult),
                      r=[f"sl{ci}", f"cv{ci}b"], w=["gT"])

                def down_mm(ct, c2):
                    for tt in range(4):
                        T(lambda e, tt=tt: e.matmul(pY[tt][:, :], gT[:, ct, tt * 128:(tt + 1) * 128],
                                                    Wd[:, ct, c2 * 512:(c2 + 1) * 512],
                                                    start=(ct == 0), stop=(ct == NCT - 1)),
                          r=["gT", f"Wd{ct}"], w=[f"pY{tt}"], inc=(ct == NCT - 1 or tt == 3))

                def bias2_for(ct):
                    pb = pY[ct % 4]
                    for ab in range(2):
                        for k in range(8):
                            wl, wn = wsl(0, ct, k, ab)
                            T(lambda e, k=k, ab=ab, wl=wl: e.matmul(pb[:, ab:ab + 1], wl, sh2T[:, k:k + 1], start=(k == 0), stop=(k == 7)),
                              r=[wn, "sh2T"], w=[f"pY{ct % 4}"], inc=(k == 7 and ab == 1))
                    A(lambda e: e.copy(bias2[:, ct:ct + 1], pb[:, 0:1]), r=[f"pY{ct % 4}"], w=["bias2"])
                    A(lambda e: e.copy(bias2[:, 22 + ct:23 + ct], pb[:, 1:2]), r=[f"pY{ct % 4}"], w=["bias2"])

                def fin_tail(qtr, tt):
                    ti = qtr * 4 + tt
                    os_ = ti % 2
                    V(lambda e: e.tensor_tensor(y2q[:, tt, :], y2q[:, tt, :], xr[tt % 2][:], ALU.add),
                      r=[f"y2q{tt}", f"xr{tt % 2}"], w=[f"y2q{tt}"])
                    if tt < 2:
                        LD(xr[tt % 2][:], x1s[(ti + 2) * 128:(ti + 3) * 128, :], f"xr{tt % 2}")
                    A(lambda e: e.activation(junk3, y2q[:, tt, :], AF.Square, accum_out=ssf[:, ti:ti + 1]),
                      r=[f"y2q{tt}"], w=["junk3", "ssf"])
                    rsqrt_cols(rsf, ssf, ti, ti + 1, 1.0 / D, "ssf", "rsf")
                    V(lambda e: e.scalar_tensor_tensor(ot[os_][:], y2q[:, tt, :], rsf[:, ti:ti + 1], gfin_b[:],
                                                       ALU.mult, ALU.mult),
                      r=[f"y2q{tt}", "rsf", "gfin_b"], w=[f"ot{os_}"])
                    P.op("sync", lambda e: e.dma_start(out=out_d[ti * 128:(ti + 1) * 128, :], in_=ot[os_][:]),
                         reads=[f"ot{os_}"], dma=f"d_out{os_}")

                ld_wu(0)
                for qtr in range(4):
                    c0 = 127 + qtr * 512
                    for ct in range(NCT):
                        zi = ct % 2
                        ci = ct % 2
                        if ct % GW == 0:
                            gg = qtr * NG + ct // GW
                            if gg + 1 < 4 * NG:
                                ld_wu(gg + 1)
                        if qtr == 0 and ct == 0:
                            bias2_for(0)
                        for ab in range(2):
                            for blk in range(2):
                                pz = pZ[ab * 2 + blk]
                                for k in range(8):
                                    wl, wn = wsl(qtr, ct, k, ab)
                                    T(lambda e, k=k, wl=wl: e.matmul(
                                        pz[:, 0:257], wl,
                                        x1nT[:, k, c0 + blk * 257:c0 + (blk + 1) * 257], start=(k == 0), stop=(k == 7)),
                                      r=[wn, "x1nT"], w=[f"pZ{ab * 2 + blk}"], inc=(k == 7))
                                A(lambda e: e.activation(
                                    zb[zi][:, ab, blk * 257:(blk + 1) * 257], pz[:, 0:257], AF.Identity,
                                    bias=bias2[:, ab * 22 + ct:ab * 22 + ct + 1]),
                                  r=[f"pZ{ab * 2 + blk}", "bias2"], w=[f"zb{zi}"])
                        if qtr >= 1 and ct >= 2:
                            down_mm(ct - 2, 0)
                        if qtr == 0 and ct + 1 < NCT:
                            bias2_for(ct + 1)
                        if qtr == 0:
                            ld_wd(ct)
                        if qtr > 0 and ct < 4:
                            fin_tail(qtr - 1, ct)
                        if ct == 4:
                            for tt in range(2):
                                LD(xr[tt][:], x1s[(qtr * 4 + tt) * 128:(qtr * 4 + tt + 1) * 128, :], f"xr{tt}")
                        if qtr == 0:
                            V(lambda e: e.tensor_scalar(zb[zi][:, :, 0:1], zb[zi][:, :, 0:1], hm[:, 0:1], None, ALU.mult),
                              r=[f"zb{zi}", "hm"], w=[f"zb{zi}"])
                        if qtr == 3:
                            V(lambda e: e.tensor_scalar(zb[zi][:, :, 513:514], zb[zi][:, :, 513:514], hm[:, 1:2], None, ALU.mult),
                              r=[f"zb{zi}", "hm"], w=[f"zb{zi}"])
                        for ab in range(2):
                            ch = ab * 22 + ct
                            if ab == 0:
                                A(lambda e: e.activation(cv[ci][:, ab, :], zb[zi][:, ab, 1:513], AF.Identity,
                                                         bias=bconv[:, ch:ch + 1], scale=wconv[:, 1, ch:ch + 1]),
                                  r=[f"zb{zi}", "wconv", "bconv"], w=[f"cv{ci}a"])
                            else:
                                G(lambda e: e.tensor_scalar(cv[ci][:, ab, :], zb[zi][:, ab, 1:513], wconv[:, 1, ch:ch + 1],
                                                            bconv[:, ch:ch + 1], ALU.mult, ALU.add),
                                  r=[f"zb{zi}", "wconv", "bconv"], w=[f"cv{ci}b"])
                        for ab in range(2):
                            ch = ab * 22 + ct
                            cvn = f"cv{ci}" + "ab"[ab]
                            V(lambda e: e.scalar_tensor_tensor(cv[ci][:, ab, :], zb[zi][:, ab, 0:512], wconv[:, 0, ch:ch + 1],
                                                               cv[ci][:, ab, :], ALU.mult, ALU.add),
                              r=[f"zb{zi}", "wconv", cvn], w=[cvn])
                            V(lambda e: e.scalar_tensor_tensor(cv[ci][:, ab, :], zb[zi][:, ab, 2:514], wconv[:, 2, ch:ch + 1],
                                                               cv[ci][:, ab, :], ALU.mult, ALU.add),
                              r=[f"zb{zi}", "wconv", cvn], w=[cvn])
                        if ct > 0:
                            up_tail(qtr, ct - 1)
                    up_tail(qtr, NCT - 1)
                    if qtr >= 1:
                        down_mm(NCT - 2, 0)
                        down_mm(NCT - 1, 0)
                    for c2 in range(2):
                        if not (qtr >= 1 and c2 == 0):
                            for ct in range(NCT):
                                down_mm(ct, c2)
                        for tt in range(4):
                            V(lambda e, tt=tt: e.tensor_tensor(y2q[:, tt, c2 * 512:(c2 + 1) * 512], pY[tt][:, :],
                                                               gt2_b[:, c2 * 512:(c2 + 1) * 512], ALU.mult),
                              r=[f"pY{tt}", "gt2_b"], w=[f"y2q{tt}"])
                    if qtr == 3:
                        for tt in range(4):
                            fin_tail(3, tt)
                P.barrier()
        print("kernel build: inst", P.ninst, "waits", P.nwait, "sems", len(P.sem), P.cnt)
    except _Stop:
        pass
    return nc


def _prep_inputs(inp):
    x = np.asarray(inp["x"], np.float32)
    c = np.asarray(inp["c"], np.float32)
    f = lambda k: np.asarray(inp[k], np.float32)
    rows = S // 64
    row_id = np.repeat(np.arange(rows), 64).astype(np.float32)
    col_id = np.tile(np.arange(64), rows).astype(np.float32)
    inv_freq = np.power(np.float32(10000.0), -np.arange(0, 32, 2, dtype=np.float32) / np.float32(32)).astype(np.float32)
    ang_r = row_id[:, None] * inv_freq[None, :]
    ang_c = col_id[:, None] * inv_freq[None, :]
    ang = np.concatenate([ang_r, ang_r, ang_c, ang_c], axis=-1).astype(np.float32)
    cos = np.cos(ang).astype(np.float32)
    sin = np.sin(ang).astype(np.float32)
    sgn = np.concatenate([-np.ones(16), np.ones(16), -np.ones(16), np.ones(16)]).astype(np.float32)
    sinS = sin * sgn[None, :]
    perm_h = [0, 4, 1, 5, 2, 6, 3, 7]
    qcols = np.concatenate([np.arange(h * 64, (h + 1) * 64) for h in perm_h])
    w_in = f("w_in")[0].copy()
    w_in[:, 0:512] = w_in[:, qcols]
    w_out = f("w_out")[0].copy()
    w_out[0:512, :] = w_out[qcols, :]
    g_attn = f("g_attn_out")[0][qcols]
    gmixv = np.concatenate([g_attn, f("g_tok_out")[0]])
    gmix = np.ascontiguousarray(gmixv.reshape(8, 128).T)
    rep = lambda v, n=128: np.ascontiguousarray(np.broadcast_to(v[None, :], (n, v.shape[0])))
    wsT = np.ascontiguousarray(f("w_s")[0].transpose(2, 0, 1))
    b_s = f("b_s")[0]
    bsbT = np.zeros((128, 4, 128), np.float32)
    for c4 in range(4):
        bsbT[0:64, c4, :] = b_s[2 * c4][None, :]
        bsbT[64:128, c4, :] = b_s[2 * c4 + 1][None, :]
    wconv = np.ascontiguousarray(f("w_conv")[0].reshape(3, 44, 128).transpose(2, 0, 1))
    bconv = np.ascontiguousarray(f("b_conv")[0].reshape(44, 128).T)
    shared = {
        "w_ada": f("w_ada")[0], "b_ada": f("b_ada"), "g1_b": rep(f("g_norm1")[0]), "g2_b": rep(f("g_norm2")[0]),
        "gfin_b": rep(f("g_final")), "w_in": w_in, "w_out": w_out, "w_up": f("w_up")[0], "w_down": f("w_down")[0],
        "gq_b": rep(f("g_q")[0]), "gk_b": rep(f("g_k")[0]), "gtv_b": rep(f("g_tok_v")[0]), "wsT": wsT,
        "bsbT": bsbT.reshape(128, 512), "gmix": gmix, "wconv": wconv, "bconv": bconv,
        "ident": np.eye(128, dtype=np.float32),
    }
    in_maps = []
    for core in range(8):
        b, j = core // 4, core % 4
        sh = j * 2048 - 128
        xr = np.roll(x[b], -sh, axis=0)
        cr = np.roll(cos, -sh, axis=0).reshape(NT, 128, 64).transpose(1, 0, 2)
        sr = np.roll(sinS, -sh, axis=0).reshape(NT, 128, 64).transpose(1, 0, 2)
        hmk = np.ones((128, 2), np.float32)
        if j == 0:
            hmk[:, 0] = 0.0
        if j == 3:
            hmk[:, 1] = 0.0
        m = dict(shared)
        m.update({"x_rot": np.ascontiguousarray(xr), "cos_t": np.ascontiguousarray(cr), "sin_t": np.ascontiguousarray(sr),
                  "cb": np.ascontiguousarray(c[b].reshape(8, 128).T), "hmask": hmk})
        in_maps.append(m)
    return in_maps


_NC_CACHE = {}


def kernel(**inputs):
    in_maps = _prep_inputs(inputs)
    if "nc" not in _NC_CACHE:
        _NC_CACHE["nc"] = build_nc()
    nc = _NC_CACHE["nc"]
    res = run_bass_kernel_spmd(nc, in_maps, core_ids=list(range(8)))
    out = np.zeros((2, S, D), np.float32)
    for core in range(8):
        b, j = core // 4, core % 4
        out[b, j * 2048:(j + 1) * 2048, :] = res.results[core]["out"]
    return out
```
